# Optimizing a Trainium2 kernel written in Bass

```python
import jax, jax.numpy as jnp
from jax import lax
import numpy as np

D_MODEL = 1024
BATCH = 8
SEQ = 4096
DEPTH = 1

MLA_HEADS = 8
MLA_NOPE_DIM = 64
MLA_ROPE_DIM = 32
MLA_V_DIM = 64
MLA_Q_RANK = 384
MLA_KV_RANK = 256
MLA_QK_DIM = MLA_NOPE_DIM + MLA_ROPE_DIM
RET_HEADS = 4
RET_QK_DIM = 64
RET_V_DIM = 128
CHUNK = 128
Q_BLOCK = 128

MLA_WIDTH = MLA_HEADS * MLA_V_DIM
RET_WIDTH = RET_HEADS * RET_V_DIM
D_MIX = MLA_WIDTH + RET_WIDTH
IN_SPLITS = (MLA_Q_RANK, MLA_KV_RANK, MLA_ROPE_DIM,
             RET_HEADS * RET_QK_DIM, RET_HEADS * RET_QK_DIM, RET_WIDTH, RET_WIDTH)
IN_COLS = sum(IN_SPLITS)
D_FF = ((8 * D_MODEL + 3 * 256 - 1) // (3 * 256)) * 256
ROPE_BASE = 10000.0
EPS = 1e-6

kernel_name = "hybrid_mla_retention_sandwich_adaln"


def rmsnorm(x, g):
    x32 = x.astype(jnp.float32)
    r = x32 * lax.rsqrt(jnp.mean(x32 * x32, axis=-1, keepdims=True) + EPS)
    return (r * g.astype(jnp.float32)).astype(x.dtype)


def rope(x, pos):
    d = x.shape[-1]
    inv = ROPE_BASE ** (-jnp.arange(0, d, 2, dtype=jnp.float32) / d)
    ang = pos.astype(jnp.float32)[:, :, None, None] * inv
    cos, sin = jnp.cos(ang), jnp.sin(ang)
    x1, x2 = jnp.split(x.astype(jnp.float32), 2, axis=-1)
    return jnp.concatenate([x1 * cos - x2 * sin, x1 * sin + x2 * cos], axis=-1).astype(x.dtype)


def mla_group(cq_raw, ckv_raw, kpe_raw, pos, q_a_norm, w_q_b, kv_a_norm, w_kv_b):
    B, S, _ = cq_raw.shape
    cq = rmsnorm(cq_raw, q_a_norm)
    q = (cq @ w_q_b).reshape(B, S, MLA_HEADS, MLA_QK_DIM)
    q_nope, q_pe = q[..., :MLA_NOPE_DIM], rope(q[..., MLA_NOPE_DIM:], pos)
    ckv = rmsnorm(ckv_raw, kv_a_norm)
    kv = (ckv @ w_kv_b).reshape(B, S, MLA_HEADS, MLA_NOPE_DIM + MLA_V_DIM)
    k_nope, v = kv[..., :MLA_NOPE_DIM], kv[..., MLA_NOPE_DIM:]
    k_pe = rope(kpe_raw[:, :, None, :], pos)
    q = jnp.concatenate([q_nope, q_pe], axis=-1)
    k = jnp.concatenate([k_nope, jnp.broadcast_to(k_pe, (B, S, MLA_HEADS, MLA_ROPE_DIM))], axis=-1)
    scale = MLA_QK_DIM ** -0.5
    nb = S // Q_BLOCK
    qb = q.reshape(B, nb, Q_BLOCK, MLA_HEADS, MLA_QK_DIM).transpose(1, 0, 3, 2, 4)
    kt = k.transpose(0, 2, 1, 3)
    vt = v.transpose(0, 2, 1, 3)
    k_idx = jnp.arange(S)

    def block(args):
        q_blk, start = args
        s = jnp.einsum('bhqd,bhkd->bhqk', q_blk, kt, preferred_element_type=jnp.float32) * scale
        q_idx = start + jnp.arange(Q_BLOCK)
        s = jnp.where(k_idx[None, :] <= q_idx[:, None], s, -jnp.inf)
        p = jax.nn.softmax(s, axis=-1).astype(vt.dtype)
        return jnp.einsum('bhqk,bhkd->bhqd', p, vt)

    o = lax.map(block, (qb, jnp.arange(nb) * Q_BLOCK))
    return o.transpose(1, 0, 3, 2, 4).reshape(B, S, MLA_WIDTH)


def retention_group(q_raw, k_raw, v_raw, g_raw, pos, gn_gain):
    B, S, _ = q_raw.shape
    f32 = jnp.float32
    q = rope(q_raw.reshape(B, S, RET_HEADS, RET_QK_DIM), pos).astype(f32)
    k = rope(k_raw.reshape(B, S, RET_HEADS, RET_QK_DIM), pos).astype(f32) * (RET_QK_DIM ** -0.5)
    v = v_raw.reshape(B, S, RET_HEADS, RET_V_DIM).astype(f32)
    log_gamma = jnp.log(1.0 - 2.0 ** (-5.0 - jnp.arange(RET_HEADS, dtype=f32)))
    nc = S // CHUNK
    to_chunks = lambda t: t.reshape(B, nc, CHUNK, RET_HEADS, t.shape[-1]).transpose(0, 3, 1, 2, 4)
    qc, kc, vc = to_chunks(q), to_chunks(k), to_chunks(v)
    idx = jnp.arange(CHUNK)
    rel = idx[:, None] - idx[None, :]
    decay_in = jnp.where(rel >= 0, jnp.exp(log_gamma[:, None, None] * jnp.maximum(rel, 0).astype(f32)), 0.0)
    scores = jnp.einsum('bhncd,bhnmd->bhncm', qc, kc) * decay_in[None, :, None]
    inner = jnp.einsum('bhncm,bhnmv->bhncv', scores, vc)
    w_k = jnp.exp(log_gamma[:, None] * (CHUNK - 1 - idx).astype(f32))
    u = jnp.einsum('bhncd,bhnce->bhnde', kc * w_k[None, :, None, :, None], vc)
    chunk_decay = jnp.exp(log_gamma * CHUNK)[None, :, None, None]

    def step(state, u_i):
        return state * chunk_decay + u_i, state

    _, s_prev = lax.scan(step, jnp.zeros((B, RET_HEADS, RET_QK_DIM, RET_V_DIM), f32),
                         u.transpose(2, 0, 1, 3, 4))
    s_prev = s_prev.transpose(1, 2, 0, 3, 4)
    w_q = jnp.exp(log_gamma[:, None] * (idx + 1).astype(f32))
    cross = jnp.einsum('bhncd,bhnde->bhnce', qc * w_q[None, :, None, :, None], s_prev)
    o = (inner + cross).transpose(0, 2, 3, 1, 4).reshape(B, S, RET_HEADS, RET_V_DIM)
    mu = jnp.mean(o, axis=-1, keepdims=True)
    var = jnp.mean(jnp.square(o - mu), axis=-1, keepdims=True)
    o = ((o - mu) * lax.rsqrt(var + EPS)).reshape(B, S, RET_WIDTH) * gn_gain.astype(f32)
    return (jax.nn.silu(g_raw.astype(f32)) * o).astype(q_raw.dtype)


def setup_inputs(seed: int = 0) -> dict:
    key = jax.random.key(seed)
    ks = jax.random.split(key, 24)
    L = DEPTH
    nrm = lambda k, shape, fan_in: jax.random.normal(k, shape, jnp.float32) * fan_in ** -0.5
    gain = lambda k, n: 1.0 + 0.05 * jax.random.normal(k, (L, n), jnp.float32)
    return {
        "x": jax.random.normal(ks[0], (BATCH, SEQ, D_MODEL), jnp.float32),
        "c": jax.random.normal(ks[1], (BATCH, D_MODEL), jnp.float32),
        "positions": (jax.random.randint(ks[2], (BATCH, 1), 0, 512, jnp.int32)
                      + jnp.arange(SEQ, dtype=jnp.int32)[None, :]),
        "w_ada": 0.5 * nrm(ks[3], (L, D_MODEL, 6 * D_MODEL), D_MODEL),
        "b_ada": 0.01 * jax.random.normal(ks[4], (L, 6 * D_MODEL), jnp.float32),
        "pre_norm_mix": gain(ks[5], D_MODEL),
        "w_in": nrm(ks[6], (L, D_MODEL, IN_COLS), D_MODEL),
        "q_a_norm": gain(ks[7], MLA_Q_RANK),
        "w_q_b": nrm(ks[8], (L, MLA_Q_RANK, MLA_HEADS * MLA_QK_DIM), MLA_Q_RANK),
        "kv_a_norm": gain(ks[9], MLA_KV_RANK),
        "w_kv_b": nrm(ks[10], (L, MLA_KV_RANK, MLA_HEADS * (MLA_NOPE_DIM + MLA_V_DIM)), MLA_KV_RANK),
        "mla_out_norm": gain(ks[11], MLA_WIDTH),
        "ret_gn_gain": gain(ks[12], RET_WIDTH),
        "w_out": nrm(ks[13], (L, D_MIX, D_MODEL), D_MIX),
        "post_norm_mix": gain(ks[14], D_MODEL),
        "pre_norm_ffn": gain(ks[15], D_MODEL),
        "w_gate": nrm(ks[16], (L, D_MODEL, D_FF), D_MODEL),
        "w_up": nrm(ks[17], (L, D_MODEL, D_FF), D_MODEL),
        "w_down": nrm(ks[18], (L, D_FF, D_MODEL), D_FF),
        "post_norm_ffn": gain(ks[19], D_MODEL),
    }


def reference(x, c, positions, w_ada, b_ada, pre_norm_mix, w_in, q_a_norm, w_q_b, kv_a_norm,
              w_kv_b, mla_out_norm, ret_gn_gain, w_out, post_norm_mix, pre_norm_ffn,
              w_gate, w_up, w_down, post_norm_ffn):
    offsets = np.cumsum(IN_SPLITS)[:-1].tolist()
    for l in range(DEPTH):
        mod = (jax.nn.silu(c) @ w_ada[l] + b_ada[l])[:, None, :]
        sh1, sc1, g1, sh2, sc2, g2 = jnp.split(mod, 6, axis=-1)
        h = rmsnorm(x, pre_norm_mix[l]) * (1.0 + sc1) + sh1
        z = h @ w_in[l]
        cq, ckv, kpe, rq, rk, rv, rg = jnp.split(z, offsets, axis=-1)
        y_mla = rmsnorm(mla_group(cq, ckv, kpe, positions, q_a_norm[l], w_q_b[l],
                                  kv_a_norm[l], w_kv_b[l]), mla_out_norm[l])
        y_ret = retention_group(rq, rk, rv, rg, positions, ret_gn_gain[l])
        mix = jnp.concatenate([y_mla, y_ret], axis=-1) @ w_out[l]
        x = x + g1 * rmsnorm(mix, post_norm_mix[l])
        h = rmsnorm(x, pre_norm_ffn[l]) * (1.0 + sc2) + sh2
        f = (jax.nn.silu(h @ w_gate[l]) * (h @ w_up[l])) @ w_down[l]
        x = x + g2 * rmsnorm(f, post_norm_ffn[l])
    return x
```

```python
import contextlib
import types
import numpy as np
import concourse.bass as bass
import concourse.mybir as mybir
from concourse.bass_utils import run_bass_kernel_spmd

F32 = mybir.dt.float32
BF16 = mybir.dt.bfloat16
I32 = mybir.dt.int32
AF = mybir.ActivationFunctionType
ALU = mybir.AluOpType
AX = mybir.AxisListType

ENGS = ("pe", "act", "dve", "pool", "sp")
STOP = 99
SUB = 99
HSEL = (0, 1, 2, 3)
TAPS = ()


class _Stop(Exception):
    pass

T = 4096
D = 1024
NSB = 8
DFF = 2816
NJ = 22
NC1 = 2752
EPS = 1e-6
PI = float(np.pi)


def _freeze(fn):
    if fn.__closure__ is None:
        return fn
    cells = []
    for c in fn.__closure__:
        try:
            cells.append(types.CellType(c.cell_contents))
        except ValueError:
            cells.append(c)
    return types.FunctionType(fn.__code__, fn.__globals__, fn.__name__, fn.__defaults__, tuple(cells))


class Op:
    __slots__ = ("eng", "idx", "emit", "deps", "is_dma", "dma_i", "marked", "count", "clock", "waits")


class Sched:
    def __init__(self, n_dma_sems=12):
        self.ops = {e: [] for e in ENGS}
        self.order = []
        self.lastw = {}
        self.readers = {}
        self.n_dma_sems = n_dma_sems
        self.dma_ops = {e: [] for e in ENGS}
        self.dma_since_bar = []

    def add(self, eng, emit, reads=(), writes=(), dma=False, deps=()):
        op = Op()
        op.eng = eng
        op.emit = _freeze(emit)
        op.is_dma = dma
        op.marked = False
        op.count = 0
        op.idx = len(self.ops[eng])
        d = set(deps)
        for k in reads:
            w = self.lastw.get(k)
            if w is not None:
                d.add(w)
        for k in writes:
            w = self.lastw.get(k)
            if w is not None:
                d.add(w)
            for r in self.readers.get(k, ()):
                d.add(r)
        for k in reads:
            self.readers.setdefault(k, []).append(op)
        for k in writes:
            self.lastw[k] = op
            self.readers[k] = []
        if dma:
            op.dma_i = len(self.dma_ops[eng])
            if op.dma_i >= self.n_dma_sems:
                d.add(self.dma_ops[eng][op.dma_i - self.n_dma_sems])
            self.dma_ops[eng].append(op)
            self.dma_since_bar.append(op)
        d.discard(op)
        op.deps = d
        self.ops[eng].append(op)
        self.order.append(op)
        return op

    def barrier(self):
        deps = [self.ops[e][-1] for e in ENGS if self.ops[e] and not self.ops[e][-1].is_dma]
        for e in ENGS:
            for o in reversed(self.ops[e]):
                if not o.is_dma:
                    deps.append(o)
                    break
        deps = list(set(deps)) + list(self.dma_since_bar)
        self.dma_since_bar = []
        for e in ENGS:
            self.add(e, lambda g: None, deps=deps)

    def resolve(self):
        known = {e: {f: -1 for f in ENGS} for e in ENGS}
        known_dma = {e: set() for e in ENGS}
        for op in self.order:
            e = op.eng
            kn = known[e]
            waits = []
            for d in sorted(op.deps, key=lambda o: -o.idx):
                if d.is_dma:
                    if d in known_dma[e]:
                        continue
                    known_dma[e].add(d)
                    waits.append(d)
                else:
                    if d.eng == "pe" and e == "pe":
                        continue
                    if kn[d.eng] >= d.idx:
                        continue
                    d.marked = True
                    waits.append(d)
                ck = d.clock
                for f in ENGS:
                    if ck[f] > kn[f]:
                        kn[f] = ck[f]
            op.waits = waits
            ck = dict(kn)
            if not op.is_dma:
                ck[e] = max(ck[e], op.idx)
            op.clock = ck
        for e in ENGS:
            c = 0
            for op in self.ops[e]:
                if op.marked:
                    c += 1
                    op.count = c

    def emit_all(self, block, esem, dsem):
        self.resolve()
        n = self.n_dma_sems

        def run(e, engobj):
            for op in self.ops[e]:
                for d in op.waits:
                    if d.is_dma:
                        engobj.wait_ge(dsem[d.eng][d.dma_i % n], 16 * (d.dma_i // n + 1))
                    else:
                        engobj.wait_ge(esem[d.eng], d.count)
                ins = op.emit(engobj)
                if op.is_dma:
                    ins.then_inc(dsem[e][op.dma_i % n], 16)
                elif op.marked:
                    if ins is None:
                        ins = engobj.nop()
                    ins.then_inc(esem[e], 1)

        block.tensor(lambda t: run("pe", t))
        block.scalar(lambda t: run("act", t))
        block.vector(lambda t: run("dve", t))
        block.gpsimd(lambda t: run("pool", t))
        block.sync(lambda t: run("sp", t))


class Arena:
    def __init__(self, ap, ncols):
        self.ap = ap
        self.n = ncols
        self.off = 0

    def alloc(self, cols, dt=F32):
        nb = cols * (4 if dt in (F32, I32) else 2)
        n32 = ((nb + 31) // 32) * 8
        assert self.off + n32 <= self.n, ("arena overflow", self.off, n32, self.n)
        v = self.ap[:, self.off:self.off + n32]
        self.off += n32
        if dt != F32:
            v = v.bitcast(dt)
        return v[:, 0:cols]

    def reset(self):
        self.off = 0


def build_nc():
    nc = bass.Bass("TRN2", target_bir_lowering=False)

    def DI(name, shape, dt=F32):
        return nc.dram_tensor(name, shape, dt, kind="ExternalInput").ap()

    x_d = DI("x", [T, D])
    c_d = DI("cT", [128, 8])
    pos_d = DI("pos", [1, T], I32)
    wada_d = DI("w_ada", [D, 6 * D])
    bada_d = DI("b_ada", [1, 6 * D])
    gpre1_d = DI("gpre1", [128, 8])
    gpre2_d = DI("gpre2", [128, 8])
    gpost1_d = DI("gpost1", [1, D])
    gpost2_d = DI("gpost2", [1, D])
    qg_d = DI("qg", [128, 3])
    kvg_d = DI("kvg", [128, 2])
    og_d = DI("og", [128, 8])
    w1_d = DI("w1", [D, NC1])
    wq_d = DI("wq", [384, 1024])
    wkv_d = DI("wkv", [256, 1536])
    wout_d = DI("wout", [D, D])
    wg_d = DI("wg", [D, DFF])
    wu_d = DI("wu", [D, DFF])
    wd_d = DI("wd", [DFF, D])
    ident_d = DI("ident", [128, 128])
    tri_d = DI("tri", [128, 128])
    dt_d = DI("dtc", [128, 512])
    wqc_d = DI("wqc", [128, 1024])
    wkc_d = DI("wkc", [128, 256])
    dec_d = DI("decc", [128, 2])
    inv_d = DI("invc", [128, 3])
    ph_d = DI("phc", [128, 3])
    out_d = nc.dram_tensor("out", [T, D], F32, kind="ExternalOutput").ap()
    yret_d = nc.dram_tensor("yret_scr", [4, 128, T], BF16).ap()

    S = Sched(n_dma_sems=12)
    A = S.add

    with contextlib.ExitStack() as ctx:
        def sbt(name, cols, dt=F32, parts=128):
            return ctx.enter_context(nc.sbuf_tensor(name, [parts, cols], dt))

        identb = sbt("identb", 128, BF16)
        trib = sbt("trib", 128, BF16)
        onesb = sbt("onesb", 128, BF16)
        onesf = sbt("onesf", 128)
        epst = sbt("epst", 1)
        DECc = sbt("DECc", 2)
        INVc = sbt("INVc", 3)
        PHc = sbt("PHc", 3)
        modc = sbt("modc", 32)
        a1 = sbt("a1", 8)
        a2 = sbt("a2", 8)
        G1b = sbt("G1b", D)
        G2b = sbt("G2b", D)
        ARN = 50000
        arena_t = sbt("arena", ARN)
        AR = Arena(arena_t, ARN)
        P = [ctx.enter_context(nc.psum_tensor(f"bank{i}", [128, 512], F32)) for i in range(8)]
        Pb = [p[:, :].bitcast(BF16) for p in P]

        esem = {e: ctx.enter_context(nc.semaphore("es_" + e)) for e in ENGS}
        dsem = {e: [ctx.enter_context(nc.semaphore(f"ds_{e}{i}")) for i in range(12)] for e in ("sp", "pool")}

        def dma(q, out, in_, r=(), w=()):
            return A(q, lambda g: g.dma_start(out=out, in_=in_), reads=r, writes=w, dma=True)

        def tap(name, ap, keys):
            if name not in TAPS:
                return
            shp = list(ap.shape)
            dd = nc.dram_tensor("dbg_" + name, shp, ap.dtype, kind="ExternalOutput").ap()
            dma("sp", dd, ap, r=keys)

        def rsqrt_ops(dst, src, scale, rk, wk):
            A("act", lambda g: g.activation(out=dst, in_=src, func=AF.Sqrt, scale=scale, bias=epst[0:dst.shape[0], :]),
              reads=list(rk) + ["epst"], writes=[wk])
            A("dve", lambda g: g.reciprocal(out=dst, in_=dst), reads=[wk], writes=[wk])

        _phase = [0]
        try:
            dma("pool", identb[:, :], ident_d, w=["identb"])
            dma("pool", trib[:, :], tri_d, w=["trib"])
            A("pool", lambda g: g.memset(onesb[:, :], 1.0), writes=["onesb"])
            A("pool", lambda g: g.memset(onesf[:, :], 1.0), writes=["onesf"])
            A("pool", lambda g: g.memset(epst[:, :], EPS), writes=["epst"])
            for t_, d_, k_ in ((DECc, dec_d, "DECc"), (INVc, inv_d, "INVc"), (PHc, ph_d, "PHc")):
                dma("sp", t_[:, :], d_, w=[k_])
            cT = AR.alloc(8)
            gp1 = AR.alloc(8)
            gp2 = AR.alloc(8)
            scb = AR.alloc(8, BF16)
            gpo1 = AR.alloc(D)
            gpo2 = AR.alloc(D)
            bada = AR.alloc(6 * D)
            modrow = AR.alloc(6 * D)
            grow1 = AR.alloc(D)
            grow2 = AR.alloc(D)
            wa = [AR.alloc(8 * 512, BF16), AR.alloc(8 * 512, BF16)]
            dma("sp", cT, c_d, w=["cT"])
            dma("sp", gp1, gpre1_d, w=["gp1"])
            dma("sp", gp2, gpre2_d, w=["gp2"])
            dma("sp", gpo1[0:1, :], gpost1_d, w=["gpo1"])
            dma("sp", gpo2[0:1, :], gpost2_d, w=["gpo2"])
            dma("sp", bada[0:1, :], bada_d, w=["bada"])
            A("act", lambda g: g.activation(out=scb, in_=cT, func=AF.Silu), reads=["cT"], writes=["scb"])
            for gi in range(12):
                wb = wa[gi % 2]
                dma("pool", wb.rearrange("p (c n) -> p c n", c=8),
                    wada_d[:, gi * 512:(gi + 1) * 512].rearrange("(c p) n -> p c n", p=128), w=[f"wa{gi % 2}"])

                def mm_ada(g, gi=gi, wb=wb):
                    for k in range(8):
                        r = g.matmul(P[gi % 2][0:1, :], lhsT=scb[:, k:k + 1], rhs=wb[:, k * 512:(k + 1) * 512],
                                     start=(k == 0), stop=(k == 7))
                    return r
                A("pe", mm_ada, reads=["scb", f"wa{gi % 2}"], writes=[f"P{gi % 2}"])
                A("dve", lambda g, gi=gi: g.tensor_tensor(out=modrow[0:1, gi * 512:(gi + 1) * 512], in0=P[gi % 2][0:1, :],
                                                         in1=bada[0:1, gi * 512:(gi + 1) * 512], op=ALU.add),
                  reads=[f"P{gi % 2}", "bada"], writes=["modrow"])
            col_offs = [0 * D, 1 * D, 3 * D, 4 * D]

            def mm_cols(g):
                for vi, off in enumerate(col_offs):
                    for c in range(8):
                        r = g.matmul(P[2][:, vi * 8 + c:vi * 8 + c + 1], lhsT=modrow[0:1, off + c * 128:off + (c + 1) * 128],
                                     rhs=onesf[0:1, 0:1], start=True, stop=True)
                return r
            A("pe", mm_cols, reads=["modrow", "onesf"], writes=["P2"])
            A("dve", lambda g: g.tensor_copy(out=modc[:, :], in_=P[2][:, 0:32]), reads=["P2"], writes=["modc"])
            A("dve", lambda g: g.scalar_tensor_tensor(out=a1[:, :], in0=modc[:, 8:16], scalar=1.0, in1=gp1, op0=ALU.add, op1=ALU.mult),
              reads=["modc", "gp1"], writes=["a1"])
            A("dve", lambda g: g.scalar_tensor_tensor(out=a2[:, :], in0=modc[:, 24:32], scalar=1.0, in1=gp2, op0=ALU.add, op1=ALU.mult),
              reads=["modc", "gp2"], writes=["a2"])
            sh1 = modc[:, 0:8]
            sh2 = modc[:, 16:24]
            A("dve", lambda g: g.tensor_tensor(out=grow1[0:1, :], in0=modrow[0:1, 2 * D:3 * D], in1=gpo1[0:1, :], op=ALU.mult),
              reads=["modrow", "gpo1"], writes=["grow1"])
            A("dve", lambda g: g.tensor_tensor(out=grow2[0:1, :], in0=modrow[0:1, 5 * D:6 * D], in1=gpo2[0:1, :], op=ALU.mult),
              reads=["modrow", "gpo2"], writes=["grow2"])
            for gi, (grow, Gb, gk) in enumerate(((grow1, G1b, "G1b"), (grow2, G2b, "G2b"))):
                for hf in range(2):
                    bk = 3 + hf
                    A("pe", lambda g, grow=grow, hf=hf, bk=bk: g.matmul(P[bk][:, :], lhsT=onesf[0:1, 0:128],
                                                                        rhs=grow[0:1, hf * 512:(hf + 1) * 512], start=True, stop=True),
                      reads=[f"grow{gi + 1}", "onesf"], writes=[f"P{bk}"])
                    A("act", lambda g, Gb=Gb, hf=hf, bk=bk: g.activation(out=Gb[:, hf * 512:(hf + 1) * 512], in_=P[bk][:, :], func=AF.Copy),
                      reads=[f"P{bk}"], writes=[gk])
            S.barrier()
            _phase[0] += 1
            if _phase[0] > STOP:
                raise _Stop()
            AR.reset()

            cqnT = AR.alloc(3 * T, BF16)
            ckvnT = AR.alloc(2 * T, BF16)
            TABm = AR.alloc(T)
            kpeT = TABm[64:96, 0:2048].bitcast(BF16)
            P12 = AR.off
            DTc = AR.alloc(512)
            WQc = AR.alloc(1024)
            WKc = AR.alloc(256)
            dma("sp", DTc, dt_d, w=["DTc"])
            dma("sp", WQc, wqc_d, w=["WQc"])
            dma("sp", WKc, wkc_d, w=["WKc"])
            W1 = AR.alloc(8 * NC1, BF16)
            xt = [AR.alloc(D), AR.alloc(D)]
            xn = [AR.alloc(D, BF16), AR.alloc(D, BF16)]
            junk = AR.alloc(D, BF16)
            ssq = [AR.alloc(1), AR.alloc(1)]
            hT = AR.alloc(8 * 512, BF16)
            cqraw = AR.alloc(3 * 512)
            ckvraw = AR.alloc(2 * 512)
            sq = AR.alloc(3 * 512, BF16)
            sq2 = AR.alloc(2 * 512, BF16)
            Rq = AR.alloc(512)
            Rkv = Rq
            posi = AR.alloc(512, I32)
            posf = AR.alloc(512)
            ang = AR.alloc(512)
            ni = posi
            nf = AR.alloc(512)
            msk = nf
            Cr = AR.alloc(512)
            Sr = AR.alloc(512)
            t1 = [AR.alloc(512)] * 2
            t2 = [AR.alloc(512)] * 2
            rqT = AR.alloc(2 * 512, BF16)
            rkT = AR.alloc(2 * 512, BF16)
            qwT = AR.alloc(2 * 512, BF16)
            rqm = AR.alloc(2 * 512, BF16)
            qwm = AR.alloc(2 * 512, BF16)
            mcol = AR.alloc(1)
            A("pool", lambda g: g.memset(mcol[0:64, :], 1.0), writes=["mcol"])
            A("pool", lambda g: g.memset(mcol[64:128, :], 0.0), writes=["mcol"])
            vtok = AR.alloc(4 * 512, BF16)
            sg = AR.alloc(4 * 512, BF16)
            kwtok = AR.alloc(256, BF16)
            scTm = AR.alloc(512, BF16)
            osb = AR.alloc(512)
            ynorm = AR.alloc(512)
            osq = ynorm
            ytok = AR.alloc(512, BF16)
            ysT = AR.alloc(4 * 512, BF16)
            Sf = AR.alloc(256)
            Sbf = AR.alloc(256, BF16)
            st = {k: AR.alloc(4) for k in ("osum", "osqs", "mean", "msq", "var", "rgn")}

            for hf in range(2):
                dma("pool", W1.rearrange("p (c n) -> p c n", c=8)[:, hf * 4:(hf + 1) * 4, :],
                    w1_d.rearrange("(c p) n -> p c n", p=128)[:, hf * 4:(hf + 1) * 4, :], w=["W1"])
            A("pool", lambda g: g.memset(Sf, 0.0), writes=["Sf"])
            A("pool", lambda g: g.memset(Sbf, 0.0), writes=["Sbf"])

            def ck(n):
                if SUB == n:
                    raise _Stop()
            ck(0)

            def w1s(c, off, n):
                return W1[:, c * NC1 + off:c * NC1 + off + n]

            def table(dst, dk, col, sbi):
                A("dve", lambda g: g.tensor_scalar(out=ang, in0=posf, scalar1=INVc[:, col:col + 1], scalar2=PHc[:, col:col + 1],
                                                   op0=ALU.mult, op1=ALU.add), reads=["posf", "INVc", "PHc"], writes=["ang"])
                A("dve", lambda g: g.tensor_scalar(out=ni, in0=ang, scalar1=float(1.0 / (2 * PI)), scalar2=None, op0=ALU.mult),
                  reads=["ang"], writes=["ibuf"])
                A("dve", lambda g: g.tensor_copy(out=nf, in_=ni), reads=["ibuf"], writes=["nf"])
                A("dve", lambda g: g.scalar_tensor_tensor(out=ang, in0=nf, scalar=-2 * PI, in1=ang, op0=ALU.mult, op1=ALU.add),
                  reads=["nf", "ang"], writes=["ang"])
                A("dve", lambda g: g.tensor_single_scalar(out=msk, in_=ang, scalar=PI, op=ALU.is_gt), reads=["ang", "nf"], writes=["nf"])
                A("dve", lambda g: g.scalar_tensor_tensor(out=ang, in0=msk, scalar=-2 * PI, in1=ang, op0=ALU.mult, op1=ALU.add),
                  reads=["nf", "ang"], writes=["ang"])
                A("dve", lambda g: g.tensor_scalar(out=ang, in0=ang, scalar1=-3.14159, scalar2=3.14159, op0=ALU.max, op1=ALU.min),
                  reads=["ang"], writes=["ang"])
                np_ = dst.shape[0]
                A("act", lambda g: g.activation(out=dst, in_=ang[0:np_, :], func=AF.Sin), reads=["ang"], writes=[dk])

            mtiles = [(0, 128, "cq", 0), (128, 128, "cq", 1), (256, 128, "cq", 2), (384, 128, "ckv", 0), (512, 128, "ckv", 1),
                      (640, 64, "kpe", 0)]
            o_ = 704
            for nm in ("rq", "rk"):
                for i in range(2):
                    mtiles.append((o_, 128, nm + "n", i))
                    mtiles.append((o_ + 128, 128, nm + "s", i))
                    o_ += 256
            RV = 1728
            RG = 2240

            for sbi in range(NSB):
                sc0 = sbi * 512
                dma("sp", posi, bass.AP(pos_d.tensor, sc0, [[0, 128], [1, 512]]), w=["ibuf"])
                A("dve", lambda g: g.tensor_copy(out=posf, in_=posi), reads=["ibuf"], writes=["posf"])
                table(TABm[0:64, sc0:sc0 + 512], "TABm", 0, sbi)
                table(Cr, "Cr", 1, sbi)
                table(Sr, "Sr", 2, sbi)
                ck(1)
                for j in range(4):
                    tb = sbi * 4 + j
                    b2 = tb % 2
                    dma("sp", xt[b2], x_d[tb * 128:(tb + 1) * 128, :], w=[f"xt{b2}"])
                    A("act", lambda g, b2=b2: g.activation(out=junk, in_=xt[b2], func=AF.Square, accum_out=ssq[b2]),
                      reads=[f"xt{b2}"], writes=["junk", f"ssq{b2}"])
                    rsqrt_ops(ssq[b2], ssq[b2], 1.0 / D, [f"ssq{b2}"], f"ssq{b2}")
                    A("act", lambda g, b2=b2: g.activation(out=xn[b2], in_=xt[b2], func=AF.Copy, scale=ssq[b2]),
                      reads=[f"xt{b2}", f"ssq{b2}"], writes=[f"xn{b2}"])

                    def tr8(g, b2=b2):
                        for c in range(8):
                            r = g.transpose(out=Pb[b2][:, c * 128:(c + 1) * 128], in_=xn[b2][:, c * 128:(c + 1) * 128], identity=identb[:, :])
                        return r
                    A("pe", tr8, reads=[f"xn{b2}", "identb"], writes=[f"P{b2}"])
                    for c in range(8):
                        dst = hT[:, c * 512 + j * 128:c * 512 + (j + 1) * 128]
                        if c % 2 == 0:
                            A("dve", lambda g, c=c, dst=dst, b2=b2: g.tensor_scalar(out=dst, in0=Pb[b2][:, c * 128:(c + 1) * 128],
                                                                                   scalar1=a1[:, c:c + 1], scalar2=sh1[:, c:c + 1],
                                                                                   op0=ALU.mult, op1=ALU.add),
                              reads=[f"P{b2}", "a1", "modc"], writes=["hT"])
                        else:
                            A("act", lambda g, c=c, dst=dst, b2=b2: g.activation(out=dst, in_=Pb[b2][:, c * 128:(c + 1) * 128], func=AF.Identity,
                                                                                scale=a1[:, c:c + 1], bias=sh1[:, c:c + 1]),
                              reads=[f"P{b2}", "a1", "modc"], writes=["hT"])
                ck(2)
                for mi, (off, M, kind, i) in enumerate(mtiles):
                    bk = 2 + mi % 2
                    pk = f"P{bk}"

                    def mmz(g, off=off, M=M, bk=bk):
                        for c in range(8):
                            r = g.matmul(P[bk][0:M, :], lhsT=w1s(c, off, M), rhs=hT[:, c * 512:(c + 1) * 512], start=(c == 0), stop=(c == 7))
                        return r
                    A("pe", mmz, reads=["W1", "hT"], writes=[pk])
                    if kind in ("cq", "ckv"):
                        raw, sqt, nt, Rt, bank, scl, dstT, rk_ = ((cqraw, sq, 3, Rq, 4, 1.0 / 384, cqnT, "Rq") if kind == "cq"
                                                                  else (ckvraw, sq2, 2, Rkv, 5, 1.0 / 256, ckvnT, "Rq"))
                        A("act", lambda g, raw=raw, i=i, bk=bk: g.activation(out=raw[:, i * 512:(i + 1) * 512], in_=P[bk][:, :], func=AF.Copy),
                          reads=[pk], writes=[f"{kind}raw{i}"])
                        A("pool", lambda g, raw=raw, sqt=sqt, i=i: g.tensor_tensor(out=sqt[:, i * 512:(i + 1) * 512], in0=raw[:, i * 512:(i + 1) * 512],
                                                                                  in1=raw[:, i * 512:(i + 1) * 512], op=ALU.mult),
                          reads=[f"{kind}raw{i}"], writes=[f"{kind}sq{i}"])
                        if i == nt - 1:
                            def mmst(g, sqt=sqt, nt=nt, bank=bank):
                                for q in range(nt):
                                    r = g.matmul(P[bank][:, :], lhsT=onesb[:, :], rhs=sqt[:, q * 512:(q + 1) * 512], start=(q == 0), stop=(q == nt - 1))
                                return r
                            A("pe", mmst, reads=[f"{kind}sq{q}" for q in range(nt)] + ["onesb"], writes=[f"P{bank}"])
                            rsqrt_ops(Rt, P[bank][:, :], scl, [f"P{bank}"], rk_)
                            for q in range(nt):
                                A("pool", lambda g, raw=raw, Rt=Rt, q=q, dstT=dstT: g.tensor_tensor(
                                    out=dstT[:, q * T + sc0:q * T + sc0 + 512], in0=raw[:, q * 512:(q + 1) * 512], in1=Rt, op=ALU.mult),
                                  reads=[f"{kind}raw{q}", rk_], writes=[f"{kind}nT"])
                    elif kind == "kpe":
                        A("dve", lambda g, bk=bk: g.tensor_tensor(out=t1[0][0:32, :], in0=P[bk][0:32, :], in1=TABm[0:32, sc0:sc0 + 512], op=ALU.mult),
                          reads=[pk, "TABm"], writes=["t1_0"])
                        A("dve", lambda g, bk=bk: g.tensor_tensor(out=t2[0][0:32, :], in0=P[bk][32:64, :], in1=TABm[32:64, sc0:sc0 + 512], op=ALU.mult),
                          reads=[pk, "TABm"], writes=["t2_0"])
                        A("dve", lambda g: g.tensor_tensor(out=kpeT[:, sc0:sc0 + 512], in0=t1[0][0:32, :], in1=t2[0][0:32, :], op=ALU.add),
                          reads=["t1_0", "t2_0"], writes=["kpeT"])
                    else:
                        nm = kind[:2]
                        if kind[2] == "n":
                            A("dve", lambda g, bk=bk, i=i: g.tensor_tensor(out=t1[i], in0=P[bk][:, :], in1=Cr, op=ALU.mult),
                              reads=[pk, "Cr"], writes=["t1_0"])
                        else:
                            A("dve", lambda g, bk=bk, i=i: g.tensor_tensor(out=t2[i], in0=P[bk][:, :], in1=Sr, op=ALU.mult),
                              reads=[pk, "Sr"], writes=["t2_0"])
                            dstq = rqT if nm == "rq" else rkT
                            A("pool", lambda g, i=i, dstq=dstq: g.tensor_tensor(out=dstq[:, i * 512:(i + 1) * 512], in0=t1[i], in1=t2[i], op=ALU.add),
                              reads=["t1_0", "t2_0"], writes=[nm + "T"])
                            if nm == "rq":
                                A("pool", lambda g, i=i: g.tensor_scalar(out=rqm[:, i * 512:(i + 1) * 512], in0=rqT[:, i * 512:(i + 1) * 512],
                                                                        scalar1=mcol[:, 0:1], scalar2=None, op0=ALU.mult),
                                  reads=["rqT", "mcol"], writes=["rqm"])
                                A("pool", lambda g, i=i: g.tensor_tensor(out=qwT[:, i * 512:(i + 1) * 512], in0=rqT[:, i * 512:(i + 1) * 512],
                                                                        in1=WQc[:, i * 512:(i + 1) * 512], op=ALU.mult),
                                  reads=["rqT", "WQc"], writes=["qwT"])
                                A("pool", lambda g, i=i: g.tensor_scalar(out=qwm[:, i * 512:(i + 1) * 512], in0=qwT[:, i * 512:(i + 1) * 512],
                                                                        scalar1=mcol[:, 0:1], scalar2=None, op0=ALU.mult),
                                  reads=["qwT", "mcol"], writes=["qwm"])
                ck(3)
                for j in range(4):
                    for which, off, bank in (("v", RV, 4), ("g", RG, 5)):
                        def mmt(g, j=j, off=off, bank=bank):
                            for c in range(8):
                                r = g.matmul(P[bank][:, :], lhsT=hT[:, c * 512 + j * 128:c * 512 + (j + 1) * 128], rhs=w1s(c, off, 512),
                                             start=(c == 0), stop=(c == 7))
                            return r
                        A("pe", mmt, reads=["W1", "hT"], writes=[f"P{bank}"])
                        if which == "v":
                            A("act", lambda g, j=j: g.activation(out=vtok[:, j * 512:(j + 1) * 512], in_=P[4][:, :], func=AF.Copy),
                              reads=["P4"], writes=["vtok"])
                        else:
                            A("act", lambda g, j=j: g.activation(out=sg[:, j * 512:(j + 1) * 512], in_=P[5][:, :], func=AF.Silu),
                              reads=["P5"], writes=["sg"])
                ck(4)
                for j in range(4):
                    jc = slice(j * 128, (j + 1) * 128)

                    def trk(g, j=j):
                        for i in range(2):
                            r = g.transpose(out=Pb[5][:, i * 128:(i + 1) * 128], in_=rkT[:, i * 512 + j * 128:i * 512 + (j + 1) * 128], identity=identb[:, :])
                        return r
                    A("pe", trk, reads=["rkT", "identb"], writes=["P5"])
                    A("dve", lambda g: g.tensor_tensor(out=kwtok, in0=Pb[5][:, 0:256], in1=WKc[:, :], op=ALU.mult),
                      reads=["P5", "WKc"], writes=["kwtok"])

                    ck(6)

                    def mmsc(g, j=j):
                        for h in range(4):
                            i, r0 = h // 2, 64 * (h % 2)
                            cs = slice(i * 512 + j * 128, i * 512 + (j + 1) * 128)
                            if r0 == 0:
                                r = g.matmul(P[6][:, h * 128:(h + 1) * 128], lhsT=rkT[:, cs], rhs=rqm[:, cs], start=True, stop=True)
                            else:
                                r = g.matmul(P[6][:, h * 128:(h + 1) * 128], lhsT=rkT[64:128, cs], rhs=rqT[64:128, cs], start=True, stop=True,
                                             tile_position=(64, 0))
                        return r
                    A("pe", mmsc, reads=["rkT", "rqT", "rqm"], writes=["P6"])
                    A("dve", lambda g: g.tensor_tensor(out=scTm, in0=P[6][:, :], in1=DTc[:, :], op=ALU.mult), reads=["P6", "DTc"], writes=["scTm"])

                    ck(7)

                    def mmo(g, j=j):
                        for h in range(4):
                            i, r0 = h // 2, 64 * (h % 2)
                            cs = slice(i * 512 + j * 128, i * 512 + (j + 1) * 128)
                            g.matmul(P[7][:, h * 128:(h + 1) * 128], lhsT=scTm[:, h * 128:(h + 1) * 128],
                                     rhs=vtok[:, j * 512 + h * 128:j * 512 + (h + 1) * 128], start=True, stop=False)
                            if r0 == 0:
                                r = g.matmul(P[7][:, h * 128:(h + 1) * 128], lhsT=qwm[:, cs], rhs=Sbf[:, i * 128:(i + 1) * 128], start=False, stop=True)
                            else:
                                r = g.matmul(P[7][:, h * 128:(h + 1) * 128], lhsT=qwT[64:128, cs], rhs=Sbf[64:128, i * 128:(i + 1) * 128],
                                             start=False, stop=True, tile_position=(64, 0))
                        return r
                    A("pe", mmo, reads=["scTm", "vtok", "qwT", "qwm", "Sbf"], writes=["P7"])
                    A("act", lambda g: g.activation(out=osb, in_=P[7][:, :], func=AF.Copy), reads=["P7"], writes=["osb"])

                    ck(8)

                    def mmu(g, j=j):
                        for h in range(4):
                            i, r0 = h // 2, 64 * (h % 2)
                            kw = dict(tile_position=(0, 64)) if r0 else {}
                            r = g.matmul(P[5][r0:r0 + 64, 256 + i * 128:256 + (i + 1) * 128], lhsT=kwtok[:, h * 64:(h + 1) * 64],
                                         rhs=vtok[:, j * 512 + h * 128:j * 512 + (h + 1) * 128], start=True, stop=True, **kw)
                        return r
                    A("pe", mmu, reads=["kwtok", "vtok"], writes=["P5"])
                    for i in range(2):
                        A("dve", lambda g, i=i: g.scalar_tensor_tensor(out=Sf[:, i * 128:(i + 1) * 128], in0=Sf[:, i * 128:(i + 1) * 128],
                                                                      scalar=DECc[:, i:i + 1], in1=P[5][:, 256 + i * 128:256 + (i + 1) * 128],
                                                                      op0=ALU.mult, op1=ALU.add),
                          reads=["P5", "DECc", "Sf"], writes=["Sf"])
                    A("pool", lambda g: g.tensor_copy(out=Sbf, in_=Sf), reads=["Sf"], writes=["Sbf"])
                    ck(9)
                    o3 = osb.rearrange("p (h v) -> p h v", h=4)
                    A("dve", lambda g, o3=o3: g.reduce_sum(out=st["osum"], in_=o3, axis=AX.X), reads=["osb"], writes=["osum"])
                    A("pool", lambda g: g.tensor_tensor(out=osq, in0=osb, in1=osb, op=ALU.mult), reads=["osb"], writes=["ynorm"])
                    A("dve", lambda g: g.reduce_sum(out=st["osqs"], in_=osq.rearrange("p (h v) -> p h v", h=4), axis=AX.X),
                      reads=["ynorm"], writes=["osqs"])
                    A("dve", lambda g: g.tensor_scalar(out=st["mean"], in0=st["osum"], scalar1=1.0 / 128, scalar2=None, op0=ALU.mult),
                      reads=["osum"], writes=["mean"])
                    A("dve", lambda g: g.tensor_tensor(out=st["msq"], in0=st["mean"], in1=st["mean"], op=ALU.mult), reads=["mean"], writes=["msq"])
                    A("dve", lambda g: g.scalar_tensor_tensor(out=st["var"], in0=st["osqs"], scalar=1.0 / 128, in1=st["msq"], op0=ALU.mult, op1=ALU.subtract),
                      reads=["osqs", "msq"], writes=["var"])
                    rsqrt_ops(st["rgn"], st["var"], 1.0, ["var"], "rgn")
                    for h in range(4):
                        A("dve", lambda g, h=h: g.tensor_scalar(out=ynorm[:, h * 128:(h + 1) * 128], in0=osb[:, h * 128:(h + 1) * 128],
                                                                scalar1=st["mean"][:, h:h + 1], scalar2=st["rgn"][:, h:h + 1],
                                                                op0=ALU.subtract, op1=ALU.mult),
                          reads=["osb", "mean", "rgn"], writes=["ynorm"])
                    A("pool", lambda g, j=j: g.tensor_tensor(out=ytok, in0=ynorm, in1=sg[:, j * 512:(j + 1) * 512], op=ALU.mult),
                      reads=["ynorm", "sg"], writes=["ytok"])

                    ck(10)

                    def try_(g):
                        for t in range(4):
                            r = g.transpose(out=Pb[4][:, t * 128:(t + 1) * 128], in_=ytok[:, t * 128:(t + 1) * 128], identity=identb[:, :])
                        return r
                    A("pe", try_, reads=["ytok", "identb"], writes=["P4"])
                    A("act", lambda g, j=j: g.activation(out=ysT.rearrange("p (t n) -> p t n", t=4)[:, :, j * 128:(j + 1) * 128],
                                                         in_=Pb[4][:, 0:512].rearrange("p (t n) -> p t n", t=4), func=AF.Copy),
                      reads=["P4"], writes=["ysT"])
                ck(5)
                dma("sp", yret_d[:, :, sc0:sc0 + 512].rearrange("t p n -> p t n"), ysT.rearrange("p (t n) -> p t n", t=4),
                    r=["ysT"], w=["yret_d"])
            tap("cqnT", cqnT, ["cqnT"])
            tap("ckvnT", ckvnT, ["ckvnT"])
            tap("kpeT", kpeT, ["kpeT"])
            tap("TABm", TABm, ["TABm"])
            tap("yret", yret_d, ["yret_d"])
            tap("hT", hT, ["hT"])
            tap("cqraw", cqraw, ["cqraw0", "cqraw1", "cqraw2"])
            tap("Rq", Rq, ["Rq"])
            tap("sq", sq, ["cqsq0", "cqsq1", "cqsq2"])
            tap("rqT", rqT, ["rqT"])
            tap("rkT", rkT, ["rkT"])
            tap("osb", osb, ["osb"])
            tap("ytok", ytok, ["ytok"])
            tap("Sf", Sf, ["Sf"])
            S.barrier()
            _phase[0] += 1
            if _phase[0] > STOP:
                raise _Stop()
            AR.off = P12

            ymlaT = AR.alloc(4 * T, BF16)
            P23 = AR.off
            Wq = AR.alloc(3 * 1024, BF16)
            Wkv = AR.alloc(2 * 1536, BF16)
            wqs = AR.alloc(3 * 1024)
            wkvs = AR.alloc(2 * 1536)
            qg = AR.alloc(3)
            kvg = AR.alloc(2)
            dma("sp", qg, qg_d, w=["qg"])
            dma("sp", kvg, kvg_d, w=["kvg"])
            dma("sp", wqs.rearrange("p (c n) -> p c n", c=3), wq_d.rearrange("(c p) n -> p c n", p=128), w=["wqs"])
            dma("sp", wkvs.rearrange("p (c n) -> p c n", c=2), wkv_d.rearrange("(c p) n -> p c n", p=128), w=["wkvs"])
            for c in range(3):
                A("dve", lambda g, c=c: g.tensor_scalar(out=Wq[:, c * 1024:(c + 1) * 1024], in0=wqs[:, c * 1024:(c + 1) * 1024],
                                                         scalar1=qg[:, c:c + 1], scalar2=None, op0=ALU.mult),
                  reads=["wqs", "qg"], writes=["Wq"])
            for c in range(2):
                A("dve", lambda g, c=c: g.tensor_scalar(out=Wkv[:, c * 1536:(c + 1) * 1536], in0=wkvs[:, c * 1536:(c + 1) * 1536],
                                                         scalar1=kvg[:, c:c + 1], scalar2=None, op0=ALU.mult),
                  reads=["wkvs", "kvg"], writes=["Wkv"])

            KT = [AR.alloc(T, BF16), AR.alloc(T, BF16)]
            QT = [AR.alloc(T, BF16), AR.alloc(T, BF16)]
            Vg = [AR.alloc(32 * 128, BF16), AR.alloc(32 * 128, BF16)]
            PT = [AR.alloc(1024, BF16) for _ in range(3)]
            u1 = AR.alloc(512)
            u2 = AR.alloc(512)
            rec = AR.alloc(512)
            for b in range(2):
                A("pool", lambda g, b=b: g.memset(KT[b][0:64, :], 0.0), writes=[f"KT{b}"])
                A("pool", lambda g, b=b: g.memset(QT[b][0:64, :], 0.0), writes=[f"QT{b}"])
                A("pool", lambda g, b=b: g.memset(Vg[b], 1.0), writes=[f"Vg{b}"])
            SCALE = float(96 ** -0.5)
            pt_i = 0
            for h in range(8):
                hb = h % 2
                kk, qk, vk = f"KT{hb}", f"QT{hb}", f"Vg{hb}"
                A("act", lambda g, hb=hb: g.activation(out=KT[hb][0:32, :], in_=kpeT[:, :], func=AF.Copy), reads=["kpeT"], writes=[kk])
                for sbi in range(NSB):
                    sc0 = sbi * 512

                    def mmk(g, h=h, sc0=sc0):
                        for c in range(2):
                            r = g.matmul(P[6][:, :], lhsT=Wkv[:, c * 1536 + h * 128:c * 1536 + (h + 1) * 128], rhs=ckvnT[:, c * T + sc0:c * T + sc0 + 512],
                                         start=(c == 0), stop=(c == 1))
                        return r
                    A("pe", mmk, reads=["Wkv", "ckvnT"], writes=["P6"])
                    A("act", lambda g, hb=hb, sc0=sc0: g.activation(out=KT[hb][64:128, sc0:sc0 + 512], in_=P[6][64:128, :], func=AF.Copy),
                      reads=["P6"], writes=[kk])

                    def mmq(g, h=h, sc0=sc0):
                        for c in range(3):
                            r = g.matmul(P[7][:, :], lhsT=Wq[:, c * 1024 + h * 128:c * 1024 + (h + 1) * 128], rhs=cqnT[:, c * T + sc0:c * T + sc0 + 512],
                                         start=(c == 0), stop=(c == 2))
                        return r
                    A("pe", mmq, reads=["Wq", "cqnT"], writes=["P7"])
                    A("dve", lambda g, sc0=sc0: g.tensor_tensor(out=u1[0:32, :], in0=P[7][0:32, :], in1=TABm[0:32, sc0:sc0 + 512], op=ALU.mult),
                      reads=["P7", "TABm"], writes=["u1"])
                    A("dve", lambda g, sc0=sc0: g.tensor_tensor(out=u2[0:32, :], in0=P[7][32:64, :], in1=TABm[32:64, sc0:sc0 + 512], op=ALU.mult),
                      reads=["P7", "TABm"], writes=["u2"])
                    A("pool", lambda g, hb=hb, sc0=sc0: g.tensor_tensor(out=QT[hb][0:32, sc0:sc0 + 512], in0=u1[0:32, :], in1=u2[0:32, :], op=ALU.add),
                      reads=["u1", "u2"], writes=[qk])
                    A("act", lambda g, hb=hb, sc0=sc0: g.activation(out=QT[hb][64:128, sc0:sc0 + 512], in_=P[7][64:128, :], func=AF.Copy),
                      reads=["P7"], writes=[qk])
                for k8 in range(4):
                    def mmv(g, h=h, k8=k8):
                        for q in range(8):
                            kb = k8 * 8 + q
                            for c in range(2):
                                r = g.matmul(P[6][:, q * 64:(q + 1) * 64], lhsT=ckvnT[:, c * T + kb * 128:c * T + (kb + 1) * 128],
                                             rhs=Wkv[:, c * 1536 + 1024 + h * 64:c * 1536 + 1024 + (h + 1) * 64], start=(c == 0), stop=(c == 1))
                        return r
                    A("pe", mmv, reads=["Wkv", "ckvnT"], writes=["P6"])
                    A("act", lambda g, hb=hb, k8=k8: g.activation(
                        out=Vg[hb].rearrange("p (k v) -> p k v", v=128)[:, k8 * 8:(k8 + 1) * 8, 0:64],
                        in_=P[6][:, :].rearrange("p (k v) -> p k v", v=64), func=AF.Copy), reads=["P6"], writes=[vk])
                for qs in range(NSB):
                    q0 = qs * 512
                    acc = 4 + qs % 2
                    ak = f"P{acc}"
                    nfull = 4 * qs
                    groups = [(kb, min(kb + 2, nfull)) for kb in range(0, nfull, 2)]
                    items = [("full", a, b) for a, b in groups] + [("diag", 4 * qs + d, d) for d in range(4)]
                    last_kb = 4 * qs + 3
                    for gi, it in enumerate(items):
                        sbank = 2 * (gi % 2)
                        sk = f"PS{gi % 2}"
                        pt = PT[pt_i % 3]
                        ptk = f"PT{pt_i % 3}"
                        pt_i += 1
                        if it[0] == "full":
                            kbs = list(range(it[1], it[2]))

                            def mms(g, hb=hb, kbs=kbs, sbank=sbank, q0=q0):
                                for n_, kb in enumerate(kbs):
                                    r = g.matmul(P[sbank + n_][:, :], lhsT=KT[hb][:, kb * 128:(kb + 1) * 128], rhs=QT[hb][:, q0:q0 + 512],
                                                 start=True, stop=True)
                                return r
                            A("pe", mms, reads=[kk, qk], writes=[sk])
                            for n_ in range(len(kbs)):
                                A("act", lambda g, pt=pt, sbank=sbank, n_=n_: g.activation(out=pt[:, n_ * 512:(n_ + 1) * 512], in_=P[sbank + n_][:, :],
                                                                                            func=AF.Exp, scale=SCALE),
                                  reads=[sk], writes=[ptk])

                            def mmpv(g, hb=hb, kbs=kbs, pt=pt, acc=acc, last_kb=last_kb):
                                for n_, kb in enumerate(kbs):
                                    r = g.matmul(P[acc][:, :], lhsT=Vg[hb][:, kb * 128:(kb + 1) * 128], rhs=pt[:, n_ * 512:(n_ + 1) * 512],
                                                 start=(kb == 0), stop=(kb == last_kb))
                                return r
                            A("pe", mmpv, reads=[vk, ptk], writes=[ak])
                        else:
                            kb, d = it[1], it[2]
                            c0 = d * 128
                            A("pe", lambda g, hb=hb, kb=kb, c0=c0, sbank=sbank, q0=q0: g.matmul(
                                P[sbank][:, c0:512], lhsT=KT[hb][:, kb * 128:(kb + 1) * 128], rhs=QT[hb][:, q0 + c0:q0 + 512], start=True, stop=True),
                              reads=[kk, qk], writes=[sk])
                            A("act", lambda g, pt=pt, sbank=sbank, c0=c0: g.activation(out=pt[:, c0:512], in_=P[sbank][:, c0:512], func=AF.Exp, scale=SCALE),
                              reads=[sk], writes=[ptk])
                            A("pool", lambda g, pt=pt, c0=c0: g.tensor_tensor(out=pt[:, c0:c0 + 128], in0=pt[:, c0:c0 + 128], in1=trib[:, :], op=ALU.mult),
                              reads=[ptk, "trib"], writes=[ptk])
                            A("pe", lambda g, hb=hb, kb=kb, c0=c0, pt=pt, acc=acc, last_kb=last_kb: g.matmul(
                                P[acc][:, c0:512], lhsT=Vg[hb][:, kb * 128:(kb + 1) * 128], rhs=pt[:, c0:512], start=(kb == 0), stop=(kb == last_kb)),
                              reads=[vk, ptk], writes=[ak])
                    A("dve", lambda g, acc=acc: g.reciprocal(out=rec[0:64, :], in_=P[acc][64:128, :]), reads=[ak], writes=["rec"])
                    r0 = 64 * (h % 2)
                    A("dve", lambda g, acc=acc, r0=r0, h=h, q0=q0: g.tensor_tensor(
                        out=ymlaT[r0:r0 + 64, (h // 2) * T + q0:(h // 2) * T + q0 + 512], in0=P[acc][0:64, :], in1=rec[0:64, :], op=ALU.mult),
                      reads=[ak, "rec"], writes=["ymlaT"])
            S.barrier()
            _phase[0] += 1
            if _phase[0] > STOP:
                raise _Stop()

            AR.off = P23
            Wo = AR.alloc(8 * D, BF16)
            wos = [AR.alloc(D), AR.alloc(D)]
            og = AR.alloc(8)
            yrTs = [AR.alloc(4 * 512, BF16), AR.alloc(4 * 512, BF16)]
            ysq = AR.alloc(4 * 128, BF16)
            xt3 = [AR.alloc(D), AR.alloc(D)]
            mB = AR.alloc(D)
            mixs = AR.alloc(D)
            tt = AR.alloc(D)
            x1 = [AR.alloc(D), AR.alloc(D)]
            junk3 = AR.alloc(D, BF16)
            rm = AR.alloc(1)
            r2 = AR.alloc(1)
            dma("sp", og, og_d, w=["og"])
            for c in range(8):
                dma("sp", wos[c % 2], wout_d[c * 128:(c + 1) * 128, :], w=[f"wos{c % 2}"])
                A("dve", lambda g, c=c: g.tensor_scalar(out=Wo[:, c * D:(c + 1) * D], in0=wos[c % 2], scalar1=og[:, c:c + 1],
                                                         scalar2=None, op0=ALU.mult), reads=[f"wos{c % 2}", "og"], writes=["Wo"])
            for tb in range(32):
                b2 = tb % 2
                tc0 = tb * 128
                sbi, j = tb // 4, tb % 4
                yb = yrTs[sbi % 2]
                ybk = f"yrT{sbi % 2}"
                if j == 0:
                    dma("sp", yb.rearrange("p (t n) -> p t n", t=4), yret_d[:, :, sbi * 512:(sbi + 1) * 512].rearrange("t p n -> p t n"),
                        r=["yret_d"], w=[ybk])
                dma("sp", xt3[b2], x_d[tc0:tc0 + 128, :], w=[f"xt3{b2}"])
                A("pool", lambda g, tc0=tc0: g.tensor_tensor(out=ysq.rearrange("p (c n) -> p c n", c=4),
                                                              in0=ymlaT.rearrange("p (c n) -> p c n", c=4)[:, :, tc0:tc0 + 128],
                                                              in1=ymlaT.rearrange("p (c n) -> p c n", c=4)[:, :, tc0:tc0 + 128], op=ALU.mult),
                  reads=["ymlaT"], writes=["ysq"])

                def mmss(g):
                    for c in range(4):
                        r = g.matmul(P[6][:, 0:1], lhsT=ysq[:, c * 128:(c + 1) * 128], rhs=onesb[:, 0:1], start=(c == 0), stop=(c == 3))
                    return r
                A("pe", mmss, reads=["ysq", "onesb"], writes=["P6"])
                rsqrt_ops(rm, P[6][:, 0:1], 1.0 / 512, ["P6"], "rm")

                def mmA(g, tc0=tc0):
                    for hf in range(2):
                        for c in range(4):
                            r = g.matmul(P[hf][:, :], lhsT=ymlaT[:, c * T + tc0:c * T + tc0 + 128], rhs=Wo[:, c * D + hf * 512:c * D + (hf + 1) * 512],
                                         start=(c == 0), stop=(c == 3))
                    return r
                A("pe", mmA, reads=["ymlaT", "Wo"], writes=["PA"])

                def mmB(g, yb=yb, j=j):
                    for hf in range(2):
                        for c in range(4):
                            r = g.matmul(P[2 + hf][:, :], lhsT=yb[:, c * 512 + j * 128:c * 512 + (j + 1) * 128],
                                         rhs=Wo[:, (4 + c) * D + hf * 512:(4 + c) * D + (hf + 1) * 512], start=(c == 0), stop=(c == 3))
                    return r
                A("pe", mmB, reads=[ybk, "Wo"], writes=["PB"])
                for hf in range(2):
                    A("act", lambda g, hf=hf: g.activation(out=mB[:, hf * 512:(hf + 1) * 512], in_=P[2 + hf][:, :], func=AF.Copy),
                      reads=["PB"], writes=["mB"])
                    A("dve", lambda g, hf=hf: g.scalar_tensor_tensor(out=mixs[:, hf * 512:(hf + 1) * 512], in0=P[hf][:, :], scalar=rm[:, 0:1],
                                                                      in1=mB[:, hf * 512:(hf + 1) * 512], op0=ALU.mult, op1=ALU.add),
                      reads=["PA", "rm", "mB"], writes=["mixs"])
                A("act", lambda g: g.activation(out=junk3, in_=mixs, func=AF.Square, accum_out=r2), reads=["mixs"], writes=["junk3", "r2"])
                rsqrt_ops(r2, r2, 1.0 / D, ["r2"], "r2")
                A("dve", lambda g: g.scalar_tensor_tensor(out=tt, in0=mixs, scalar=r2[:, 0:1], in1=G1b[:, :], op0=ALU.mult, op1=ALU.mult),
                  reads=["mixs", "r2", "G1b"], writes=["tt"])
                A("pool", lambda g, b2=b2: g.tensor_tensor(out=x1[b2], in0=xt3[b2], in1=tt, op=ALU.add), reads=[f"xt3{b2}", "tt"], writes=[f"x1{b2}"])
                dma("sp", out_d[tc0:tc0 + 128, :], x1[b2], r=[f"x1{b2}"], w=[f"out{tb}"])
            S.barrier()
            _phase[0] += 1
            if _phase[0] > STOP:
                raise _Stop()
            AR.reset()

            Wg = AR.alloc(8 * DFF, BF16)
            Wu = AR.alloc(8 * DFF, BF16)
            Wd = AR.alloc(NJ * D, BF16)
            x1t = AR.alloc(4 * D)
            xn4 = AR.alloc(D, BF16)
            junk4 = AR.alloc(D, BF16)
            s4 = [AR.alloc(1), AR.alloc(1)]
            h2T = AR.alloc(8 * 512, BF16)
            h1T = AR.alloc(NJ * 512, BF16)
            sgt = [AR.alloc(512), AR.alloc(512)]
            t4 = AR.alloc(D)
            r3 = AR.alloc(1)
            for hf in range(2):
                dma("pool", Wg.rearrange("p (c n) -> p c n", c=8)[:, hf * 4:(hf + 1) * 4, :],
                    wg_d.rearrange("(c p) n -> p c n", p=128)[:, hf * 4:(hf + 1) * 4, :], w=["Wg"])
                dma("pool", Wu.rearrange("p (c n) -> p c n", c=8)[:, hf * 4:(hf + 1) * 4, :],
                    wu_d.rearrange("(c p) n -> p c n", p=128)[:, hf * 4:(hf + 1) * 4, :], w=["Wu"])
            for hf in range(2):
                dma("pool", Wd.rearrange("p (c n) -> p c n", c=NJ)[:, hf * 11:(hf + 1) * 11, :],
                    wd_d.rearrange("(c p) n -> p c n", p=128)[:, hf * 11:(hf + 1) * 11, :], w=["Wd"])
            fin = []
            for sbi in range(NSB):
                for j in range(4):
                    tb = sbi * 4 + j
                    b2 = tb % 2
                    xv = x1t[:, j * D:(j + 1) * D]
                    dma("sp", xv, out_d[tb * 128:(tb + 1) * 128, :], r=[f"out{tb}"], w=[f"x1t{j}"])
                    A("act", lambda g, xv=xv, b2=b2: g.activation(out=junk4, in_=xv, func=AF.Square, accum_out=s4[b2]),
                      reads=[f"x1t{j}"], writes=["junk4", f"s4{b2}"])
                    rsqrt_ops(s4[b2], s4[b2], 1.0 / D, [f"s4{b2}"], f"s4{b2}")
                    A("act", lambda g, xv=xv, b2=b2: g.activation(out=xn4, in_=xv, func=AF.Copy, scale=s4[b2]),
                      reads=[f"x1t{j}", f"s4{b2}"], writes=["xn4"])

                    def tr8b(g, b2=b2):
                        for c in range(8):
                            r = g.transpose(out=Pb[b2][:, c * 128:(c + 1) * 128], in_=xn4[:, c * 128:(c + 1) * 128], identity=identb[:, :])
                        return r
                    A("pe", tr8b, reads=["xn4", "identb"], writes=[f"P{b2}"])
                    for c in range(8):
                        dst = h2T[:, c * 512 + j * 128:c * 512 + (j + 1) * 128]
                        if c % 2 == 0:
                            A("dve", lambda g, c=c, dst=dst, b2=b2: g.tensor_scalar(out=dst, in0=Pb[b2][:, c * 128:(c + 1) * 128],
                                                                                   scalar1=a2[:, c:c + 1], scalar2=sh2[:, c:c + 1],
                                                                                   op0=ALU.mult, op1=ALU.add),
                              reads=[f"P{b2}", "a2", "modc"], writes=["h2T"])
                        else:
                            A("act", lambda g, c=c, dst=dst, b2=b2: g.activation(out=dst, in_=Pb[b2][:, c * 128:(c + 1) * 128], func=AF.Identity,
                                                                                scale=a2[:, c:c + 1], bias=sh2[:, c:c + 1]),
                              reads=[f"P{b2}", "a2", "modc"], writes=["h2T"])
                for jj in range(NJ):
                    gb = 2 + jj % 2
                    ub = 4 + jj % 2

                    def mmg(g, jj=jj, gb=gb, ub=ub):
                        for c in range(8):
                            g.matmul(P[gb][:, :], lhsT=Wg[:, c * DFF + jj * 128:c * DFF + (jj + 1) * 128], rhs=h2T[:, c * 512:(c + 1) * 512],
                                     start=(c == 0), stop=(c == 7))
                        for c in range(8):
                            r = g.matmul(P[ub][:, :], lhsT=Wu[:, c * DFF + jj * 128:c * DFF + (jj + 1) * 128], rhs=h2T[:, c * 512:(c + 1) * 512],
                                         start=(c == 0), stop=(c == 7))
                        return r
                    A("pe", mmg, reads=["Wg", "Wu", "h2T"], writes=[f"P{gb}", f"P{ub}"])
                    A("act", lambda g, jj=jj, gb=gb: g.activation(out=sgt[jj % 2], in_=P[gb][:, :], func=AF.Silu), reads=[f"P{gb}"], writes=[f"sgt{jj % 2}"])
                    A("dve", lambda g, jj=jj, ub=ub: g.tensor_tensor(out=h1T[:, jj * 512:(jj + 1) * 512], in0=P[ub][:, :], in1=sgt[jj % 2], op=ALU.mult),
                      reads=[f"P{ub}", f"sgt{jj % 2}"], writes=["h1T"])
                for j in range(4):
                    tb = sbi * 4 + j
                    xv = x1t[:, j * D:(j + 1) * D]

                    def mmd(g, j=j):
                        for hf in range(2):
                            for jj in range(NJ):
                                r = g.matmul(P[6 + hf][:, :], lhsT=h1T[:, jj * 512 + j * 128:jj * 512 + (j + 1) * 128],
                                             rhs=Wd[:, jj * D + hf * 512:jj * D + (hf + 1) * 512], start=(jj == 0), stop=(jj == NJ - 1))
                        return r
                    A("pe", mmd, reads=["h1T", "Wd"], writes=["PF"])
                    for hf in range(2):
                        A("act", lambda g, hf=hf: g.activation(out=t4[:, hf * 512:(hf + 1) * 512], in_=P[6 + hf][:, :], func=AF.Copy),
                          reads=["PF"], writes=["t4"])
                    A("act", lambda g: g.activation(out=junk4, in_=t4, func=AF.Square, accum_out=r3), reads=["t4"], writes=["junk4", "r3"])
                    rsqrt_ops(r3, r3, 1.0 / D, ["r3"], "r3")
                    A("dve", lambda g: g.scalar_tensor_tensor(out=t4, in0=t4, scalar=r3[:, 0:1], in1=G2b[:, :], op0=ALU.mult, op1=ALU.mult),
                      reads=["t4", "r3", "G2b"], writes=["t4"])
                    A("pool", lambda g, xv=xv: g.tensor_tensor(out=xv, in0=xv, in1=t4, op=ALU.add), reads=[f"x1t{j}", "t4"], writes=[f"x1t{j}"])
                    fin.append(dma("sp", out_d[tb * 128:(tb + 1) * 128, :], xv, r=[f"x1t{j}"], w=[f"out{tb}"]))
            A("sp", lambda g: None, deps=fin)

        except _Stop:
            pass
        with nc.Block() as block:
            S.emit_all(block, esem, dsem)
    return nc


def _consts():
    f = np.float32
    gam = 1.0 - 2.0 ** (-5.0 - np.arange(4, dtype=np.float64))
    idx = np.arange(128)
    ident = np.eye(128, dtype=f)
    tri = (idx[None, :] >= idx[:, None]).astype(f)
    dtc = np.zeros((128, 4, 128), np.float64)
    rel = idx[None, :] - idx[:, None]
    for h in range(4):
        dtc[:, h, :] = np.where(rel >= 0, gam[h] ** np.maximum(rel, 0), 0.0) * 0.125
    wqc = np.zeros((128, 2, 512), np.float64)
    wkc = np.zeros((128, 2, 128), np.float64)
    decc = np.zeros((128, 2), np.float64)
    for i in range(2):
        for r in range(128):
            h = 2 * i + r // 64
            wqc[r, i, :] = np.tile(gam[h] ** (idx + 1.0), 4)
            decc[r, i] = gam[h] ** 128
        for ft in range(128):
            h = 2 * i + ft // 64
            wkc[:, i, ft] = gam[h] ** (127.0 - idx) * 0.125
    inv_m = 10000.0 ** (-np.arange(16, dtype=np.float64) / 16.0)
    inv_r = 10000.0 ** (-np.arange(32, dtype=np.float64) / 32.0)
    invc = np.zeros((128, 3), np.float64)
    phc = np.zeros((128, 3), np.float64)
    for r in range(64):
        invc[r, 0] = inv_m[r % 16]
        phc[r, 0] = np.pi / 2 if r < 32 else (np.pi if r < 48 else 0.0)
    for r in range(128):
        invc[r, 1] = inv_r[r % 32]
        invc[r, 2] = inv_r[r % 32]
        phc[r, 1] = np.pi / 2
        phc[r, 2] = np.pi if (r % 64) < 32 else 0.0
    return dict(ident=ident, tri=tri, dtc=dtc.reshape(128, 512).astype(f), wqc=wqc.reshape(128, 1024).astype(f),
                wkc=wkc.reshape(128, 256).astype(f), decc=decc.astype(f), invc=invc.astype(f), phc=phc.astype(f))


def _colmajor(v, n):
    return np.ascontiguousarray(np.asarray(v, np.float32).reshape(n, 128).T)


def _prep_shared(inp):
    f = np.float32
    w_in = np.asarray(inp["w_in"], f)[0]
    cols = list(range(0, 640))
    cols += list(range(640, 672)) + [640 + k for k in list(range(16, 32)) + list(range(0, 16))]
    for base in (672, 928):
        for i in range(2):
            nat, sw = [], []
            for hh in (2 * i, 2 * i + 1):
                b = base + hh * 64
                nat += list(range(b, b + 64))
                sw += list(range(b + 32, b + 64)) + list(range(b, b + 32))
            cols += nat + sw
    cols += list(range(1184, 2208))
    w1 = np.ascontiguousarray(w_in[:, cols])
    assert w1.shape[1] == NC1
    wqb = np.asarray(inp["w_q_b"], f)[0]
    qc = []
    for h in range(8):
        b = h * 96
        qc += list(range(b + 64, b + 96)) + [b + 64 + k for k in list(range(16, 32)) + list(range(0, 16))] + list(range(b, b + 64))
    wq = np.ascontiguousarray(wqb[:, qc])
    wkvb = np.asarray(inp["w_kv_b"], f)[0]
    wkv = np.zeros((256, 1536), f)
    for h in range(8):
        wkv[:, h * 128 + 64:h * 128 + 128] = wkvb[:, h * 128:h * 128 + 64]
        wkv[:, 1024 + h * 64:1024 + (h + 1) * 64] = wkvb[:, h * 128 + 64:h * 128 + 128]
    sh = dict(
        w_ada=np.ascontiguousarray(np.asarray(inp["w_ada"], f)[0]),
        b_ada=np.ascontiguousarray(np.asarray(inp["b_ada"], f)[0][None, :]),
        gpre1=_colmajor(inp["pre_norm_mix"][0], 8), gpre2=_colmajor(inp["pre_norm_ffn"][0], 8),
        gpost1=np.ascontiguousarray(np.asarray(inp["post_norm_mix"], f)[0][None, :]),
        gpost2=np.ascontiguousarray(np.asarray(inp["post_norm_ffn"], f)[0][None, :]),
        qg=_colmajor(inp["q_a_norm"][0], 3), kvg=_colmajor(inp["kv_a_norm"][0], 2),
        og=_colmajor(np.concatenate([np.asarray(inp["mla_out_norm"], f)[0], np.asarray(inp["ret_gn_gain"], f)[0]]), 8),
        w1=w1, wq=wq, wkv=wkv,
        wout=np.ascontiguousarray(np.asarray(inp["w_out"], f)[0]),
        wg=np.ascontiguousarray(np.asarray(inp["w_gate"], f)[0]),
        wu=np.ascontiguousarray(np.asarray(inp["w_up"], f)[0]),
        wd=np.ascontiguousarray(np.asarray(inp["w_down"], f)[0]),
    )
    sh.update(_consts())
    return sh


def make_in_maps(inp, cores):
    sh = _prep_shared(inp)
    x = np.asarray(inp["x"], np.float32)
    c = np.asarray(inp["c"], np.float32)
    pos = np.asarray(inp["positions"], np.int32)
    maps = []
    for b in cores:
        m = dict(sh)
        m["x"] = np.ascontiguousarray(x[b])
        m["cT"] = _colmajor(c[b], 8)
        m["pos"] = np.ascontiguousarray(pos[b][None, :])
        maps.append(m)
    return maps


_NC = None


def kernel(**inputs):
    global _NC
    if _NC is None:
        _NC = build_nc()
    maps = make_in_maps(inputs, list(range(8)))
    res = run_bass_kernel_spmd(_NC, maps, core_ids=list(range(8)))
    return np.stack([np.asarray(r["out"], np.float32) for r in res.results], axis=0)
```

```python
import contextlib
import types
import numpy as np
import concourse.bass as bass
import concourse.mybir as mybir
from concourse.bass_utils import run_bass_kernel_spmd

F32 = mybir.dt.float32
BF16 = mybir.dt.bfloat16
I32 = mybir.dt.int32
AF = mybir.ActivationFunctionType
ALU = mybir.AluOpType
AX = mybir.AxisListType

ENGS = ("pe", "act", "dve", "pool", "sp")
STOP = 99
SUB = 99
HSEL = (0, 1, 2, 3)
TAPS = ()
REORDER = True


class _Stop(Exception):
    pass

T = 4096
D = 1024
NSB = 8
DFF = 2816
NJ = 22
NC1 = 2752
EPS = 1e-6
PI = float(np.pi)


def _freeze(fn):
    if fn.__closure__ is None:
        return fn
    cells = []
    for c in fn.__closure__:
        try:
            cells.append(types.CellType(c.cell_contents))
        except ValueError:
            cells.append(c)
    return types.FunctionType(fn.__code__, fn.__globals__, fn.__name__, fn.__defaults__, tuple(cells))


class Op:
    __slots__ = ("eng", "idx", "emit", "deps", "is_dma", "dma_i", "marked", "count", "clock", "waits", "is_bar", "busy", "lat", "seq")


def _nfree(ap):
    n = 1
    for d in ap.shape[1:]:
        n *= int(d)
    return n


class _Fake:
    def __init__(self, eng):
        self.eng = eng
        self.busy = 0.0
        self.lat = None

    def matmul(self, out, lhsT=None, rhs=None, **kw):
        n = max(_nfree(rhs), 64)
        f = 4.0 if rhs.dtype == F32 else 1.0
        self.busy += f * n / 1950.0 + 0.012
        return self

    def transpose(self, out=None, in_=None, identity=None, **kw):
        self.busy += 0.08
        return self

    def activation(self, out=None, in_=None, **kw):
        self.busy += 0.22 + _nfree(in_) / 1150.0
        return self

    def dma_start(self, out=None, in_=None, **kw):
        nb = _nfree(out) * int(out.shape[0]) * (4 if out.dtype in (F32, I32) else 2)
        self.busy += 0.15 if self.eng == "sp" else 1.2
        self.lat = 2.5 + nb / 150e3
        return self

    def _dve(self, out, **kw):
        n = _nfree(out)
        if self.eng == "pool":
            self.busy += 0.2 + n / 500.0
        else:
            self.busy += 0.12 + n / 900.0
        return self

    def tensor_tensor(self, out=None, **kw):
        return self._dve(out)

    def tensor_scalar(self, out=None, **kw):
        return self._dve(out)

    def tensor_copy(self, out=None, **kw):
        return self._dve(out)

    def scalar_tensor_tensor(self, out=None, **kw):
        return self._dve(out)

    def tensor_single_scalar(self, out=None, **kw):
        return self._dve(out)

    def reciprocal(self, out=None, **kw):
        return self._dve(out)

    def memset(self, ap, *a, **kw):
        return self._dve(ap)

    def reduce_sum(self, out=None, in_=None, **kw):
        return self._dve(in_)

    def then_inc(self, *a, **kw):
        return self


class Sched:
    def __init__(self, n_dma_sems=12):
        self.ops = {e: [] for e in ENGS}
        self.order = []
        self.lastw = {}
        self.readers = {}
        self.n_dma_sems = n_dma_sems
        self.dma_ops = {e: [] for e in ENGS}
        self.dma_since_bar = []

    def add(self, eng, emit, reads=(), writes=(), dma=False, deps=()):
        op = Op()
        op.eng = eng
        op.emit = _freeze(emit)
        op.is_dma = dma
        op.marked = False
        op.count = 0
        op.idx = len(self.ops[eng])
        d = set(deps)
        for k in reads:
            w = self.lastw.get(k)
            if w is not None:
                d.add(w)
        for k in writes:
            w = self.lastw.get(k)
            if w is not None:
                d.add(w)
            for r in self.readers.get(k, ()):
                d.add(r)
        for k in reads:
            self.readers.setdefault(k, []).append(op)
        for k in writes:
            self.lastw[k] = op
            self.readers[k] = []
        d.discard(op)
        op.deps = d
        op.is_bar = False
        op.seq = len(self.order)
        self.ops[eng].append(op)
        self.order.append(op)
        return op

    def barrier(self):
        for e in ENGS:
            self.add(e, lambda g: None).is_bar = True

    def _list_schedule(self, seg):
        import heapq
        segset = set(seg)
        succ = {o: [] for o in seg}
        indeg = {}
        for o in seg:
            fk = _Fake(o.eng)
            o.emit(fk)
            o.busy = fk.busy
            o.lat = fk.lat if fk.lat is not None else fk.busy + 0.06
            k = 0
            for d in o.deps:
                if d in segset:
                    succ[d].append(o)
                    k += 1
            indeg[o] = k
        bl = {}
        for o in reversed(seg):
            m = 0.0
            for s_ in succ[o]:
                if bl[s_] > m:
                    m = bl[s_]
            bl[o] = o.lat + m
        free = {e: 0.0 for e in ENGS}
        avail = {e: [] for e in ENGS}
        future = {e: [] for e in ENGS}
        rtime = {o: 0.0 for o in seg}
        for o in seg:
            if indeg[o] == 0:
                heapq.heappush(future[o.eng], (0.0, o.seq, o))
        out = []
        n = len(seg)
        while len(out) < n:
            best_e, best_t = None, None
            for e in ENGS:
                fu, av = future[e], avail[e]
                while fu and fu[0][0] <= free[e]:
                    _, sq, o = heapq.heappop(fu)
                    heapq.heappush(av, (-bl[o], sq, o))
                if av:
                    t = free[e]
                elif fu:
                    t = fu[0][0]
                else:
                    continue
                if best_t is None or t < best_t:
                    best_e, best_t = e, t
            e = best_e
            if not avail[e]:
                free[e] = best_t
                fu, av = future[e], avail[e]
                while fu and fu[0][0] <= free[e]:
                    _, sq, o = heapq.heappop(fu)
                    heapq.heappush(av, (-bl[o], sq, o))
            _, sq, o = heapq.heappop(avail[e])
            st = free[e]
            free[e] = st + o.busy
            fin = st + o.lat
            out.append(o)
            for s_ in succ[o]:
                if fin > rtime[s_]:
                    rtime[s_] = fin
                indeg[s_] -= 1
                if indeg[s_] == 0:
                    heapq.heappush(future[s_.eng], (rtime[s_], s_.seq, s_))
        return out

    def schedule(self, reorder=True):
        segs, cur = [], []
        for o in self.order:
            if o.is_bar:
                if cur:
                    segs.append(("seg", cur))
                    cur = []
                if segs and segs[-1][0] == "bar":
                    segs[-1][1].append(o)
                else:
                    segs.append(("bar", [o]))
            else:
                cur.append(o)
        if cur:
            segs.append(("seg", cur))
        new = []
        last_seg = []
        for kind, lst in segs:
            if kind == "seg":
                lst2 = self._list_schedule(lst) if reorder else lst
                new += lst2
                last_seg = lst2
            else:
                deps = [o for o in last_seg if o.is_dma]
                for e in ENGS:
                    for o in reversed(last_seg):
                        if o.eng == e and not o.is_dma:
                            deps.append(o)
                            break
                for o in lst:
                    o.deps = set(deps)
                new += lst
        self.order = new
        self.ops = {e: [] for e in ENGS}
        self.dma_ops = {e: [] for e in ENGS}
        for o in new:
            o.idx = len(self.ops[o.eng])
            self.ops[o.eng].append(o)
            if o.is_dma:
                o.dma_i = len(self.dma_ops[o.eng])
                if o.dma_i >= self.n_dma_sems:
                    o.deps.add(self.dma_ops[o.eng][o.dma_i - self.n_dma_sems])
                self.dma_ops[o.eng].append(o)

    def resolve(self):
        known = {e: {f: -1 for f in ENGS} for e in ENGS}
        known_dma = {e: set() for e in ENGS}
        for op in self.order:
            e = op.eng
            kn = known[e]
            waits = []
            for d in sorted(op.deps, key=lambda o: -o.idx):
                if d.is_dma:
                    if d in known_dma[e]:
                        continue
                    known_dma[e].add(d)
                    waits.append(d)
                else:
                    if d.eng == "pe" and e == "pe":
                        continue
                    if kn[d.eng] >= d.idx:
                        continue
                    d.marked = True
                    waits.append(d)
                ck = d.clock
                for f in ENGS:
                    if ck[f] > kn[f]:
                        kn[f] = ck[f]
            op.waits = waits
            ck = dict(kn)
            if not op.is_dma:
                ck[e] = max(ck[e], op.idx)
            op.clock = ck
        for e in ENGS:
            c = 0
            for op in self.ops[e]:
                if op.marked:
                    c += 1
                    op.count = c

    def emit_all(self, block, esem, dsem):
        self.schedule(reorder=REORDER)
        self.resolve()
        n = self.n_dma_sems

        def run(e, engobj):
            for op in self.ops[e]:
                for d in op.waits:
                    if d.is_dma:
                        engobj.wait_ge(dsem[d.eng][d.dma_i % n], 16 * (d.dma_i // n + 1))
                    else:
                        engobj.wait_ge(esem[d.eng], d.count)
                ins = op.emit(engobj)
                if op.is_dma:
                    ins.then_inc(dsem[e][op.dma_i % n], 16)
                elif op.marked:
                    if ins is None:
                        ins = engobj.nop()
                    ins.then_inc(esem[e], 1)

        block.tensor(lambda t: run("pe", t))
        block.scalar(lambda t: run("act", t))
        block.vector(lambda t: run("dve", t))
        block.gpsimd(lambda t: run("pool", t))
        block.sync(lambda t: run("sp", t))


class Arena:
    def __init__(self, ap, ncols):
        self.ap = ap
        self.n = ncols
        self.off = 0

    def alloc(self, cols, dt=F32):
        nb = cols * (4 if dt in (F32, I32) else 2)
        n32 = ((nb + 31) // 32) * 8
        assert self.off + n32 <= self.n, ("arena overflow", self.off, n32, self.n)
        v = self.ap[:, self.off:self.off + n32]
        self.off += n32
        if dt != F32:
            v = v.bitcast(dt)
        return v[:, 0:cols]

    def reset(self):
        self.off = 0


def build_nc():
    nc = bass.Bass("TRN2", target_bir_lowering=False)

    def DI(name, shape, dt=F32):
        return nc.dram_tensor(name, shape, dt, kind="ExternalInput").ap()

    x_d = DI("x", [T, D])
    c_d = DI("cT", [128, 8])
    pos_d = DI("pos", [1, T], I32)
    wada_d = DI("w_ada", [D, 6 * D])
    bada_d = DI("b_ada", [1, 6 * D])
    gpre1_d = DI("gpre1", [128, 8])
    gpre2_d = DI("gpre2", [128, 8])
    gpost1_d = DI("gpost1", [1, D])
    gpost2_d = DI("gpost2", [1, D])
    qg_d = DI("qg", [128, 3])
    kvg_d = DI("kvg", [128, 2])
    og_d = DI("og", [128, 8])
    w1_d = DI("w1", [D, NC1])
    wq_d = DI("wq", [384, 1024])
    wkv_d = DI("wkv", [256, 1536])
    wout_d = DI("wout", [D, D])
    wg_d = DI("wg", [D, DFF])
    wu_d = DI("wu", [D, DFF])
    wd_d = DI("wd", [DFF, D])
    ident_d = DI("ident", [128, 128])
    tri_d = DI("tri", [128, 128])
    dt_d = DI("dtc", [128, 512])
    wqc_d = DI("wqc", [128, 1024])
    wkc_d = DI("wkc", [128, 256])
    dec_d = DI("decc", [128, 2])
    inv_d = DI("invc", [128, 3])
    ph_d = DI("phc", [128, 3])
    out_d = nc.dram_tensor("out", [T, D], F32, kind="ExternalOutput").ap()
    yret_d = nc.dram_tensor("yret_scr", [4, 128, T], BF16).ap()

    S = Sched(n_dma_sems=12)
    A = S.add

    with contextlib.ExitStack() as ctx:
        def sbt(name, cols, dt=F32, parts=128):
            return ctx.enter_context(nc.sbuf_tensor(name, [parts, cols], dt))

        identb = sbt("identb", 128, BF16)
        trib = sbt("trib", 128, BF16)
        onesb = sbt("onesb", 128, BF16)
        onesf = sbt("onesf", 128)
        epst = sbt("epst", 1)
        DECc = sbt("DECc", 2)
        INVc = sbt("INVc", 3)
        PHc = sbt("PHc", 3)
        modc = sbt("modc", 32)
        a1 = sbt("a1", 8)
        a2 = sbt("a2", 8)
        G1b = sbt("G1b", D)
        G2b = sbt("G2b", D)
        ARN = 50000
        arena_t = sbt("arena", ARN)
        AR = Arena(arena_t, ARN)
        P = [ctx.enter_context(nc.psum_tensor(f"bank{i}", [128, 512], F32)) for i in range(8)]
        Pb = [p[:, :].bitcast(BF16) for p in P]

        esem = {e: ctx.enter_context(nc.semaphore("es_" + e)) for e in ENGS}
        dsem = {e: [ctx.enter_context(nc.semaphore(f"ds_{e}{i}")) for i in range(12)] for e in ("sp", "pool")}

        def dma(q, out, in_, r=(), w=()):
            return A(q, lambda g: g.dma_start(out=out, in_=in_), reads=r, writes=w, dma=True)

        def tap(name, ap, keys):
            if name not in TAPS:
                return
            shp = list(ap.shape)
            dd = nc.dram_tensor("dbg_" + name, shp, ap.dtype, kind="ExternalOutput").ap()
            dma("sp", dd, ap, r=keys)

        def rsqrt_ops(dst, src, scale, rk, wk):
            A("act", lambda g: g.activation(out=dst, in_=src, func=AF.Sqrt, scale=scale, bias=epst[0:dst.shape[0], :]),
              reads=list(rk) + ["epst"], writes=[wk])
            A("dve", lambda g: g.reciprocal(out=dst, in_=dst), reads=[wk], writes=[wk])

        _phase = [0]
        try:
            dma("pool", identb[:, :], ident_d, w=["identb"])
            dma("pool", trib[:, :], tri_d, w=["trib"])
            A("pool", lambda g: g.memset(onesb[:, :], 1.0), writes=["onesb"])
            A("pool", lambda g: g.memset(onesf[:, :], 1.0), writes=["onesf"])
            A("pool", lambda g: g.memset(epst[:, :], EPS), writes=["epst"])
            for t_, d_, k_ in ((DECc, dec_d, "DECc"), (INVc, inv_d, "INVc"), (PHc, ph_d, "PHc")):
                dma("sp", t_[:, :], d_, w=[k_])
            cT = AR.alloc(8)
            gp1 = AR.alloc(8)
            gp2 = AR.alloc(8)
            scb = AR.alloc(8, BF16)
            gpo1 = AR.alloc(D)
            gpo2 = AR.alloc(D)
            bada = AR.alloc(6 * D)
            modrow = AR.alloc(6 * D)
            grow1 = AR.alloc(D)
            grow2 = AR.alloc(D)
            wa = [AR.alloc(8 * 512, BF16), AR.alloc(8 * 512, BF16)]
            dma("sp", cT, c_d, w=["cT"])
            dma("sp", gp1, gpre1_d, w=["gp1"])
            dma("sp", gp2, gpre2_d, w=["gp2"])
            dma("sp", gpo1[0:1, :], gpost1_d, w=["gpo1"])
            dma("sp", gpo2[0:1, :], gpost2_d, w=["gpo2"])
            dma("sp", bada[0:1, :], bada_d, w=["bada"])
            A("act", lambda g: g.activation(out=scb, in_=cT, func=AF.Silu), reads=["cT"], writes=["scb"])
            for gi in range(12):
                wb = wa[gi % 2]
                dma("pool", wb.rearrange("p (c n) -> p c n", c=8),
                    wada_d[:, gi * 512:(gi + 1) * 512].rearrange("(c p) n -> p c n", p=128), w=[f"wa{gi % 2}"])

                def mm_ada(g, gi=gi, wb=wb):
                    for k in range(8):
                        r = g.matmul(P[gi % 2][0:1, :], lhsT=scb[:, k:k + 1], rhs=wb[:, k * 512:(k + 1) * 512],
                                     start=(k == 0), stop=(k == 7))
                    return r
                A("pe", mm_ada, reads=["scb", f"wa{gi % 2}"], writes=[f"P{gi % 2}"])
                A("dve", lambda g, gi=gi: g.tensor_tensor(out=modrow[0:1, gi * 512:(gi + 1) * 512], in0=P[gi % 2][0:1, :],
                                                         in1=bada[0:1, gi * 512:(gi + 1) * 512], op=ALU.add),
                  reads=[f"P{gi % 2}", "bada"], writes=["modrow"])
            col_offs = [0 * D, 1 * D, 3 * D, 4 * D]

            def mm_cols(g):
                for vi, off in enumerate(col_offs):
                    for c in range(8):
                        r = g.matmul(P[2][:, vi * 8 + c:vi * 8 + c + 1], lhsT=modrow[0:1, off + c * 128:off + (c + 1) * 128],
                                     rhs=onesf[0:1, 0:1], start=True, stop=True)
                return r
            A("pe", mm_cols, reads=["modrow", "onesf"], writes=["P2"])
            A("dve", lambda g: g.tensor_copy(out=modc[:, :], in_=P[2][:, 0:32]), reads=["P2"], writes=["modc"])
            A("dve", lambda g: g.scalar_tensor_tensor(out=a1[:, :], in0=modc[:, 8:16], scalar=1.0, in1=gp1, op0=ALU.add, op1=ALU.mult),
              reads=["modc", "gp1"], writes=["a1"])
            A("dve", lambda g: g.scalar_tensor_tensor(out=a2[:, :], in0=modc[:, 24:32], scalar=1.0, in1=gp2, op0=ALU.add, op1=ALU.mult),
              reads=["modc", "gp2"], writes=["a2"])
            sh1 = modc[:, 0:8]
            sh2 = modc[:, 16:24]
            A("dve", lambda g: g.tensor_tensor(out=grow1[0:1, :], in0=modrow[0:1, 2 * D:3 * D], in1=gpo1[0:1, :], op=ALU.mult),
              reads=["modrow", "gpo1"], writes=["grow1"])
            A("dve", lambda g: g.tensor_tensor(out=grow2[0:1, :], in0=modrow[0:1, 5 * D:6 * D], in1=gpo2[0:1, :], op=ALU.mult),
              reads=["modrow", "gpo2"], writes=["grow2"])
            for gi, (grow, Gb, gk) in enumerate(((grow1, G1b, "G1b"), (grow2, G2b, "G2b"))):
                for hf in range(2):
                    bk = 3 + hf
                    A("pe", lambda g, grow=grow, hf=hf, bk=bk: g.matmul(P[bk][:, :], lhsT=onesf[0:1, 0:128],
                                                                        rhs=grow[0:1, hf * 512:(hf + 1) * 512], start=True, stop=True),
                      reads=[f"grow{gi + 1}", "onesf"], writes=[f"P{bk}"])
                    A("act", lambda g, Gb=Gb, hf=hf, bk=bk: g.activation(out=Gb[:, hf * 512:(hf + 1) * 512], in_=P[bk][:, :], func=AF.Copy),
                      reads=[f"P{bk}"], writes=[gk])
            S.barrier()
            _phase[0] += 1
            if _phase[0] > STOP:
                raise _Stop()
            AR.reset()

            cqnT = AR.alloc(3 * T, BF16)
            ckvnT = AR.alloc(2 * T, BF16)
            TABm = AR.alloc(T)
            kpeT = TABm[64:96, 0:2048].bitcast(BF16)
            P12 = AR.off
            DTc = AR.alloc(512)
            WQc = AR.alloc(1024)
            WKc = AR.alloc(256)
            dma("sp", DTc, dt_d, w=["DTc"])
            dma("sp", WQc, wqc_d, w=["WQc"])
            dma("sp", WKc, wkc_d, w=["WKc"])
            W1 = AR.alloc(8 * NC1, BF16)
            xt = [AR.alloc(D), AR.alloc(D)]
            xn = [AR.alloc(D, BF16), AR.alloc(D, BF16)]
            junk = AR.alloc(D, BF16)
            ssq = [AR.alloc(1), AR.alloc(1)]
            hT = AR.alloc(8 * 512, BF16)
            cqraw = AR.alloc(3 * 512)
            ckvraw = AR.alloc(2 * 512)
            sq = AR.alloc(3 * 512, BF16)
            sq2 = AR.alloc(2 * 512, BF16)
            Rq = AR.alloc(512)
            Rkv = Rq
            posi = AR.alloc(512, I32)
            posf = AR.alloc(512)
            ang = AR.alloc(512)
            ni = posi
            nf = AR.alloc(512)
            msk = nf
            Cr = AR.alloc(512)
            Sr = AR.alloc(512)
            t1 = [AR.alloc(512)] * 2
            t2 = [AR.alloc(512)] * 2
            rqT = AR.alloc(2 * 512, BF16)
            rkT = AR.alloc(2 * 512, BF16)
            qwT = AR.alloc(2 * 512, BF16)
            rqm = AR.alloc(2 * 512, BF16)
            qwm = AR.alloc(2 * 512, BF16)
            mcol = AR.alloc(1)
            A("pool", lambda g: g.memset(mcol[0:64, :], 1.0), writes=["mcol"])
            A("pool", lambda g: g.memset(mcol[64:128, :], 0.0), writes=["mcol"])
            vtok = AR.alloc(4 * 512, BF16)
            sg = AR.alloc(4 * 512, BF16)
            kwtok = AR.alloc(256, BF16)
            scTm = AR.alloc(512, BF16)
            osb = AR.alloc(512)
            ynorm = AR.alloc(512)
            osq = ynorm
            ytok = AR.alloc(512, BF16)
            ysT = AR.alloc(4 * 512, BF16)
            Sf = AR.alloc(256)
            Sbf = AR.alloc(256, BF16)
            st = {k: AR.alloc(4) for k in ("osum", "osqs", "mean", "msq", "var", "rgn")}

            for hf in range(2):
                dma("pool", W1.rearrange("p (c n) -> p c n", c=8)[:, hf * 4:(hf + 1) * 4, :],
                    w1_d.rearrange("(c p) n -> p c n", p=128)[:, hf * 4:(hf + 1) * 4, :], w=["W1"])
            A("pool", lambda g: g.memset(Sf, 0.0), writes=["Sf"])
            A("pool", lambda g: g.memset(Sbf, 0.0), writes=["Sbf"])

            def ck(n):
                if SUB == n:
                    raise _Stop()
            ck(0)

            def w1s(c, off, n):
                return W1[:, c * NC1 + off:c * NC1 + off + n]

            def table(dst, dk, col, sbi):
                A("dve", lambda g: g.tensor_scalar(out=ang, in0=posf, scalar1=INVc[:, col:col + 1], scalar2=PHc[:, col:col + 1],
                                                   op0=ALU.mult, op1=ALU.add), reads=["posf", "INVc", "PHc"], writes=["ang"])
                A("dve", lambda g: g.tensor_scalar(out=ni, in0=ang, scalar1=float(1.0 / (2 * PI)), scalar2=None, op0=ALU.mult),
                  reads=["ang"], writes=["ibuf"])
                A("dve", lambda g: g.tensor_copy(out=nf, in_=ni), reads=["ibuf"], writes=["nf"])
                A("dve", lambda g: g.scalar_tensor_tensor(out=ang, in0=nf, scalar=-2 * PI, in1=ang, op0=ALU.mult, op1=ALU.add),
                  reads=["nf", "ang"], writes=["ang"])
                A("dve", lambda g: g.tensor_single_scalar(out=msk, in_=ang, scalar=PI, op=ALU.is_gt), reads=["ang", "nf"], writes=["nf"])
                A("dve", lambda g: g.scalar_tensor_tensor(out=ang, in0=msk, scalar=-2 * PI, in1=ang, op0=ALU.mult, op1=ALU.add),
                  reads=["nf", "ang"], writes=["ang"])
                A("dve", lambda g: g.tensor_scalar(out=ang, in0=ang, scalar1=-3.14159, scalar2=3.14159, op0=ALU.max, op1=ALU.min),
                  reads=["ang"], writes=["ang"])
                np_ = dst.shape[0]
                A("act", lambda g: g.activation(out=dst, in_=ang[0:np_, :], func=AF.Sin), reads=["ang"], writes=[dk])

            mtiles = [(0, 128, "cq", 0), (128, 128, "cq", 1), (256, 128, "cq", 2), (384, 128, "ckv", 0), (512, 128, "ckv", 1),
                      (640, 64, "kpe", 0)]
            o_ = 704
            for nm in ("rq", "rk"):
                for i in range(2):
                    mtiles.append((o_, 128, nm + "n", i))
                    mtiles.append((o_ + 128, 128, nm + "s", i))
                    o_ += 256
            RV = 1728
            RG = 2240

            for sbi in range(NSB):
                sc0 = sbi * 512
                dma("sp", posi, bass.AP(pos_d.tensor, sc0, [[0, 128], [1, 512]]), w=["ibuf"])
                A("dve", lambda g: g.tensor_copy(out=posf, in_=posi), reads=["ibuf"], writes=["posf"])
                table(TABm[0:64, sc0:sc0 + 512], "TABm", 0, sbi)
                table(Cr, "Cr", 1, sbi)
                table(Sr, "Sr", 2, sbi)
                ck(1)
                for j in range(4):
                    tb = sbi * 4 + j
                    b2 = tb % 2
                    dma("sp", xt[b2], x_d[tb * 128:(tb + 1) * 128, :], w=[f"xt{b2}"])
                    A("act", lambda g, b2=b2: g.activation(out=junk, in_=xt[b2], func=AF.Square, accum_out=ssq[b2]),
                      reads=[f"xt{b2}"], writes=["junk", f"ssq{b2}"])
                    rsqrt_ops(ssq[b2], ssq[b2], 1.0 / D, [f"ssq{b2}"], f"ssq{b2}")
                    A("act", lambda g, b2=b2: g.activation(out=xn[b2], in_=xt[b2], func=AF.Copy, scale=ssq[b2]),
                      reads=[f"xt{b2}", f"ssq{b2}"], writes=[f"xn{b2}"])

                    def tr8(g, b2=b2):
                        for c in range(8):
                            r = g.transpose(out=Pb[b2][:, c * 128:(c + 1) * 128], in_=xn[b2][:, c * 128:(c + 1) * 128], identity=identb[:, :])
                        return r
                    A("pe", tr8, reads=[f"xn{b2}", "identb"], writes=[f"P{b2}"])
                    for c in range(8):
                        dst = hT[:, c * 512 + j * 128:c * 512 + (j + 1) * 128]
                        if c % 2 == 0:
                            A("dve", lambda g, c=c, dst=dst, b2=b2: g.tensor_scalar(out=dst, in0=Pb[b2][:, c * 128:(c + 1) * 128],
                                                                                   scalar1=a1[:, c:c + 1], scalar2=sh1[:, c:c + 1],
                                                                                   op0=ALU.mult, op1=ALU.add),
                              reads=[f"P{b2}", "a1", "modc"], writes=["hT"])
                        else:
                            A("act", lambda g, c=c, dst=dst, b2=b2: g.activation(out=dst, in_=Pb[b2][:, c * 128:(c + 1) * 128], func=AF.Identity,
                                                                                scale=a1[:, c:c + 1], bias=sh1[:, c:c + 1]),
                              reads=[f"P{b2}", "a1", "modc"], writes=["hT"])
                ck(2)
                for mi, (off, M, kind, i) in enumerate(mtiles):
                    bk = 2 + mi % 2
                    pk = f"P{bk}"

                    def mmz(g, off=off, M=M, bk=bk):
                        for c in range(8):
                            r = g.matmul(P[bk][0:M, :], lhsT=w1s(c, off, M), rhs=hT[:, c * 512:(c + 1) * 512], start=(c == 0), stop=(c == 7))
                        return r
                    A("pe", mmz, reads=["W1", "hT"], writes=[pk])
                    if kind in ("cq", "ckv"):
                        raw, sqt, nt, Rt, bank, scl, dstT, rk_ = ((cqraw, sq, 3, Rq, 4, 1.0 / 384, cqnT, "Rq") if kind == "cq"
                                                                  else (ckvraw, sq2, 2, Rkv, 5, 1.0 / 256, ckvnT, "Rq"))
                        A("act", lambda g, raw=raw, i=i, bk=bk: g.activation(out=raw[:, i * 512:(i + 1) * 512], in_=P[bk][:, :], func=AF.Copy),
                          reads=[pk], writes=[f"{kind}raw{i}"])
                        A("pool", lambda g, raw=raw, sqt=sqt, i=i: g.tensor_tensor(out=sqt[:, i * 512:(i + 1) * 512], in0=raw[:, i * 512:(i + 1) * 512],
                                                                                  in1=raw[:, i * 512:(i + 1) * 512], op=ALU.mult),
                          reads=[f"{kind}raw{i}"], writes=[f"{kind}sq{i}"])
                        if i == nt - 1:
                            def mmst(g, sqt=sqt, nt=nt, bank=bank):
                                for q in range(nt):
                                    r = g.matmul(P[bank][:, :], lhsT=onesb[:, :], rhs=sqt[:, q * 512:(q + 1) * 512], start=(q == 0), stop=(q == nt - 1))
                                return r
                            A("pe", mmst, reads=[f"{kind}sq{q}" for q in range(nt)] + ["onesb"], writes=[f"P{bank}"])
                            rsqrt_ops(Rt, P[bank][:, :], scl, [f"P{bank}"], rk_)
                            for q in range(nt):
                                A("pool", lambda g, raw=raw, Rt=Rt, q=q, dstT=dstT: g.tensor_tensor(
                                    out=dstT[:, q * T + sc0:q * T + sc0 + 512], in0=raw[:, q * 512:(q + 1) * 512], in1=Rt, op=ALU.mult),
                                  reads=[f"{kind}raw{q}", rk_], writes=[f"{kind}nT"])
                    elif kind == "kpe":
                        A("dve", lambda g, bk=bk: g.tensor_tensor(out=t1[0][0:32, :], in0=P[bk][0:32, :], in1=TABm[0:32, sc0:sc0 + 512], op=ALU.mult),
                          reads=[pk, "TABm"], writes=["t1_0"])
                        A("dve", lambda g, bk=bk: g.tensor_tensor(out=t2[0][0:32, :], in0=P[bk][32:64, :], in1=TABm[32:64, sc0:sc0 + 512], op=ALU.mult),
                          reads=[pk, "TABm"], writes=["t2_0"])
                        A("dve", lambda g: g.tensor_tensor(out=kpeT[:, sc0:sc0 + 512], in0=t1[0][0:32, :], in1=t2[0][0:32, :], op=ALU.add),
                          reads=["t1_0", "t2_0"], writes=["kpeT"])
                    else:
                        nm = kind[:2]
                        if kind[2] == "n":
                            A("dve", lambda g, bk=bk, i=i: g.tensor_tensor(out=t1[i], in0=P[bk][:, :], in1=Cr, op=ALU.mult),
                              reads=[pk, "Cr"], writes=["t1_0"])
                        else:
                            A("dve", lambda g, bk=bk, i=i: g.tensor_tensor(out=t2[i], in0=P[bk][:, :], in1=Sr, op=ALU.mult),
                              reads=[pk, "Sr"], writes=["t2_0"])
                            dstq = rqT if nm == "rq" else rkT
                            A("pool", lambda g, i=i, dstq=dstq: g.tensor_tensor(out=dstq[:, i * 512:(i + 1) * 512], in0=t1[i], in1=t2[i], op=ALU.add),
                              reads=["t1_0", "t2_0"], writes=[nm + "T"])
                            if nm == "rq":
                                A("pool", lambda g, i=i: g.tensor_scalar(out=rqm[:, i * 512:(i + 1) * 512], in0=rqT[:, i * 512:(i + 1) * 512],
                                                                        scalar1=mcol[:, 0:1], scalar2=None, op0=ALU.mult),
                                  reads=["rqT", "mcol"], writes=["rqm"])
                                A("pool", lambda g, i=i: g.tensor_tensor(out=qwT[:, i * 512:(i + 1) * 512], in0=rqT[:, i * 512:(i + 1) * 512],
                                                                        in1=WQc[:, i * 512:(i + 1) * 512], op=ALU.mult),
                                  reads=["rqT", "WQc"], writes=["qwT"])
                                A("pool", lambda g, i=i: g.tensor_scalar(out=qwm[:, i * 512:(i + 1) * 512], in0=qwT[:, i * 512:(i + 1) * 512],
                                                                        scalar1=mcol[:, 0:1], scalar2=None, op0=ALU.mult),
                                  reads=["qwT", "mcol"], writes=["qwm"])
                ck(3)
                for j in range(4):
                    for which, off, bank in (("v", RV, 4), ("g", RG, 5)):
                        def mmt(g, j=j, off=off, bank=bank):
                            for c in range(8):
                                r = g.matmul(P[bank][:, :], lhsT=hT[:, c * 512 + j * 128:c * 512 + (j + 1) * 128], rhs=w1s(c, off, 512),
                                             start=(c == 0), stop=(c == 7))
                            return r
                        A("pe", mmt, reads=["W1", "hT"], writes=[f"P{bank}"])
                        if which == "v":
                            A("act", lambda g, j=j: g.activation(out=vtok[:, j * 512:(j + 1) * 512], in_=P[4][:, :], func=AF.Copy),
                              reads=["P4"], writes=["vtok"])
                        else:
                            A("act", lambda g, j=j: g.activation(out=sg[:, j * 512:(j + 1) * 512], in_=P[5][:, :], func=AF.Silu),
                              reads=["P5"], writes=["sg"])
                ck(4)
                for j in range(4):
                    jc = slice(j * 128, (j + 1) * 128)

                    def trk(g, j=j):
                        for i in range(2):
                            r = g.transpose(out=Pb[5][:, i * 128:(i + 1) * 128], in_=rkT[:, i * 512 + j * 128:i * 512 + (j + 1) * 128], identity=identb[:, :])
                        return r
                    A("pe", trk, reads=["rkT", "identb"], writes=["P5"])
                    A("dve", lambda g: g.tensor_tensor(out=kwtok, in0=Pb[5][:, 0:256], in1=WKc[:, :], op=ALU.mult),
                      reads=["P5", "WKc"], writes=["kwtok"])

                    ck(6)

                    def mmsc(g, j=j):
                        for h in range(4):
                            i, r0 = h // 2, 64 * (h % 2)
                            cs = slice(i * 512 + j * 128, i * 512 + (j + 1) * 128)
                            if r0 == 0:
                                r = g.matmul(P[6][:, h * 128:(h + 1) * 128], lhsT=rkT[:, cs], rhs=rqm[:, cs], start=True, stop=True)
                            else:
                                r = g.matmul(P[6][:, h * 128:(h + 1) * 128], lhsT=rkT[64:128, cs], rhs=rqT[64:128, cs], start=True, stop=True,
                                             tile_position=(64, 0))
                        return r
                    A("pe", mmsc, reads=["rkT", "rqT", "rqm"], writes=["P6"])
                    A("dve", lambda g: g.tensor_tensor(out=scTm, in0=P[6][:, :], in1=DTc[:, :], op=ALU.mult), reads=["P6", "DTc"], writes=["scTm"])

                    ck(7)

                    def mmo(g, j=j):
                        for h in range(4):
                            i, r0 = h // 2, 64 * (h % 2)
                            cs = slice(i * 512 + j * 128, i * 512 + (j + 1) * 128)
                            g.matmul(P[7][:, h * 128:(h + 1) * 128], lhsT=scTm[:, h * 128:(h + 1) * 128],
                                     rhs=vtok[:, j * 512 + h * 128:j * 512 + (h + 1) * 128], start=True, stop=False)
                            if r0 == 0:
                                r = g.matmul(P[7][:, h * 128:(h + 1) * 128], lhsT=qwm[:, cs], rhs=Sbf[:, i * 128:(i + 1) * 128], start=False, stop=True)
                            else:
                                r = g.matmul(P[7][:, h * 128:(h + 1) * 128], lhsT=qwT[64:128, cs], rhs=Sbf[64:128, i * 128:(i + 1) * 128],
                                             start=False, stop=True, tile_position=(64, 0))
                        return r
                    A("pe", mmo, reads=["scTm", "vtok", "qwT", "qwm", "Sbf"], writes=["P7"])
                    A("act", lambda g: g.activation(out=osb, in_=P[7][:, :], func=AF.Copy), reads=["P7"], writes=["osb"])

                    ck(8)

                    def mmu(g, j=j):
                        for h in range(4):
                            i, r0 = h // 2, 64 * (h % 2)
                            kw = dict(tile_position=(0, 64)) if r0 else {}
                            r = g.matmul(P[5][r0:r0 + 64, 256 + i * 128:256 + (i + 1) * 128], lhsT=kwtok[:, h * 64:(h + 1) * 64],
                                         rhs=vtok[:, j * 512 + h * 128:j * 512 + (h + 1) * 128], start=True, stop=True, **kw)
                        return r
                    A("pe", mmu, reads=["kwtok", "vtok"], writes=["P5"])
                    for i in range(2):
                        A("dve", lambda g, i=i: g.scalar_tensor_tensor(out=Sf[:, i * 128:(i + 1) * 128], in0=Sf[:, i * 128:(i + 1) * 128],
                                                                      scalar=DECc[:, i:i + 1], in1=P[5][:, 256 + i * 128:256 + (i + 1) * 128],
                                                                      op0=ALU.mult, op1=ALU.add),
                          reads=["P5", "DECc", "Sf"], writes=["Sf"])
                    A("pool", lambda g: g.tensor_copy(out=Sbf, in_=Sf), reads=["Sf"], writes=["Sbf"])
                    ck(9)
                    o3 = osb.rearrange("p (h v) -> p h v", h=4)
                    A("dve", lambda g, o3=o3: g.reduce_sum(out=st["osum"], in_=o3, axis=AX.X), reads=["osb"], writes=["osum"])
                    A("pool", lambda g: g.tensor_tensor(out=osq, in0=osb, in1=osb, op=ALU.mult), reads=["osb"], writes=["ynorm"])
                    A("dve", lambda g: g.reduce_sum(out=st["osqs"], in_=osq.rearrange("p (h v) -> p h v", h=4), axis=AX.X),
                      reads=["ynorm"], writes=["osqs"])
                    A("dve", lambda g: g.tensor_scalar(out=st["mean"], in0=st["osum"], scalar1=1.0 / 128, scalar2=None, op0=ALU.mult),
                      reads=["osum"], writes=["mean"])
                    A("dve", lambda g: g.tensor_tensor(out=st["msq"], in0=st["mean"], in1=st["mean"], op=ALU.mult), reads=["mean"], writes=["msq"])
                    A("dve", lambda g: g.scalar_tensor_tensor(out=st["var"], in0=st["osqs"], scalar=1.0 / 128, in1=st["msq"], op0=ALU.mult, op1=ALU.subtract),
                      reads=["osqs", "msq"], writes=["var"])
                    rsqrt_ops(st["rgn"], st["var"], 1.0, ["var"], "rgn")
                    for h in range(4):
                        A("dve", lambda g, h=h: g.tensor_scalar(out=ynorm[:, h * 128:(h + 1) * 128], in0=osb[:, h * 128:(h + 1) * 128],
                                                                scalar1=st["mean"][:, h:h + 1], scalar2=st["rgn"][:, h:h + 1],
                                                                op0=ALU.subtract, op1=ALU.mult),
                          reads=["osb", "mean", "rgn"], writes=["ynorm"])
                    A("pool", lambda g, j=j: g.tensor_tensor(out=ytok, in0=ynorm, in1=sg[:, j * 512:(j + 1) * 512], op=ALU.mult),
                      reads=["ynorm", "sg"], writes=["ytok"])

                    ck(10)

                    def try_(g):
                        for t in range(4):
                            r = g.transpose(out=Pb[4][:, t * 128:(t + 1) * 128], in_=ytok[:, t * 128:(t + 1) * 128], identity=identb[:, :])
                        return r
                    A("pe", try_, reads=["ytok", "identb"], writes=["P4"])
                    A("act", lambda g, j=j: g.activation(out=ysT.rearrange("p (t n) -> p t n", t=4)[:, :, j * 128:(j + 1) * 128],
                                                         in_=Pb[4][:, 0:512].rearrange("p (t n) -> p t n", t=4), func=AF.Copy),
                      reads=["P4"], writes=["ysT"])
                ck(5)
                dma("sp", yret_d[:, :, sc0:sc0 + 512].rearrange("t p n -> p t n"), ysT.rearrange("p (t n) -> p t n", t=4),
                    r=["ysT"], w=["yret_d"])
            tap("cqnT", cqnT, ["cqnT"])
            tap("ckvnT", ckvnT, ["ckvnT"])
            tap("kpeT", kpeT, ["kpeT"])
            tap("TABm", TABm, ["TABm"])
            tap("yret", yret_d, ["yret_d"])
            tap("hT", hT, ["hT"])
            tap("cqraw", cqraw, ["cqraw0", "cqraw1", "cqraw2"])
            tap("Rq", Rq, ["Rq"])
            tap("sq", sq, ["cqsq0", "cqsq1", "cqsq2"])
            tap("rqT", rqT, ["rqT"])
            tap("rkT", rkT, ["rkT"])
            tap("osb", osb, ["osb"])
            tap("ytok", ytok, ["ytok"])
            tap("Sf", Sf, ["Sf"])
            S.barrier()
            _phase[0] += 1
            if _phase[0] > STOP:
                raise _Stop()
            AR.off = P12

            ymlaT = AR.alloc(4 * T, BF16)
            P23 = AR.off
            Wq = AR.alloc(3 * 1024, BF16)
            Wkv = AR.alloc(2 * 1536, BF16)
            wqs = AR.alloc(3 * 1024)
            wkvs = AR.alloc(2 * 1536)
            qg = AR.alloc(3)
            kvg = AR.alloc(2)
            dma("sp", qg, qg_d, w=["qg"])
            dma("sp", kvg, kvg_d, w=["kvg"])
            dma("sp", wqs.rearrange("p (c n) -> p c n", c=3), wq_d.rearrange("(c p) n -> p c n", p=128), w=["wqs"])
            dma("sp", wkvs.rearrange("p (c n) -> p c n", c=2), wkv_d.rearrange("(c p) n -> p c n", p=128), w=["wkvs"])
            for c in range(3):
                A("dve", lambda g, c=c: g.tensor_scalar(out=Wq[:, c * 1024:(c + 1) * 1024], in0=wqs[:, c * 1024:(c + 1) * 1024],
                                                         scalar1=qg[:, c:c + 1], scalar2=None, op0=ALU.mult),
                  reads=["wqs", "qg"], writes=["Wq"])
            for c in range(2):
                A("dve", lambda g, c=c: g.tensor_scalar(out=Wkv[:, c * 1536:(c + 1) * 1536], in0=wkvs[:, c * 1536:(c + 1) * 1536],
                                                         scalar1=kvg[:, c:c + 1], scalar2=None, op0=ALU.mult),
                  reads=["wkvs", "kvg"], writes=["Wkv"])

            KT = [AR.alloc(T, BF16), AR.alloc(T, BF16)]
            QT = [AR.alloc(T, BF16), AR.alloc(T, BF16)]
            Vg = [AR.alloc(32 * 128, BF16), AR.alloc(32 * 128, BF16)]
            PT = [AR.alloc(1024, BF16) for _ in range(3)]
            u1 = AR.alloc(512)
            u2 = AR.alloc(512)
            rec = AR.alloc(512)
            for b in range(2):
                A("pool", lambda g, b=b: g.memset(KT[b][0:64, :], 0.0), writes=[f"KT{b}"])
                A("pool", lambda g, b=b: g.memset(QT[b][0:64, :], 0.0), writes=[f"QT{b}"])
                A("pool", lambda g, b=b: g.memset(Vg[b], 1.0), writes=[f"Vg{b}"])
            SCALE = float(96 ** -0.5)
            pt_i = 0
            for h in range(8):
                hb = h % 2
                kk, qk, vk = f"KT{hb}", f"QT{hb}", f"Vg{hb}"
                A("act", lambda g, hb=hb: g.activation(out=KT[hb][0:32, :], in_=kpeT[:, :], func=AF.Copy), reads=["kpeT"], writes=[kk])
                for sbi in range(NSB):
                    sc0 = sbi * 512

                    def mmk(g, h=h, sc0=sc0):
                        for c in range(2):
                            r = g.matmul(P[6][:, :], lhsT=Wkv[:, c * 1536 + h * 128:c * 1536 + (h + 1) * 128], rhs=ckvnT[:, c * T + sc0:c * T + sc0 + 512],
                                         start=(c == 0), stop=(c == 1))
                        return r
                    A("pe", mmk, reads=["Wkv", "ckvnT"], writes=["P6"])
                    A("act", lambda g, hb=hb, sc0=sc0: g.activation(out=KT[hb][64:128, sc0:sc0 + 512], in_=P[6][64:128, :], func=AF.Copy),
                      reads=["P6"], writes=[kk])

                    def mmq(g, h=h, sc0=sc0):
                        for c in range(3):
                            r = g.matmul(P[7][:, :], lhsT=Wq[:, c * 1024 + h * 128:c * 1024 + (h + 1) * 128], rhs=cqnT[:, c * T + sc0:c * T + sc0 + 512],
                                         start=(c == 0), stop=(c == 2))
                        return r
                    A("pe", mmq, reads=["Wq", "cqnT"], writes=["P7"])
                    A("dve", lambda g, sc0=sc0: g.tensor_tensor(out=u1[0:32, :], in0=P[7][0:32, :], in1=TABm[0:32, sc0:sc0 + 512], op=ALU.mult),
                      reads=["P7", "TABm"], writes=["u1"])
                    A("dve", lambda g, sc0=sc0: g.tensor_tensor(out=u2[0:32, :], in0=P[7][32:64, :], in1=TABm[32:64, sc0:sc0 + 512], op=ALU.mult),
                      reads=["P7", "TABm"], writes=["u2"])
                    A("pool", lambda g, hb=hb, sc0=sc0: g.tensor_tensor(out=QT[hb][0:32, sc0:sc0 + 512], in0=u1[0:32, :], in1=u2[0:32, :], op=ALU.add),
                      reads=["u1", "u2"], writes=[qk])
                    A("act", lambda g, hb=hb, sc0=sc0: g.activation(out=QT[hb][64:128, sc0:sc0 + 512], in_=P[7][64:128, :], func=AF.Copy),
                      reads=["P7"], writes=[qk])
                for k8 in range(4):
                    def mmv(g, h=h, k8=k8):
                        for q in range(8):
                            kb = k8 * 8 + q
                            for c in range(2):
                                r = g.matmul(P[6][:, q * 64:(q + 1) * 64], lhsT=ckvnT[:, c * T + kb * 128:c * T + (kb + 1) * 128],
                                             rhs=Wkv[:, c * 1536 + 1024 + h * 64:c * 1536 + 1024 + (h + 1) * 64], start=(c == 0), stop=(c == 1))
                        return r
                    A("pe", mmv, reads=["Wkv", "ckvnT"], writes=["P6"])
                    A("act", lambda g, hb=hb, k8=k8: g.activation(
                        out=Vg[hb].rearrange("p (k v) -> p k v", v=128)[:, k8 * 8:(k8 + 1) * 8, 0:64],
                        in_=P[6][:, :].rearrange("p (k v) -> p k v", v=64), func=AF.Copy), reads=["P6"], writes=[vk])
                for qs in range(NSB):
                    q0 = qs * 512
                    acc = 4 + qs % 2
                    ak = f"P{acc}"
                    nfull = 4 * qs
                    groups = [(kb, min(kb + 2, nfull)) for kb in range(0, nfull, 2)]
                    items = [("full", a, b) for a, b in groups] + [("diag", 4 * qs + d, d) for d in range(4)]
                    last_kb = 4 * qs + 3
                    for gi, it in enumerate(items):
                        sbank = 2 * (gi % 2)
                        sk = f"PS{gi % 2}"
                        pt = PT[pt_i % 3]
                        ptk = f"PT{pt_i % 3}"
                        pt_i += 1
                        if it[0] == "full":
                            kbs = list(range(it[1], it[2]))

                            def mms(g, hb=hb, kbs=kbs, sbank=sbank, q0=q0):
                                for n_, kb in enumerate(kbs):
                                    r = g.matmul(P[sbank + n_][:, :], lhsT=KT[hb][:, kb * 128:(kb + 1) * 128], rhs=QT[hb][:, q0:q0 + 512],
                                                 start=True, stop=True)
                                return r
                            A("pe", mms, reads=[kk, qk], writes=[sk])
                            for n_ in range(len(kbs)):
                                A("act", lambda g, pt=pt, sbank=sbank, n_=n_: g.activation(out=pt[:, n_ * 512:(n_ + 1) * 512], in_=P[sbank + n_][:, :],
                                                                                            func=AF.Exp, scale=SCALE),
                                  reads=[sk], writes=[ptk])

                            def mmpv(g, hb=hb, kbs=kbs, pt=pt, acc=acc, last_kb=last_kb):
                                for n_, kb in enumerate(kbs):
                                    r = g.matmul(P[acc][:, :], lhsT=Vg[hb][:, kb * 128:(kb + 1) * 128], rhs=pt[:, n_ * 512:(n_ + 1) * 512],
                                                 start=(kb == 0), stop=(kb == last_kb))
                                return r
                            A("pe", mmpv, reads=[vk, ptk], writes=[ak])
                        else:
                            kb, d = it[1], it[2]
                            c0 = d * 128
                            A("pe", lambda g, hb=hb, kb=kb, c0=c0, sbank=sbank, q0=q0: g.matmul(
                                P[sbank][:, c0:512], lhsT=KT[hb][:, kb * 128:(kb + 1) * 128], rhs=QT[hb][:, q0 + c0:q0 + 512], start=True, stop=True),
                              reads=[kk, qk], writes=[sk])
                            A("act", lambda g, pt=pt, sbank=sbank, c0=c0: g.activation(out=pt[:, c0:512], in_=P[sbank][:, c0:512], func=AF.Exp, scale=SCALE),
                              reads=[sk], writes=[ptk])
                            A("pool", lambda g, pt=pt, c0=c0: g.tensor_tensor(out=pt[:, c0:c0 + 128], in0=pt[:, c0:c0 + 128], in1=trib[:, :], op=ALU.mult),
                              reads=[ptk, "trib"], writes=[ptk])
                            A("pe", lambda g, hb=hb, kb=kb, c0=c0, pt=pt, acc=acc, last_kb=last_kb: g.matmul(
                                P[acc][:, c0:512], lhsT=Vg[hb][:, kb * 128:(kb + 1) * 128], rhs=pt[:, c0:512], start=(kb == 0), stop=(kb == last_kb)),
                              reads=[vk, ptk], writes=[ak])
                    A("dve", lambda g, acc=acc: g.reciprocal(out=rec[0:64, :], in_=P[acc][64:128, :]), reads=[ak], writes=["rec"])
                    r0 = 64 * (h % 2)
                    A("dve", lambda g, acc=acc, r0=r0, h=h, q0=q0: g.tensor_tensor(
                        out=ymlaT[r0:r0 + 64, (h // 2) * T + q0:(h // 2) * T + q0 + 512], in0=P[acc][0:64, :], in1=rec[0:64, :], op=ALU.mult),
                      reads=[ak, "rec"], writes=["ymlaT"])
            S.barrier()
            _phase[0] += 1
            if _phase[0] > STOP:
                raise _Stop()

            AR.off = P23
            Wo = AR.alloc(8 * D, BF16)
            wos = [AR.alloc(D), AR.alloc(D)]
            og = AR.alloc(8)
            yrTs = [AR.alloc(4 * 512, BF16), AR.alloc(4 * 512, BF16)]
            ysq = AR.alloc(4 * 128, BF16)
            xt3 = [AR.alloc(D), AR.alloc(D)]
            mB = AR.alloc(D)
            mixs = AR.alloc(D)
            tt = AR.alloc(D)
            x1 = [AR.alloc(D), AR.alloc(D)]
            junk3 = AR.alloc(D, BF16)
            rm = AR.alloc(1)
            r2 = AR.alloc(1)
            dma("sp", og, og_d, w=["og"])
            for c in range(8):
                dma("sp", wos[c % 2], wout_d[c * 128:(c + 1) * 128, :], w=[f"wos{c % 2}"])
                A("dve", lambda g, c=c: g.tensor_scalar(out=Wo[:, c * D:(c + 1) * D], in0=wos[c % 2], scalar1=og[:, c:c + 1],
                                                         scalar2=None, op0=ALU.mult), reads=[f"wos{c % 2}", "og"], writes=["Wo"])
            for tb in range(32):
                b2 = tb % 2
                tc0 = tb * 128
                sbi, j = tb // 4, tb % 4
                yb = yrTs[sbi % 2]
                ybk = f"yrT{sbi % 2}"
                if j == 0:
                    dma("sp", yb.rearrange("p (t n) -> p t n", t=4), yret_d[:, :, sbi * 512:(sbi + 1) * 512].rearrange("t p n -> p t n"),
                        r=["yret_d"], w=[ybk])
                dma("sp", xt3[b2], x_d[tc0:tc0 + 128, :], w=[f"xt3{b2}"])
                A("pool", lambda g, tc0=tc0: g.tensor_tensor(out=ysq.rearrange("p (c n) -> p c n", c=4),
                                                              in0=ymlaT.rearrange("p (c n) -> p c n", c=4)[:, :, tc0:tc0 + 128],
                                                              in1=ymlaT.rearrange("p (c n) -> p c n", c=4)[:, :, tc0:tc0 + 128], op=ALU.mult),
                  reads=["ymlaT"], writes=["ysq"])

                def mmss(g):
                    for c in range(4):
                        r = g.matmul(P[6][:, 0:1], lhsT=ysq[:, c * 128:(c + 1) * 128], rhs=onesb[:, 0:1], start=(c == 0), stop=(c == 3))
                    return r
                A("pe", mmss, reads=["ysq", "onesb"], writes=["P6"])
                rsqrt_ops(rm, P[6][:, 0:1], 1.0 / 512, ["P6"], "rm")

                def mmA(g, tc0=tc0):
                    for hf in range(2):
                        for c in range(4):
                            r = g.matmul(P[hf][:, :], lhsT=ymlaT[:, c * T + tc0:c * T + tc0 + 128], rhs=Wo[:, c * D + hf * 512:c * D + (hf + 1) * 512],
                                         start=(c == 0), stop=(c == 3))
                    return r
                A("pe", mmA, reads=["ymlaT", "Wo"], writes=["PA"])

                def mmB(g, yb=yb, j=j):
                    for hf in range(2):
                        for c in range(4):
                            r = g.matmul(P[2 + hf][:, :], lhsT=yb[:, c * 512 + j * 128:c * 512 + (j + 1) * 128],
                                         rhs=Wo[:, (4 + c) * D + hf * 512:(4 + c) * D + (hf + 1) * 512], start=(c == 0), stop=(c == 3))
                    return r
                A("pe", mmB, reads=[ybk, "Wo"], writes=["PB"])
                for hf in range(2):
                    A("act", lambda g, hf=hf: g.activation(out=mB[:, hf * 512:(hf + 1) * 512], in_=P[2 + hf][:, :], func=AF.Copy),
                      reads=["PB"], writes=["mB"])
                    A("dve", lambda g, hf=hf: g.scalar_tensor_tensor(out=mixs[:, hf * 512:(hf + 1) * 512], in0=P[hf][:, :], scalar=rm[:, 0:1],
                                                                      in1=mB[:, hf * 512:(hf + 1) * 512], op0=ALU.mult, op1=ALU.add),
                      reads=["PA", "rm", "mB"], writes=["mixs"])
                A("act", lambda g: g.activation(out=junk3, in_=mixs, func=AF.Square, accum_out=r2), reads=["mixs"], writes=["junk3", "r2"])
                rsqrt_ops(r2, r2, 1.0 / D, ["r2"], "r2")
                A("dve", lambda g: g.scalar_tensor_tensor(out=tt, in0=mixs, scalar=r2[:, 0:1], in1=G1b[:, :], op0=ALU.mult, op1=ALU.mult),
                  reads=["mixs", "r2", "G1b"], writes=["tt"])
                A("pool", lambda g, b2=b2: g.tensor_tensor(out=x1[b2], in0=xt3[b2], in1=tt, op=ALU.add), reads=[f"xt3{b2}", "tt"], writes=[f"x1{b2}"])
                dma("sp", out_d[tc0:tc0 + 128, :], x1[b2], r=[f"x1{b2}"], w=[f"out{tb}"])
            S.barrier()
            _phase[0] += 1
            if _phase[0] > STOP:
                raise _Stop()
            AR.reset()

            Wg = AR.alloc(8 * DFF, BF16)
            Wu = AR.alloc(8 * DFF, BF16)
            Wd = AR.alloc(NJ * D, BF16)
            x1t = AR.alloc(4 * D)
            xn4 = AR.alloc(D, BF16)
            junk4 = AR.alloc(D, BF16)
            s4 = [AR.alloc(1), AR.alloc(1)]
            h2T = AR.alloc(8 * 512, BF16)
            h1T = AR.alloc(NJ * 512, BF16)
            sgt = [AR.alloc(512), AR.alloc(512)]
            t4 = AR.alloc(D)
            r3 = AR.alloc(1)
            for hf in range(2):
                dma("pool", Wg.rearrange("p (c n) -> p c n", c=8)[:, hf * 4:(hf + 1) * 4, :],
                    wg_d.rearrange("(c p) n -> p c n", p=128)[:, hf * 4:(hf + 1) * 4, :], w=["Wg"])
                dma("pool", Wu.rearrange("p (c n) -> p c n", c=8)[:, hf * 4:(hf + 1) * 4, :],
                    wu_d.rearrange("(c p) n -> p c n", p=128)[:, hf * 4:(hf + 1) * 4, :], w=["Wu"])
            for hf in range(2):
                dma("pool", Wd.rearrange("p (c n) -> p c n", c=NJ)[:, hf * 11:(hf + 1) * 11, :],
                    wd_d.rearrange("(c p) n -> p c n", p=128)[:, hf * 11:(hf + 1) * 11, :], w=["Wd"])
            fin = []
            for sbi in range(NSB):
                for j in range(4):
                    tb = sbi * 4 + j
                    b2 = tb % 2
                    xv = x1t[:, j * D:(j + 1) * D]
                    dma("sp", xv, out_d[tb * 128:(tb + 1) * 128, :], r=[f"out{tb}"], w=[f"x1t{j}"])
                    A("act", lambda g, xv=xv, b2=b2: g.activation(out=junk4, in_=xv, func=AF.Square, accum_out=s4[b2]),
                      reads=[f"x1t{j}"], writes=["junk4", f"s4{b2}"])
                    rsqrt_ops(s4[b2], s4[b2], 1.0 / D, [f"s4{b2}"], f"s4{b2}")
                    A("act", lambda g, xv=xv, b2=b2: g.activation(out=xn4, in_=xv, func=AF.Copy, scale=s4[b2]),
                      reads=[f"x1t{j}", f"s4{b2}"], writes=["xn4"])

                    def tr8b(g, b2=b2):
                        for c in range(8):
                            r = g.transpose(out=Pb[b2][:, c * 128:(c + 1) * 128], in_=xn4[:, c * 128:(c + 1) * 128], identity=identb[:, :])
                        return r
                    A("pe", tr8b, reads=["xn4", "identb"], writes=[f"P{b2}"])
                    for c in range(8):
                        dst = h2T[:, c * 512 + j * 128:c * 512 + (j + 1) * 128]
                        if c % 2 == 0:
                            A("dve", lambda g, c=c, dst=dst, b2=b2: g.tensor_scalar(out=dst, in0=Pb[b2][:, c * 128:(c + 1) * 128],
                                                                                   scalar1=a2[:, c:c + 1], scalar2=sh2[:, c:c + 1],
                                                                                   op0=ALU.mult, op1=ALU.add),
                              reads=[f"P{b2}", "a2", "modc"], writes=["h2T"])
                        else:
                            A("act", lambda g, c=c, dst=dst, b2=b2: g.activation(out=dst, in_=Pb[b2][:, c * 128:(c + 1) * 128], func=AF.Identity,
                                                                                scale=a2[:, c:c + 1], bias=sh2[:, c:c + 1]),
                              reads=[f"P{b2}", "a2", "modc"], writes=["h2T"])
                for jj in range(NJ):
                    gb = 2 + jj % 2
                    ub = 4 + jj % 2

                    def mmg(g, jj=jj, gb=gb, ub=ub):
                        for c in range(8):
                            g.matmul(P[gb][:, :], lhsT=Wg[:, c * DFF + jj * 128:c * DFF + (jj + 1) * 128], rhs=h2T[:, c * 512:(c + 1) * 512],
                                     start=(c == 0), stop=(c == 7))
                        for c in range(8):
                            r = g.matmul(P[ub][:, :], lhsT=Wu[:, c * DFF + jj * 128:c * DFF + (jj + 1) * 128], rhs=h2T[:, c * 512:(c + 1) * 512],
                                         start=(c == 0), stop=(c == 7))
                        return r
                    A("pe", mmg, reads=["Wg", "Wu", "h2T"], writes=[f"P{gb}", f"P{ub}"])
                    A("act", lambda g, jj=jj, gb=gb: g.activation(out=sgt[jj % 2], in_=P[gb][:, :], func=AF.Silu), reads=[f"P{gb}"], writes=[f"sgt{jj % 2}"])
                    A("dve", lambda g, jj=jj, ub=ub: g.tensor_tensor(out=h1T[:, jj * 512:(jj + 1) * 512], in0=P[ub][:, :], in1=sgt[jj % 2], op=ALU.mult),
                      reads=[f"P{ub}", f"sgt{jj % 2}"], writes=["h1T"])
                for j in range(4):
                    tb = sbi * 4 + j
                    xv = x1t[:, j * D:(j + 1) * D]

                    def mmd(g, j=j):
                        for hf in range(2):
                            for jj in range(NJ):
                                r = g.matmul(P[6 + hf][:, :], lhsT=h1T[:, jj * 512 + j * 128:jj * 512 + (j + 1) * 128],
                                             rhs=Wd[:, jj * D + hf * 512:jj * D + (hf + 1) * 512], start=(jj == 0), stop=(jj == NJ - 1))
                        return r
                    A("pe", mmd, reads=["h1T", "Wd"], writes=["PF"])
                    for hf in range(2):
                        A("act", lambda g, hf=hf: g.activation(out=t4[:, hf * 512:(hf + 1) * 512], in_=P[6 + hf][:, :], func=AF.Copy),
                          reads=["PF"], writes=["t4"])
                    A("act", lambda g: g.activation(out=junk4, in_=t4, func=AF.Square, accum_out=r3), reads=["t4"], writes=["junk4", "r3"])
                    rsqrt_ops(r3, r3, 1.0 / D, ["r3"], "r3")
                    A("dve", lambda g: g.scalar_tensor_tensor(out=t4, in0=t4, scalar=r3[:, 0:1], in1=G2b[:, :], op0=ALU.mult, op1=ALU.mult),
                      reads=["t4", "r3", "G2b"], writes=["t4"])
                    A("pool", lambda g, xv=xv: g.tensor_tensor(out=xv, in0=xv, in1=t4, op=ALU.add), reads=[f"x1t{j}", "t4"], writes=[f"x1t{j}"])
                    fin.append(dma("sp", out_d[tb * 128:(tb + 1) * 128, :], xv, r=[f"x1t{j}"], w=[f"out{tb}"]))
            A("sp", lambda g: None, deps=fin)

        except _Stop:
            pass
        with nc.Block() as block:
            S.emit_all(block, esem, dsem)
    return nc


def _consts():
    f = np.float32
    gam = 1.0 - 2.0 ** (-5.0 - np.arange(4, dtype=np.float64))
    idx = np.arange(128)
    ident = np.eye(128, dtype=f)
    tri = (idx[None, :] >= idx[:, None]).astype(f)
    dtc = np.zeros((128, 4, 128), np.float64)
    rel = idx[None, :] - idx[:, None]
    for h in range(4):
        dtc[:, h, :] = np.where(rel >= 0, gam[h] ** np.maximum(rel, 0), 0.0) * 0.125
    wqc = np.zeros((128, 2, 512), np.float64)
    wkc = np.zeros((128, 2, 128), np.float64)
    decc = np.zeros((128, 2), np.float64)
    for i in range(2):
        for r in range(128):
            h = 2 * i + r // 64
            wqc[r, i, :] = np.tile(gam[h] ** (idx + 1.0), 4)
            decc[r, i] = gam[h] ** 128
        for ft in range(128):
            h = 2 * i + ft // 64
            wkc[:, i, ft] = gam[h] ** (127.0 - idx) * 0.125
    inv_m = 10000.0 ** (-np.arange(16, dtype=np.float64) / 16.0)
    inv_r = 10000.0 ** (-np.arange(32, dtype=np.float64) / 32.0)
    invc = np.zeros((128, 3), np.float64)
    phc = np.zeros((128, 3), np.float64)
    for r in range(64):
        invc[r, 0] = inv_m[r % 16]
        phc[r, 0] = np.pi / 2 if r < 32 else (np.pi if r < 48 else 0.0)
    for r in range(128):
        invc[r, 1] = inv_r[r % 32]
        invc[r, 2] = inv_r[r % 32]
        phc[r, 1] = np.pi / 2
        phc[r, 2] = np.pi if (r % 64) < 32 else 0.0
    return dict(ident=ident, tri=tri, dtc=dtc.reshape(128, 512).astype(f), wqc=wqc.reshape(128, 1024).astype(f),
                wkc=wkc.reshape(128, 256).astype(f), decc=decc.astype(f), invc=invc.astype(f), phc=phc.astype(f))


def _colmajor(v, n):
    return np.ascontiguousarray(np.asarray(v, np.float32).reshape(n, 128).T)


def _prep_shared(inp):
    f = np.float32
    w_in = np.asarray(inp["w_in"], f)[0]
    cols = list(range(0, 640))
    cols += list(range(640, 672)) + [640 + k for k in list(range(16, 32)) + list(range(0, 16))]
    for base in (672, 928):
        for i in range(2):
            nat, sw = [], []
            for hh in (2 * i, 2 * i + 1):
                b = base + hh * 64
                nat += list(range(b, b + 64))
                sw += list(range(b + 32, b + 64)) + list(range(b, b + 32))
            cols += nat + sw
    cols += list(range(1184, 2208))
    w1 = np.ascontiguousarray(w_in[:, cols])
    assert w1.shape[1] == NC1
    wqb = np.asarray(inp["w_q_b"], f)[0]
    qc = []
    for h in range(8):
        b = h * 96
        qc += list(range(b + 64, b + 96)) + [b + 64 + k for k in list(range(16, 32)) + list(range(0, 16))] + list(range(b, b + 64))
    wq = np.ascontiguousarray(wqb[:, qc])
    wkvb = np.asarray(inp["w_kv_b"], f)[0]
    wkv = np.zeros((256, 1536), f)
    for h in range(8):
        wkv[:, h * 128 + 64:h * 128 + 128] = wkvb[:, h * 128:h * 128 + 64]
        wkv[:, 1024 + h * 64:1024 + (h + 1) * 64] = wkvb[:, h * 128 + 64:h * 128 + 128]
    sh = dict(
        w_ada=np.ascontiguousarray(np.asarray(inp["w_ada"], f)[0]),
        b_ada=np.ascontiguousarray(np.asarray(inp["b_ada"], f)[0][None, :]),
        gpre1=_colmajor(inp["pre_norm_mix"][0], 8), gpre2=_colmajor(inp["pre_norm_ffn"][0], 8),
        gpost1=np.ascontiguousarray(np.asarray(inp["post_norm_mix"], f)[0][None, :]),
        gpost2=np.ascontiguousarray(np.asarray(inp["post_norm_ffn"], f)[0][None, :]),
        qg=_colmajor(inp["q_a_norm"][0], 3), kvg=_colmajor(inp["kv_a_norm"][0], 2),
        og=_colmajor(np.concatenate([np.asarray(inp["mla_out_norm"], f)[0], np.asarray(inp["ret_gn_gain"], f)[0]]), 8),
        w1=w1, wq=wq, wkv=wkv,
        wout=np.ascontiguousarray(np.asarray(inp["w_out"], f)[0]),
        wg=np.ascontiguousarray(np.asarray(inp["w_gate"], f)[0]),
        wu=np.ascontiguousarray(np.asarray(inp["w_up"], f)[0]),
        wd=np.ascontiguousarray(np.asarray(inp["w_down"], f)[0]),
    )
    sh.update(_consts())
    return sh


def make_in_maps(inp, cores):
    sh = _prep_shared(inp)
    x = np.asarray(inp["x"], np.float32)
    c = np.asarray(inp["c"], np.float32)
    pos = np.asarray(inp["positions"], np.int32)
    maps = []
    for b in cores:
        m = dict(sh)
        m["x"] = np.ascontiguousarray(x[b])
        m["cT"] = _colmajor(c[b], 8)
        m["pos"] = np.ascontiguousarray(pos[b][None, :])
        maps.append(m)
    return maps


_NC = None


def kernel(**inputs):
    global _NC
    if _NC is None:
        _NC = build_nc()
    maps = make_in_maps(inputs, list(range(8)))
    res = run_bass_kernel_spmd(_NC, maps, core_ids=list(range(8)))
    return np.stack([np.asarray(r["out"], np.float32) for r in res.results], axis=0)
```

```python
import contextlib
import types
import numpy as np
import concourse.bass as bass
import concourse.mybir as mybir
from concourse.bass_utils import run_bass_kernel_spmd

F32 = mybir.dt.float32
BF16 = mybir.dt.bfloat16
I32 = mybir.dt.int32
AF = mybir.ActivationFunctionType
ALU = mybir.AluOpType
AX = mybir.AxisListType

ENGS = ("pe", "act", "dve", "pool", "sp")
STOP = 99
SUB = 99
HSEL = (0, 1, 2, 3)
TAPS = ()
REORDER = True


class _Stop(Exception):
    pass

T = 4096
D = 1024
NSB = 8
DFF = 2816
NJ = 22
NC1 = 2752
EPS = 1e-6
PI = float(np.pi)


def _freeze(fn):
    if fn.__closure__ is None:
        return fn
    cells = []
    for c in fn.__closure__:
        try:
            cells.append(types.CellType(c.cell_contents))
        except ValueError:
            cells.append(c)
    return types.FunctionType(fn.__code__, fn.__globals__, fn.__name__, fn.__defaults__, tuple(cells))


class Op:
    __slots__ = ("eng", "idx", "emit", "deps", "is_dma", "dma_i", "marked", "count", "clock", "waits", "is_bar", "busy", "lat", "seq")


def _nfree(ap):
    n = 1
    for d in ap.shape[1:]:
        n *= int(d)
    return n


class _Fake:
    def __init__(self, eng):
        self.eng = eng
        self.busy = 0.0
        self.lat = None

    def matmul(self, out, lhsT=None, rhs=None, **kw):
        n = max(_nfree(rhs), 64)
        f = 4.0 if rhs.dtype == F32 else 1.0
        self.busy += f * n / 2370.0 + 0.004
        return self

    def transpose(self, out=None, in_=None, identity=None, **kw):
        self.busy += 0.08
        return self

    def activation(self, out=None, in_=None, **kw):
        self.busy += 0.1 + _nfree(in_) / 1150.0
        return self

    def dma_start(self, out=None, in_=None, **kw):
        nb = _nfree(out) * int(out.shape[0]) * (4 if out.dtype in (F32, I32) else 2)
        self.busy += 0.15 if self.eng == "sp" else 1.2
        self.lat = 2.5 + nb / 150e3
        return self

    def _dve(self, out, **kw):
        n = _nfree(out)
        if self.eng == "pool":
            self.busy += 0.2 + n / 500.0
        else:
            self.busy += 0.12 + n / 900.0
        return self

    def tensor_tensor(self, out=None, **kw):
        return self._dve(out)

    def tensor_scalar(self, out=None, **kw):
        return self._dve(out)

    def tensor_copy(self, out=None, **kw):
        return self._dve(out)

    def scalar_tensor_tensor(self, out=None, **kw):
        return self._dve(out)

    def tensor_single_scalar(self, out=None, **kw):
        return self._dve(out)

    def reciprocal(self, out=None, **kw):
        self.busy += 0.1 + _nfree(out) / 150.0
        return self

    def memset(self, ap, *a, **kw):
        return self._dve(ap)

    def reduce_sum(self, out=None, in_=None, **kw):
        return self._dve(in_)

    def then_inc(self, *a, **kw):
        return self


class Sched:
    def __init__(self, n_dma_sems=12):
        self.ops = {e: [] for e in ENGS}
        self.order = []
        self.lastw = {}
        self.readers = {}
        self.n_dma_sems = n_dma_sems
        self.dma_ops = {e: [] for e in ENGS}
        self.dma_since_bar = []

    def add(self, eng, emit, reads=(), writes=(), dma=False, deps=()):
        op = Op()
        op.eng = eng
        op.emit = _freeze(emit)
        op.is_dma = dma
        op.marked = False
        op.count = 0
        op.idx = len(self.ops[eng])
        d = set(deps)
        for k in reads:
            w = self.lastw.get(k)
            if w is not None:
                d.add(w)
        for k in writes:
            w = self.lastw.get(k)
            if w is not None:
                d.add(w)
            for r in self.readers.get(k, ()):
                d.add(r)
        for k in reads:
            self.readers.setdefault(k, []).append(op)
        for k in writes:
            self.lastw[k] = op
            self.readers[k] = []
        d.discard(op)
        op.deps = d
        op.is_bar = False
        op.seq = len(self.order)
        self.ops[eng].append(op)
        self.order.append(op)
        return op

    def barrier(self):
        for e in ENGS:
            self.add(e, lambda g: None).is_bar = True

    def _list_schedule(self, seg):
        import heapq
        segset = set(seg)
        succ = {o: [] for o in seg}
        indeg = {}
        for o in seg:
            fk = _Fake(o.eng)
            o.emit(fk)
            o.busy = fk.busy
            o.lat = fk.lat if fk.lat is not None else fk.busy + 0.06
            k = 0
            for d in o.deps:
                if d in segset:
                    succ[d].append(o)
                    k += 1
            indeg[o] = k
        bl = {}
        for o in reversed(seg):
            m = 0.0
            for s_ in succ[o]:
                if bl[s_] > m:
                    m = bl[s_]
            bl[o] = o.lat + m
        free = {e: 0.0 for e in ENGS}
        avail = {e: [] for e in ENGS}
        future = {e: [] for e in ENGS}
        rtime = {o: 0.0 for o in seg}
        for o in seg:
            if indeg[o] == 0:
                heapq.heappush(future[o.eng], (0.0, o.seq, o))
        out = []
        n = len(seg)
        while len(out) < n:
            best_e, best_t = None, None
            for e in ENGS:
                fu, av = future[e], avail[e]
                while fu and fu[0][0] <= free[e]:
                    _, sq, o = heapq.heappop(fu)
                    heapq.heappush(av, (-bl[o], sq, o))
                if av:
                    t = free[e]
                elif fu:
                    t = fu[0][0]
                else:
                    continue
                if best_t is None or t < best_t:
                    best_e, best_t = e, t
            e = best_e
            if not avail[e]:
                free[e] = best_t
                fu, av = future[e], avail[e]
                while fu and fu[0][0] <= free[e]:
                    _, sq, o = heapq.heappop(fu)
                    heapq.heappush(av, (-bl[o], sq, o))
            _, sq, o = heapq.heappop(avail[e])
            st = free[e]
            free[e] = st + o.busy
            fin = st + o.lat
            out.append(o)
            for s_ in succ[o]:
                if fin > rtime[s_]:
                    rtime[s_] = fin
                indeg[s_] -= 1
                if indeg[s_] == 0:
                    heapq.heappush(future[s_.eng], (rtime[s_], s_.seq, s_))
        return out

    def schedule(self, reorder=True):
        segs, cur = [], []
        for o in self.order:
            if o.is_bar:
                if cur:
                    segs.append(("seg", cur))
                    cur = []
                if segs and segs[-1][0] == "bar":
                    segs[-1][1].append(o)
                else:
                    segs.append(("bar", [o]))
            else:
                cur.append(o)
        if cur:
            segs.append(("seg", cur))
        new = []
        last_seg = []
        for kind, lst in segs:
            if kind == "seg":
                lst2 = self._list_schedule(lst) if reorder else lst
                new += lst2
                last_seg = lst2
            else:
                deps = [o for o in last_seg if o.is_dma]
                for e in ENGS:
                    for o in reversed(last_seg):
                        if o.eng == e and not o.is_dma:
                            deps.append(o)
                            break
                for o in lst:
                    o.deps = set(deps)
                new += lst
        self.order = new
        self.ops = {e: [] for e in ENGS}
        self.dma_ops = {e: [] for e in ENGS}
        for o in new:
            o.idx = len(self.ops[o.eng])
            self.ops[o.eng].append(o)
            if o.is_dma:
                o.dma_i = len(self.dma_ops[o.eng])
                if o.dma_i >= self.n_dma_sems:
                    o.deps.add(self.dma_ops[o.eng][o.dma_i - self.n_dma_sems])
                self.dma_ops[o.eng].append(o)

    def resolve(self):
        known = {e: {f: -1 for f in ENGS} for e in ENGS}
        known_dma = {e: set() for e in ENGS}
        for op in self.order:
            e = op.eng
            kn = known[e]
            waits = []
            for d in sorted(op.deps, key=lambda o: -o.idx):
                if d.is_dma:
                    if d in known_dma[e]:
                        continue
                    known_dma[e].add(d)
                    waits.append(d)
                else:
                    if d.eng == "pe" and e == "pe":
                        continue
                    if kn[d.eng] >= d.idx:
                        continue
                    d.marked = True
                    waits.append(d)
                ck = d.clock
                for f in ENGS:
                    if ck[f] > kn[f]:
                        kn[f] = ck[f]
            op.waits = waits
            ck = dict(kn)
            if not op.is_dma:
                ck[e] = max(ck[e], op.idx)
            op.clock = ck
        for e in ENGS:
            c = 0
            for op in self.ops[e]:
                if op.marked:
                    c += 1
                    op.count = c

    def emit_all(self, block, esem, dsem):
        self.schedule(reorder=REORDER)
        self.resolve()
        n = self.n_dma_sems

        def run(e, engobj):
            for op in self.ops[e]:
                for d in op.waits:
                    if d.is_dma:
                        engobj.wait_ge(dsem[d.eng][d.dma_i % n], 16 * (d.dma_i // n + 1))
                    else:
                        engobj.wait_ge(esem[d.eng], d.count)
                ins = op.emit(engobj)
                if op.is_dma:
                    ins.then_inc(dsem[e][op.dma_i % n], 16)
                elif op.marked:
                    if ins is None:
                        ins = engobj.nop()
                    ins.then_inc(esem[e], 1)

        block.tensor(lambda t: run("pe", t))
        block.scalar(lambda t: run("act", t))
        block.vector(lambda t: run("dve", t))
        block.gpsimd(lambda t: run("pool", t))
        block.sync(lambda t: run("sp", t))


class Arena:
    def __init__(self, ap, ncols):
        self.ap = ap
        self.n = ncols
        self.off = 0

    def alloc(self, cols, dt=F32):
        nb = cols * (4 if dt in (F32, I32) else 2)
        n32 = ((nb + 31) // 32) * 8
        assert self.off + n32 <= self.n, ("arena overflow", self.off, n32, self.n)
        v = self.ap[:, self.off:self.off + n32]
        self.off += n32
        if dt != F32:
            v = v.bitcast(dt)
        return v[:, 0:cols]

    def reset(self):
        self.off = 0

    def alloc_at(self, off32, cols, dt=F32):
        save = self.off
        self.off = off32
        v = self.alloc(cols, dt)
        end = self.off
        self.off = save
        return v, end


def build_nc():
    nc = bass.Bass("TRN2", target_bir_lowering=False)

    def DI(name, shape, dt=F32):
        return nc.dram_tensor(name, shape, dt, kind="ExternalInput").ap()

    x_d = DI("x", [T, D])
    c_d = DI("cT", [128, 8])
    pos_d = DI("pos", [1, T], I32)
    wada_d = DI("w_ada", [D, 6 * D])
    bada_d = DI("b_ada", [1, 6 * D])
    gpre1_d = DI("gpre1", [128, 8])
    gpre2_d = DI("gpre2", [128, 8])
    gpost1_d = DI("gpost1", [1, D])
    gpost2_d = DI("gpost2", [1, D])
    qg_d = DI("qg", [128, 3])
    kvg_d = DI("kvg", [128, 2])
    og_d = DI("og", [128, 8])
    w1_d = DI("w1", [D, NC1])
    wq_d = DI("wq", [384, 1024])
    wkv_d = DI("wkv", [256, 1536])
    wout_d = DI("wout", [D, D])
    wg_d = DI("wg", [D, DFF])
    wu_d = DI("wu", [D, DFF])
    wd_d = DI("wd", [DFF, D])
    ident_d = DI("ident", [128, 128])
    tri_d = DI("tri", [128, 128])
    dt_d = DI("dtc", [128, 512])
    wqc_d = DI("wqc", [128, 1024])
    wkc_d = DI("wkc", [128, 256])
    dec_d = DI("decc", [128, 2])
    inv_d = DI("invc", [128, 3])
    ph_d = DI("phc", [128, 3])
    out_d = nc.dram_tensor("out", [T, D], F32, kind="ExternalOutput").ap()
    yret_d = nc.dram_tensor("yret_scr", [4, 128, T], BF16).ap()

    S = Sched(n_dma_sems=12)
    A = S.add

    with contextlib.ExitStack() as ctx:
        def sbt(name, cols, dt=F32, parts=128):
            return ctx.enter_context(nc.sbuf_tensor(name, [parts, cols], dt))

        identb = sbt("identb", 128, BF16)
        trib = sbt("trib", 128, BF16)
        onesb = sbt("onesb", 128, BF16)
        onesf = sbt("onesf", 128)
        epst = sbt("epst", 1)
        DECc = sbt("DECc", 2)
        INVc = sbt("INVc", 3)
        PHc = sbt("PHc", 3)
        modc = sbt("modc", 32)
        a1 = sbt("a1", 8)
        a2 = sbt("a2", 8)
        G1b = sbt("G1b", D)
        G2b = sbt("G2b", D)
        ARN = 50000
        arena_t = sbt("arena", ARN)
        AR = Arena(arena_t, ARN)
        P = [ctx.enter_context(nc.psum_tensor(f"bank{i}", [128, 512], F32)) for i in range(8)]
        Pb = [p[:, :].bitcast(BF16) for p in P]

        esem = {e: ctx.enter_context(nc.semaphore("es_" + e)) for e in ENGS}
        dsem = {e: [ctx.enter_context(nc.semaphore(f"ds_{e}{i}")) for i in range(12)] for e in ("sp", "pool")}

        def dma(q, out, in_, r=(), w=()):
            return A(q, lambda g: g.dma_start(out=out, in_=in_), reads=r, writes=w, dma=True)

        def tap(name, ap, keys):
            if name not in TAPS:
                return
            shp = list(ap.shape)
            dd = nc.dram_tensor("dbg_" + name, shp, ap.dtype, kind="ExternalOutput").ap()
            dma("sp", dd, ap, r=keys)

        def rsqrt_ops(dst, src, scale, rk, wk):
            A("act", lambda g: g.activation(out=dst, in_=src, func=AF.Sqrt, scale=scale, bias=epst[0:dst.shape[0], :]),
              reads=list(rk) + ["epst"], writes=[wk])
            A("dve", lambda g: g.reciprocal(out=dst, in_=dst), reads=[wk], writes=[wk])

        _phase = [0]
        try:
            dma("pool", identb[:, :], ident_d, w=["identb"])
            dma("pool", trib[:, :], tri_d, w=["trib"])
            A("pool", lambda g: g.memset(onesb[:, :], 1.0), writes=["onesb"])
            A("pool", lambda g: g.memset(onesf[:, :], 1.0), writes=["onesf"])
            A("pool", lambda g: g.memset(epst[:, :], EPS), writes=["epst"])
            for t_, d_, k_ in ((DECc, dec_d, "DECc"), (INVc, inv_d, "INVc"), (PHc, ph_d, "PHc")):
                dma("sp", t_[:, :], d_, w=[k_])
            cT = AR.alloc(8)
            gp1 = AR.alloc(8)
            gp2 = AR.alloc(8)
            scb = AR.alloc(8, BF16)
            gpo1 = AR.alloc(D)
            gpo2 = AR.alloc(D)
            bada = AR.alloc(6 * D)
            modrow = AR.alloc(6 * D)
            grow1 = AR.alloc(D)
            grow2 = AR.alloc(D)
            wa = [AR.alloc(8 * 512, BF16), AR.alloc(8 * 512, BF16)]
            dma("sp", cT, c_d, w=["cT"])
            dma("sp", gp1, gpre1_d, w=["gp1"])
            dma("sp", gp2, gpre2_d, w=["gp2"])
            dma("sp", gpo1[0:1, :], gpost1_d, w=["gpo1"])
            dma("sp", gpo2[0:1, :], gpost2_d, w=["gpo2"])
            dma("sp", bada[0:1, :], bada_d, w=["bada"])
            A("act", lambda g: g.activation(out=scb, in_=cT, func=AF.Silu), reads=["cT"], writes=["scb"])
            for gi in range(12):
                wb = wa[gi % 2]
                dma("pool", wb.rearrange("p (c n) -> p c n", c=8),
                    wada_d[:, gi * 512:(gi + 1) * 512].rearrange("(c p) n -> p c n", p=128), w=[f"wa{gi % 2}"])

                def mm_ada(g, gi=gi, wb=wb):
                    for k in range(8):
                        r = g.matmul(P[gi % 2][0:1, :], lhsT=scb[:, k:k + 1], rhs=wb[:, k * 512:(k + 1) * 512],
                                     start=(k == 0), stop=(k == 7))
                    return r
                A("pe", mm_ada, reads=["scb", f"wa{gi % 2}"], writes=[f"P{gi % 2}"])
                A("dve", lambda g, gi=gi: g.tensor_tensor(out=modrow[0:1, gi * 512:(gi + 1) * 512], in0=P[gi % 2][0:1, :],
                                                         in1=bada[0:1, gi * 512:(gi + 1) * 512], op=ALU.add),
                  reads=[f"P{gi % 2}", "bada"], writes=["modrow"])
            col_offs = [0 * D, 1 * D, 3 * D, 4 * D]

            def mm_cols(g):
                for vi, off in enumerate(col_offs):
                    for c in range(8):
                        r = g.matmul(P[2][:, vi * 8 + c:vi * 8 + c + 1], lhsT=modrow[0:1, off + c * 128:off + (c + 1) * 128],
                                     rhs=onesf[0:1, 0:1], start=True, stop=True)
                return r
            A("pe", mm_cols, reads=["modrow", "onesf"], writes=["P2"])
            A("dve", lambda g: g.tensor_copy(out=modc[:, :], in_=P[2][:, 0:32]), reads=["P2"], writes=["modc"])
            A("dve", lambda g: g.scalar_tensor_tensor(out=a1[:, :], in0=modc[:, 8:16], scalar=1.0, in1=gp1, op0=ALU.add, op1=ALU.mult),
              reads=["modc", "gp1"], writes=["a1"])
            A("dve", lambda g: g.scalar_tensor_tensor(out=a2[:, :], in0=modc[:, 24:32], scalar=1.0, in1=gp2, op0=ALU.add, op1=ALU.mult),
              reads=["modc", "gp2"], writes=["a2"])
            sh1 = modc[:, 0:8]
            sh2 = modc[:, 16:24]
            A("dve", lambda g: g.tensor_tensor(out=grow1[0:1, :], in0=modrow[0:1, 2 * D:3 * D], in1=gpo1[0:1, :], op=ALU.mult),
              reads=["modrow", "gpo1"], writes=["grow1"])
            A("dve", lambda g: g.tensor_tensor(out=grow2[0:1, :], in0=modrow[0:1, 5 * D:6 * D], in1=gpo2[0:1, :], op=ALU.mult),
              reads=["modrow", "gpo2"], writes=["grow2"])
            for gi, (grow, Gb, gk) in enumerate(((grow1, G1b, "G1b"), (grow2, G2b, "G2b"))):
                for hf in range(2):
                    bk = 3 + hf
                    A("pe", lambda g, grow=grow, hf=hf, bk=bk: g.matmul(P[bk][:, :], lhsT=onesf[0:1, 0:128],
                                                                        rhs=grow[0:1, hf * 512:(hf + 1) * 512], start=True, stop=True),
                      reads=[f"grow{gi + 1}", "onesf"], writes=[f"P{bk}"])
                    A("act", lambda g, Gb=Gb, hf=hf, bk=bk: g.activation(out=Gb[:, hf * 512:(hf + 1) * 512], in_=P[bk][:, :], func=AF.Copy),
                      reads=[f"P{bk}"], writes=[gk])
            S.barrier()
            _phase[0] += 1
            if _phase[0] > STOP:
                raise _Stop()
            AR.reset()

            cqnT = AR.alloc(3 * T, BF16)
            ckvnT = AR.alloc(2 * T, BF16)
            TABm = AR.alloc(T)
            kpeT = TABm[64:96, 0:2048].bitcast(BF16)
            P12 = AR.off
            DTc = AR.alloc(512)
            WQc = AR.alloc(1024)
            WKc = AR.alloc(256)
            dma("sp", DTc, dt_d, w=["DTc"])
            dma("sp", WQc, wqc_d, w=["WQc"])
            dma("sp", WKc, wkc_d, w=["WKc"])
            W1 = AR.alloc(8 * NC1, BF16)
            xt = [AR.alloc(D), AR.alloc(D)]
            xn = [AR.alloc(D, BF16), AR.alloc(D, BF16)]
            junk = AR.alloc(D, BF16)
            ssq = [AR.alloc(1), AR.alloc(1)]
            hT = AR.alloc(8 * 512, BF16)
            cqraw = AR.alloc(3 * 512)
            ckvraw = AR.alloc(2 * 512)
            sq = AR.alloc(3 * 512, BF16)
            sq2 = AR.alloc(2 * 512, BF16)
            Rq = AR.alloc(512)
            Rkv = Rq
            posi = AR.alloc(512, I32)
            posf = AR.alloc(512)
            ang = AR.alloc(512)
            ni = posi
            nf = AR.alloc(512)
            msk = nf
            Cr = AR.alloc(512)
            Sr = AR.alloc(512)
            t1 = [AR.alloc(512)] * 2
            t2 = [AR.alloc(512)] * 2
            rqT = AR.alloc(2 * 512, BF16)
            rkT = AR.alloc(2 * 512, BF16)
            qwT = AR.alloc(2 * 512, BF16)
            rqm = AR.alloc(2 * 512, BF16)
            qwm = AR.alloc(2 * 512, BF16)
            A("pool", lambda g: g.memset(rqm[64:128, :], 0.0), writes=["rqm"])
            A("pool", lambda g: g.memset(qwm[64:128, :], 0.0), writes=["qwm"])
            vtok = AR.alloc(4 * 512, BF16)
            sg = AR.alloc(4 * 512, BF16)
            kwtok = AR.alloc(256, BF16)
            scTm = AR.alloc(512, BF16)
            osb = AR.alloc(512)
            ynorm = AR.alloc(512)
            osq = ynorm
            ytok = AR.alloc(512, BF16)
            ysT = AR.alloc(4 * 512, BF16)
            Sf = AR.alloc(256)
            Sbf = AR.alloc(256, BF16)
            st = {k: AR.alloc(4) for k in ("osum", "osqs", "mean", "msq", "var", "rgn")}

            for hf in range(2):
                dma("pool", W1.rearrange("p (c n) -> p c n", c=8)[:, hf * 4:(hf + 1) * 4, :],
                    w1_d.rearrange("(c p) n -> p c n", p=128)[:, hf * 4:(hf + 1) * 4, :], w=["W1"])
            A("pool", lambda g: g.memset(Sf, 0.0), writes=["Sf"])
            A("pool", lambda g: g.memset(Sbf, 0.0), writes=["Sbf"])

            def ck(n):
                if SUB == n:
                    raise _Stop()
            ck(0)

            def w1s(c, off, n):
                return W1[:, c * NC1 + off:c * NC1 + off + n]

            def table(dst, dk, col, sbi):
                A("dve", lambda g: g.tensor_scalar(out=ang, in0=posf, scalar1=INVc[:, col:col + 1], scalar2=PHc[:, col:col + 1],
                                                   op0=ALU.mult, op1=ALU.add), reads=["posf", "INVc", "PHc"], writes=["ang"])
                A("dve", lambda g: g.tensor_scalar(out=ni, in0=ang, scalar1=float(1.0 / (2 * PI)), scalar2=None, op0=ALU.mult),
                  reads=["ang"], writes=["ibuf"])
                A("dve", lambda g: g.tensor_copy(out=nf, in_=ni), reads=["ibuf"], writes=["nf"])
                A("dve", lambda g: g.scalar_tensor_tensor(out=ang, in0=nf, scalar=-2 * PI, in1=ang, op0=ALU.mult, op1=ALU.add),
                  reads=["nf", "ang"], writes=["ang"])
                A("dve", lambda g: g.tensor_single_scalar(out=msk, in_=ang, scalar=PI, op=ALU.is_gt), reads=["ang", "nf"], writes=["nf"])
                A("dve", lambda g: g.scalar_tensor_tensor(out=ang, in0=msk, scalar=-2 * PI, in1=ang, op0=ALU.mult, op1=ALU.add),
                  reads=["nf", "ang"], writes=["ang"])
                A("dve", lambda g: g.tensor_scalar(out=ang, in0=ang, scalar1=-3.14159, scalar2=3.14159, op0=ALU.max, op1=ALU.min),
                  reads=["ang"], writes=["ang"])
                np_ = dst.shape[0]
                A("act", lambda g: g.activation(out=dst, in_=ang[0:np_, :], func=AF.Sin), reads=["ang"], writes=[dk])

            mtiles = [(0, 128, "cq", 0), (128, 128, "cq", 1), (256, 128, "cq", 2), (384, 128, "ckv", 0), (512, 128, "ckv", 1),
                      (640, 64, "kpe", 0)]
            o_ = 704
            for nm in ("rq", "rk"):
                for i in range(2):
                    mtiles.append((o_, 128, nm + "n", i))
                    mtiles.append((o_ + 128, 128, nm + "s", i))
                    o_ += 256
            RV = 1728
            RG = 2240

            for sbi in range(NSB):
                sc0 = sbi * 512
                dma("sp", posi, bass.AP(pos_d.tensor, sc0, [[0, 128], [1, 512]]), w=["ibuf"])
                A("dve", lambda g: g.tensor_copy(out=posf, in_=posi), reads=["ibuf"], writes=["posf"])
                table(TABm[0:64, sc0:sc0 + 512], "TABm", 0, sbi)
                table(Cr, "Cr", 1, sbi)
                table(Sr, "Sr", 2, sbi)
                ck(1)
                for j in range(4):
                    tb = sbi * 4 + j
                    b2 = tb % 2
                    dma("sp", xt[b2], x_d[tb * 128:(tb + 1) * 128, :], w=[f"xt{b2}"])
                    A("act", lambda g, b2=b2: g.activation(out=junk, in_=xt[b2], func=AF.Square, accum_out=ssq[b2]),
                      reads=[f"xt{b2}"], writes=["junk", f"ssq{b2}"])
                    rsqrt_ops(ssq[b2], ssq[b2], 1.0 / D, [f"ssq{b2}"], f"ssq{b2}")
                    A("act", lambda g, b2=b2: g.activation(out=xn[b2], in_=xt[b2], func=AF.Copy, scale=ssq[b2]),
                      reads=[f"xt{b2}", f"ssq{b2}"], writes=[f"xn{b2}"])

                    def tr8(g, b2=b2):
                        for c in range(8):
                            r = g.transpose(out=Pb[b2][:, c * 128:(c + 1) * 128], in_=xn[b2][:, c * 128:(c + 1) * 128], identity=identb[:, :])
                        return r
                    A("pe", tr8, reads=[f"xn{b2}", "identb"], writes=[f"P{b2}"])
                    for c in range(8):
                        dst = hT[:, c * 512 + j * 128:c * 512 + (j + 1) * 128]
                        if c % 2 == 0:
                            A("dve", lambda g, c=c, dst=dst, b2=b2: g.tensor_scalar(out=dst, in0=Pb[b2][:, c * 128:(c + 1) * 128],
                                                                                   scalar1=a1[:, c:c + 1], scalar2=sh1[:, c:c + 1],
                                                                                   op0=ALU.mult, op1=ALU.add),
                              reads=[f"P{b2}", "a1", "modc"], writes=["hT"])
                        else:
                            A("act", lambda g, c=c, dst=dst, b2=b2: g.activation(out=dst, in_=Pb[b2][:, c * 128:(c + 1) * 128], func=AF.Identity,
                                                                                scale=a1[:, c:c + 1], bias=sh1[:, c:c + 1]),
                              reads=[f"P{b2}", "a1", "modc"], writes=["hT"])
                ck(2)
                for mi, (off, M, kind, i) in enumerate(mtiles):
                    bk = 2 + mi % 2
                    pk = f"P{bk}"

                    def mmz(g, off=off, M=M, bk=bk):
                        for c in range(8):
                            r = g.matmul(P[bk][0:M, :], lhsT=w1s(c, off, M), rhs=hT[:, c * 512:(c + 1) * 512], start=(c == 0), stop=(c == 7))
                        return r
                    A("pe", mmz, reads=["W1", "hT"], writes=[pk])
                    if kind in ("cq", "ckv"):
                        raw, sqt, nt, Rt, bank, scl, dstT, rk_ = ((cqraw, sq, 3, Rq, 4, 1.0 / 384, cqnT, "Rq") if kind == "cq"
                                                                  else (ckvraw, sq2, 2, Rkv, 5, 1.0 / 256, ckvnT, "Rq"))
                        A("act", lambda g, raw=raw, i=i, bk=bk: g.activation(out=raw[:, i * 512:(i + 1) * 512], in_=P[bk][:, :], func=AF.Copy),
                          reads=[pk], writes=[f"{kind}raw{i}"])
                        A("act", lambda g, sqt=sqt, i=i, bk=bk: g.activation(out=sqt[:, i * 512:(i + 1) * 512], in_=P[bk][:, :], func=AF.Square),
                          reads=[pk], writes=[f"{kind}sq{i}"])
                        if i == nt - 1:
                            def mmst(g, sqt=sqt, nt=nt, bank=bank):
                                for q in range(nt):
                                    r = g.matmul(P[bank][:, :], lhsT=onesb[:, :], rhs=sqt[:, q * 512:(q + 1) * 512], start=(q == 0), stop=(q == nt - 1))
                                return r
                            A("pe", mmst, reads=[f"{kind}sq{q}" for q in range(nt)] + ["onesb"], writes=[f"P{bank}"])
                            rsqrt_ops(Rt, P[bank][:, :], scl, [f"P{bank}"], rk_)
                            for q in range(nt):
                                A("pool", lambda g, raw=raw, Rt=Rt, q=q, dstT=dstT: g.tensor_tensor(
                                    out=dstT[:, q * T + sc0:q * T + sc0 + 512], in0=raw[:, q * 512:(q + 1) * 512], in1=Rt, op=ALU.mult),
                                  reads=[f"{kind}raw{q}", rk_], writes=[f"{kind}nT"])
                    elif kind == "kpe":
                        A("dve", lambda g, bk=bk: g.tensor_tensor(out=t1[0][0:32, :], in0=P[bk][0:32, :], in1=TABm[0:32, sc0:sc0 + 512], op=ALU.mult),
                          reads=[pk, "TABm"], writes=["t1_0"])
                        A("dve", lambda g, bk=bk: g.tensor_tensor(out=t2[0][0:32, :], in0=P[bk][32:64, :], in1=TABm[32:64, sc0:sc0 + 512], op=ALU.mult),
                          reads=[pk, "TABm"], writes=["t2_0"])
                        A("dve", lambda g: g.tensor_tensor(out=kpeT[:, sc0:sc0 + 512], in0=t1[0][0:32, :], in1=t2[0][0:32, :], op=ALU.add),
                          reads=["t1_0", "t2_0"], writes=["kpeT"])
                    else:
                        nm = kind[:2]
                        if kind[2] == "n":
                            A("dve", lambda g, bk=bk, i=i: g.tensor_tensor(out=t1[i], in0=P[bk][:, :], in1=Cr, op=ALU.mult),
                              reads=[pk, "Cr"], writes=["t1_0"])
                        else:
                            A("dve", lambda g, bk=bk, i=i: g.tensor_tensor(out=t2[i], in0=P[bk][:, :], in1=Sr, op=ALU.mult),
                              reads=[pk, "Sr"], writes=["t2_0"])
                            dstq = rqT if nm == "rq" else rkT
                            A("pool", lambda g, i=i, dstq=dstq: g.tensor_tensor(out=dstq[:, i * 512:(i + 1) * 512], in0=t1[i], in1=t2[i], op=ALU.add),
                              reads=["t1_0", "t2_0"], writes=[nm + "T"])
                            if nm == "rq":
                                A("pool", lambda g, i=i: g.tensor_tensor(out=rqm[0:64, i * 512:(i + 1) * 512], in0=t1[i][0:64, :], in1=t2[i][0:64, :],
                                                                        op=ALU.add),
                                  reads=["t1_0", "t2_0"], writes=["rqm"])
                                A("pool", lambda g, i=i: g.tensor_tensor(out=qwT[:, i * 512:(i + 1) * 512], in0=rqT[:, i * 512:(i + 1) * 512],
                                                                        in1=WQc[:, i * 512:(i + 1) * 512], op=ALU.mult),
                                  reads=["rqT", "WQc"], writes=["qwT"])
                                A("pool", lambda g, i=i: g.tensor_tensor(out=qwm[0:64, i * 512:(i + 1) * 512], in0=rqT[0:64, i * 512:(i + 1) * 512],
                                                                        in1=WQc[0:64, i * 512:(i + 1) * 512], op=ALU.mult),
                                  reads=["rqT", "WQc"], writes=["qwm"])
                ck(3)
                for j in range(4):
                    for which, off, bank in (("v", RV, 4), ("g", RG, 5)):
                        def mmt(g, j=j, off=off, bank=bank):
                            for c in range(8):
                                r = g.matmul(P[bank][:, :], lhsT=hT[:, c * 512 + j * 128:c * 512 + (j + 1) * 128], rhs=w1s(c, off, 512),
                                             start=(c == 0), stop=(c == 7))
                            return r
                        A("pe", mmt, reads=["W1", "hT"], writes=[f"P{bank}"])
                        if which == "v":
                            A("act", lambda g, j=j: g.activation(out=vtok[:, j * 512:(j + 1) * 512], in_=P[4][:, :], func=AF.Copy),
                              reads=["P4"], writes=["vtok"])
                        else:
                            A("act", lambda g, j=j: g.activation(out=sg[:, j * 512:(j + 1) * 512], in_=P[5][:, :], func=AF.Silu),
                              reads=["P5"], writes=["sg"])
                ck(4)
                for j in range(4):
                    jc = slice(j * 128, (j + 1) * 128)

                    def trk(g, j=j):
                        for i in range(2):
                            r = g.transpose(out=Pb[5][:, i * 128:(i + 1) * 128], in_=rkT[:, i * 512 + j * 128:i * 512 + (j + 1) * 128], identity=identb[:, :])
                        return r
                    A("pe", trk, reads=["rkT", "identb"], writes=["P5"])
                    A("dve", lambda g: g.tensor_tensor(out=kwtok, in0=Pb[5][:, 0:256], in1=WKc[:, :], op=ALU.mult),
                      reads=["P5", "WKc"], writes=["kwtok"])

                    ck(6)

                    def mmsc(g, j=j):
                        for h in range(4):
                            i, r0 = h // 2, 64 * (h % 2)
                            cs = slice(i * 512 + j * 128, i * 512 + (j + 1) * 128)
                            if r0 == 0:
                                r = g.matmul(P[6][:, h * 128:(h + 1) * 128], lhsT=rkT[:, cs], rhs=rqm[:, cs], start=True, stop=True)
                            else:
                                r = g.matmul(P[6][:, h * 128:(h + 1) * 128], lhsT=rkT[64:128, cs], rhs=rqT[64:128, cs], start=True, stop=True,
                                             tile_position=(64, 0))
                        return r
                    A("pe", mmsc, reads=["rkT", "rqT", "rqm"], writes=["P6"])
                    A("dve", lambda g: g.tensor_tensor(out=scTm, in0=P[6][:, :], in1=DTc[:, :], op=ALU.mult), reads=["P6", "DTc"], writes=["scTm"])

                    ck(7)

                    def mmo(g, j=j):
                        for h in range(4):
                            i, r0 = h // 2, 64 * (h % 2)
                            cs = slice(i * 512 + j * 128, i * 512 + (j + 1) * 128)
                            g.matmul(P[7][:, h * 128:(h + 1) * 128], lhsT=scTm[:, h * 128:(h + 1) * 128],
                                     rhs=vtok[:, j * 512 + h * 128:j * 512 + (h + 1) * 128], start=True, stop=False)
                            if r0 == 0:
                                r = g.matmul(P[7][:, h * 128:(h + 1) * 128], lhsT=qwm[:, cs], rhs=Sbf[:, i * 128:(i + 1) * 128], start=False, stop=True)
                            else:
                                r = g.matmul(P[7][:, h * 128:(h + 1) * 128], lhsT=qwT[64:128, cs], rhs=Sbf[64:128, i * 128:(i + 1) * 128],
                                             start=False, stop=True, tile_position=(64, 0))
                        return r
                    A("pe", mmo, reads=["scTm", "vtok", "qwT", "qwm", "Sbf"], writes=["P7"])
                    A("act", lambda g: g.activation(out=osb, in_=P[7][:, :], func=AF.Copy), reads=["P7"], writes=["osb"])

                    ck(8)

                    def mmu(g, j=j):
                        for h in range(4):
                            i, r0 = h // 2, 64 * (h % 2)
                            kw = dict(tile_position=(0, 64)) if r0 else {}
                            r = g.matmul(P[5][r0:r0 + 64, 256 + i * 128:256 + (i + 1) * 128], lhsT=kwtok[:, h * 64:(h + 1) * 64],
                                         rhs=vtok[:, j * 512 + h * 128:j * 512 + (h + 1) * 128], start=True, stop=True, **kw)
                        return r
                    A("pe", mmu, reads=["kwtok", "vtok"], writes=["P5"])
                    for i in range(2):
                        A("dve", lambda g, i=i: g.scalar_tensor_tensor(out=Sf[:, i * 128:(i + 1) * 128], in0=Sf[:, i * 128:(i + 1) * 128],
                                                                      scalar=DECc[:, i:i + 1], in1=P[5][:, 256 + i * 128:256 + (i + 1) * 128],
                                                                      op0=ALU.mult, op1=ALU.add),
                          reads=["P5", "DECc", "Sf"], writes=["Sf"])
                    A("pool", lambda g: g.tensor_copy(out=Sbf, in_=Sf), reads=["Sf"], writes=["Sbf"])
                    ck(9)
                    o3 = osb.rearrange("p (h v) -> p h v", h=4)
                    A("dve", lambda g, o3=o3: g.reduce_sum(out=st["osum"], in_=o3, axis=AX.X), reads=["osb"], writes=["osum"])
                    A("pool", lambda g: g.tensor_tensor(out=osq, in0=osb, in1=osb, op=ALU.mult), reads=["osb"], writes=["ynorm"])
                    A("dve", lambda g: g.reduce_sum(out=st["osqs"], in_=osq.rearrange("p (h v) -> p h v", h=4), axis=AX.X),
                      reads=["ynorm"], writes=["osqs"])
                    A("dve", lambda g: g.tensor_scalar(out=st["mean"], in0=st["osum"], scalar1=1.0 / 128, scalar2=None, op0=ALU.mult),
                      reads=["osum"], writes=["mean"])
                    A("dve", lambda g: g.tensor_tensor(out=st["msq"], in0=st["mean"], in1=st["mean"], op=ALU.mult), reads=["mean"], writes=["msq"])
                    A("dve", lambda g: g.scalar_tensor_tensor(out=st["var"], in0=st["osqs"], scalar=1.0 / 128, in1=st["msq"], op0=ALU.mult, op1=ALU.subtract),
                      reads=["osqs", "msq"], writes=["var"])
                    rsqrt_ops(st["rgn"], st["var"], 1.0, ["var"], "rgn")
                    for h in range(4):
                        A("dve", lambda g, h=h: g.tensor_scalar(out=ynorm[:, h * 128:(h + 1) * 128], in0=osb[:, h * 128:(h + 1) * 128],
                                                                scalar1=st["mean"][:, h:h + 1], scalar2=st["rgn"][:, h:h + 1],
                                                                op0=ALU.subtract, op1=ALU.mult),
                          reads=["osb", "mean", "rgn"], writes=["ynorm"])
                    A("pool", lambda g, j=j: g.tensor_tensor(out=ytok, in0=ynorm, in1=sg[:, j * 512:(j + 1) * 512], op=ALU.mult),
                      reads=["ynorm", "sg"], writes=["ytok"])

                    ck(10)

                    def try_(g):
                        for t in range(4):
                            r = g.transpose(out=Pb[4][:, t * 128:(t + 1) * 128], in_=ytok[:, t * 128:(t + 1) * 128], identity=identb[:, :])
                        return r
                    A("pe", try_, reads=["ytok", "identb"], writes=["P4"])
                    A("act", lambda g, j=j: g.activation(out=ysT.rearrange("p (t n) -> p t n", t=4)[:, :, j * 128:(j + 1) * 128],
                                                         in_=Pb[4][:, 0:512].rearrange("p (t n) -> p t n", t=4), func=AF.Copy),
                      reads=["P4"], writes=["ysT"])
                ck(5)
                dma("sp", yret_d[:, :, sc0:sc0 + 512].rearrange("t p n -> p t n"), ysT.rearrange("p (t n) -> p t n", t=4),
                    r=["ysT"], w=["yret_d"])
            tap("cqnT", cqnT, ["cqnT"])
            tap("ckvnT", ckvnT, ["ckvnT"])
            tap("kpeT", kpeT, ["kpeT"])
            tap("TABm", TABm, ["TABm"])
            tap("yret", yret_d, ["yret_d"])
            tap("hT", hT, ["hT"])
            tap("cqraw", cqraw, ["cqraw0", "cqraw1", "cqraw2"])
            tap("Rq", Rq, ["Rq"])
            tap("sq", sq, ["cqsq0", "cqsq1", "cqsq2"])
            tap("rqT", rqT, ["rqT"])
            tap("rkT", rkT, ["rkT"])
            tap("osb", osb, ["osb"])
            tap("ytok", ytok, ["ytok"])
            tap("Sf", Sf, ["Sf"])
            S.barrier()
            _phase[0] += 1
            if _phase[0] > STOP:
                raise _Stop()
            AR.off = P12

            ymlaT = AR.alloc(4 * T, BF16)
            P23 = AR.off
            Wq = AR.alloc(3 * 1024, BF16)
            Wkv = AR.alloc(2 * 1536, BF16)
            wqs = AR.alloc(3 * 1024)
            wkvs = AR.alloc(2 * 1536)
            qg = AR.alloc(3)
            kvg = AR.alloc(2)
            dma("sp", qg, qg_d, w=["qg"])
            dma("sp", kvg, kvg_d, w=["kvg"])
            dma("sp", wqs.rearrange("p (c n) -> p c n", c=3), wq_d.rearrange("(c p) n -> p c n", p=128), w=["wqs"])
            dma("sp", wkvs.rearrange("p (c n) -> p c n", c=2), wkv_d.rearrange("(c p) n -> p c n", p=128), w=["wkvs"])
            for c in range(3):
                A("dve", lambda g, c=c: g.tensor_scalar(out=Wq[:, c * 1024:(c + 1) * 1024], in0=wqs[:, c * 1024:(c + 1) * 1024],
                                                         scalar1=qg[:, c:c + 1], scalar2=None, op0=ALU.mult),
                  reads=["wqs", "qg"], writes=["Wq"])
            for c in range(2):
                A("dve", lambda g, c=c: g.tensor_scalar(out=Wkv[:, c * 1536:(c + 1) * 1536], in0=wkvs[:, c * 1536:(c + 1) * 1536],
                                                         scalar1=kvg[:, c:c + 1], scalar2=None, op0=ALU.mult),
                  reads=["wkvs", "kvg"], writes=["Wkv"])

            KT = [AR.alloc(T, BF16), AR.alloc(T, BF16)]
            QT = [AR.alloc(T, BF16), AR.alloc(T, BF16)]
            Vg = [AR.alloc(32 * 128, BF16), AR.alloc(32 * 128, BF16)]
            PT = [AR.alloc(1024, BF16) for _ in range(3)]
            u1 = AR.alloc(512)
            u2 = AR.alloc(512)
            rec = AR.alloc(512)
            for b in range(2):
                A("pool", lambda g, b=b: g.memset(KT[b][0:64, :], 0.0), writes=[f"KT{b}"])
                A("pool", lambda g, b=b: g.memset(QT[b][0:64, :], 0.0), writes=[f"QT{b}"])
                A("pool", lambda g, b=b: g.memset(Vg[b], 1.0), writes=[f"Vg{b}"])
            SCALE = float(96 ** -0.5)
            pt_i = 0
            for h in range(8):
                hb = h % 2
                kk, qk, vk = f"KT{hb}", f"QT{hb}", f"Vg{hb}"
                A("act", lambda g, hb=hb: g.activation(out=KT[hb][0:32, :], in_=kpeT[:, :], func=AF.Copy), reads=["kpeT"], writes=[kk])
                for sbi in range(NSB):
                    sc0 = sbi * 512

                    def mmk(g, h=h, sc0=sc0):
                        for c in range(2):
                            r = g.matmul(P[6][:, :], lhsT=Wkv[:, c * 1536 + h * 128:c * 1536 + (h + 1) * 128], rhs=ckvnT[:, c * T + sc0:c * T + sc0 + 512],
                                         start=(c == 0), stop=(c == 1))
                        return r
                    A("pe", mmk, reads=["Wkv", "ckvnT"], writes=["P6"])
                    A("dve", lambda g, hb=hb, sc0=sc0: g.tensor_copy(out=KT[hb][64:128, sc0:sc0 + 512], in_=P[6][64:128, :]),
                      reads=["P6"], writes=[kk])

                    def mmq(g, h=h, sc0=sc0):
                        for c in range(3):
                            r = g.matmul(P[7][:, :], lhsT=Wq[:, c * 1024 + h * 128:c * 1024 + (h + 1) * 128], rhs=cqnT[:, c * T + sc0:c * T + sc0 + 512],
                                         start=(c == 0), stop=(c == 2))
                        return r
                    A("pe", mmq, reads=["Wq", "cqnT"], writes=["P7"])
                    A("dve", lambda g, sc0=sc0: g.tensor_tensor(out=u1[0:32, :], in0=P[7][0:32, :], in1=TABm[0:32, sc0:sc0 + 512], op=ALU.mult),
                      reads=["P7", "TABm"], writes=["u1"])
                    A("dve", lambda g, sc0=sc0: g.tensor_tensor(out=u2[0:32, :], in0=P[7][32:64, :], in1=TABm[32:64, sc0:sc0 + 512], op=ALU.mult),
                      reads=["P7", "TABm"], writes=["u2"])
                    A("pool", lambda g, hb=hb, sc0=sc0: g.tensor_tensor(out=QT[hb][0:32, sc0:sc0 + 512], in0=u1[0:32, :], in1=u2[0:32, :], op=ALU.add),
                      reads=["u1", "u2"], writes=[qk])
                    A("dve", lambda g, hb=hb, sc0=sc0: g.tensor_copy(out=QT[hb][64:128, sc0:sc0 + 512], in_=P[7][64:128, :]),
                      reads=["P7"], writes=[qk])
                for k8 in range(4):
                    def mmv(g, h=h, k8=k8):
                        for q in range(8):
                            kb = k8 * 8 + q
                            for c in range(2):
                                r = g.matmul(P[6][:, q * 64:(q + 1) * 64], lhsT=ckvnT[:, c * T + kb * 128:c * T + (kb + 1) * 128],
                                             rhs=Wkv[:, c * 1536 + 1024 + h * 64:c * 1536 + 1024 + (h + 1) * 64], start=(c == 0), stop=(c == 1))
                        return r
                    A("pe", mmv, reads=["Wkv", "ckvnT"], writes=["P6"])
                    A("dve", lambda g, hb=hb, k8=k8: g.tensor_copy(
                        out=Vg[hb].rearrange("p (k v) -> p k v", v=128)[:, k8 * 8:(k8 + 1) * 8, 0:64],
                        in_=P[6][:, :].rearrange("p (k v) -> p k v", v=64)), reads=["P6"], writes=[vk])
                for qs in range(NSB):
                    q0 = qs * 512
                    acc = 4 + qs % 2
                    ak = f"P{acc}"
                    nfull = 4 * qs
                    groups = [(kb, min(kb + 2, nfull)) for kb in range(0, nfull, 2)]
                    items = [("full", a, b) for a, b in groups] + [("diag", 4 * qs + d, d) for d in range(4)]
                    last_kb = 4 * qs + 3
                    for gi, it in enumerate(items):
                        sbank = 2 * (gi % 2)
                        sk = f"PS{gi % 2}"
                        pt = PT[pt_i % 3]
                        ptk = f"PT{pt_i % 3}"
                        pt_i += 1
                        if it[0] == "full":
                            kbs = list(range(it[1], it[2]))

                            def mms(g, hb=hb, kbs=kbs, sbank=sbank, q0=q0):
                                for n_, kb in enumerate(kbs):
                                    r = g.matmul(P[sbank + n_][:, :], lhsT=KT[hb][:, kb * 128:(kb + 1) * 128], rhs=QT[hb][:, q0:q0 + 512],
                                                 start=True, stop=True)
                                return r
                            A("pe", mms, reads=[kk, qk], writes=[sk])
                            for n_ in range(len(kbs)):
                                A("act", lambda g, pt=pt, sbank=sbank, n_=n_: g.activation(out=pt[:, n_ * 512:(n_ + 1) * 512], in_=P[sbank + n_][:, :],
                                                                                            func=AF.Exp, scale=SCALE),
                                  reads=[sk], writes=[ptk])

                            def mmpv(g, hb=hb, kbs=kbs, pt=pt, acc=acc, last_kb=last_kb):
                                for n_, kb in enumerate(kbs):
                                    r = g.matmul(P[acc][:, :], lhsT=Vg[hb][:, kb * 128:(kb + 1) * 128], rhs=pt[:, n_ * 512:(n_ + 1) * 512],
                                                 start=(kb == 0), stop=(kb == last_kb))
                                return r
                            A("pe", mmpv, reads=[vk, ptk], writes=[ak])
                        else:
                            kb, d = it[1], it[2]
                            c0 = d * 128
                            A("pe", lambda g, hb=hb, kb=kb, c0=c0, sbank=sbank, q0=q0: g.matmul(
                                P[sbank][:, c0:512], lhsT=KT[hb][:, kb * 128:(kb + 1) * 128], rhs=QT[hb][:, q0 + c0:q0 + 512], start=True, stop=True),
                              reads=[kk, qk], writes=[sk])
                            A("act", lambda g, pt=pt, sbank=sbank, c0=c0: g.activation(out=pt[:, c0:512], in_=P[sbank][:, c0:512], func=AF.Exp, scale=SCALE),
                              reads=[sk], writes=[ptk])
                            A("pool", lambda g, pt=pt, c0=c0: g.tensor_tensor(out=pt[:, c0:c0 + 128], in0=pt[:, c0:c0 + 128], in1=trib[:, :], op=ALU.mult),
                              reads=[ptk, "trib"], writes=[ptk])
                            A("pe", lambda g, hb=hb, kb=kb, c0=c0, pt=pt, acc=acc, last_kb=last_kb: g.matmul(
                                P[acc][:, c0:512], lhsT=Vg[hb][:, kb * 128:(kb + 1) * 128], rhs=pt[:, c0:512], start=(kb == 0), stop=(kb == last_kb)),
                              reads=[vk, ptk], writes=[ak])
                    A("dve", lambda g, acc=acc: g.reciprocal(out=rec[0:64, :], in_=P[acc][64:128, :]), reads=[ak], writes=["rec"])
                    r0 = 64 * (h % 2)
                    A("dve", lambda g, acc=acc, r0=r0, h=h, q0=q0: g.tensor_tensor(
                        out=ymlaT[r0:r0 + 64, (h // 2) * T + q0:(h // 2) * T + q0 + 512], in0=P[acc][0:64, :], in1=rec[0:64, :], op=ALU.mult),
                      reads=[ak, "rec"], writes=["ymlaT"])
            S.barrier()
            _phase[0] += 1
            if _phase[0] > STOP:
                raise _Stop()

            AR.off = P23
            WU_OFF = ARN - (8 * DFF * 2) // 4
            Wg, wg_end = AR.alloc_at(0, 8 * DFF, BF16)
            Wu, _ = AR.alloc_at(WU_OFF, 8 * DFF, BF16)
            assert wg_end <= P12
            for hf in range(2):
                dma("pool", Wg.rearrange("p (c n) -> p c n", c=8)[:, hf * 4:(hf + 1) * 4, :],
                    wg_d.rearrange("(c p) n -> p c n", p=128)[:, hf * 4:(hf + 1) * 4, :], w=["Wg"])
                dma("pool", Wu.rearrange("p (c n) -> p c n", c=8)[:, hf * 4:(hf + 1) * 4, :],
                    wu_d.rearrange("(c p) n -> p c n", p=128)[:, hf * 4:(hf + 1) * 4, :], w=["Wu"])
            Wo = AR.alloc(8 * D, BF16)
            wos = [AR.alloc(D), AR.alloc(D)]
            og = AR.alloc(8)
            yrTs = [AR.alloc(4 * 512, BF16), AR.alloc(4 * 512, BF16)]
            ysq = AR.alloc(4 * 128, BF16)
            xt3 = [AR.alloc(D), AR.alloc(D)]
            mB = AR.alloc(D)
            mixs = AR.alloc(D)
            tt = AR.alloc(D)
            x1 = [AR.alloc(D), AR.alloc(D)]
            junk3 = AR.alloc(D, BF16)
            rm = AR.alloc(1)
            r2 = AR.alloc(1)
            dma("sp", og, og_d, w=["og"])
            for c in range(8):
                dma("sp", wos[c % 2], wout_d[c * 128:(c + 1) * 128, :], w=[f"wos{c % 2}"])
                A("dve", lambda g, c=c: g.tensor_scalar(out=Wo[:, c * D:(c + 1) * D], in0=wos[c % 2], scalar1=og[:, c:c + 1],
                                                         scalar2=None, op0=ALU.mult), reads=[f"wos{c % 2}", "og"], writes=["Wo"])
            for tb in range(32):
                b2 = tb % 2
                tc0 = tb * 128
                sbi, j = tb // 4, tb % 4
                yb = yrTs[sbi % 2]
                ybk = f"yrT{sbi % 2}"
                if j == 0:
                    dma("sp", yb.rearrange("p (t n) -> p t n", t=4), yret_d[:, :, sbi * 512:(sbi + 1) * 512].rearrange("t p n -> p t n"),
                        r=["yret_d"], w=[ybk])
                dma("sp", xt3[b2], x_d[tc0:tc0 + 128, :], w=[f"xt3{b2}"])
                A("pool", lambda g, tc0=tc0: g.tensor_tensor(out=ysq.rearrange("p (c n) -> p c n", c=4),
                                                              in0=ymlaT.rearrange("p (c n) -> p c n", c=4)[:, :, tc0:tc0 + 128],
                                                              in1=ymlaT.rearrange("p (c n) -> p c n", c=4)[:, :, tc0:tc0 + 128], op=ALU.mult),
                  reads=["ymlaT"], writes=["ysq"])

                def mmss(g):
                    for c in range(4):
                        r = g.matmul(P[6][:, 0:1], lhsT=ysq[:, c * 128:(c + 1) * 128], rhs=onesb[:, 0:1], start=(c == 0), stop=(c == 3))
                    return r
                A("pe", mmss, reads=["ysq", "onesb"], writes=["P6"])
                rsqrt_ops(rm, P[6][:, 0:1], 1.0 / 512, ["P6"], "rm")

                def mmA(g, tc0=tc0):
                    for hf in range(2):
                        for c in range(4):
                            r = g.matmul(P[hf][:, :], lhsT=ymlaT[:, c * T + tc0:c * T + tc0 + 128], rhs=Wo[:, c * D + hf * 512:c * D + (hf + 1) * 512],
                                         start=(c == 0), stop=(c == 3))
                    return r
                A("pe", mmA, reads=["ymlaT", "Wo"], writes=["PA"])

                def mmB(g, yb=yb, j=j):
                    for hf in range(2):
                        for c in range(4):
                            r = g.matmul(P[2 + hf][:, :], lhsT=yb[:, c * 512 + j * 128:c * 512 + (j + 1) * 128],
                                         rhs=Wo[:, (4 + c) * D + hf * 512:(4 + c) * D + (hf + 1) * 512], start=(c == 0), stop=(c == 3))
                    return r
                A("pe", mmB, reads=[ybk, "Wo"], writes=["PB"])
                for hf in range(2):
                    A("act", lambda g, hf=hf: g.activation(out=mB[:, hf * 512:(hf + 1) * 512], in_=P[2 + hf][:, :], func=AF.Copy),
                      reads=["PB"], writes=["mB"])
                    A("dve", lambda g, hf=hf: g.scalar_tensor_tensor(out=mixs[:, hf * 512:(hf + 1) * 512], in0=P[hf][:, :], scalar=rm[:, 0:1],
                                                                      in1=mB[:, hf * 512:(hf + 1) * 512], op0=ALU.mult, op1=ALU.add),
                      reads=["PA", "rm", "mB"], writes=["mixs"])
                A("act", lambda g: g.activation(out=junk3, in_=mixs, func=AF.Square, accum_out=r2), reads=["mixs"], writes=["junk3", "r2"])
                rsqrt_ops(r2, r2, 1.0 / D, ["r2"], "r2")
                A("dve", lambda g: g.scalar_tensor_tensor(out=tt, in0=mixs, scalar=r2[:, 0:1], in1=G1b[:, :], op0=ALU.mult, op1=ALU.mult),
                  reads=["mixs", "r2", "G1b"], writes=["tt"])
                A("pool", lambda g, b2=b2: g.tensor_tensor(out=x1[b2], in0=xt3[b2], in1=tt, op=ALU.add), reads=[f"xt3{b2}", "tt"], writes=[f"x1{b2}"])
                dma("sp", out_d[tc0:tc0 + 128, :], x1[b2], r=[f"x1{b2}"], w=[f"out{tb}"])
            assert AR.off <= WU_OFF, (AR.off, WU_OFF)
            S.barrier()
            _phase[0] += 1
            if _phase[0] > STOP:
                raise _Stop()

            Wd, wd_end = AR.alloc_at(wg_end, NJ * D, BF16)
            assert wd_end <= P23
            AR.off = P23
            x1t = AR.alloc(4 * D)
            xn4 = AR.alloc(D, BF16)
            junk4 = AR.alloc(D, BF16)
            s4 = [AR.alloc(1), AR.alloc(1)]
            h2T = AR.alloc(8 * 512, BF16)
            h1T = AR.alloc(NJ * 512, BF16)
            sgt = [AR.alloc(512), AR.alloc(512)]
            t4 = AR.alloc(D)
            r3 = AR.alloc(1)
            for hf in range(2):
                dma("pool", Wd.rearrange("p (c n) -> p c n", c=NJ)[:, hf * 11:(hf + 1) * 11, :],
                    wd_d.rearrange("(c p) n -> p c n", p=128)[:, hf * 11:(hf + 1) * 11, :], w=["Wd"])
            fin = []
            for sbi in range(NSB):
                for j in range(4):
                    tb = sbi * 4 + j
                    b2 = tb % 2
                    xv = x1t[:, j * D:(j + 1) * D]
                    dma("sp", xv, out_d[tb * 128:(tb + 1) * 128, :], r=[f"out{tb}"], w=[f"x1t{j}"])
                    A("act", lambda g, xv=xv, b2=b2: g.activation(out=junk4, in_=xv, func=AF.Square, accum_out=s4[b2]),
                      reads=[f"x1t{j}"], writes=["junk4", f"s4{b2}"])
                    rsqrt_ops(s4[b2], s4[b2], 1.0 / D, [f"s4{b2}"], f"s4{b2}")
                    A("act", lambda g, xv=xv, b2=b2: g.activation(out=xn4, in_=xv, func=AF.Copy, scale=s4[b2]),
                      reads=[f"x1t{j}", f"s4{b2}"], writes=["xn4"])

                    def tr8b(g, b2=b2):
                        for c in range(8):
                            r = g.transpose(out=Pb[b2][:, c * 128:(c + 1) * 128], in_=xn4[:, c * 128:(c + 1) * 128], identity=identb[:, :])
                        return r
                    A("pe", tr8b, reads=["xn4", "identb"], writes=[f"P{b2}"])
                    for c in range(8):
                        dst = h2T[:, c * 512 + j * 128:c * 512 + (j + 1) * 128]
                        if c % 2 == 0:
                            A("dve", lambda g, c=c, dst=dst, b2=b2: g.tensor_scalar(out=dst, in0=Pb[b2][:, c * 128:(c + 1) * 128],
                                                                                   scalar1=a2[:, c:c + 1], scalar2=sh2[:, c:c + 1],
                                                                                   op0=ALU.mult, op1=ALU.add),
                              reads=[f"P{b2}", "a2", "modc"], writes=["h2T"])
                        else:
                            A("act", lambda g, c=c, dst=dst, b2=b2: g.activation(out=dst, in_=Pb[b2][:, c * 128:(c + 1) * 128], func=AF.Identity,
                                                                                scale=a2[:, c:c + 1], bias=sh2[:, c:c + 1]),
                              reads=[f"P{b2}", "a2", "modc"], writes=["h2T"])
                for jj in range(NJ):
                    gb = 2 + jj % 2
                    ub = 4 + jj % 2

                    def mmg(g, jj=jj, gb=gb, ub=ub):
                        for c in range(8):
                            g.matmul(P[gb][:, :], lhsT=Wg[:, c * DFF + jj * 128:c * DFF + (jj + 1) * 128], rhs=h2T[:, c * 512:(c + 1) * 512],
                                     start=(c == 0), stop=(c == 7))
                        for c in range(8):
                            r = g.matmul(P[ub][:, :], lhsT=Wu[:, c * DFF + jj * 128:c * DFF + (jj + 1) * 128], rhs=h2T[:, c * 512:(c + 1) * 512],
                                         start=(c == 0), stop=(c == 7))
                        return r
                    A("pe", mmg, reads=["Wg", "Wu", "h2T"], writes=[f"P{gb}", f"P{ub}"])
                    A("act", lambda g, jj=jj, gb=gb: g.activation(out=sgt[jj % 2], in_=P[gb][:, :], func=AF.Silu), reads=[f"P{gb}"], writes=[f"sgt{jj % 2}"])
                    A("dve", lambda g, jj=jj, ub=ub: g.tensor_tensor(out=h1T[:, jj * 512:(jj + 1) * 512], in0=P[ub][:, :], in1=sgt[jj % 2], op=ALU.mult),
                      reads=[f"P{ub}", f"sgt{jj % 2}"], writes=["h1T"])
                for j in range(4):
                    tb = sbi * 4 + j
                    xv = x1t[:, j * D:(j + 1) * D]

                    def mmd(g, j=j):
                        for hf in range(2):
                            for jj in range(NJ):
                                r = g.matmul(P[6 + hf][:, :], lhsT=h1T[:, jj * 512 + j * 128:jj * 512 + (j + 1) * 128],
                                             rhs=Wd[:, jj * D + hf * 512:jj * D + (hf + 1) * 512], start=(jj == 0), stop=(jj == NJ - 1))
                        return r
                    A("pe", mmd, reads=["h1T", "Wd"], writes=["PF"])
                    for hf in range(2):
                        A("act", lambda g, hf=hf: g.activation(out=t4[:, hf * 512:(hf + 1) * 512], in_=P[6 + hf][:, :], func=AF.Copy),
                          reads=["PF"], writes=["t4"])
                    A("act", lambda g: g.activation(out=junk4, in_=t4, func=AF.Square, accum_out=r3), reads=["t4"], writes=["junk4", "r3"])
                    rsqrt_ops(r3, r3, 1.0 / D, ["r3"], "r3")
                    A("dve", lambda g: g.scalar_tensor_tensor(out=t4, in0=t4, scalar=r3[:, 0:1], in1=G2b[:, :], op0=ALU.mult, op1=ALU.mult),
                      reads=["t4", "r3", "G2b"], writes=["t4"])
                    A("pool", lambda g, xv=xv: g.tensor_tensor(out=xv, in0=xv, in1=t4, op=ALU.add), reads=[f"x1t{j}", "t4"], writes=[f"x1t{j}"])
                    fin.append(dma("sp", out_d[tb * 128:(tb + 1) * 128, :], xv, r=[f"x1t{j}"], w=[f"out{tb}"]))
            A("sp", lambda g: None, deps=fin)

        except _Stop:
            pass
        with nc.Block() as block:
            S.emit_all(block, esem, dsem)
    return nc


def _consts():
    f = np.float32
    gam = 1.0 - 2.0 ** (-5.0 - np.arange(4, dtype=np.float64))
    idx = np.arange(128)
    ident = np.eye(128, dtype=f)
    tri = (idx[None, :] >= idx[:, None]).astype(f)
    dtc = np.zeros((128, 4, 128), np.float64)
    rel = idx[None, :] - idx[:, None]
    for h in range(4):
        dtc[:, h, :] = np.where(rel >= 0, gam[h] ** np.maximum(rel, 0), 0.0) * 0.125
    wqc = np.zeros((128, 2, 512), np.float64)
    wkc = np.zeros((128, 2, 128), np.float64)
    decc = np.zeros((128, 2), np.float64)
    for i in range(2):
        for r in range(128):
            h = 2 * i + r // 64
            wqc[r, i, :] = np.tile(gam[h] ** (idx + 1.0), 4)
            decc[r, i] = gam[h] ** 128
        for ft in range(128):
            h = 2 * i + ft // 64
            wkc[:, i, ft] = gam[h] ** (127.0 - idx) * 0.125
    inv_m = 10000.0 ** (-np.arange(16, dtype=np.float64) / 16.0)
    inv_r = 10000.0 ** (-np.arange(32, dtype=np.float64) / 32.0)
    invc = np.zeros((128, 3), np.float64)
    phc = np.zeros((128, 3), np.float64)
    for r in range(64):
        invc[r, 0] = inv_m[r % 16]
        phc[r, 0] = np.pi / 2 if r < 32 else (np.pi if r < 48 else 0.0)
    for r in range(128):
        invc[r, 1] = inv_r[r % 32]
        invc[r, 2] = inv_r[r % 32]
        phc[r, 1] = np.pi / 2
        phc[r, 2] = np.pi if (r % 64) < 32 else 0.0
    return dict(ident=ident, tri=tri, dtc=dtc.reshape(128, 512).astype(f), wqc=wqc.reshape(128, 1024).astype(f),
                wkc=wkc.reshape(128, 256).astype(f), decc=decc.astype(f), invc=invc.astype(f), phc=phc.astype(f))


def _colmajor(v, n):
    return np.ascontiguousarray(np.asarray(v, np.float32).reshape(n, 128).T)


def _prep_shared(inp):
    f = np.float32
    w_in = np.asarray(inp["w_in"], f)[0]
    cols = list(range(0, 640))
    cols += list(range(640, 672)) + [640 + k for k in list(range(16, 32)) + list(range(0, 16))]
    for base in (672, 928):
        for i in range(2):
            nat, sw = [], []
            for hh in (2 * i, 2 * i + 1):
                b = base + hh * 64
                nat += list(range(b, b + 64))
                sw += list(range(b + 32, b + 64)) + list(range(b, b + 32))
            cols += nat + sw
    cols += list(range(1184, 2208))
    w1 = np.ascontiguousarray(w_in[:, cols])
    assert w1.shape[1] == NC1
    wqb = np.asarray(inp["w_q_b"], f)[0]
    qc = []
    for h in range(8):
        b = h * 96
        qc += list(range(b + 64, b + 96)) + [b + 64 + k for k in list(range(16, 32)) + list(range(0, 16))] + list(range(b, b + 64))
    wq = np.ascontiguousarray(wqb[:, qc])
    wkvb = np.asarray(inp["w_kv_b"], f)[0]
    wkv = np.zeros((256, 1536), f)
    for h in range(8):
        wkv[:, h * 128 + 64:h * 128 + 128] = wkvb[:, h * 128:h * 128 + 64]
        wkv[:, 1024 + h * 64:1024 + (h + 1) * 64] = wkvb[:, h * 128 + 64:h * 128 + 128]
    sh = dict(
        w_ada=np.ascontiguousarray(np.asarray(inp["w_ada"], f)[0]),
        b_ada=np.ascontiguousarray(np.asarray(inp["b_ada"], f)[0][None, :]),
        gpre1=_colmajor(inp["pre_norm_mix"][0], 8), gpre2=_colmajor(inp["pre_norm_ffn"][0], 8),
        gpost1=np.ascontiguousarray(np.asarray(inp["post_norm_mix"], f)[0][None, :]),
        gpost2=np.ascontiguousarray(np.asarray(inp["post_norm_ffn"], f)[0][None, :]),
        qg=_colmajor(inp["q_a_norm"][0], 3), kvg=_colmajor(inp["kv_a_norm"][0], 2),
        og=_colmajor(np.concatenate([np.asarray(inp["mla_out_norm"], f)[0], np.asarray(inp["ret_gn_gain"], f)[0]]), 8),
        w1=w1, wq=wq, wkv=wkv,
        wout=np.ascontiguousarray(np.asarray(inp["w_out"], f)[0]),
        wg=np.ascontiguousarray(np.asarray(inp["w_gate"], f)[0]),
        wu=np.ascontiguousarray(np.asarray(inp["w_up"], f)[0]),
        wd=np.ascontiguousarray(np.asarray(inp["w_down"], f)[0]),
    )
    sh.update(_consts())
    return sh


def make_in_maps(inp, cores):
    sh = _prep_shared(inp)
    x = np.asarray(inp["x"], np.float32)
    c = np.asarray(inp["c"], np.float32)
    pos = np.asarray(inp["positions"], np.int32)
    maps = []
    for b in cores:
        m = dict(sh)
        m["x"] = np.ascontiguousarray(x[b])
        m["cT"] = _colmajor(c[b], 8)
        m["pos"] = np.ascontiguousarray(pos[b][None, :])
        maps.append(m)
    return maps


_NC = None


def kernel(**inputs):
    global _NC
    if _NC is None:
        _NC = build_nc()
    maps = make_in_maps(inputs, list(range(8)))
    res = run_bass_kernel_spmd(_NC, maps, core_ids=list(range(8)))
    return np.stack([np.asarray(r["out"], np.float32) for r in res.results], axis=0)
```

```python
import contextlib
import types
import numpy as np
import concourse.bass as bass
import concourse.mybir as mybir
from concourse.bass_utils import run_bass_kernel_spmd

F32 = mybir.dt.float32
BF16 = mybir.dt.bfloat16
I32 = mybir.dt.int32
AF = mybir.ActivationFunctionType
ALU = mybir.AluOpType
AX = mybir.AxisListType

ENGS = ("pe", "act", "dve", "pool", "sp")
STOP = 99
SUB = 99
HSEL = (0, 1, 2, 3)
TAPS = ()
REORDER = True


class _Stop(Exception):
    pass

T = 4096
D = 1024
NSB = 8
DFF = 2816
NJ = 22
NC1 = 2752
EPS = 1e-6
PI = float(np.pi)


def _freeze(fn):
    if fn.__closure__ is None:
        return fn
    cells = []
    for c in fn.__closure__:
        try:
            cells.append(types.CellType(c.cell_contents))
        except ValueError:
            cells.append(c)
    return types.FunctionType(fn.__code__, fn.__globals__, fn.__name__, fn.__defaults__, tuple(cells))


class Op:
    __slots__ = ("eng", "idx", "emit", "deps", "is_dma", "dma_i", "marked", "count", "clock", "waits", "is_bar", "busy", "lat", "seq")


def _nfree(ap):
    n = 1
    for d in ap.shape[1:]:
        n *= int(d)
    return n


class _Fake:
    def __init__(self, eng):
        self.eng = eng
        self.busy = 0.0
        self.lat = None

    def matmul(self, out, lhsT=None, rhs=None, **kw):
        n = max(_nfree(rhs), 64)
        f = 4.0 if rhs.dtype == F32 else 1.0
        self.busy += f * n / 2370.0 + 0.004
        return self

    def transpose(self, out=None, in_=None, identity=None, **kw):
        self.busy += 0.08
        return self

    def activation(self, out=None, in_=None, **kw):
        self.busy += 0.1 + _nfree(in_) / 1150.0
        return self

    def dma_start(self, out=None, in_=None, **kw):
        nb = _nfree(out) * int(out.shape[0]) * (4 if out.dtype in (F32, I32) else 2)
        self.busy += 0.15 if self.eng == "sp" else 1.2
        self.lat = 2.5 + nb / 150e3
        return self

    def _dve(self, out, **kw):
        n = _nfree(out)
        if self.eng == "pool":
            self.busy += 0.2 + n / 500.0
        else:
            self.busy += 0.12 + n / 900.0
        return self

    def tensor_tensor(self, out=None, **kw):
        return self._dve(out)

    def tensor_scalar(self, out=None, **kw):
        return self._dve(out)

    def tensor_copy(self, out=None, **kw):
        return self._dve(out)

    def scalar_tensor_tensor(self, out=None, **kw):
        return self._dve(out)

    def tensor_single_scalar(self, out=None, **kw):
        return self._dve(out)

    def reciprocal(self, out=None, **kw):
        self.busy += 0.1 + _nfree(out) / 150.0
        return self

    def memset(self, ap, *a, **kw):
        return self._dve(ap)

    def reduce_sum(self, out=None, in_=None, **kw):
        return self._dve(in_)

    def then_inc(self, *a, **kw):
        return self


class Sched:
    def __init__(self, n_dma_sems=12):
        self.ops = {e: [] for e in ENGS}
        self.order = []
        self.lastw = {}
        self.readers = {}
        self.n_dma_sems = n_dma_sems
        self.dma_ops = {e: [] for e in ENGS}
        self.dma_since_bar = []

    def add(self, eng, emit, reads=(), writes=(), dma=False, deps=()):
        op = Op()
        op.eng = eng
        op.emit = _freeze(emit)
        op.is_dma = dma
        op.marked = False
        op.count = 0
        op.idx = len(self.ops[eng])
        d = set(deps)
        for k in reads:
            w = self.lastw.get(k)
            if w is not None:
                d.add(w)
        for k in writes:
            w = self.lastw.get(k)
            if w is not None:
                d.add(w)
            for r in self.readers.get(k, ()):
                d.add(r)
        for k in reads:
            self.readers.setdefault(k, []).append(op)
        for k in writes:
            self.lastw[k] = op
            self.readers[k] = []
        d.discard(op)
        op.deps = d
        op.is_bar = False
        op.seq = len(self.order)
        self.ops[eng].append(op)
        self.order.append(op)
        return op

    def barrier(self):
        for e in ENGS:
            self.add(e, lambda g: None).is_bar = True

    def _list_schedule(self, seg):
        import heapq
        segset = set(seg)
        succ = {o: [] for o in seg}
        indeg = {}
        for o in seg:
            fk = _Fake(o.eng)
            o.emit(fk)
            o.busy = fk.busy
            o.lat = fk.lat if fk.lat is not None else fk.busy + 0.06
            k = 0
            for d in o.deps:
                if d in segset:
                    succ[d].append(o)
                    k += 1
            indeg[o] = k
        bl = {}
        for o in reversed(seg):
            m = 0.0
            for s_ in succ[o]:
                if bl[s_] > m:
                    m = bl[s_]
            bl[o] = o.lat + m
        free = {e: 0.0 for e in ENGS}
        avail = {e: [] for e in ENGS}
        future = {e: [] for e in ENGS}
        rtime = {o: 0.0 for o in seg}
        for o in seg:
            if indeg[o] == 0:
                heapq.heappush(future[o.eng], (0.0, o.seq, o))
        out = []
        n = len(seg)
        while len(out) < n:
            best_e, best_t = None, None
            for e in ENGS:
                fu, av = future[e], avail[e]
                while fu and fu[0][0] <= free[e]:
                    _, sq, o = heapq.heappop(fu)
                    heapq.heappush(av, (-bl[o], sq, o))
                if av:
                    t = free[e]
                elif fu:
                    t = fu[0][0]
                else:
                    continue
                if best_t is None or t < best_t:
                    best_e, best_t = e, t
            e = best_e
            if not avail[e]:
                free[e] = best_t
                fu, av = future[e], avail[e]
                while fu and fu[0][0] <= free[e]:
                    _, sq, o = heapq.heappop(fu)
                    heapq.heappush(av, (-bl[o], sq, o))
            _, sq, o = heapq.heappop(avail[e])
            st = free[e]
            free[e] = st + o.busy
            fin = st + o.lat
            out.append(o)
            for s_ in succ[o]:
                if fin > rtime[s_]:
                    rtime[s_] = fin
                indeg[s_] -= 1
                if indeg[s_] == 0:
                    heapq.heappush(future[s_.eng], (rtime[s_], s_.seq, s_))
        return out

    def schedule(self, reorder=True):
        segs, cur = [], []
        for o in self.order:
            if o.is_bar:
                if cur:
                    segs.append(("seg", cur))
                    cur = []
                if segs and segs[-1][0] == "bar":
                    segs[-1][1].append(o)
                else:
                    segs.append(("bar", [o]))
            else:
                cur.append(o)
        if cur:
            segs.append(("seg", cur))
        new = []
        last_seg = []
        for kind, lst in segs:
            if kind == "seg":
                lst2 = self._list_schedule(lst) if reorder else lst
                new += lst2
                last_seg = lst2
            else:
                deps = [o for o in last_seg if o.is_dma]
                for e in ENGS:
                    for o in reversed(last_seg):
                        if o.eng == e and not o.is_dma:
                            deps.append(o)
                            break
                for o in lst:
                    o.deps = set(deps)
                new += lst
        self.order = new
        self.ops = {e: [] for e in ENGS}
        self.dma_ops = {e: [] for e in ENGS}
        for o in new:
            o.idx = len(self.ops[o.eng])
            self.ops[o.eng].append(o)
            if o.is_dma:
                o.dma_i = len(self.dma_ops[o.eng])
                if o.dma_i >= self.n_dma_sems:
                    o.deps.add(self.dma_ops[o.eng][o.dma_i - self.n_dma_sems])
                self.dma_ops[o.eng].append(o)

    def resolve(self):
        known = {e: {f: -1 for f in ENGS} for e in ENGS}
        known_dma = {e: set() for e in ENGS}
        for op in self.order:
            e = op.eng
            kn = known[e]
            waits = []
            for d in sorted(op.deps, key=lambda o: -o.idx):
                if d.is_dma:
                    if d in known_dma[e]:
                        continue
                    known_dma[e].add(d)
                    waits.append(d)
                else:
                    if d.eng == "pe" and e == "pe":
                        continue
                    if kn[d.eng] >= d.idx:
                        continue
                    d.marked = True
                    waits.append(d)
                ck = d.clock
                for f in ENGS:
                    if ck[f] > kn[f]:
                        kn[f] = ck[f]
            op.waits = waits
            ck = dict(kn)
            if not op.is_dma:
                ck[e] = max(ck[e], op.idx)
            op.clock = ck
        for e in ENGS:
            c = 0
            for op in self.ops[e]:
                if op.marked:
                    c += 1
                    op.count = c

    def emit_all(self, block, esem, dsem):
        self.schedule(reorder=REORDER)
        self.resolve()
        n = self.n_dma_sems

        def run(e, engobj):
            for op in self.ops[e]:
                for d in op.waits:
                    if d.is_dma:
                        engobj.wait_ge(dsem[d.eng][d.dma_i % n], 16 * (d.dma_i // n + 1))
                    else:
                        engobj.wait_ge(esem[d.eng], d.count)
                ins = op.emit(engobj)
                if op.is_dma:
                    ins.then_inc(dsem[e][op.dma_i % n], 16)
                elif op.marked:
                    if ins is None:
                        ins = engobj.nop()
                    ins.then_inc(esem[e], 1)

        block.tensor(lambda t: run("pe", t))
        block.scalar(lambda t: run("act", t))
        block.vector(lambda t: run("dve", t))
        block.gpsimd(lambda t: run("pool", t))
        block.sync(lambda t: run("sp", t))


class Arena:
    def __init__(self, ap, ncols):
        self.ap = ap
        self.n = ncols
        self.off = 0

    def alloc(self, cols, dt=F32):
        nb = cols * (4 if dt in (F32, I32) else 2)
        n32 = ((nb + 31) // 32) * 8
        assert self.off + n32 <= self.n, ("arena overflow", self.off, n32, self.n)
        v = self.ap[:, self.off:self.off + n32]
        self.off += n32
        if dt != F32:
            v = v.bitcast(dt)
        return v[:, 0:cols]

    def reset(self):
        self.off = 0

    def alloc_at(self, off32, cols, dt=F32):
        save = self.off
        self.off = off32
        v = self.alloc(cols, dt)
        end = self.off
        self.off = save
        return v, end


def build_nc():
    nc = bass.Bass("TRN2", target_bir_lowering=False)

    def DI(name, shape, dt=F32):
        return nc.dram_tensor(name, shape, dt, kind="ExternalInput").ap()

    x_d = DI("x", [T, D])
    c_d = DI("cT", [128, 8])
    pos_d = DI("pos", [1, T], I32)
    wada_d = DI("w_ada", [D, 6 * D])
    bada_d = DI("b_ada", [1, 6 * D])
    gpre1_d = DI("gpre1", [128, 8])
    gpre2_d = DI("gpre2", [128, 8])
    gpost1_d = DI("gpost1", [1, D])
    gpost2_d = DI("gpost2", [1, D])
    qg_d = DI("qg", [128, 3])
    kvg_d = DI("kvg", [128, 2])
    og_d = DI("og", [128, 8])
    w1_d = DI("w1", [D, NC1])
    wq_d = DI("wq", [384, 1024])
    wkv_d = DI("wkv", [256, 1536])
    wout_d = DI("wout", [D, D])
    wg_d = DI("wg", [D, DFF])
    wu_d = DI("wu", [D, DFF])
    wd_d = DI("wd", [DFF, D])
    ident_d = DI("ident", [128, 128])
    tri_d = DI("tri", [128, 128])
    dt_d = DI("dtc", [128, 512])
    wqc_d = DI("wqc", [128, 1024])
    wkc_d = DI("wkc", [128, 256])
    dec_d = DI("decc", [128, 2])
    inv_d = DI("invc", [128, 3])
    ph_d = DI("phc", [128, 3])
    out_d = nc.dram_tensor("out", [T, D], F32, kind="ExternalOutput").ap()
    yret_d = nc.dram_tensor("yret_scr", [4, 128, T], BF16).ap()

    S = Sched(n_dma_sems=12)
    A = S.add

    with contextlib.ExitStack() as ctx:
        def sbt(name, cols, dt=F32, parts=128):
            return ctx.enter_context(nc.sbuf_tensor(name, [parts, cols], dt))

        identb = sbt("identb", 128, BF16)
        trib = sbt("trib", 128, BF16)
        onesb = sbt("onesb", 128, BF16)
        onesf = sbt("onesf", 128)
        epst = sbt("epst", 1)
        DECc = sbt("DECc", 2)
        INVc = sbt("INVc", 3)
        PHc = sbt("PHc", 3)
        modc = sbt("modc", 32)
        a1 = sbt("a1", 8)
        a2 = sbt("a2", 8)
        G1b = sbt("G1b", D)
        G2b = sbt("G2b", D)
        ARN = 50000
        arena_t = sbt("arena", ARN)
        AR = Arena(arena_t, ARN)
        P = [ctx.enter_context(nc.psum_tensor(f"bank{i}", [128, 512], F32)) for i in range(8)]
        Pb = [p[:, :].bitcast(BF16) for p in P]

        esem = {e: ctx.enter_context(nc.semaphore("es_" + e)) for e in ENGS}
        dsem = {e: [ctx.enter_context(nc.semaphore(f"ds_{e}{i}")) for i in range(12)] for e in ("sp", "pool")}

        def dma(q, out, in_, r=(), w=()):
            return A(q, lambda g: g.dma_start(out=out, in_=in_), reads=r, writes=w, dma=True)

        def tap(name, ap, keys):
            if name not in TAPS:
                return
            shp = list(ap.shape)
            dd = nc.dram_tensor("dbg_" + name, shp, ap.dtype, kind="ExternalOutput").ap()
            dma("sp", dd, ap, r=keys)

        def rsqrt_ops(dst, src, scale, rk, wk):
            A("act", lambda g: g.activation(out=dst, in_=src, func=AF.Sqrt, scale=scale, bias=epst[0:dst.shape[0], :]),
              reads=list(rk) + ["epst"], writes=[wk])
            A("dve", lambda g: g.reciprocal(out=dst, in_=dst), reads=[wk], writes=[wk])

        _phase = [0]
        try:
            dma("pool", identb[:, :], ident_d, w=["identb"])
            dma("pool", trib[:, :], tri_d, w=["trib"])
            A("pool", lambda g: g.memset(onesb[:, :], 1.0), writes=["onesb"])
            A("pool", lambda g: g.memset(onesf[:, :], 1.0), writes=["onesf"])
            A("pool", lambda g: g.memset(epst[:, :], EPS), writes=["epst"])
            for t_, d_, k_ in ((DECc, dec_d, "DECc"), (INVc, inv_d, "INVc"), (PHc, ph_d, "PHc")):
                dma("sp", t_[:, :], d_, w=[k_])
            cT = AR.alloc(8)
            gp1 = AR.alloc(8)
            gp2 = AR.alloc(8)
            scb = AR.alloc(8, BF16)
            gpo1 = AR.alloc(D)
            gpo2 = AR.alloc(D)
            bada = AR.alloc(6 * D)
            modrow = AR.alloc(6 * D)
            grow1 = AR.alloc(D)
            grow2 = AR.alloc(D)
            wa = [AR.alloc(8 * 512, BF16), AR.alloc(8 * 512, BF16)]
            dma("sp", cT, c_d, w=["cT"])
            dma("sp", gp1, gpre1_d, w=["gp1"])
            dma("sp", gp2, gpre2_d, w=["gp2"])
            dma("sp", gpo1[0:1, :], gpost1_d, w=["gpo1"])
            dma("sp", gpo2[0:1, :], gpost2_d, w=["gpo2"])
            dma("sp", bada[0:1, :], bada_d, w=["bada"])
            A("act", lambda g: g.activation(out=scb, in_=cT, func=AF.Silu), reads=["cT"], writes=["scb"])
            for gi in range(12):
                wb = wa[gi % 2]
                dma("pool", wb.rearrange("p (c n) -> p c n", c=8),
                    wada_d[:, gi * 512:(gi + 1) * 512].rearrange("(c p) n -> p c n", p=128), w=[f"wa{gi % 2}"])

                def mm_ada(g, gi=gi, wb=wb):
                    for k in range(8):
                        r = g.matmul(P[gi % 2][0:1, :], lhsT=scb[:, k:k + 1], rhs=wb[:, k * 512:(k + 1) * 512],
                                     start=(k == 0), stop=(k == 7))
                    return r
                A("pe", mm_ada, reads=["scb", f"wa{gi % 2}"], writes=[f"P{gi % 2}"])
                A("dve", lambda g, gi=gi: g.tensor_tensor(out=modrow[0:1, gi * 512:(gi + 1) * 512], in0=P[gi % 2][0:1, :],
                                                         in1=bada[0:1, gi * 512:(gi + 1) * 512], op=ALU.add),
                  reads=[f"P{gi % 2}", "bada"], writes=["modrow"])
            col_offs = [0 * D, 1 * D, 3 * D, 4 * D]

            def mm_cols(g):
                for vi, off in enumerate(col_offs):
                    for c in range(8):
                        r = g.matmul(P[2][:, vi * 8 + c:vi * 8 + c + 1], lhsT=modrow[0:1, off + c * 128:off + (c + 1) * 128],
                                     rhs=onesf[0:1, 0:1], start=True, stop=True)
                return r
            A("pe", mm_cols, reads=["modrow", "onesf"], writes=["P2"])
            A("dve", lambda g: g.tensor_copy(out=modc[:, :], in_=P[2][:, 0:32]), reads=["P2"], writes=["modc"])
            A("dve", lambda g: g.scalar_tensor_tensor(out=a1[:, :], in0=modc[:, 8:16], scalar=1.0, in1=gp1, op0=ALU.add, op1=ALU.mult),
              reads=["modc", "gp1"], writes=["a1"])
            A("dve", lambda g: g.scalar_tensor_tensor(out=a2[:, :], in0=modc[:, 24:32], scalar=1.0, in1=gp2, op0=ALU.add, op1=ALU.mult),
              reads=["modc", "gp2"], writes=["a2"])
            sh1 = modc[:, 0:8]
            sh2 = modc[:, 16:24]
            A("dve", lambda g: g.tensor_tensor(out=grow1[0:1, :], in0=modrow[0:1, 2 * D:3 * D], in1=gpo1[0:1, :], op=ALU.mult),
              reads=["modrow", "gpo1"], writes=["grow1"])
            A("dve", lambda g: g.tensor_tensor(out=grow2[0:1, :], in0=modrow[0:1, 5 * D:6 * D], in1=gpo2[0:1, :], op=ALU.mult),
              reads=["modrow", "gpo2"], writes=["grow2"])
            for gi, (grow, Gb, gk) in enumerate(((grow1, G1b, "G1b"), (grow2, G2b, "G2b"))):
                for hf in range(2):
                    bk = 3 + hf
                    A("pe", lambda g, grow=grow, hf=hf, bk=bk: g.matmul(P[bk][:, :], lhsT=onesf[0:1, 0:128],
                                                                        rhs=grow[0:1, hf * 512:(hf + 1) * 512], start=True, stop=True),
                      reads=[f"grow{gi + 1}", "onesf"], writes=[f"P{bk}"])
                    A("act", lambda g, Gb=Gb, hf=hf, bk=bk: g.activation(out=Gb[:, hf * 512:(hf + 1) * 512], in_=P[bk][:, :], func=AF.Copy),
                      reads=[f"P{bk}"], writes=[gk])
            S.barrier()
            _phase[0] += 1
            if _phase[0] > STOP:
                raise _Stop()
            AR.reset()

            cqnT = AR.alloc(3 * T, BF16)
            ckvnT = AR.alloc(2 * T, BF16)
            TABm = AR.alloc(T)
            kpeT = TABm[64:96, 0:2048].bitcast(BF16)
            P12 = AR.off
            DTc = AR.alloc(512)
            WQc = AR.alloc(1024)
            WKc = AR.alloc(256)
            dma("sp", DTc, dt_d, w=["DTc"])
            dma("sp", WQc, wqc_d, w=["WQc"])
            dma("sp", WKc, wkc_d, w=["WKc"])
            W1 = AR.alloc(8 * NC1, BF16)
            xt = [AR.alloc(D), AR.alloc(D)]
            xn = [AR.alloc(D, BF16), AR.alloc(D, BF16)]
            junk = AR.alloc(D, BF16)
            ssq = [AR.alloc(1), AR.alloc(1)]
            hT = AR.alloc(8 * 512, BF16)
            cqraw = AR.alloc(3 * 512)
            ckvraw = AR.alloc(2 * 512)
            sq = AR.alloc(3 * 512, BF16)
            sq2 = AR.alloc(2 * 512, BF16)
            Rq = AR.alloc(512)
            Rkv = Rq
            posi = AR.alloc(512, I32)
            posf = AR.alloc(512)
            ang = AR.alloc(512)
            ni = posi
            nf = AR.alloc(512)
            msk = nf
            Cr = AR.alloc(512)
            Sr = AR.alloc(512)
            t1 = [AR.alloc(512)] * 2
            t2 = [AR.alloc(512)] * 2
            rqT = AR.alloc(2 * 512, BF16)
            rkT = AR.alloc(2 * 512, BF16)
            qwT = AR.alloc(2 * 512, BF16)
            rqm = AR.alloc(2 * 512, BF16)
            qwm = AR.alloc(2 * 512, BF16)
            A("pool", lambda g: g.memset(rqm[64:128, :], 0.0), writes=["rqm"])
            A("pool", lambda g: g.memset(qwm[64:128, :], 0.0), writes=["qwm"])
            vtok = AR.alloc(4 * 512, BF16)
            sg = AR.alloc(4 * 512, BF16)
            kwtok = AR.alloc(256, BF16)
            scTm = AR.alloc(512, BF16)
            osb = AR.alloc(512)
            ynorm = AR.alloc(512)
            osq = ynorm
            ytok = AR.alloc(512, BF16)
            ysT = AR.alloc(4 * 512, BF16)
            Sf = AR.alloc(256)
            Sbf = AR.alloc(256, BF16)
            st = {k: AR.alloc(4) for k in ("osum", "osqs", "mean", "msq", "var", "rgn")}

            for hf in range(2):
                dma("pool", W1.rearrange("p (c n) -> p c n", c=8)[:, hf * 4:(hf + 1) * 4, :],
                    w1_d.rearrange("(c p) n -> p c n", p=128)[:, hf * 4:(hf + 1) * 4, :], w=["W1"])
            A("pool", lambda g: g.memset(Sf, 0.0), writes=["Sf"])
            A("pool", lambda g: g.memset(Sbf, 0.0), writes=["Sbf"])

            def ck(n):
                if SUB == n:
                    raise _Stop()
            ck(0)

            def w1s(c, off, n):
                return W1[:, c * NC1 + off:c * NC1 + off + n]

            def table(dst, dk, col, sbi):
                A("dve", lambda g: g.tensor_scalar(out=ang, in0=posf, scalar1=INVc[:, col:col + 1], scalar2=PHc[:, col:col + 1],
                                                   op0=ALU.mult, op1=ALU.add), reads=["posf", "INVc", "PHc"], writes=["ang"])
                A("dve", lambda g: g.tensor_scalar(out=ni, in0=ang, scalar1=float(1.0 / (2 * PI)), scalar2=None, op0=ALU.mult),
                  reads=["ang"], writes=["ibuf"])
                A("dve", lambda g: g.tensor_copy(out=nf, in_=ni), reads=["ibuf"], writes=["nf"])
                A("dve", lambda g: g.scalar_tensor_tensor(out=ang, in0=nf, scalar=-2 * PI, in1=ang, op0=ALU.mult, op1=ALU.add),
                  reads=["nf", "ang"], writes=["ang"])
                A("dve", lambda g: g.tensor_single_scalar(out=msk, in_=ang, scalar=PI, op=ALU.is_gt), reads=["ang", "nf"], writes=["nf"])
                A("dve", lambda g: g.scalar_tensor_tensor(out=ang, in0=msk, scalar=-2 * PI, in1=ang, op0=ALU.mult, op1=ALU.add),
                  reads=["nf", "ang"], writes=["ang"])
                A("dve", lambda g: g.tensor_scalar(out=ang, in0=ang, scalar1=-3.14159, scalar2=3.14159, op0=ALU.max, op1=ALU.min),
                  reads=["ang"], writes=["ang"])
                np_ = dst.shape[0]
                A("act", lambda g: g.activation(out=dst, in_=ang[0:np_, :], func=AF.Sin), reads=["ang"], writes=[dk])

            mtiles = [(0, 128, "cq", 0), (128, 128, "cq", 1), (256, 128, "cq", 2), (384, 128, "ckv", 0), (512, 128, "ckv", 1),
                      (640, 64, "kpe", 0)]
            o_ = 704
            for nm in ("rq", "rk"):
                for i in range(2):
                    mtiles.append((o_, 128, nm + "n", i))
                    mtiles.append((o_ + 128, 128, nm + "s", i))
                    o_ += 256
            RV = 1728
            RG = 2240

            for sbi in range(NSB):
                sc0 = sbi * 512
                dma("sp", posi, bass.AP(pos_d.tensor, sc0, [[0, 128], [1, 512]]), w=["ibuf"])
                A("dve", lambda g: g.tensor_copy(out=posf, in_=posi), reads=["ibuf"], writes=["posf"])
                table(TABm[0:64, sc0:sc0 + 512], "TABm", 0, sbi)
                table(Cr, "Cr", 1, sbi)
                table(Sr, "Sr", 2, sbi)
                ck(1)
                for j in range(4):
                    tb = sbi * 4 + j
                    b2 = tb % 2
                    dma("sp", xt[b2], x_d[tb * 128:(tb + 1) * 128, :], w=[f"xt{b2}"])
                    A("act", lambda g, b2=b2: g.activation(out=junk, in_=xt[b2], func=AF.Square, accum_out=ssq[b2]),
                      reads=[f"xt{b2}"], writes=["junk", f"ssq{b2}"])
                    rsqrt_ops(ssq[b2], ssq[b2], 1.0 / D, [f"ssq{b2}"], f"ssq{b2}")
                    A("act", lambda g, b2=b2: g.activation(out=xn[b2], in_=xt[b2], func=AF.Copy, scale=ssq[b2]),
                      reads=[f"xt{b2}", f"ssq{b2}"], writes=[f"xn{b2}"])

                    def tr8(g, b2=b2):
                        for c in range(8):
                            r = g.transpose(out=Pb[b2][:, c * 128:(c + 1) * 128], in_=xn[b2][:, c * 128:(c + 1) * 128], identity=identb[:, :])
                        return r
                    A("pe", tr8, reads=[f"xn{b2}", "identb"], writes=[f"P{b2}"])
                    for c in range(8):
                        dst = hT[:, c * 512 + j * 128:c * 512 + (j + 1) * 128]
                        if c % 2 == 0:
                            A("dve", lambda g, c=c, dst=dst, b2=b2: g.tensor_scalar(out=dst, in0=Pb[b2][:, c * 128:(c + 1) * 128],
                                                                                   scalar1=a1[:, c:c + 1], scalar2=sh1[:, c:c + 1],
                                                                                   op0=ALU.mult, op1=ALU.add),
                              reads=[f"P{b2}", "a1", "modc"], writes=["hT"])
                        else:
                            A("act", lambda g, c=c, dst=dst, b2=b2: g.activation(out=dst, in_=Pb[b2][:, c * 128:(c + 1) * 128], func=AF.Identity,
                                                                                scale=a1[:, c:c + 1], bias=sh1[:, c:c + 1]),
                              reads=[f"P{b2}", "a1", "modc"], writes=["hT"])
                ck(2)
                for mi, (off, M, kind, i) in enumerate(mtiles):
                    bk = 2 + mi % 2
                    pk = f"P{bk}"

                    def mmz(g, off=off, M=M, bk=bk):
                        for c in range(8):
                            r = g.matmul(P[bk][0:M, :], lhsT=w1s(c, off, M), rhs=hT[:, c * 512:(c + 1) * 512], start=(c == 0), stop=(c == 7))
                        return r
                    A("pe", mmz, reads=["W1", "hT"], writes=[pk])
                    if kind in ("cq", "ckv"):
                        raw, sqt, nt, Rt, bank, scl, dstT, rk_ = ((cqraw, sq, 3, Rq, 4, 1.0 / 384, cqnT, "Rq") if kind == "cq"
                                                                  else (ckvraw, sq2, 2, Rkv, 5, 1.0 / 256, ckvnT, "Rq"))
                        A("act", lambda g, raw=raw, i=i, bk=bk: g.activation(out=raw[:, i * 512:(i + 1) * 512], in_=P[bk][:, :], func=AF.Copy),
                          reads=[pk], writes=[f"{kind}raw{i}"])
                        A("act", lambda g, sqt=sqt, i=i, bk=bk: g.activation(out=sqt[:, i * 512:(i + 1) * 512], in_=P[bk][:, :], func=AF.Square),
                          reads=[pk], writes=[f"{kind}sq{i}"])
                        if i == nt - 1:
                            def mmst(g, sqt=sqt, nt=nt, bank=bank):
                                for q in range(nt):
                                    r = g.matmul(P[bank][:, :], lhsT=onesb[:, :], rhs=sqt[:, q * 512:(q + 1) * 512], start=(q == 0), stop=(q == nt - 1))
                                return r
                            A("pe", mmst, reads=[f"{kind}sq{q}" for q in range(nt)] + ["onesb"], writes=[f"P{bank}"])
                            rsqrt_ops(Rt, P[bank][:, :], scl, [f"P{bank}"], rk_)
                            for q in range(nt):
                                A("pool", lambda g, raw=raw, Rt=Rt, q=q, dstT=dstT: g.tensor_tensor(
                                    out=dstT[:, q * T + sc0:q * T + sc0 + 512], in0=raw[:, q * 512:(q + 1) * 512], in1=Rt, op=ALU.mult),
                                  reads=[f"{kind}raw{q}", rk_], writes=[f"{kind}nT"])
                    elif kind == "kpe":
                        A("dve", lambda g, bk=bk: g.tensor_tensor(out=t1[0][0:32, :], in0=P[bk][0:32, :], in1=TABm[0:32, sc0:sc0 + 512], op=ALU.mult),
                          reads=[pk, "TABm"], writes=["t1_0"])
                        A("dve", lambda g, bk=bk: g.tensor_tensor(out=t2[0][0:32, :], in0=P[bk][32:64, :], in1=TABm[32:64, sc0:sc0 + 512], op=ALU.mult),
                          reads=[pk, "TABm"], writes=["t2_0"])
                        A("dve", lambda g: g.tensor_tensor(out=kpeT[:, sc0:sc0 + 512], in0=t1[0][0:32, :], in1=t2[0][0:32, :], op=ALU.add),
                          reads=["t1_0", "t2_0"], writes=["kpeT"])
                    else:
                        nm = kind[:2]
                        if kind[2] == "n":
                            A("dve", lambda g, bk=bk, i=i: g.tensor_tensor(out=t1[i], in0=P[bk][:, :], in1=Cr, op=ALU.mult),
                              reads=[pk, "Cr"], writes=["t1_0"])
                        else:
                            A("dve", lambda g, bk=bk, i=i: g.tensor_tensor(out=t2[i], in0=P[bk][:, :], in1=Sr, op=ALU.mult),
                              reads=[pk, "Sr"], writes=["t2_0"])
                            dstq = rqT if nm == "rq" else rkT
                            A("pool", lambda g, i=i, dstq=dstq: g.tensor_tensor(out=dstq[:, i * 512:(i + 1) * 512], in0=t1[i], in1=t2[i], op=ALU.add),
                              reads=["t1_0", "t2_0"], writes=[nm + "T"])
                            if nm == "rq":
                                A("pool", lambda g, i=i: g.tensor_tensor(out=rqm[0:64, i * 512:(i + 1) * 512], in0=t1[i][0:64, :], in1=t2[i][0:64, :],
                                                                        op=ALU.add),
                                  reads=["t1_0", "t2_0"], writes=["rqm"])
                                A("pool", lambda g, i=i: g.tensor_tensor(out=qwT[:, i * 512:(i + 1) * 512], in0=rqT[:, i * 512:(i + 1) * 512],
                                                                        in1=WQc[:, i * 512:(i + 1) * 512], op=ALU.mult),
                                  reads=["rqT", "WQc"], writes=["qwT"])
                                A("pool", lambda g, i=i: g.tensor_tensor(out=qwm[0:64, i * 512:(i + 1) * 512], in0=rqT[0:64, i * 512:(i + 1) * 512],
                                                                        in1=WQc[0:64, i * 512:(i + 1) * 512], op=ALU.mult),
                                  reads=["rqT", "WQc"], writes=["qwm"])
                ck(3)
                for j in range(4):
                    for which, off, bank in (("v", RV, 4), ("g", RG, 5)):
                        def mmt(g, j=j, off=off, bank=bank):
                            for c in range(8):
                                r = g.matmul(P[bank][:, :], lhsT=hT[:, c * 512 + j * 128:c * 512 + (j + 1) * 128], rhs=w1s(c, off, 512),
                                             start=(c == 0), stop=(c == 7))
                            return r
                        A("pe", mmt, reads=["W1", "hT"], writes=[f"P{bank}"])
                        if which == "v":
                            A("act", lambda g, j=j: g.activation(out=vtok[:, j * 512:(j + 1) * 512], in_=P[4][:, :], func=AF.Copy),
                              reads=["P4"], writes=["vtok"])
                        else:
                            A("act", lambda g, j=j: g.activation(out=sg[:, j * 512:(j + 1) * 512], in_=P[5][:, :], func=AF.Silu),
                              reads=["P5"], writes=["sg"])
                ck(4)
                for j in range(4):
                    jc = slice(j * 128, (j + 1) * 128)

                    def trk(g, j=j):
                        for i in range(2):
                            r = g.transpose(out=Pb[5][:, i * 128:(i + 1) * 128], in_=rkT[:, i * 512 + j * 128:i * 512 + (j + 1) * 128], identity=identb[:, :])
                        return r
                    A("pe", trk, reads=["rkT", "identb"], writes=["P5"])
                    A("dve", lambda g: g.tensor_tensor(out=kwtok, in0=Pb[5][:, 0:256], in1=WKc[:, :], op=ALU.mult),
                      reads=["P5", "WKc"], writes=["kwtok"])

                    ck(6)

                    def mmsc(g, j=j):
                        for h in range(4):
                            i, r0 = h // 2, 64 * (h % 2)
                            cs = slice(i * 512 + j * 128, i * 512 + (j + 1) * 128)
                            if r0 == 0:
                                r = g.matmul(P[6][:, h * 128:(h + 1) * 128], lhsT=rkT[:, cs], rhs=rqm[:, cs], start=True, stop=True)
                            else:
                                r = g.matmul(P[6][:, h * 128:(h + 1) * 128], lhsT=rkT[64:128, cs], rhs=rqT[64:128, cs], start=True, stop=True,
                                             tile_position=(64, 0))
                        return r
                    A("pe", mmsc, reads=["rkT", "rqT", "rqm"], writes=["P6"])
                    A("dve", lambda g: g.tensor_tensor(out=scTm, in0=P[6][:, :], in1=DTc[:, :], op=ALU.mult), reads=["P6", "DTc"], writes=["scTm"])

                    ck(7)

                    def mmo(g, j=j):
                        for h in range(4):
                            i, r0 = h // 2, 64 * (h % 2)
                            cs = slice(i * 512 + j * 128, i * 512 + (j + 1) * 128)
                            g.matmul(P[7][:, h * 128:(h + 1) * 128], lhsT=scTm[:, h * 128:(h + 1) * 128],
                                     rhs=vtok[:, j * 512 + h * 128:j * 512 + (h + 1) * 128], start=True, stop=False)
                            if r0 == 0:
                                r = g.matmul(P[7][:, h * 128:(h + 1) * 128], lhsT=qwm[:, cs], rhs=Sbf[:, i * 128:(i + 1) * 128], start=False, stop=True)
                            else:
                                r = g.matmul(P[7][:, h * 128:(h + 1) * 128], lhsT=qwT[64:128, cs], rhs=Sbf[64:128, i * 128:(i + 1) * 128],
                                             start=False, stop=True, tile_position=(64, 0))
                        return r
                    A("pe", mmo, reads=["scTm", "vtok", "qwT", "qwm", "Sbf"], writes=["P7"])
                    A("act", lambda g: g.activation(out=osb, in_=P[7][:, :], func=AF.Copy), reads=["P7"], writes=["osb"])

                    ck(8)

                    def mmu(g, j=j):
                        for h in range(4):
                            i, r0 = h // 2, 64 * (h % 2)
                            kw = dict(tile_position=(0, 64)) if r0 else {}
                            r = g.matmul(P[5][r0:r0 + 64, 256 + i * 128:256 + (i + 1) * 128], lhsT=kwtok[:, h * 64:(h + 1) * 64],
                                         rhs=vtok[:, j * 512 + h * 128:j * 512 + (h + 1) * 128], start=True, stop=True, **kw)
                        return r
                    A("pe", mmu, reads=["kwtok", "vtok"], writes=["P5"])
                    for i in range(2):
                        A("dve", lambda g, i=i: g.scalar_tensor_tensor(out=Sf[:, i * 128:(i + 1) * 128], in0=Sf[:, i * 128:(i + 1) * 128],
                                                                      scalar=DECc[:, i:i + 1], in1=P[5][:, 256 + i * 128:256 + (i + 1) * 128],
                                                                      op0=ALU.mult, op1=ALU.add),
                          reads=["P5", "DECc", "Sf"], writes=["Sf"])
                    A("pool", lambda g: g.tensor_copy(out=Sbf, in_=Sf), reads=["Sf"], writes=["Sbf"])
                    ck(9)
                    o3 = osb.rearrange("p (h v) -> p h v", h=4)
                    A("dve", lambda g, o3=o3: g.reduce_sum(out=st["osum"], in_=o3, axis=AX.X), reads=["osb"], writes=["osum"])
                    A("pool", lambda g: g.tensor_tensor(out=osq, in0=osb, in1=osb, op=ALU.mult), reads=["osb"], writes=["ynorm"])
                    A("dve", lambda g: g.reduce_sum(out=st["osqs"], in_=osq.rearrange("p (h v) -> p h v", h=4), axis=AX.X),
                      reads=["ynorm"], writes=["osqs"])
                    A("dve", lambda g: g.tensor_scalar(out=st["mean"], in0=st["osum"], scalar1=1.0 / 128, scalar2=None, op0=ALU.mult),
                      reads=["osum"], writes=["mean"])
                    A("dve", lambda g: g.tensor_tensor(out=st["msq"], in0=st["mean"], in1=st["mean"], op=ALU.mult), reads=["mean"], writes=["msq"])
                    A("dve", lambda g: g.scalar_tensor_tensor(out=st["var"], in0=st["osqs"], scalar=1.0 / 128, in1=st["msq"], op0=ALU.mult, op1=ALU.subtract),
                      reads=["osqs", "msq"], writes=["var"])
                    rsqrt_ops(st["rgn"], st["var"], 1.0, ["var"], "rgn")
                    for h in range(4):
                        A("dve", lambda g, h=h: g.tensor_scalar(out=ynorm[:, h * 128:(h + 1) * 128], in0=osb[:, h * 128:(h + 1) * 128],
                                                                scalar1=st["mean"][:, h:h + 1], scalar2=st["rgn"][:, h:h + 1],
                                                                op0=ALU.subtract, op1=ALU.mult),
                          reads=["osb", "mean", "rgn"], writes=["ynorm"])
                    A("pool", lambda g, j=j: g.tensor_tensor(out=ytok, in0=ynorm, in1=sg[:, j * 512:(j + 1) * 512], op=ALU.mult),
                      reads=["ynorm", "sg"], writes=["ytok"])

                    ck(10)

                    def try_(g):
                        for t in range(4):
                            r = g.transpose(out=Pb[4][:, t * 128:(t + 1) * 128], in_=ytok[:, t * 128:(t + 1) * 128], identity=identb[:, :])
                        return r
                    A("pe", try_, reads=["ytok", "identb"], writes=["P4"])
                    A("act", lambda g, j=j: g.activation(out=ysT.rearrange("p (t n) -> p t n", t=4)[:, :, j * 128:(j + 1) * 128],
                                                         in_=Pb[4][:, 0:512].rearrange("p (t n) -> p t n", t=4), func=AF.Copy),
                      reads=["P4"], writes=["ysT"])
                ck(5)
                dma("sp", yret_d[:, :, sc0:sc0 + 512].rearrange("t p n -> p t n"), ysT.rearrange("p (t n) -> p t n", t=4),
                    r=["ysT"], w=["yret_d"])
            tap("cqnT", cqnT, ["cqnT"])
            tap("ckvnT", ckvnT, ["ckvnT"])
            tap("kpeT", kpeT, ["kpeT"])
            tap("TABm", TABm, ["TABm"])
            tap("yret", yret_d, ["yret_d"])
            tap("hT", hT, ["hT"])
            tap("cqraw", cqraw, ["cqraw0", "cqraw1", "cqraw2"])
            tap("Rq", Rq, ["Rq"])
            tap("sq", sq, ["cqsq0", "cqsq1", "cqsq2"])
            tap("rqT", rqT, ["rqT"])
            tap("rkT", rkT, ["rkT"])
            tap("osb", osb, ["osb"])
            tap("ytok", ytok, ["ytok"])
            tap("Sf", Sf, ["Sf"])
            S.barrier()
            _phase[0] += 1
            if _phase[0] > STOP:
                raise _Stop()
            AR.off = P12

            ymlaT = AR.alloc(4 * T, BF16)
            P23 = AR.off
            Wq = AR.alloc(3 * 1024, BF16)
            Wkv = AR.alloc(2 * 1536, BF16)
            wqs = AR.alloc(3 * 1024)
            wkvs = AR.alloc(2 * 1536)
            qg = AR.alloc(3)
            kvg = AR.alloc(2)
            dma("sp", qg, qg_d, w=["qg"])
            dma("sp", kvg, kvg_d, w=["kvg"])
            dma("sp", wqs.rearrange("p (c n) -> p c n", c=3), wq_d.rearrange("(c p) n -> p c n", p=128), w=["wqs"])
            dma("sp", wkvs.rearrange("p (c n) -> p c n", c=2), wkv_d.rearrange("(c p) n -> p c n", p=128), w=["wkvs"])
            for c in range(3):
                A("dve", lambda g, c=c: g.tensor_scalar(out=Wq[:, c * 1024:(c + 1) * 1024], in0=wqs[:, c * 1024:(c + 1) * 1024],
                                                         scalar1=qg[:, c:c + 1], scalar2=None, op0=ALU.mult),
                  reads=["wqs", "qg"], writes=["Wq"])
            for c in range(2):
                A("dve", lambda g, c=c: g.tensor_scalar(out=Wkv[:, c * 1536:(c + 1) * 1536], in0=wkvs[:, c * 1536:(c + 1) * 1536],
                                                         scalar1=kvg[:, c:c + 1], scalar2=None, op0=ALU.mult),
                  reads=["wkvs", "kvg"], writes=["Wkv"])

            KT = [AR.alloc(T, BF16), AR.alloc(T, BF16)]
            QT = [AR.alloc(T, BF16), AR.alloc(T, BF16)]
            Vg = [AR.alloc(32 * 128, BF16), AR.alloc(32 * 128, BF16)]
            PT = [AR.alloc(1024, BF16) for _ in range(3)]
            u1 = AR.alloc(512)
            u2 = AR.alloc(512)
            rec = AR.alloc(512)
            for b in range(2):
                A("pool", lambda g, b=b: g.memset(KT[b][0:64, :], 0.0), writes=[f"KT{b}"])
                A("pool", lambda g, b=b: g.memset(QT[b][0:64, :], 0.0), writes=[f"QT{b}"])
                A("pool", lambda g, b=b: g.memset(Vg[b], 1.0), writes=[f"Vg{b}"])
            SCALE = float(96 ** -0.5)
            pt_i = 0
            for h in range(8):
                hb = h % 2
                kk, qk, vk = f"KT{hb}", f"QT{hb}", f"Vg{hb}"
                A("act", lambda g, hb=hb: g.activation(out=KT[hb][0:32, :], in_=kpeT[:, :], func=AF.Copy), reads=["kpeT"], writes=[kk])
                for sbi in range(NSB):
                    sc0 = sbi * 512

                    def mmk(g, h=h, sc0=sc0):
                        for c in range(2):
                            r = g.matmul(P[6][:, :], lhsT=Wkv[:, c * 1536 + h * 128:c * 1536 + (h + 1) * 128], rhs=ckvnT[:, c * T + sc0:c * T + sc0 + 512],
                                         start=(c == 0), stop=(c == 1))
                        return r
                    A("pe", mmk, reads=["Wkv", "ckvnT"], writes=["P6"])
                    A("dve", lambda g, hb=hb, sc0=sc0: g.tensor_copy(out=KT[hb][64:128, sc0:sc0 + 512], in_=P[6][64:128, :]),
                      reads=["P6"], writes=[kk])

                    def mmq(g, h=h, sc0=sc0):
                        for c in range(3):
                            r = g.matmul(P[7][:, :], lhsT=Wq[:, c * 1024 + h * 128:c * 1024 + (h + 1) * 128], rhs=cqnT[:, c * T + sc0:c * T + sc0 + 512],
                                         start=(c == 0), stop=(c == 2))
                        return r
                    A("pe", mmq, reads=["Wq", "cqnT"], writes=["P7"])
                    A("dve", lambda g, sc0=sc0: g.tensor_tensor(out=u1[0:32, :], in0=P[7][0:32, :], in1=TABm[0:32, sc0:sc0 + 512], op=ALU.mult),
                      reads=["P7", "TABm"], writes=["u1"])
                    A("dve", lambda g, sc0=sc0: g.tensor_tensor(out=u2[0:32, :], in0=P[7][32:64, :], in1=TABm[32:64, sc0:sc0 + 512], op=ALU.mult),
                      reads=["P7", "TABm"], writes=["u2"])
                    A("pool", lambda g, hb=hb, sc0=sc0: g.tensor_tensor(out=QT[hb][0:32, sc0:sc0 + 512], in0=u1[0:32, :], in1=u2[0:32, :], op=ALU.add),
                      reads=["u1", "u2"], writes=[qk])
                    A("dve", lambda g, hb=hb, sc0=sc0: g.tensor_copy(out=QT[hb][64:128, sc0:sc0 + 512], in_=P[7][64:128, :]),
                      reads=["P7"], writes=[qk])
                for k8 in range(4):
                    def mmv(g, h=h, k8=k8):
                        for q in range(8):
                            kb = k8 * 8 + q
                            for c in range(2):
                                r = g.matmul(P[6][:, q * 64:(q + 1) * 64], lhsT=ckvnT[:, c * T + kb * 128:c * T + (kb + 1) * 128],
                                             rhs=Wkv[:, c * 1536 + 1024 + h * 64:c * 1536 + 1024 + (h + 1) * 64], start=(c == 0), stop=(c == 1))
                        return r
                    A("pe", mmv, reads=["Wkv", "ckvnT"], writes=["P6"])
                    A("dve", lambda g, hb=hb, k8=k8: g.tensor_copy(
                        out=Vg[hb].rearrange("p (k v) -> p k v", v=128)[:, k8 * 8:(k8 + 1) * 8, 0:64],
                        in_=P[6][:, :].rearrange("p (k v) -> p k v", v=64)), reads=["P6"], writes=[vk])
                for qs in range(NSB):
                    q0 = qs * 512
                    acc = 4 + qs % 2
                    ak = f"P{acc}"
                    nfull = 4 * qs
                    groups = [(kb, min(kb + 2, nfull)) for kb in range(0, nfull, 2)]
                    items = [("full", a, b) for a, b in groups] + [("diag", 4 * qs + d, d) for d in range(4)]
                    last_kb = 4 * qs + 3
                    for gi, it in enumerate(items):
                        sbank = 2 * (gi % 2)
                        sk = f"PS{gi % 2}"
                        pt = PT[pt_i % 3]
                        ptk = f"PT{pt_i % 3}"
                        pt_i += 1
                        if it[0] == "full":
                            kbs = list(range(it[1], it[2]))

                            def mms(g, hb=hb, kbs=kbs, sbank=sbank, q0=q0):
                                for n_, kb in enumerate(kbs):
                                    r = g.matmul(P[sbank + n_][:, :], lhsT=KT[hb][:, kb * 128:(kb + 1) * 128], rhs=QT[hb][:, q0:q0 + 512],
                                                 start=True, stop=True)
                                return r
                            A("pe", mms, reads=[kk, qk], writes=[sk])
                            for n_ in range(len(kbs)):
                                A("act", lambda g, pt=pt, sbank=sbank, n_=n_: g.activation(out=pt[:, n_ * 512:(n_ + 1) * 512], in_=P[sbank + n_][:, :],
                                                                                            func=AF.Exp, scale=SCALE),
                                  reads=[sk], writes=[ptk])

                            def mmpv(g, hb=hb, kbs=kbs, pt=pt, acc=acc, last_kb=last_kb):
                                for n_, kb in enumerate(kbs):
                                    r = g.matmul(P[acc][:, :], lhsT=Vg[hb][:, kb * 128:(kb + 1) * 128], rhs=pt[:, n_ * 512:(n_ + 1) * 512],
                                                 start=(kb == 0), stop=(kb == last_kb))
                                return r
                            A("pe", mmpv, reads=[vk, ptk], writes=[ak])
                        else:
                            kb, d = it[1], it[2]
                            c0 = d * 128
                            A("pe", lambda g, hb=hb, kb=kb, c0=c0, sbank=sbank, q0=q0: g.matmul(
                                P[sbank][:, c0:512], lhsT=KT[hb][:, kb * 128:(kb + 1) * 128], rhs=QT[hb][:, q0 + c0:q0 + 512], start=True, stop=True),
                              reads=[kk, qk], writes=[sk])
                            A("act", lambda g, pt=pt, sbank=sbank, c0=c0: g.activation(out=pt[:, c0:512], in_=P[sbank][:, c0:512], func=AF.Exp, scale=SCALE),
                              reads=[sk], writes=[ptk])
                            A("pool", lambda g, pt=pt, c0=c0: g.tensor_tensor(out=pt[:, c0:c0 + 128], in0=pt[:, c0:c0 + 128], in1=trib[:, :], op=ALU.mult),
                              reads=[ptk, "trib"], writes=[ptk])
                            A("pe", lambda g, hb=hb, kb=kb, c0=c0, pt=pt, acc=acc, last_kb=last_kb: g.matmul(
                                P[acc][:, c0:512], lhsT=Vg[hb][:, kb * 128:(kb + 1) * 128], rhs=pt[:, c0:512], start=(kb == 0), stop=(kb == last_kb)),
                              reads=[vk, ptk], writes=[ak])
                    A("dve", lambda g, acc=acc: g.reciprocal(out=rec[0:64, :], in_=P[acc][64:128, :]), reads=[ak], writes=["rec"])
                    r0 = 64 * (h % 2)
                    A("dve", lambda g, acc=acc, r0=r0, h=h, q0=q0: g.tensor_tensor(
                        out=ymlaT[r0:r0 + 64, (h // 2) * T + q0:(h // 2) * T + q0 + 512], in0=P[acc][0:64, :], in1=rec[0:64, :], op=ALU.mult),
                      reads=[ak, "rec"], writes=["ymlaT"])
            S.barrier()
            _phase[0] += 1
            if _phase[0] > STOP:
                raise _Stop()

            AR.off = P23
            WU_OFF = ARN - (8 * DFF * 2) // 4
            Wg, wg_end = AR.alloc_at(0, 8 * DFF, BF16)
            Wu, _ = AR.alloc_at(WU_OFF, 8 * DFF, BF16)
            assert wg_end <= P12
            for hf in range(2):
                dma("pool", Wg.rearrange("p (c n) -> p c n", c=8)[:, hf * 4:(hf + 1) * 4, :],
                    wg_d.rearrange("(c p) n -> p c n", p=128)[:, hf * 4:(hf + 1) * 4, :], w=["Wg"])
                dma("pool", Wu.rearrange("p (c n) -> p c n", c=8)[:, hf * 4:(hf + 1) * 4, :],
                    wu_d.rearrange("(c p) n -> p c n", p=128)[:, hf * 4:(hf + 1) * 4, :], w=["Wu"])
            Wo = AR.alloc(8 * D, BF16)
            wos = [AR.alloc(D), AR.alloc(D)]
            og = AR.alloc(8)
            yrTs = [AR.alloc(4 * 512, BF16), AR.alloc(4 * 512, BF16)]
            ysq = AR.alloc(4 * 128, BF16)
            xt3 = [AR.alloc(D), AR.alloc(D)]
            mB = AR.alloc(D)
            mixs = AR.alloc(D)
            tt = AR.alloc(D)
            x1 = [AR.alloc(D), AR.alloc(D)]
            junk3 = AR.alloc(D, BF16)
            rm = AR.alloc(1)
            r2 = AR.alloc(1)
            dma("sp", og, og_d, w=["og"])
            for c in range(8):
                dma("sp", wos[c % 2], wout_d[c * 128:(c + 1) * 128, :], w=[f"wos{c % 2}"])
                A("dve", lambda g, c=c: g.tensor_scalar(out=Wo[:, c * D:(c + 1) * D], in0=wos[c % 2], scalar1=og[:, c:c + 1],
                                                         scalar2=None, op0=ALU.mult), reads=[f"wos{c % 2}", "og"], writes=["Wo"])
            for tb in range(32):
                b2 = tb % 2
                tc0 = tb * 128
                sbi, j = tb // 4, tb % 4
                yb = yrTs[sbi % 2]
                ybk = f"yrT{sbi % 2}"
                if j == 0:
                    dma("sp", yb.rearrange("p (t n) -> p t n", t=4), yret_d[:, :, sbi * 512:(sbi + 1) * 512].rearrange("t p n -> p t n"),
                        r=["yret_d"], w=[ybk])
                dma("sp", xt3[b2], x_d[tc0:tc0 + 128, :], w=[f"xt3{b2}"])
                A("pool", lambda g, tc0=tc0: g.tensor_tensor(out=ysq.rearrange("p (c n) -> p c n", c=4),
                                                              in0=ymlaT.rearrange("p (c n) -> p c n", c=4)[:, :, tc0:tc0 + 128],
                                                              in1=ymlaT.rearrange("p (c n) -> p c n", c=4)[:, :, tc0:tc0 + 128], op=ALU.mult),
                  reads=["ymlaT"], writes=["ysq"])

                def mmss(g):
                    for c in range(4):
                        r = g.matmul(P[6][:, 0:1], lhsT=ysq[:, c * 128:(c + 1) * 128], rhs=onesb[:, 0:1], start=(c == 0), stop=(c == 3))
                    return r
                A("pe", mmss, reads=["ysq", "onesb"], writes=["P6"])
                rsqrt_ops(rm, P[6][:, 0:1], 1.0 / 512, ["P6"], "rm")

                def mmA(g, tc0=tc0):
                    for hf in range(2):
                        for c in range(4):
                            r = g.matmul(P[hf][:, :], lhsT=ymlaT[:, c * T + tc0:c * T + tc0 + 128], rhs=Wo[:, c * D + hf * 512:c * D + (hf + 1) * 512],
                                         start=(c == 0), stop=(c == 3))
                    return r
                A("pe", mmA, reads=["ymlaT", "Wo"], writes=["PA"])

                def mmB(g, yb=yb, j=j):
                    for hf in range(2):
                        for c in range(4):
                            r = g.matmul(P[2 + hf][:, :], lhsT=yb[:, c * 512 + j * 128:c * 512 + (j + 1) * 128],
                                         rhs=Wo[:, (4 + c) * D + hf * 512:(4 + c) * D + (hf + 1) * 512], start=(c == 0), stop=(c == 3))
                    return r
                A("pe", mmB, reads=[ybk, "Wo"], writes=["PB"])
                for hf in range(2):
                    A("act", lambda g, hf=hf: g.activation(out=mB[:, hf * 512:(hf + 1) * 512], in_=P[2 + hf][:, :], func=AF.Copy),
                      reads=["PB"], writes=["mB"])
                    A("dve", lambda g, hf=hf: g.scalar_tensor_tensor(out=mixs[:, hf * 512:(hf + 1) * 512], in0=P[hf][:, :], scalar=rm[:, 0:1],
                                                                      in1=mB[:, hf * 512:(hf + 1) * 512], op0=ALU.mult, op1=ALU.add),
                      reads=["PA", "rm", "mB"], writes=["mixs"])
                A("act", lambda g: g.activation(out=junk3, in_=mixs, func=AF.Square, accum_out=r2), reads=["mixs"], writes=["junk3", "r2"])
                rsqrt_ops(r2, r2, 1.0 / D, ["r2"], "r2")
                A("dve", lambda g: g.scalar_tensor_tensor(out=tt, in0=mixs, scalar=r2[:, 0:1], in1=G1b[:, :], op0=ALU.mult, op1=ALU.mult),
                  reads=["mixs", "r2", "G1b"], writes=["tt"])
                A("pool", lambda g, b2=b2: g.tensor_tensor(out=x1[b2], in0=xt3[b2], in1=tt, op=ALU.add), reads=[f"xt3{b2}", "tt"], writes=[f"x1{b2}"])
                dma("sp", out_d[tc0:tc0 + 128, :], x1[b2], r=[f"x1{b2}"], w=[f"out{tb}"])
            assert AR.off <= WU_OFF, (AR.off, WU_OFF)
            S.barrier()
            _phase[0] += 1
            if _phase[0] > STOP:
                raise _Stop()

            Wd, wd_end = AR.alloc_at(wg_end, NJ * D, BF16)
            assert wd_end <= P23
            AR.off = P23
            xa = [AR.alloc(D), AR.alloc(D)]
            xb = [AR.alloc(D), AR.alloc(D)]
            xn4_ = AR.alloc(D, BF16)
            xn4 = [xn4_, xn4_]
            junk4 = AR.alloc(D, BF16)
            junk5 = junk4
            s4 = [AR.alloc(1), AR.alloc(1)]
            h2T = AR.alloc(8 * 512, BF16)
            h1T = AR.alloc(NJ * 512, BF16)
            sgt = [AR.alloc(512), AR.alloc(512)]
            t4 = [AR.alloc(D), AR.alloc(D)]
            r3 = [AR.alloc(1), AR.alloc(1)]
            assert AR.off <= WU_OFF, (AR.off, WU_OFF)
            for hf in range(2):
                dma("pool", Wd.rearrange("p (c n) -> p c n", c=NJ)[:, hf * 11:(hf + 1) * 11, :],
                    wd_d.rearrange("(c p) n -> p c n", p=128)[:, hf * 11:(hf + 1) * 11, :], w=["Wd"])
            fin = []
            for sbi in range(NSB):
                for j in range(4):
                    tb = sbi * 4 + j
                    b2 = tb % 2
                    xv = xa[b2]
                    dma("sp", xv, out_d[tb * 128:(tb + 1) * 128, :], r=[f"out{tb}"], w=[f"xa{b2}"])
                    A("act", lambda g, xv=xv, b2=b2: g.activation(out=junk4, in_=xv, func=AF.Square, accum_out=s4[b2]),
                      reads=[f"xa{b2}"], writes=["junk4", f"s4{b2}"])
                    rsqrt_ops(s4[b2], s4[b2], 1.0 / D, [f"s4{b2}"], f"s4{b2}")
                    A("act", lambda g, xv=xv, b2=b2: g.activation(out=xn4[b2], in_=xv, func=AF.Copy, scale=s4[b2]),
                      reads=[f"xa{b2}", f"s4{b2}"], writes=["xn4"])

                    def tr8b(g, b2=b2):
                        for c in range(8):
                            r = g.transpose(out=Pb[b2][:, c * 128:(c + 1) * 128], in_=xn4[b2][:, c * 128:(c + 1) * 128], identity=identb[:, :])
                        return r
                    A("pe", tr8b, reads=["xn4", "identb"], writes=[f"P{b2}"])
                    for c in range(8):
                        dst = h2T[:, c * 512 + j * 128:c * 512 + (j + 1) * 128]
                        if c % 2 == 0:
                            A("dve", lambda g, c=c, dst=dst, b2=b2: g.tensor_scalar(out=dst, in0=Pb[b2][:, c * 128:(c + 1) * 128],
                                                                                   scalar1=a2[:, c:c + 1], scalar2=sh2[:, c:c + 1],
                                                                                   op0=ALU.mult, op1=ALU.add),
                              reads=[f"P{b2}", "a2", "modc"], writes=["h2T"])
                        else:
                            A("act", lambda g, c=c, dst=dst, b2=b2: g.activation(out=dst, in_=Pb[b2][:, c * 128:(c + 1) * 128], func=AF.Identity,
                                                                                scale=a2[:, c:c + 1], bias=sh2[:, c:c + 1]),
                              reads=[f"P{b2}", "a2", "modc"], writes=["h2T"])
                for jj in range(NJ):
                    gb = 2 + jj % 2
                    ub = 4 + jj % 2

                    def mmg(g, jj=jj, gb=gb, ub=ub):
                        for c in range(8):
                            g.matmul(P[gb][:, :], lhsT=Wg[:, c * DFF + jj * 128:c * DFF + (jj + 1) * 128], rhs=h2T[:, c * 512:(c + 1) * 512],
                                     start=(c == 0), stop=(c == 7))
                        for c in range(8):
                            r = g.matmul(P[ub][:, :], lhsT=Wu[:, c * DFF + jj * 128:c * DFF + (jj + 1) * 128], rhs=h2T[:, c * 512:(c + 1) * 512],
                                         start=(c == 0), stop=(c == 7))
                        return r
                    A("pe", mmg, reads=["Wg", "Wu", "h2T"], writes=[f"P{gb}", f"P{ub}"])
                    A("act", lambda g, jj=jj, gb=gb: g.activation(out=sgt[jj % 2], in_=P[gb][:, :], func=AF.Silu), reads=[f"P{gb}"], writes=[f"sgt{jj % 2}"])
                    A("dve", lambda g, jj=jj, ub=ub: g.tensor_tensor(out=h1T[:, jj * 512:(jj + 1) * 512], in0=P[ub][:, :], in1=sgt[jj % 2], op=ALU.mult),
                      reads=[f"P{ub}", f"sgt{jj % 2}"], writes=["h1T"])
                for j in range(4):
                    tb = sbi * 4 + j
                    b2 = tb % 2
                    xv = xb[b2]
                    tv = t4[b2]
                    dma("sp", xv, out_d[tb * 128:(tb + 1) * 128, :], r=[f"out{tb}"], w=[f"xb{b2}"])

                    def mmd(g, j=j):
                        for hf in range(2):
                            for jj in range(NJ):
                                r = g.matmul(P[6 + hf][:, :], lhsT=h1T[:, jj * 512 + j * 128:jj * 512 + (j + 1) * 128],
                                             rhs=Wd[:, jj * D + hf * 512:jj * D + (hf + 1) * 512], start=(jj == 0), stop=(jj == NJ - 1))
                        return r
                    A("pe", mmd, reads=["h1T", "Wd"], writes=["PF"])
                    A("act", lambda g, tv=tv: g.activation(out=tv[:, 0:512], in_=P[6][:, :], func=AF.Copy), reads=["PF"], writes=[f"t4{b2}"])
                    A("dve", lambda g, tv=tv: g.tensor_copy(out=tv[:, 512:1024], in_=P[7][:, :]), reads=["PF"], writes=[f"t4{b2}"])
                    A("act", lambda g, tv=tv, b2=b2: g.activation(out=junk5, in_=tv, func=AF.Square, accum_out=r3[b2]), reads=[f"t4{b2}"], writes=["junk4", f"r3{b2}"])
                    rsqrt_ops(r3[b2], r3[b2], 1.0 / D, [f"r3{b2}"], f"r3{b2}")
                    A("dve", lambda g, tv=tv, b2=b2: g.scalar_tensor_tensor(out=tv, in0=tv, scalar=r3[b2][:, 0:1], in1=G2b[:, :], op0=ALU.mult, op1=ALU.mult),
                      reads=[f"t4{b2}", f"r3{b2}", "G2b"], writes=[f"t4{b2}"])
                    A("pool", lambda g, xv=xv, tv=tv: g.tensor_tensor(out=xv, in0=xv, in1=tv, op=ALU.add), reads=[f"xb{b2}", f"t4{b2}"], writes=[f"xb{b2}"])
                    fin.append(dma("sp", out_d[tb * 128:(tb + 1) * 128, :], xv, r=[f"xb{b2}"], w=[f"out{tb}"]))
            A("sp", lambda g: None, deps=fin)

        except _Stop:
            pass
        with nc.Block() as block:
            S.emit_all(block, esem, dsem)
    return nc


def _consts():
    f = np.float32
    gam = 1.0 - 2.0 ** (-5.0 - np.arange(4, dtype=np.float64))
    idx = np.arange(128)
    ident = np.eye(128, dtype=f)
    tri = (idx[None, :] >= idx[:, None]).astype(f)
    dtc = np.zeros((128, 4, 128), np.float64)
    rel = idx[None, :] - idx[:, None]
    for h in range(4):
        dtc[:, h, :] = np.where(rel >= 0, gam[h] ** np.maximum(rel, 0), 0.0) * 0.125
    wqc = np.zeros((128, 2, 512), np.float64)
    wkc = np.zeros((128, 2, 128), np.float64)
    decc = np.zeros((128, 2), np.float64)
    for i in range(2):
        for r in range(128):
            h = 2 * i + r // 64
            wqc[r, i, :] = np.tile(gam[h] ** (idx + 1.0), 4)
            decc[r, i] = gam[h] ** 128
        for ft in range(128):
            h = 2 * i + ft // 64
            wkc[:, i, ft] = gam[h] ** (127.0 - idx) * 0.125
    inv_m = 10000.0 ** (-np.arange(16, dtype=np.float64) / 16.0)
    inv_r = 10000.0 ** (-np.arange(32, dtype=np.float64) / 32.0)
    invc = np.zeros((128, 3), np.float64)
    phc = np.zeros((128, 3), np.float64)
    for r in range(64):
        invc[r, 0] = inv_m[r % 16]
        phc[r, 0] = np.pi / 2 if r < 32 else (np.pi if r < 48 else 0.0)
    for r in range(128):
        invc[r, 1] = inv_r[r % 32]
        invc[r, 2] = inv_r[r % 32]
        phc[r, 1] = np.pi / 2
        phc[r, 2] = np.pi if (r % 64) < 32 else 0.0
    return dict(ident=ident, tri=tri, dtc=dtc.reshape(128, 512).astype(f), wqc=wqc.reshape(128, 1024).astype(f),
                wkc=wkc.reshape(128, 256).astype(f), decc=decc.astype(f), invc=invc.astype(f), phc=phc.astype(f))


def _colmajor(v, n):
    return np.ascontiguousarray(np.asarray(v, np.float32).reshape(n, 128).T)


def _prep_shared(inp):
    f = np.float32
    w_in = np.asarray(inp["w_in"], f)[0]
    cols = list(range(0, 640))
    cols += list(range(640, 672)) + [640 + k for k in list(range(16, 32)) + list(range(0, 16))]
    for base in (672, 928):
        for i in range(2):
            nat, sw = [], []
            for hh in (2 * i, 2 * i + 1):
                b = base + hh * 64
                nat += list(range(b, b + 64))
                sw += list(range(b + 32, b + 64)) + list(range(b, b + 32))
            cols += nat + sw
    cols += list(range(1184, 2208))
    w1 = np.ascontiguousarray(w_in[:, cols])
    assert w1.shape[1] == NC1
    wqb = np.asarray(inp["w_q_b"], f)[0]
    qc = []
    for h in range(8):
        b = h * 96
        qc += list(range(b + 64, b + 96)) + [b + 64 + k for k in list(range(16, 32)) + list(range(0, 16))] + list(range(b, b + 64))
    wq = np.ascontiguousarray(wqb[:, qc])
    wkvb = np.asarray(inp["w_kv_b"], f)[0]
    wkv = np.zeros((256, 1536), f)
    for h in range(8):
        wkv[:, h * 128 + 64:h * 128 + 128] = wkvb[:, h * 128:h * 128 + 64]
        wkv[:, 1024 + h * 64:1024 + (h + 1) * 64] = wkvb[:, h * 128 + 64:h * 128 + 128]
    sh = dict(
        w_ada=np.ascontiguousarray(np.asarray(inp["w_ada"], f)[0]),
        b_ada=np.ascontiguousarray(np.asarray(inp["b_ada"], f)[0][None, :]),
        gpre1=_colmajor(inp["pre_norm_mix"][0], 8), gpre2=_colmajor(inp["pre_norm_ffn"][0], 8),
        gpost1=np.ascontiguousarray(np.asarray(inp["post_norm_mix"], f)[0][None, :]),
        gpost2=np.ascontiguousarray(np.asarray(inp["post_norm_ffn"], f)[0][None, :]),
        qg=_colmajor(inp["q_a_norm"][0], 3), kvg=_colmajor(inp["kv_a_norm"][0], 2),
        og=_colmajor(np.concatenate([np.asarray(inp["mla_out_norm"], f)[0], np.asarray(inp["ret_gn_gain"], f)[0]]), 8),
        w1=w1, wq=wq, wkv=wkv,
        wout=np.ascontiguousarray(np.asarray(inp["w_out"], f)[0]),
        wg=np.ascontiguousarray(np.asarray(inp["w_gate"], f)[0]),
        wu=np.ascontiguousarray(np.asarray(inp["w_up"], f)[0]),
        wd=np.ascontiguousarray(np.asarray(inp["w_down"], f)[0]),
    )
    sh.update(_consts())
    return sh


def make_in_maps(inp, cores):
    sh = _prep_shared(inp)
    x = np.asarray(inp["x"], np.float32)
    c = np.asarray(inp["c"], np.float32)
    pos = np.asarray(inp["positions"], np.int32)
    maps = []
    for b in cores:
        m = dict(sh)
        m["x"] = np.ascontiguousarray(x[b])
        m["cT"] = _colmajor(c[b], 8)
        m["pos"] = np.ascontiguousarray(pos[b][None, :])
        maps.append(m)
    return maps


_NC = None


def kernel(**inputs):
    global _NC
    if _NC is None:
        _NC = build_nc()
    maps = make_in_maps(inputs, list(range(8)))
    res = run_bass_kernel_spmd(_NC, maps, core_ids=list(range(8)))
    return np.stack([np.asarray(r["out"], np.float32) for r in res.results], axis=0)
```

```python
import contextlib
import types
import numpy as np
import concourse.bass as bass
import concourse.mybir as mybir
from concourse.bass_utils import run_bass_kernel_spmd

F32 = mybir.dt.float32
BF16 = mybir.dt.bfloat16
I32 = mybir.dt.int32
AF = mybir.ActivationFunctionType
ALU = mybir.AluOpType
AX = mybir.AxisListType

ENGS = ("pe", "act", "dve", "pool", "sp")
STOP = 99
SUB = 99
HSEL = (0, 1, 2, 3)
TAPS = ()
REORDER = True
QS_ORDER = (0, 7, 1, 6, 2, 5, 3, 4)
GS = 1
NPT = 6


class _Stop(Exception):
    pass

T = 4096
D = 1024
NSB = 8
DFF = 2816
NJ = 22
NC1 = 2752
EPS = 1e-6
PI = float(np.pi)


def _freeze(fn):
    if fn.__closure__ is None:
        return fn
    cells = []
    for c in fn.__closure__:
        try:
            cells.append(types.CellType(c.cell_contents))
        except ValueError:
            cells.append(c)
    return types.FunctionType(fn.__code__, fn.__globals__, fn.__name__, fn.__defaults__, tuple(cells))


class Op:
    __slots__ = ("eng", "idx", "emit", "deps", "is_dma", "dma_i", "marked", "count", "clock", "waits", "is_bar", "busy", "lat", "seq", "st")


def _nfree(ap):
    n = 1
    for d in ap.shape[1:]:
        n *= int(d)
    return n


class _Fake:
    def __init__(self, eng):
        self.eng = eng
        self.busy = 0.0
        self.lat = None

    def matmul(self, out, lhsT=None, rhs=None, **kw):
        n = max(_nfree(rhs), 64)
        f = 4.0 if rhs.dtype == F32 else 1.0
        self.busy += f * n / 2370.0 + 0.004
        return self

    def transpose(self, out=None, in_=None, identity=None, **kw):
        self.busy += 0.08
        return self

    def activation(self, out=None, in_=None, **kw):
        self.busy += 0.1 + _nfree(in_) / 1150.0
        return self

    def dma_start(self, out=None, in_=None, **kw):
        nb = _nfree(out) * int(out.shape[0]) * (4 if out.dtype in (F32, I32) else 2)
        self.busy += 0.15 if self.eng == "sp" else 1.2
        self.lat = 2.5 + nb / 150e3
        return self

    def _dve(self, out, **kw):
        n = _nfree(out)
        if self.eng == "pool":
            self.busy += 0.2 + n / 500.0
        else:
            self.busy += 0.12 + n / 900.0
        return self

    def tensor_tensor(self, out=None, **kw):
        return self._dve(out)

    def tensor_scalar(self, out=None, **kw):
        return self._dve(out)

    def tensor_copy(self, out=None, **kw):
        return self._dve(out)

    def scalar_tensor_tensor(self, out=None, **kw):
        return self._dve(out)

    def tensor_single_scalar(self, out=None, **kw):
        return self._dve(out)

    def reciprocal(self, out=None, **kw):
        self.busy += 0.1 + _nfree(out) / 150.0
        return self

    def memset(self, ap, *a, **kw):
        return self._dve(ap)

    def reduce_sum(self, out=None, in_=None, **kw):
        return self._dve(in_)

    def then_inc(self, *a, **kw):
        return self


class Sched:
    def __init__(self, n_dma_sems=12):
        self.ops = {e: [] for e in ENGS}
        self.order = []
        self.lastw = {}
        self.readers = {}
        self.n_dma_sems = n_dma_sems
        self.dma_ops = {e: [] for e in ENGS}
        self.dma_since_bar = []

    def add(self, eng, emit, reads=(), writes=(), dma=False, deps=()):
        op = Op()
        op.eng = eng
        op.emit = _freeze(emit)
        op.is_dma = dma
        op.marked = False
        op.count = 0
        op.idx = len(self.ops[eng])
        d = set(deps)
        for k in reads:
            w = self.lastw.get(k)
            if w is not None:
                d.add(w)
        for k in writes:
            w = self.lastw.get(k)
            if w is not None:
                d.add(w)
            for r in self.readers.get(k, ()):
                d.add(r)
        for k in reads:
            self.readers.setdefault(k, []).append(op)
        for k in writes:
            self.lastw[k] = op
            self.readers[k] = []
        d.discard(op)
        op.deps = d
        op.is_bar = False
        op.seq = len(self.order)
        self.ops[eng].append(op)
        self.order.append(op)
        return op

    def barrier(self):
        for e in ENGS:
            self.add(e, lambda g: None).is_bar = True

    def _list_schedule(self, seg):
        import heapq
        segset = set(seg)
        succ = {o: [] for o in seg}
        indeg = {}
        for o in seg:
            fk = _Fake(o.eng)
            o.emit(fk)
            o.busy = fk.busy
            o.lat = fk.lat if fk.lat is not None else fk.busy + 0.06
            k = 0
            for d in o.deps:
                if d in segset:
                    succ[d].append(o)
                    k += 1
            indeg[o] = k
        bl = {}
        for o in reversed(seg):
            m = 0.0
            for s_ in succ[o]:
                if bl[s_] > m:
                    m = bl[s_]
            bl[o] = o.lat + m
        free = {e: 0.0 for e in ENGS}
        avail = {e: [] for e in ENGS}
        future = {e: [] for e in ENGS}
        rtime = {o: 0.0 for o in seg}
        for o in seg:
            if indeg[o] == 0:
                heapq.heappush(future[o.eng], (0.0, o.seq, o))
        out = []
        n = len(seg)
        while len(out) < n:
            best_e, best_t = None, None
            for e in ENGS:
                fu, av = future[e], avail[e]
                while fu and fu[0][0] <= free[e]:
                    _, sq, o = heapq.heappop(fu)
                    heapq.heappush(av, (-bl[o], sq, o))
                if av:
                    t = free[e]
                elif fu:
                    t = fu[0][0]
                else:
                    continue
                if best_t is None or t < best_t:
                    best_e, best_t = e, t
            e = best_e
            if not avail[e]:
                free[e] = best_t
                fu, av = future[e], avail[e]
                while fu and fu[0][0] <= free[e]:
                    _, sq, o = heapq.heappop(fu)
                    heapq.heappush(av, (-bl[o], sq, o))
            _, sq, o = heapq.heappop(avail[e])
            st = free[e]
            o.st = st
            free[e] = st + o.busy
            fin = st + o.lat
            out.append(o)
            for s_ in succ[o]:
                if fin > rtime[s_]:
                    rtime[s_] = fin
                indeg[s_] -= 1
                if indeg[s_] == 0:
                    heapq.heappush(future[s_.eng], (rtime[s_], s_.seq, s_))
        return out

    def schedule(self, reorder=True):
        segs, cur = [], []
        for o in self.order:
            if o.is_bar:
                if cur:
                    segs.append(("seg", cur))
                    cur = []
                if segs and segs[-1][0] == "bar":
                    segs[-1][1].append(o)
                else:
                    segs.append(("bar", [o]))
            else:
                cur.append(o)
        if cur:
            segs.append(("seg", cur))
        new = []
        last_seg = []
        for kind, lst in segs:
            if kind == "seg":
                lst2 = self._list_schedule(lst) if reorder else lst
                new += lst2
                last_seg = lst2
            else:
                deps = [o for o in last_seg if o.is_dma]
                for e in ENGS:
                    for o in reversed(last_seg):
                        if o.eng == e and not o.is_dma:
                            deps.append(o)
                            break
                for o in lst:
                    o.deps = set(deps)
                new += lst
        self.order = new
        self.ops = {e: [] for e in ENGS}
        self.dma_ops = {e: [] for e in ENGS}
        for o in new:
            o.idx = len(self.ops[o.eng])
            self.ops[o.eng].append(o)
            if o.is_dma:
                o.dma_i = len(self.dma_ops[o.eng])
                if o.dma_i >= self.n_dma_sems:
                    o.deps.add(self.dma_ops[o.eng][o.dma_i - self.n_dma_sems])
                self.dma_ops[o.eng].append(o)

    def resolve(self):
        known = {e: {f: -1 for f in ENGS} for e in ENGS}
        known_dma = {e: set() for e in ENGS}
        for op in self.order:
            e = op.eng
            kn = known[e]
            waits = []
            for d in sorted(op.deps, key=lambda o: -o.idx):
                if d.is_dma:
                    if d in known_dma[e]:
                        continue
                    known_dma[e].add(d)
                    waits.append(d)
                else:
                    if d.eng == "pe" and e == "pe":
                        continue
                    if kn[d.eng] >= d.idx:
                        continue
                    d.marked = True
                    waits.append(d)
                ck = d.clock
                for f in ENGS:
                    if ck[f] > kn[f]:
                        kn[f] = ck[f]
            op.waits = waits
            ck = dict(kn)
            if not op.is_dma:
                ck[e] = max(ck[e], op.idx)
            op.clock = ck
        for e in ENGS:
            c = 0
            for op in self.ops[e]:
                if op.marked:
                    c += 1
                    op.count = c

    def emit_all(self, block, esem, dsem):
        self.schedule(reorder=REORDER)
        self.resolve()
        n = self.n_dma_sems

        def run(e, engobj):
            for op in self.ops[e]:
                for d in op.waits:
                    if d.is_dma:
                        engobj.wait_ge(dsem[d.eng][d.dma_i % n], 16 * (d.dma_i // n + 1))
                    else:
                        engobj.wait_ge(esem[d.eng], d.count)
                ins = op.emit(engobj)
                if op.is_dma:
                    ins.then_inc(dsem[e][op.dma_i % n], 16)
                elif op.marked:
                    if ins is None:
                        ins = engobj.nop()
                    ins.then_inc(esem[e], 1)

        block.tensor(lambda t: run("pe", t))
        block.scalar(lambda t: run("act", t))
        block.vector(lambda t: run("dve", t))
        block.gpsimd(lambda t: run("pool", t))
        block.sync(lambda t: run("sp", t))


class Arena:
    def __init__(self, ap, ncols):
        self.ap = ap
        self.n = ncols
        self.off = 0

    def alloc(self, cols, dt=F32):
        nb = cols * (4 if dt in (F32, I32) else 2)
        n32 = ((nb + 31) // 32) * 8
        assert self.off + n32 <= self.n, ("arena overflow", self.off, n32, self.n)
        v = self.ap[:, self.off:self.off + n32]
        self.off += n32
        if dt != F32:
            v = v.bitcast(dt)
        return v[:, 0:cols]

    def reset(self):
        self.off = 0

    def alloc_at(self, off32, cols, dt=F32):
        save = self.off
        self.off = off32
        v = self.alloc(cols, dt)
        end = self.off
        self.off = save
        return v, end


def build_nc():
    nc = bass.Bass("TRN2", target_bir_lowering=False)

    def DI(name, shape, dt=F32):
        return nc.dram_tensor(name, shape, dt, kind="ExternalInput").ap()

    x_d = DI("x", [T, D])
    c_d = DI("cT", [128, 8])
    pos_d = DI("pos", [1, T], I32)
    wada_d = DI("w_ada", [D, 6 * D])
    bada_d = DI("b_ada", [1, 6 * D])
    gpre1_d = DI("gpre1", [128, 8])
    gpre2_d = DI("gpre2", [128, 8])
    gpost1_d = DI("gpost1", [1, D])
    gpost2_d = DI("gpost2", [1, D])
    qg_d = DI("qg", [128, 3])
    kvg_d = DI("kvg", [128, 2])
    og_d = DI("og", [128, 8])
    w1_d = DI("w1", [D, NC1])
    wq_d = DI("wq", [384, 1024])
    wkv_d = DI("wkv", [256, 1536])
    wout_d = DI("wout", [D, D])
    wg_d = DI("wg", [D, DFF])
    wu_d = DI("wu", [D, DFF])
    wd_d = DI("wd", [DFF, D])
    ident_d = DI("ident", [128, 128])
    tri_d = DI("tri", [128, 128])
    dt_d = DI("dtc", [128, 512])
    wqc_d = DI("wqc", [128, 1024])
    wkc_d = DI("wkc", [128, 256])
    dec_d = DI("decc", [128, 2])
    inv_d = DI("invc", [128, 3])
    ph_d = DI("phc", [128, 3])
    out_d = nc.dram_tensor("out", [T, D], F32, kind="ExternalOutput").ap()
    yret_d = nc.dram_tensor("yret_scr", [4, 128, T], BF16).ap()

    S = Sched(n_dma_sems=12)
    A = S.add

    with contextlib.ExitStack() as ctx:
        def sbt(name, cols, dt=F32, parts=128):
            return ctx.enter_context(nc.sbuf_tensor(name, [parts, cols], dt))

        identb = sbt("identb", 128, BF16)
        trib = sbt("trib", 128, BF16)
        onesb = sbt("onesb", 128, BF16)
        onesf = sbt("onesf", 128)
        epst = sbt("epst", 1)
        DECc = sbt("DECc", 2)
        INVc = sbt("INVc", 3)
        PHc = sbt("PHc", 3)
        modc = sbt("modc", 32)
        a1 = sbt("a1", 8)
        a2 = sbt("a2", 8)
        G1b = sbt("G1b", D)
        G2b = sbt("G2b", D)
        ARN = 50000
        arena_t = sbt("arena", ARN)
        AR = Arena(arena_t, ARN)
        P = [ctx.enter_context(nc.psum_tensor(f"bank{i}", [128, 512], F32)) for i in range(8)]
        Pb = [p[:, :].bitcast(BF16) for p in P]

        esem = {e: ctx.enter_context(nc.semaphore("es_" + e)) for e in ENGS}
        dsem = {e: [ctx.enter_context(nc.semaphore(f"ds_{e}{i}")) for i in range(12)] for e in ("sp", "pool")}

        def dma(q, out, in_, r=(), w=()):
            return A(q, lambda g: g.dma_start(out=out, in_=in_), reads=r, writes=w, dma=True)

        def tap(name, ap, keys):
            if name not in TAPS:
                return
            shp = list(ap.shape)
            dd = nc.dram_tensor("dbg_" + name, shp, ap.dtype, kind="ExternalOutput").ap()
            dma("sp", dd, ap, r=keys)

        def rsqrt_ops(dst, src, scale, rk, wk):
            A("act", lambda g: g.activation(out=dst, in_=src, func=AF.Sqrt, scale=scale, bias=epst[0:dst.shape[0], :]),
              reads=list(rk) + ["epst"], writes=[wk])
            A("dve", lambda g: g.reciprocal(out=dst, in_=dst), reads=[wk], writes=[wk])

        _phase = [0]
        try:
            dma("pool", identb[:, :], ident_d, w=["identb"])
            dma("pool", trib[:, :], tri_d, w=["trib"])
            A("pool", lambda g: g.memset(onesb[:, :], 1.0), writes=["onesb"])
            A("pool", lambda g: g.memset(onesf[:, :], 1.0), writes=["onesf"])
            A("pool", lambda g: g.memset(epst[:, :], EPS), writes=["epst"])
            for t_, d_, k_ in ((DECc, dec_d, "DECc"), (INVc, inv_d, "INVc"), (PHc, ph_d, "PHc")):
                dma("sp", t_[:, :], d_, w=[k_])
            cT = AR.alloc(8)
            gp1 = AR.alloc(8)
            gp2 = AR.alloc(8)
            scb = AR.alloc(8, BF16)
            gpo1 = AR.alloc(D)
            gpo2 = AR.alloc(D)
            bada = AR.alloc(6 * D)
            modrow = AR.alloc(6 * D)
            grow1 = AR.alloc(D)
            grow2 = AR.alloc(D)
            wa = [AR.alloc(8 * 512, BF16), AR.alloc(8 * 512, BF16)]
            dma("sp", cT, c_d, w=["cT"])
            dma("sp", gp1, gpre1_d, w=["gp1"])
            dma("sp", gp2, gpre2_d, w=["gp2"])
            dma("sp", gpo1[0:1, :], gpost1_d, w=["gpo1"])
            dma("sp", gpo2[0:1, :], gpost2_d, w=["gpo2"])
            dma("sp", bada[0:1, :], bada_d, w=["bada"])
            A("act", lambda g: g.activation(out=scb, in_=cT, func=AF.Silu), reads=["cT"], writes=["scb"])
            for gi in range(12):
                wb = wa[gi % 2]
                dma("pool", wb.rearrange("p (c n) -> p c n", c=8),
                    wada_d[:, gi * 512:(gi + 1) * 512].rearrange("(c p) n -> p c n", p=128), w=[f"wa{gi % 2}"])

                def mm_ada(g, gi=gi, wb=wb):
                    for k in range(8):
                        r = g.matmul(P[gi % 2][0:1, :], lhsT=scb[:, k:k + 1], rhs=wb[:, k * 512:(k + 1) * 512],
                                     start=(k == 0), stop=(k == 7))
                    return r
                A("pe", mm_ada, reads=["scb", f"wa{gi % 2}"], writes=[f"P{gi % 2}"])
                A("dve", lambda g, gi=gi: g.tensor_tensor(out=modrow[0:1, gi * 512:(gi + 1) * 512], in0=P[gi % 2][0:1, :],
                                                         in1=bada[0:1, gi * 512:(gi + 1) * 512], op=ALU.add),
                  reads=[f"P{gi % 2}", "bada"], writes=["modrow"])
            col_offs = [0 * D, 1 * D, 3 * D, 4 * D]

            def mm_cols(g):
                for vi, off in enumerate(col_offs):
                    for c in range(8):
                        r = g.matmul(P[2][:, vi * 8 + c:vi * 8 + c + 1], lhsT=modrow[0:1, off + c * 128:off + (c + 1) * 128],
                                     rhs=onesf[0:1, 0:1], start=True, stop=True)
                return r
            A("pe", mm_cols, reads=["modrow", "onesf"], writes=["P2"])
            A("dve", lambda g: g.tensor_copy(out=modc[:, :], in_=P[2][:, 0:32]), reads=["P2"], writes=["modc"])
            A("dve", lambda g: g.scalar_tensor_tensor(out=a1[:, :], in0=modc[:, 8:16], scalar=1.0, in1=gp1, op0=ALU.add, op1=ALU.mult),
              reads=["modc", "gp1"], writes=["a1"])
            A("dve", lambda g: g.scalar_tensor_tensor(out=a2[:, :], in0=modc[:, 24:32], scalar=1.0, in1=gp2, op0=ALU.add, op1=ALU.mult),
              reads=["modc", "gp2"], writes=["a2"])
            sh1 = modc[:, 0:8]
            sh2 = modc[:, 16:24]
            A("dve", lambda g: g.tensor_tensor(out=grow1[0:1, :], in0=modrow[0:1, 2 * D:3 * D], in1=gpo1[0:1, :], op=ALU.mult),
              reads=["modrow", "gpo1"], writes=["grow1"])
            A("dve", lambda g: g.tensor_tensor(out=grow2[0:1, :], in0=modrow[0:1, 5 * D:6 * D], in1=gpo2[0:1, :], op=ALU.mult),
              reads=["modrow", "gpo2"], writes=["grow2"])
            for gi, (grow, Gb, gk) in enumerate(((grow1, G1b, "G1b"), (grow2, G2b, "G2b"))):
                for hf in range(2):
                    bk = 3 + hf
                    A("pe", lambda g, grow=grow, hf=hf, bk=bk: g.matmul(P[bk][:, :], lhsT=onesf[0:1, 0:128],
                                                                        rhs=grow[0:1, hf * 512:(hf + 1) * 512], start=True, stop=True),
                      reads=[f"grow{gi + 1}", "onesf"], writes=[f"P{bk}"])
                    A("act", lambda g, Gb=Gb, hf=hf, bk=bk: g.activation(out=Gb[:, hf * 512:(hf + 1) * 512], in_=P[bk][:, :], func=AF.Copy),
                      reads=[f"P{bk}"], writes=[gk])
            S.barrier()
            _phase[0] += 1
            if _phase[0] > STOP:
                raise _Stop()
            AR.reset()

            cqnT = AR.alloc(3 * T, BF16)
            ckvnT = AR.alloc(2 * T, BF16)
            TABm = AR.alloc(T)
            kpeT = TABm[64:96, 0:2048].bitcast(BF16)
            P12 = AR.off
            DTc = AR.alloc(512)
            WQc = AR.alloc(1024)
            WKc = AR.alloc(256)
            dma("sp", DTc, dt_d, w=["DTc"])
            dma("sp", WQc, wqc_d, w=["WQc"])
            dma("sp", WKc, wkc_d, w=["WKc"])
            W1 = AR.alloc(8 * NC1, BF16)
            xt = [AR.alloc(D), AR.alloc(D)]
            xn = [AR.alloc(D, BF16), AR.alloc(D, BF16)]
            junk = AR.alloc(D, BF16)
            ssq = [AR.alloc(1), AR.alloc(1)]
            hT = AR.alloc(8 * 512, BF16)
            cqraw = AR.alloc(3 * 512)
            ckvraw = AR.alloc(2 * 512)
            sq = AR.alloc(3 * 512, BF16)
            sq2 = AR.alloc(2 * 512, BF16)
            Rq = AR.alloc(512)
            Rkv = Rq
            posi = AR.alloc(512, I32)
            posf = AR.alloc(512)
            ang = AR.alloc(512)
            ni = posi
            nf = AR.alloc(512)
            msk = nf
            Cr = AR.alloc(512)
            Sr = AR.alloc(512)
            t1 = [AR.alloc(512), AR.alloc(512)]
            t2 = [AR.alloc(512), AR.alloc(512)]
            rqT = AR.alloc(2 * 512, BF16)
            rkT = AR.alloc(2 * 512, BF16)
            qwT = AR.alloc(2 * 512, BF16)
            rqm = AR.alloc(2 * 512, BF16)
            qwm = AR.alloc(2 * 512, BF16)
            A("pool", lambda g: g.memset(rqm[64:128, :], 0.0), writes=["rqm"])
            A("pool", lambda g: g.memset(qwm[64:128, :], 0.0), writes=["qwm"])
            vtok = AR.alloc(4 * 512, BF16)
            sg = AR.alloc(4 * 512, BF16)
            kwtok = AR.alloc(256, BF16)
            scTm = AR.alloc(512, BF16)
            osb = AR.alloc(512)
            ynorm = AR.alloc(512)
            osq = ynorm
            ytok = AR.alloc(512, BF16)
            ysT = AR.alloc(4 * 512, BF16)
            Sf = AR.alloc(256)
            Sbf = AR.alloc(256, BF16)
            st = {k: AR.alloc(4) for k in ("osum", "osqs", "mean", "msq", "var", "rgn")}

            for hf in range(2):
                dma("pool", W1.rearrange("p (c n) -> p c n", c=8)[:, hf * 4:(hf + 1) * 4, :],
                    w1_d.rearrange("(c p) n -> p c n", p=128)[:, hf * 4:(hf + 1) * 4, :], w=["W1"])
            A("pool", lambda g: g.memset(Sf, 0.0), writes=["Sf"])
            A("pool", lambda g: g.memset(Sbf, 0.0), writes=["Sbf"])

            def ck(n):
                if SUB == n:
                    raise _Stop()
            ck(0)

            def w1s(c, off, n):
                return W1[:, c * NC1 + off:c * NC1 + off + n]

            def table(dst, dk, col, sbi):
                A("dve", lambda g: g.tensor_scalar(out=ang, in0=posf, scalar1=INVc[:, col:col + 1], scalar2=PHc[:, col:col + 1],
                                                   op0=ALU.mult, op1=ALU.add), reads=["posf", "INVc", "PHc"], writes=["ang"])
                A("dve", lambda g: g.tensor_scalar(out=ni, in0=ang, scalar1=float(1.0 / (2 * PI)), scalar2=None, op0=ALU.mult),
                  reads=["ang"], writes=["ibuf"])
                A("dve", lambda g: g.tensor_copy(out=nf, in_=ni), reads=["ibuf"], writes=["nf"])
                A("dve", lambda g: g.scalar_tensor_tensor(out=ang, in0=nf, scalar=-2 * PI, in1=ang, op0=ALU.mult, op1=ALU.add),
                  reads=["nf", "ang"], writes=["ang"])
                A("dve", lambda g: g.tensor_single_scalar(out=msk, in_=ang, scalar=PI, op=ALU.is_gt), reads=["ang", "nf"], writes=["nf"])
                A("dve", lambda g: g.scalar_tensor_tensor(out=ang, in0=msk, scalar=-2 * PI, in1=ang, op0=ALU.mult, op1=ALU.add),
                  reads=["nf", "ang"], writes=["ang"])
                A("dve", lambda g: g.tensor_scalar(out=ang, in0=ang, scalar1=-3.14159, scalar2=3.14159, op0=ALU.max, op1=ALU.min),
                  reads=["ang"], writes=["ang"])
                np_ = dst.shape[0]
                A("act", lambda g: g.activation(out=dst, in_=ang[0:np_, :], func=AF.Sin), reads=["ang"], writes=[dk])

            mtiles = [(0, 128, "cq", 0), (128, 128, "cq", 1), (256, 128, "cq", 2), (384, 128, "ckv", 0), (512, 128, "ckv", 1),
                      (640, 64, "kpe", 0)]
            o_ = 704
            for nm in ("rq", "rk"):
                for i in range(2):
                    mtiles.append((o_, 128, nm + "n", i))
                    mtiles.append((o_ + 128, 128, nm + "s", i))
                    o_ += 256
            RV = 1728
            RG = 2240

            for sbi in range(NSB):
                sc0 = sbi * 512
                dma("sp", posi, bass.AP(pos_d.tensor, sc0, [[0, 128], [1, 512]]), w=["ibuf"])
                A("dve", lambda g: g.tensor_copy(out=posf, in_=posi), reads=["ibuf"], writes=["posf"])
                table(TABm[0:64, sc0:sc0 + 512], "TABm", 0, sbi)
                table(Cr, "Cr", 1, sbi)
                table(Sr, "Sr", 2, sbi)
                ck(1)
                for j in range(4):
                    tb = sbi * 4 + j
                    b2 = tb % 2
                    dma("sp", xt[b2], x_d[tb * 128:(tb + 1) * 128, :], w=[f"xt{b2}"])
                    A("act", lambda g, b2=b2: g.activation(out=junk, in_=xt[b2], func=AF.Square, accum_out=ssq[b2]),
                      reads=[f"xt{b2}"], writes=["junk", f"ssq{b2}"])
                    rsqrt_ops(ssq[b2], ssq[b2], 1.0 / D, [f"ssq{b2}"], f"ssq{b2}")
                    A("act", lambda g, b2=b2: g.activation(out=xn[b2], in_=xt[b2], func=AF.Copy, scale=ssq[b2]),
                      reads=[f"xt{b2}", f"ssq{b2}"], writes=[f"xn{b2}"])

                    def tr8(g, b2=b2):
                        for c in range(8):
                            r = g.transpose(out=Pb[b2][:, c * 128:(c + 1) * 128], in_=xn[b2][:, c * 128:(c + 1) * 128], identity=identb[:, :])
                        return r
                    A("pe", tr8, reads=[f"xn{b2}", "identb"], writes=[f"P{b2}"])
                    for c in range(8):
                        dst = hT[:, c * 512 + j * 128:c * 512 + (j + 1) * 128]
                        if c % 2 == 0:
                            A("dve", lambda g, c=c, dst=dst, b2=b2: g.tensor_scalar(out=dst, in0=Pb[b2][:, c * 128:(c + 1) * 128],
                                                                                   scalar1=a1[:, c:c + 1], scalar2=sh1[:, c:c + 1],
                                                                                   op0=ALU.mult, op1=ALU.add),
                              reads=[f"P{b2}", "a1", "modc"], writes=["hT"])
                        else:
                            A("act", lambda g, c=c, dst=dst, b2=b2: g.activation(out=dst, in_=Pb[b2][:, c * 128:(c + 1) * 128], func=AF.Identity,
                                                                                scale=a1[:, c:c + 1], bias=sh1[:, c:c + 1]),
                              reads=[f"P{b2}", "a1", "modc"], writes=["hT"])
                ck(2)
                for mi, (off, M, kind, i) in enumerate(mtiles):
                    bk = 2 + mi % 2
                    pk = f"P{bk}"

                    def mmz(g, off=off, M=M, bk=bk):
                        for c in range(8):
                            r = g.matmul(P[bk][0:M, :], lhsT=w1s(c, off, M), rhs=hT[:, c * 512:(c + 1) * 512], start=(c == 0), stop=(c == 7))
                        return r
                    A("pe", mmz, reads=["W1", "hT"], writes=[pk])
                    if kind in ("cq", "ckv"):
                        raw, sqt, nt, Rt, bank, scl, dstT, rk_ = ((cqraw, sq, 3, Rq, 4, 1.0 / 384, cqnT, "Rq") if kind == "cq"
                                                                  else (ckvraw, sq2, 2, Rkv, 5, 1.0 / 256, ckvnT, "Rq"))
                        A("act", lambda g, raw=raw, i=i, bk=bk: g.activation(out=raw[:, i * 512:(i + 1) * 512], in_=P[bk][:, :], func=AF.Copy),
                          reads=[pk], writes=[f"{kind}raw{i}"])
                        A("act", lambda g, sqt=sqt, i=i, bk=bk: g.activation(out=sqt[:, i * 512:(i + 1) * 512], in_=P[bk][:, :], func=AF.Square),
                          reads=[pk], writes=[f"{kind}sq{i}"])
                        if i == nt - 1:
                            def mmst(g, sqt=sqt, nt=nt, bank=bank):
                                for q in range(nt):
                                    r = g.matmul(P[bank][:, :], lhsT=onesb[:, :], rhs=sqt[:, q * 512:(q + 1) * 512], start=(q == 0), stop=(q == nt - 1))
                                return r
                            A("pe", mmst, reads=[f"{kind}sq{q}" for q in range(nt)] + ["onesb"], writes=[f"P{bank}"])
                            rsqrt_ops(Rt, P[bank][:, :], scl, [f"P{bank}"], rk_)
                            for q in range(nt):
                                A("pool", lambda g, raw=raw, Rt=Rt, q=q, dstT=dstT: g.tensor_tensor(
                                    out=dstT[:, q * T + sc0:q * T + sc0 + 512], in0=raw[:, q * 512:(q + 1) * 512], in1=Rt, op=ALU.mult),
                                  reads=[f"{kind}raw{q}", rk_], writes=[f"{kind}nT"])
                    elif kind == "kpe":
                        A("dve", lambda g, bk=bk: g.tensor_tensor(out=t1[0][0:32, :], in0=P[bk][0:32, :], in1=TABm[0:32, sc0:sc0 + 512], op=ALU.mult),
                          reads=[pk, "TABm"], writes=["t1_0"])
                        A("dve", lambda g, bk=bk: g.tensor_tensor(out=t2[0][0:32, :], in0=P[bk][32:64, :], in1=TABm[32:64, sc0:sc0 + 512], op=ALU.mult),
                          reads=[pk, "TABm"], writes=["t2_0"])
                        A("dve", lambda g: g.tensor_tensor(out=kpeT[:, sc0:sc0 + 512], in0=t1[0][0:32, :], in1=t2[0][0:32, :], op=ALU.add),
                          reads=["t1_0", "t2_0"], writes=["kpeT"])
                    else:
                        nm = kind[:2]
                        if kind[2] == "n":
                            A("dve", lambda g, bk=bk, i=i: g.tensor_tensor(out=t1[i], in0=P[bk][:, :], in1=Cr, op=ALU.mult),
                              reads=[pk, "Cr"], writes=[f"t1_{i}"])
                        else:
                            A("dve", lambda g, bk=bk, i=i: g.tensor_tensor(out=t2[i], in0=P[bk][:, :], in1=Sr, op=ALU.mult),
                              reads=[pk, "Sr"], writes=[f"t2_{i}"])
                            dstq = rqT if nm == "rq" else rkT
                            A("pool", lambda g, i=i, dstq=dstq: g.tensor_tensor(out=dstq[:, i * 512:(i + 1) * 512], in0=t1[i], in1=t2[i], op=ALU.add),
                              reads=[f"t1_{i}", f"t2_{i}"], writes=[nm + "T"])
                            if nm == "rq":
                                A("pool", lambda g, i=i: g.tensor_tensor(out=rqm[0:64, i * 512:(i + 1) * 512], in0=t1[i][0:64, :], in1=t2[i][0:64, :],
                                                                        op=ALU.add),
                                  reads=[f"t1_{i}", f"t2_{i}"], writes=["rqm"])
                                A("pool", lambda g, i=i: g.tensor_tensor(out=qwT[:, i * 512:(i + 1) * 512], in0=rqT[:, i * 512:(i + 1) * 512],
                                                                        in1=WQc[:, i * 512:(i + 1) * 512], op=ALU.mult),
                                  reads=["rqT", "WQc"], writes=["qwT"])
                                A("pool", lambda g, i=i: g.tensor_tensor(out=qwm[0:64, i * 512:(i + 1) * 512], in0=rqT[0:64, i * 512:(i + 1) * 512],
                                                                        in1=WQc[0:64, i * 512:(i + 1) * 512], op=ALU.mult),
                                  reads=["rqT", "WQc"], writes=["qwm"])
                ck(3)
                for j in range(4):
                    for which, off, bank in (("v", RV, 4), ("g", RG, 5)):
                        def mmt(g, j=j, off=off, bank=bank):
                            for c in range(8):
                                r = g.matmul(P[bank][:, :], lhsT=hT[:, c * 512 + j * 128:c * 512 + (j + 1) * 128], rhs=w1s(c, off, 512),
                                             start=(c == 0), stop=(c == 7))
                            return r
                        A("pe", mmt, reads=["W1", "hT"], writes=[f"P{bank}"])
                        if which == "v":
                            A("act", lambda g, j=j: g.activation(out=vtok[:, j * 512:(j + 1) * 512], in_=P[4][:, :], func=AF.Copy),
                              reads=["P4"], writes=["vtok"])
                        else:
                            A("act", lambda g, j=j: g.activation(out=sg[:, j * 512:(j + 1) * 512], in_=P[5][:, :], func=AF.Silu),
                              reads=["P5"], writes=["sg"])
                ck(4)
                for j in range(4):
                    jc = slice(j * 128, (j + 1) * 128)

                    def trk(g, j=j):
                        for i in range(2):
                            r = g.transpose(out=Pb[5][:, i * 128:(i + 1) * 128], in_=rkT[:, i * 512 + j * 128:i * 512 + (j + 1) * 128], identity=identb[:, :])
                        return r
                    A("pe", trk, reads=["rkT", "identb"], writes=["P5"])
                    A("dve", lambda g: g.tensor_tensor(out=kwtok, in0=Pb[5][:, 0:256], in1=WKc[:, :], op=ALU.mult),
                      reads=["P5", "WKc"], writes=["kwtok"])

                    ck(6)

                    def mmsc(g, j=j):
                        for h in range(4):
                            i, r0 = h // 2, 64 * (h % 2)
                            cs = slice(i * 512 + j * 128, i * 512 + (j + 1) * 128)
                            if r0 == 0:
                                r = g.matmul(P[6][:, h * 128:(h + 1) * 128], lhsT=rkT[:, cs], rhs=rqm[:, cs], start=True, stop=True)
                            else:
                                r = g.matmul(P[6][:, h * 128:(h + 1) * 128], lhsT=rkT[64:128, cs], rhs=rqT[64:128, cs], start=True, stop=True,
                                             tile_position=(64, 0))
                        return r
                    A("pe", mmsc, reads=["rkT", "rqT", "rqm"], writes=["P6"])
                    A("dve", lambda g: g.tensor_tensor(out=scTm, in0=P[6][:, :], in1=DTc[:, :], op=ALU.mult), reads=["P6", "DTc"], writes=["scTm"])

                    ck(7)

                    def mmo(g, j=j):
                        for h in range(4):
                            i, r0 = h // 2, 64 * (h % 2)
                            cs = slice(i * 512 + j * 128, i * 512 + (j + 1) * 128)
                            g.matmul(P[7][:, h * 128:(h + 1) * 128], lhsT=scTm[:, h * 128:(h + 1) * 128],
                                     rhs=vtok[:, j * 512 + h * 128:j * 512 + (h + 1) * 128], start=True, stop=False)
                            if r0 == 0:
                                r = g.matmul(P[7][:, h * 128:(h + 1) * 128], lhsT=qwm[:, cs], rhs=Sbf[:, i * 128:(i + 1) * 128], start=False, stop=True)
                            else:
                                r = g.matmul(P[7][:, h * 128:(h + 1) * 128], lhsT=qwT[64:128, cs], rhs=Sbf[64:128, i * 128:(i + 1) * 128],
                                             start=False, stop=True, tile_position=(64, 0))
                        return r
                    A("pe", mmo, reads=["scTm", "vtok", "qwT", "qwm", "Sbf"], writes=["P7"])
                    A("act", lambda g: g.activation(out=osb, in_=P[7][:, :], func=AF.Copy), reads=["P7"], writes=["osb"])

                    ck(8)

                    def mmu(g, j=j):
                        for h in range(4):
                            i, r0 = h // 2, 64 * (h % 2)
                            kw = dict(tile_position=(0, 64)) if r0 else {}
                            r = g.matmul(P[5][r0:r0 + 64, 256 + i * 128:256 + (i + 1) * 128], lhsT=kwtok[:, h * 64:(h + 1) * 64],
                                         rhs=vtok[:, j * 512 + h * 128:j * 512 + (h + 1) * 128], start=True, stop=True, **kw)
                        return r
                    A("pe", mmu, reads=["kwtok", "vtok"], writes=["P5"])
                    for i in range(2):
                        A("dve", lambda g, i=i: g.scalar_tensor_tensor(out=Sf[:, i * 128:(i + 1) * 128], in0=Sf[:, i * 128:(i + 1) * 128],
                                                                      scalar=DECc[:, i:i + 1], in1=P[5][:, 256 + i * 128:256 + (i + 1) * 128],
                                                                      op0=ALU.mult, op1=ALU.add),
                          reads=["P5", "DECc", "Sf"], writes=["Sf"])
                    A("pool", lambda g: g.tensor_copy(out=Sbf, in_=Sf), reads=["Sf"], writes=["Sbf"])
                    ck(9)
                    o3 = osb.rearrange("p (h v) -> p h v", h=4)
                    A("dve", lambda g, o3=o3: g.reduce_sum(out=st["osum"], in_=o3, axis=AX.X), reads=["osb"], writes=["osum"])
                    A("pool", lambda g: g.tensor_tensor(out=osq, in0=osb, in1=osb, op=ALU.mult), reads=["osb"], writes=["ynorm"])
                    A("dve", lambda g: g.reduce_sum(out=st["osqs"], in_=osq.rearrange("p (h v) -> p h v", h=4), axis=AX.X),
                      reads=["ynorm"], writes=["osqs"])
                    A("dve", lambda g: g.tensor_scalar(out=st["mean"], in0=st["osum"], scalar1=1.0 / 128, scalar2=None, op0=ALU.mult),
                      reads=["osum"], writes=["mean"])
                    A("dve", lambda g: g.tensor_tensor(out=st["msq"], in0=st["mean"], in1=st["mean"], op=ALU.mult), reads=["mean"], writes=["msq"])
                    A("dve", lambda g: g.scalar_tensor_tensor(out=st["var"], in0=st["osqs"], scalar=1.0 / 128, in1=st["msq"], op0=ALU.mult, op1=ALU.subtract),
                      reads=["osqs", "msq"], writes=["var"])
                    rsqrt_ops(st["rgn"], st["var"], 1.0, ["var"], "rgn")
                    for h in range(4):
                        A("dve", lambda g, h=h: g.tensor_scalar(out=ynorm[:, h * 128:(h + 1) * 128], in0=osb[:, h * 128:(h + 1) * 128],
                                                                scalar1=st["mean"][:, h:h + 1], scalar2=st["rgn"][:, h:h + 1],
                                                                op0=ALU.subtract, op1=ALU.mult),
                          reads=["osb", "mean", "rgn"], writes=["ynorm"])
                    A("pool", lambda g, j=j: g.tensor_tensor(out=ytok, in0=ynorm, in1=sg[:, j * 512:(j + 1) * 512], op=ALU.mult),
                      reads=["ynorm", "sg"], writes=["ytok"])

                    ck(10)

                    def try_(g):
                        for t in range(4):
                            r = g.transpose(out=Pb[4][:, t * 128:(t + 1) * 128], in_=ytok[:, t * 128:(t + 1) * 128], identity=identb[:, :])
                        return r
                    A("pe", try_, reads=["ytok", "identb"], writes=["P4"])
                    A("act", lambda g, j=j: g.activation(out=ysT.rearrange("p (t n) -> p t n", t=4)[:, :, j * 128:(j + 1) * 128],
                                                         in_=Pb[4][:, 0:512].rearrange("p (t n) -> p t n", t=4), func=AF.Copy),
                      reads=["P4"], writes=["ysT"])
                ck(5)
                dma("sp", yret_d[:, :, sc0:sc0 + 512].rearrange("t p n -> p t n"), ysT.rearrange("p (t n) -> p t n", t=4),
                    r=["ysT"], w=["yret_d"])
            tap("cqnT", cqnT, ["cqnT"])
            tap("ckvnT", ckvnT, ["ckvnT"])
            tap("kpeT", kpeT, ["kpeT"])
            tap("TABm", TABm, ["TABm"])
            tap("yret", yret_d, ["yret_d"])
            tap("hT", hT, ["hT"])
            tap("cqraw", cqraw, ["cqraw0", "cqraw1", "cqraw2"])
            tap("Rq", Rq, ["Rq"])
            tap("sq", sq, ["cqsq0", "cqsq1", "cqsq2"])
            tap("rqT", rqT, ["rqT"])
            tap("rkT", rkT, ["rkT"])
            tap("osb", osb, ["osb"])
            tap("ytok", ytok, ["ytok"])
            tap("Sf", Sf, ["Sf"])
            S.barrier()
            _phase[0] += 1
            if _phase[0] > STOP:
                raise _Stop()
            AR.off = P12

            ymlaT = AR.alloc(4 * T, BF16)
            P23 = AR.off
            Wq = AR.alloc(3 * 1024, BF16)
            Wkv = AR.alloc(2 * 1536, BF16)
            wqs = AR.alloc(3 * 1024)
            wkvs = AR.alloc(2 * 1536)
            qg = AR.alloc(3)
            kvg = AR.alloc(2)
            dma("sp", qg, qg_d, w=["qg"])
            dma("sp", kvg, kvg_d, w=["kvg"])
            dma("sp", wqs.rearrange("p (c n) -> p c n", c=3), wq_d.rearrange("(c p) n -> p c n", p=128), w=["wqs"])
            dma("sp", wkvs.rearrange("p (c n) -> p c n", c=2), wkv_d.rearrange("(c p) n -> p c n", p=128), w=["wkvs"])
            for c in range(3):
                A("dve", lambda g, c=c: g.tensor_scalar(out=Wq[:, c * 1024:(c + 1) * 1024], in0=wqs[:, c * 1024:(c + 1) * 1024],
                                                         scalar1=qg[:, c:c + 1], scalar2=None, op0=ALU.mult),
                  reads=["wqs", "qg"], writes=["Wq"])
            for c in range(2):
                A("dve", lambda g, c=c: g.tensor_scalar(out=Wkv[:, c * 1536:(c + 1) * 1536], in0=wkvs[:, c * 1536:(c + 1) * 1536],
                                                         scalar1=kvg[:, c:c + 1], scalar2=None, op0=ALU.mult),
                  reads=["wkvs", "kvg"], writes=["Wkv"])

            KT = [AR.alloc(T, BF16), AR.alloc(T, BF16)]
            QT = [AR.alloc(T, BF16), AR.alloc(T, BF16)]
            Vg = [AR.alloc(32 * 128, BF16), AR.alloc(32 * 128, BF16)]
            PT = [AR.alloc(GS * 512, BF16) for _ in range(NPT)]
            NSLOT = 4 // GS
            it_i = 0
            u1 = AR.alloc(512)
            u2 = AR.alloc(512)
            rec = AR.alloc(512)
            for b in range(2):
                A("dve", lambda g, b=b: g.memset(KT[b][0:64, :], 0.0), writes=[f"KT{b}"])
                A("pool", lambda g, b=b: g.memset(QT[b][0:64, :], 0.0), writes=[f"QT{b}"])
                A("dve", lambda g, b=b: g.memset(Vg[b].rearrange("p (k v) -> p k v", v=128)[:, :, 64:128], 1.0), writes=[f"Vg{b}"])
            SCALE = float(96 ** -0.5)
            pt_i = 0
            for h in range(8):
                hb = h % 2
                kk, qk, vk = f"KT{hb}", f"QT{hb}", f"Vg{hb}"
                A("dve", lambda g, hb=hb: g.tensor_copy(out=KT[hb][0:32, :], in_=kpeT[:, :]), reads=["kpeT"], writes=[kk])
                for sbi in range(NSB):
                    sc0 = sbi * 512

                    def mmk(g, h=h, sc0=sc0):
                        for c in range(2):
                            r = g.matmul(P[6][:, :], lhsT=Wkv[:, c * 1536 + h * 128:c * 1536 + (h + 1) * 128], rhs=ckvnT[:, c * T + sc0:c * T + sc0 + 512],
                                         start=(c == 0), stop=(c == 1))
                        return r
                    A("pe", mmk, reads=["Wkv", "ckvnT"], writes=["P6"])
                    A("dve", lambda g, hb=hb, sc0=sc0: g.tensor_copy(out=KT[hb][64:128, sc0:sc0 + 512], in_=P[6][64:128, :]),
                      reads=["P6"], writes=[kk])

                    def mmq(g, h=h, sc0=sc0):
                        for c in range(3):
                            r = g.matmul(P[7][:, :], lhsT=Wq[:, c * 1024 + h * 128:c * 1024 + (h + 1) * 128], rhs=cqnT[:, c * T + sc0:c * T + sc0 + 512],
                                         start=(c == 0), stop=(c == 2))
                        return r
                    A("pe", mmq, reads=["Wq", "cqnT"], writes=["P7"])
                    A("dve", lambda g, sc0=sc0: g.tensor_tensor(out=u1[0:32, :], in0=P[7][0:32, :], in1=TABm[0:32, sc0:sc0 + 512], op=ALU.mult),
                      reads=["P7", "TABm"], writes=["u1"])
                    A("dve", lambda g, sc0=sc0: g.tensor_tensor(out=u2[0:32, :], in0=P[7][32:64, :], in1=TABm[32:64, sc0:sc0 + 512], op=ALU.mult),
                      reads=["P7", "TABm"], writes=["u2"])
                    A("pool", lambda g, hb=hb, sc0=sc0: g.tensor_tensor(out=QT[hb][0:32, sc0:sc0 + 512], in0=u1[0:32, :], in1=u2[0:32, :], op=ALU.add),
                      reads=["u1", "u2"], writes=[qk])
                    A("dve", lambda g, hb=hb, sc0=sc0: g.tensor_copy(out=QT[hb][64:128, sc0:sc0 + 512], in_=P[7][64:128, :]),
                      reads=["P7"], writes=[qk])
                for k8 in range(4):
                    def mmv(g, h=h, k8=k8):
                        for q in range(8):
                            kb = k8 * 8 + q
                            for c in range(2):
                                r = g.matmul(P[6][:, q * 64:(q + 1) * 64], lhsT=ckvnT[:, c * T + kb * 128:c * T + (kb + 1) * 128],
                                             rhs=Wkv[:, c * 1536 + 1024 + h * 64:c * 1536 + 1024 + (h + 1) * 64], start=(c == 0), stop=(c == 1))
                        return r
                    A("pe", mmv, reads=["Wkv", "ckvnT"], writes=["P6"])
                    A("dve", lambda g, hb=hb, k8=k8: g.tensor_copy(
                        out=Vg[hb].rearrange("p (k v) -> p k v", v=128)[:, k8 * 8:(k8 + 1) * 8, 0:64],
                        in_=P[6][:, :].rearrange("p (k v) -> p k v", v=64)), reads=["P6"], writes=[vk])
                for qi, qs in enumerate(QS_ORDER):
                    q0 = qs * 512
                    acc = 4 + qi % 2
                    ak = f"P{acc}"
                    nfull = 4 * qs
                    groups = [(kb, min(kb + GS, nfull)) for kb in range(0, nfull, GS)]
                    items = [("full", a, b) for a, b in groups] + [("diag", 4 * qs + d, d) for d in range(4)]
                    last_kb = 4 * qs + 3
                    for gi, it in enumerate(items):
                        slot = it_i % NSLOT
                        it_i += 1
                        sbank = GS * slot
                        sk = f"PS{slot}"
                        pt = PT[pt_i % NPT]
                        ptk = f"PT{pt_i % NPT}"
                        pt_i += 1
                        if it[0] == "full":
                            kbs = list(range(it[1], it[2]))

                            def mms(g, hb=hb, kbs=kbs, sbank=sbank, q0=q0):
                                for n_, kb in enumerate(kbs):
                                    r = g.matmul(P[sbank + n_][:, :], lhsT=KT[hb][:, kb * 128:(kb + 1) * 128], rhs=QT[hb][:, q0:q0 + 512],
                                                 start=True, stop=True)
                                return r
                            A("pe", mms, reads=[kk, qk], writes=[sk])
                            for n_ in range(len(kbs)):
                                A("act", lambda g, pt=pt, sbank=sbank, n_=n_: g.activation(out=pt[:, n_ * 512:(n_ + 1) * 512], in_=P[sbank + n_][:, :],
                                                                                            func=AF.Exp, scale=SCALE),
                                  reads=[sk], writes=[ptk])

                            def mmpv(g, hb=hb, kbs=kbs, pt=pt, acc=acc, last_kb=last_kb):
                                for n_, kb in enumerate(kbs):
                                    r = g.matmul(P[acc][:, :], lhsT=Vg[hb][:, kb * 128:(kb + 1) * 128], rhs=pt[:, n_ * 512:(n_ + 1) * 512],
                                                 start=(kb == 0), stop=(kb == last_kb))
                                return r
                            A("pe", mmpv, reads=[vk, ptk], writes=[ak])
                        else:
                            kb, d = it[1], it[2]
                            c0 = d * 128
                            A("pe", lambda g, hb=hb, kb=kb, c0=c0, sbank=sbank, q0=q0: g.matmul(
                                P[sbank][:, c0:512], lhsT=KT[hb][:, kb * 128:(kb + 1) * 128], rhs=QT[hb][:, q0 + c0:q0 + 512], start=True, stop=True),
                              reads=[kk, qk], writes=[sk])
                            A("act", lambda g, pt=pt, sbank=sbank, c0=c0: g.activation(out=pt[:, c0:512], in_=P[sbank][:, c0:512], func=AF.Exp, scale=SCALE),
                              reads=[sk], writes=[ptk])
                            A("pool", lambda g, pt=pt, c0=c0: g.tensor_tensor(out=pt[:, c0:c0 + 128], in0=pt[:, c0:c0 + 128], in1=trib[:, :], op=ALU.mult),
                              reads=[ptk, "trib"], writes=[ptk])
                            A("pe", lambda g, hb=hb, kb=kb, c0=c0, pt=pt, acc=acc, last_kb=last_kb: g.matmul(
                                P[acc][:, c0:512], lhsT=Vg[hb][:, kb * 128:(kb + 1) * 128], rhs=pt[:, c0:512], start=(kb == 0), stop=(kb == last_kb)),
                              reads=[vk, ptk], writes=[ak])
                    A("dve", lambda g, acc=acc: g.reciprocal(out=rec[0:64, :], in_=P[acc][64:128, :]), reads=[ak], writes=["rec"])
                    r0 = 64 * (h % 2)
                    A("dve", lambda g, acc=acc, r0=r0, h=h, q0=q0: g.tensor_tensor(
                        out=ymlaT[r0:r0 + 64, (h // 2) * T + q0:(h // 2) * T + q0 + 512], in0=P[acc][0:64, :], in1=rec[0:64, :], op=ALU.mult),
                      reads=[ak, "rec"], writes=["ymlaT"])
            S.barrier()
            _phase[0] += 1
            if _phase[0] > STOP:
                raise _Stop()

            AR.off = P23
            WU_OFF = ARN - (8 * DFF * 2) // 4
            Wg, wg_end = AR.alloc_at(0, 8 * DFF, BF16)
            Wu, _ = AR.alloc_at(WU_OFF, 8 * DFF, BF16)
            assert wg_end <= P12
            for hf in range(2):
                dma("pool", Wg.rearrange("p (c n) -> p c n", c=8)[:, hf * 4:(hf + 1) * 4, :],
                    wg_d.rearrange("(c p) n -> p c n", p=128)[:, hf * 4:(hf + 1) * 4, :], w=["Wg"])
                dma("pool", Wu.rearrange("p (c n) -> p c n", c=8)[:, hf * 4:(hf + 1) * 4, :],
                    wu_d.rearrange("(c p) n -> p c n", p=128)[:, hf * 4:(hf + 1) * 4, :], w=["Wu"])
            Wo = AR.alloc(8 * D, BF16)
            wos = [AR.alloc(D), AR.alloc(D)]
            og = AR.alloc(8)
            yrTs = [AR.alloc(4 * 512, BF16), AR.alloc(4 * 512, BF16)]
            ysq = [AR.alloc(4 * 128, BF16), AR.alloc(4 * 128, BF16)]
            xt3 = [AR.alloc(D), AR.alloc(D)]
            x1 = [AR.alloc(D), AR.alloc(D)]
            mB = [AR.alloc(D)]
            mixs = [AR.alloc(D)]
            tt = [AR.alloc(D)]
            rm = [AR.alloc(1), AR.alloc(1)]
            r2 = [AR.alloc(1), AR.alloc(1)]
            hole = wg_end
            for lst in (mB, mixs, tt):
                v_, hole = AR.alloc_at(hole, D)
                lst.append(v_)
            assert hole <= P12
            dma("sp", og, og_d, w=["og"])
            for c in range(8):
                dma("sp", wos[c % 2], wout_d[c * 128:(c + 1) * 128, :], w=[f"wos{c % 2}"])
                A("dve", lambda g, c=c: g.tensor_scalar(out=Wo[:, c * D:(c + 1) * D], in0=wos[c % 2], scalar1=og[:, c:c + 1],
                                                         scalar2=None, op0=ALU.mult), reads=[f"wos{c % 2}", "og"], writes=["Wo"])
            for tb in range(32):
                b2 = tb % 2
                tc0 = tb * 128
                sbi, j = tb // 4, tb % 4
                yb = yrTs[sbi % 2]
                ybk = f"yrT{sbi % 2}"
                pa = (0, 1) if b2 == 0 else (4, 5)
                pak = [f"P{pa[0]}", f"P{pa[1]}"]
                stb = 6 + b2
                if j == 0:
                    dma("sp", yb.rearrange("p (t n) -> p t n", t=4), yret_d[:, :, sbi * 512:(sbi + 1) * 512].rearrange("t p n -> p t n"),
                        r=["yret_d"], w=[ybk])
                dma("sp", xt3[b2], x_d[tc0:tc0 + 128, :], w=[f"xt3{b2}"])
                A("pool", lambda g, tc0=tc0, b2=b2: g.tensor_tensor(out=ysq[b2].rearrange("p (c n) -> p c n", c=4),
                                                                     in0=ymlaT.rearrange("p (c n) -> p c n", c=4)[:, :, tc0:tc0 + 128],
                                                                     in1=ymlaT.rearrange("p (c n) -> p c n", c=4)[:, :, tc0:tc0 + 128], op=ALU.mult),
                  reads=["ymlaT"], writes=[f"ysq{b2}"])

                def mmss(g, b2=b2, stb=stb):
                    for c in range(4):
                        r = g.matmul(P[stb][:, 0:1], lhsT=ysq[b2][:, c * 128:(c + 1) * 128], rhs=onesb[:, 0:1], start=(c == 0), stop=(c == 3))
                    return r
                A("pe", mmss, reads=[f"ysq{b2}", "onesb"], writes=[f"P{stb}"])
                rsqrt_ops(rm[b2], P[stb][:, 0:1], 1.0 / 512, [f"P{stb}"], f"rm{b2}")

                def mmA(g, tc0=tc0, pa=pa):
                    for hf in range(2):
                        for c in range(4):
                            r = g.matmul(P[pa[hf]][:, :], lhsT=ymlaT[:, c * T + tc0:c * T + tc0 + 128], rhs=Wo[:, c * D + hf * 512:c * D + (hf + 1) * 512],
                                         start=(c == 0), stop=(c == 3))
                    return r
                A("pe", mmA, reads=["ymlaT", "Wo"], writes=pak)

                def mmB(g, yb=yb, j=j):
                    for hf in range(2):
                        for c in range(4):
                            r = g.matmul(P[2 + hf][:, :], lhsT=yb[:, c * 512 + j * 128:c * 512 + (j + 1) * 128],
                                         rhs=Wo[:, (4 + c) * D + hf * 512:(4 + c) * D + (hf + 1) * 512], start=(c == 0), stop=(c == 3))
                    return r
                A("pe", mmB, reads=[ybk, "Wo"], writes=["PB"])
                for hf in range(2):
                    A("act", lambda g, hf=hf, b2=b2: g.activation(out=mB[b2][:, hf * 512:(hf + 1) * 512], in_=P[2 + hf][:, :], func=AF.Copy),
                      reads=["PB"], writes=[f"mB{b2}"])
                    A("dve", lambda g, hf=hf, b2=b2, pa=pa: g.scalar_tensor_tensor(out=mixs[b2][:, hf * 512:(hf + 1) * 512], in0=P[pa[hf]][:, :],
                                                                                scalar=rm[b2][:, 0:1], in1=mB[b2][:, hf * 512:(hf + 1) * 512],
                                                                                op0=ALU.mult, op1=ALU.add),
                      reads=[pak[hf], f"rm{b2}", f"mB{b2}"], writes=[f"mixs{b2}"])
                A("act", lambda g, b2=b2: g.activation(out=tt[b2].bitcast(BF16)[:, 0:D], in_=mixs[b2], func=AF.Square, accum_out=r2[b2]),
                  reads=[f"mixs{b2}"], writes=[f"tt{b2}", f"r2{b2}"])
                rsqrt_ops(r2[b2], r2[b2], 1.0 / D, [f"r2{b2}"], f"r2{b2}")
                A("dve", lambda g, b2=b2: g.scalar_tensor_tensor(out=tt[b2], in0=mixs[b2], scalar=r2[b2][:, 0:1], in1=G1b[:, :], op0=ALU.mult, op1=ALU.mult),
                  reads=[f"mixs{b2}", f"r2{b2}", "G1b"], writes=[f"tt{b2}"])
                A("pool", lambda g, b2=b2: g.tensor_tensor(out=x1[b2], in0=xt3[b2], in1=tt[b2], op=ALU.add), reads=[f"xt3{b2}", f"tt{b2}"], writes=[f"x1{b2}"])
                dma("sp", out_d[tc0:tc0 + 128, :], x1[b2], r=[f"x1{b2}"], w=[f"out{tb}"])
            assert AR.off <= WU_OFF, (AR.off, WU_OFF)
            S.barrier()
            _phase[0] += 1
            if _phase[0] > STOP:
                raise _Stop()

            Wd, wd_end = AR.alloc_at(wg_end, NJ * D, BF16)
            assert wd_end <= P23
            AR.off = P23
            xa = [AR.alloc(D), AR.alloc(D)]
            xb = [AR.alloc(D), AR.alloc(D)]
            xn4_ = AR.alloc(D, BF16)
            xn4 = [xn4_, xn4_]
            junk4 = AR.alloc(D, BF16)
            junk5 = junk4
            s4 = [AR.alloc(1), AR.alloc(1)]
            h2T = AR.alloc(8 * 512, BF16)
            h1T = AR.alloc(NJ * 512, BF16)
            sgt = [AR.alloc(512), AR.alloc(512)]
            t4 = [AR.alloc(D), AR.alloc(D)]
            r3 = [AR.alloc(1), AR.alloc(1)]
            assert AR.off <= WU_OFF, (AR.off, WU_OFF)
            for hf in range(2):
                dma("pool", Wd.rearrange("p (c n) -> p c n", c=NJ)[:, hf * 11:(hf + 1) * 11, :],
                    wd_d.rearrange("(c p) n -> p c n", p=128)[:, hf * 11:(hf + 1) * 11, :], w=["Wd"])
            fin = []
            for sbi in range(NSB):
                for j in range(4):
                    tb = sbi * 4 + j
                    b2 = tb % 2
                    xv = xa[b2]
                    dma("sp", xv, out_d[tb * 128:(tb + 1) * 128, :], r=[f"out{tb}"], w=[f"xa{b2}"])
                    A("act", lambda g, xv=xv, b2=b2: g.activation(out=xn4[b2], in_=xv, func=AF.Square, accum_out=s4[b2]),
                      reads=[f"xa{b2}"], writes=["xn4", f"s4{b2}"])
                    rsqrt_ops(s4[b2], s4[b2], 1.0 / D, [f"s4{b2}"], f"s4{b2}")
                    A("act", lambda g, xv=xv, b2=b2: g.activation(out=xn4[b2], in_=xv, func=AF.Copy, scale=s4[b2]),
                      reads=[f"xa{b2}", f"s4{b2}"], writes=["xn4"])

                    def tr8b(g, b2=b2):
                        for c in range(8):
                            r = g.transpose(out=Pb[b2][:, c * 128:(c + 1) * 128], in_=xn4[b2][:, c * 128:(c + 1) * 128], identity=identb[:, :])
                        return r
                    A("pe", tr8b, reads=["xn4", "identb"], writes=[f"P{b2}"])
                    for c in range(8):
                        dst = h2T[:, c * 512 + j * 128:c * 512 + (j + 1) * 128]
                        if c % 2 == 0:
                            A("dve", lambda g, c=c, dst=dst, b2=b2: g.tensor_scalar(out=dst, in0=Pb[b2][:, c * 128:(c + 1) * 128],
                                                                                   scalar1=a2[:, c:c + 1], scalar2=sh2[:, c:c + 1],
                                                                                   op0=ALU.mult, op1=ALU.add),
                              reads=[f"P{b2}", "a2", "modc"], writes=["h2T"])
                        else:
                            A("act", lambda g, c=c, dst=dst, b2=b2: g.activation(out=dst, in_=Pb[b2][:, c * 128:(c + 1) * 128], func=AF.Identity,
                                                                                scale=a2[:, c:c + 1], bias=sh2[:, c:c + 1]),
                              reads=[f"P{b2}", "a2", "modc"], writes=["h2T"])
                for jj in range(NJ):
                    gb = 2 + jj % 2
                    ub = 4 + jj % 2

                    def mmg(g, jj=jj, gb=gb, ub=ub):
                        for c in range(8):
                            g.matmul(P[gb][:, :], lhsT=Wg[:, c * DFF + jj * 128:c * DFF + (jj + 1) * 128], rhs=h2T[:, c * 512:(c + 1) * 512],
                                     start=(c == 0), stop=(c == 7))
                        for c in range(8):
                            r = g.matmul(P[ub][:, :], lhsT=Wu[:, c * DFF + jj * 128:c * DFF + (jj + 1) * 128], rhs=h2T[:, c * 512:(c + 1) * 512],
                                         start=(c == 0), stop=(c == 7))
                        return r
                    A("pe", mmg, reads=["Wg", "Wu", "h2T"], writes=[f"P{gb}", f"P{ub}"])
                    A("act", lambda g, jj=jj, gb=gb: g.activation(out=sgt[jj % 2], in_=P[gb][:, :], func=AF.Silu), reads=[f"P{gb}"], writes=[f"sgt{jj % 2}"])
                    A("dve", lambda g, jj=jj, ub=ub: g.tensor_tensor(out=h1T[:, jj * 512:(jj + 1) * 512], in0=P[ub][:, :], in1=sgt[jj % 2], op=ALU.mult),
                      reads=[f"P{ub}", f"sgt{jj % 2}"], writes=["h1T"])
                for j in range(4):
                    tb = sbi * 4 + j
                    b2 = tb % 2
                    xv = xb[b2]
                    tv = t4[b2]
                    dma("sp", xv, out_d[tb * 128:(tb + 1) * 128, :], r=[f"out{tb}"], w=[f"xb{b2}"])

                    fb = (6, 7) if j % 2 == 0 else (3, 5)
                    fk_ = [f"P{fb[0]}", f"P{fb[1]}"]

                    def mmd(g, j=j, fb=fb):
                        for hf in range(2):
                            for jj in range(NJ):
                                r = g.matmul(P[fb[hf]][:, :], lhsT=h1T[:, jj * 512 + j * 128:jj * 512 + (j + 1) * 128],
                                             rhs=Wd[:, jj * D + hf * 512:jj * D + (hf + 1) * 512], start=(jj == 0), stop=(jj == NJ - 1))
                        return r
                    A("pe", mmd, reads=["h1T", "Wd"], writes=fk_)
                    A("act", lambda g, tv=tv, fb=fb: g.activation(out=tv[:, 0:512], in_=P[fb[0]][:, :], func=AF.Copy), reads=[fk_[0]], writes=[f"t4{b2}"])
                    A("dve", lambda g, tv=tv, fb=fb: g.tensor_copy(out=tv[:, 512:1024], in_=P[fb[1]][:, :]), reads=[fk_[1]], writes=[f"t4{b2}"])
                    A("act", lambda g, tv=tv, b2=b2: g.activation(out=junk5, in_=tv, func=AF.Square, accum_out=r3[b2]), reads=[f"t4{b2}"], writes=["junk4", f"r3{b2}"])
                    rsqrt_ops(r3[b2], r3[b2], 1.0 / D, [f"r3{b2}"], f"r3{b2}")
                    A("dve", lambda g, tv=tv, b2=b2: g.scalar_tensor_tensor(out=tv, in0=tv, scalar=r3[b2][:, 0:1], in1=G2b[:, :], op0=ALU.mult, op1=ALU.mult),
                      reads=[f"t4{b2}", f"r3{b2}", "G2b"], writes=[f"t4{b2}"])
                    A("pool", lambda g, xv=xv, tv=tv: g.tensor_tensor(out=xv, in0=xv, in1=tv, op=ALU.add), reads=[f"xb{b2}", f"t4{b2}"], writes=[f"xb{b2}"])
                    fin.append(dma("sp", out_d[tb * 128:(tb + 1) * 128, :], xv, r=[f"xb{b2}"], w=[f"out{tb}"]))
            A("sp", lambda g: None, deps=fin)

        except _Stop:
            pass
        with nc.Block() as block:
            S.emit_all(block, esem, dsem)
    return nc


def _consts():
    f = np.float32
    gam = 1.0 - 2.0 ** (-5.0 - np.arange(4, dtype=np.float64))
    idx = np.arange(128)
    ident = np.eye(128, dtype=f)
    tri = (idx[None, :] >= idx[:, None]).astype(f)
    dtc = np.zeros((128, 4, 128), np.float64)
    rel = idx[None, :] - idx[:, None]
    for h in range(4):
        dtc[:, h, :] = np.where(rel >= 0, gam[h] ** np.maximum(rel, 0), 0.0) * 0.125
    wqc = np.zeros((128, 2, 512), np.float64)
    wkc = np.zeros((128, 2, 128), np.float64)
    decc = np.zeros((128, 2), np.float64)
    for i in range(2):
        for r in range(128):
            h = 2 * i + r // 64
            wqc[r, i, :] = np.tile(gam[h] ** (idx + 1.0), 4)
            decc[r, i] = gam[h] ** 128
        for ft in range(128):
            h = 2 * i + ft // 64
            wkc[:, i, ft] = gam[h] ** (127.0 - idx) * 0.125
    inv_m = 10000.0 ** (-np.arange(16, dtype=np.float64) / 16.0)
    inv_r = 10000.0 ** (-np.arange(32, dtype=np.float64) / 32.0)
    invc = np.zeros((128, 3), np.float64)
    phc = np.zeros((128, 3), np.float64)
    for r in range(64):
        invc[r, 0] = inv_m[r % 16]
        phc[r, 0] = np.pi / 2 if r < 32 else (np.pi if r < 48 else 0.0)
    for r in range(128):
        invc[r, 1] = inv_r[r % 32]
        invc[r, 2] = inv_r[r % 32]
        phc[r, 1] = np.pi / 2
        phc[r, 2] = np.pi if (r % 64) < 32 else 0.0
    return dict(ident=ident, tri=tri, dtc=dtc.reshape(128, 512).astype(f), wqc=wqc.reshape(128, 1024).astype(f),
                wkc=wkc.reshape(128, 256).astype(f), decc=decc.astype(f), invc=invc.astype(f), phc=phc.astype(f))


def _colmajor(v, n):
    return np.ascontiguousarray(np.asarray(v, np.float32).reshape(n, 128).T)


def _prep_shared(inp):
    f = np.float32
    w_in = np.asarray(inp["w_in"], f)[0]
    cols = list(range(0, 640))
    cols += list(range(640, 672)) + [640 + k for k in list(range(16, 32)) + list(range(0, 16))]
    for base in (672, 928):
        for i in range(2):
            nat, sw = [], []
            for hh in (2 * i, 2 * i + 1):
                b = base + hh * 64
                nat += list(range(b, b + 64))
                sw += list(range(b + 32, b + 64)) + list(range(b, b + 32))
            cols += nat + sw
    cols += list(range(1184, 2208))
    w1 = np.ascontiguousarray(w_in[:, cols])
    assert w1.shape[1] == NC1
    wqb = np.asarray(inp["w_q_b"], f)[0]
    qc = []
    for h in range(8):
        b = h * 96
        qc += list(range(b + 64, b + 96)) + [b + 64 + k for k in list(range(16, 32)) + list(range(0, 16))] + list(range(b, b + 64))
    wq = np.ascontiguousarray(wqb[:, qc])
    wkvb = np.asarray(inp["w_kv_b"], f)[0]
    wkv = np.zeros((256, 1536), f)
    for h in range(8):
        wkv[:, h * 128 + 64:h * 128 + 128] = wkvb[:, h * 128:h * 128 + 64]
        wkv[:, 1024 + h * 64:1024 + (h + 1) * 64] = wkvb[:, h * 128 + 64:h * 128 + 128]
    sh = dict(
        w_ada=np.ascontiguousarray(np.asarray(inp["w_ada"], f)[0]),
        b_ada=np.ascontiguousarray(np.asarray(inp["b_ada"], f)[0][None, :]),
        gpre1=_colmajor(inp["pre_norm_mix"][0], 8), gpre2=_colmajor(inp["pre_norm_ffn"][0], 8),
        gpost1=np.ascontiguousarray(np.asarray(inp["post_norm_mix"], f)[0][None, :]),
        gpost2=np.ascontiguousarray(np.asarray(inp["post_norm_ffn"], f)[0][None, :]),
        qg=_colmajor(inp["q_a_norm"][0], 3), kvg=_colmajor(inp["kv_a_norm"][0], 2),
        og=_colmajor(np.concatenate([np.asarray(inp["mla_out_norm"], f)[0], np.asarray(inp["ret_gn_gain"], f)[0]]), 8),
        w1=w1, wq=wq, wkv=wkv,
        wout=np.ascontiguousarray(np.asarray(inp["w_out"], f)[0]),
        wg=np.ascontiguousarray(np.asarray(inp["w_gate"], f)[0]),
        wu=np.ascontiguousarray(np.asarray(inp["w_up"], f)[0]),
        wd=np.ascontiguousarray(np.asarray(inp["w_down"], f)[0]),
    )
    sh.update(_consts())
    return sh


def make_in_maps(inp, cores):
    sh = _prep_shared(inp)
    x = np.asarray(inp["x"], np.float32)
    c = np.asarray(inp["c"], np.float32)
    pos = np.asarray(inp["positions"], np.int32)
    maps = []
    for b in cores:
        m = dict(sh)
        m["x"] = np.ascontiguousarray(x[b])
        m["cT"] = _colmajor(c[b], 8)
        m["pos"] = np.ascontiguousarray(pos[b][None, :])
        maps.append(m)
    return maps


_NC = None


def kernel(**inputs):
    global _NC
    if _NC is None:
        _NC = build_nc()
    maps = make_in_maps(inputs, list(range(8)))
    res = run_bass_kernel_spmd(_NC, maps, core_ids=list(range(8)))
    return np.stack([np.asarray(r["out"], np.float32) for r in res.results], axis=0)
```

```python
import contextlib
import types
import numpy as np
import concourse.bass as bass
import concourse.mybir as mybir
from concourse.bass_utils import run_bass_kernel_spmd

F32 = mybir.dt.float32
BF16 = mybir.dt.bfloat16
I32 = mybir.dt.int32
AF = mybir.ActivationFunctionType
ALU = mybir.AluOpType
AX = mybir.AxisListType

ENGS = ("pe", "act", "dve", "pool", "sp")
STOP = 99
SUB = 99
HSEL = (0, 1, 2, 3)
TAPS = ()
REORDER = True
QS_ORDER = (0, 7, 1, 6, 2, 5, 3, 4)
GS = 1
NPT = 6


class _Stop(Exception):
    pass

T = 4096
D = 1024
NSB = 8
DFF = 2816
NJ = 22
NC1 = 2752
EPS = 1e-6
PI = float(np.pi)


def _freeze(fn):
    if fn.__closure__ is None:
        return fn
    cells = []
    for c in fn.__closure__:
        try:
            cells.append(types.CellType(c.cell_contents))
        except ValueError:
            cells.append(c)
    return types.FunctionType(fn.__code__, fn.__globals__, fn.__name__, fn.__defaults__, tuple(cells))


class Op:
    __slots__ = ("eng", "idx", "emit", "deps", "is_dma", "dma_i", "marked", "count", "clock", "waits", "is_bar", "busy", "lat", "seq", "st")


def _nfree(ap):
    n = 1
    for d in ap.shape[1:]:
        n *= int(d)
    return n


class _Fake:
    def __init__(self, eng):
        self.eng = eng
        self.busy = 0.0
        self.lat = None

    def matmul(self, out, lhsT=None, rhs=None, **kw):
        n = max(_nfree(rhs), 64)
        f = 4.0 if rhs.dtype == F32 else 1.0
        self.busy += f * n / 2370.0 + 0.004
        return self

    def transpose(self, out=None, in_=None, identity=None, **kw):
        self.busy += 0.08
        return self

    def activation(self, out=None, in_=None, **kw):
        self.busy += 0.1 + _nfree(in_) / 1150.0
        return self

    def dma_start(self, out=None, in_=None, **kw):
        nb = _nfree(out) * int(out.shape[0]) * (4 if out.dtype in (F32, I32) else 2)
        self.busy += 0.15 if self.eng == "sp" else 1.2
        self.lat = 2.5 + nb / 150e3
        return self

    def _dve(self, out, **kw):
        n = _nfree(out)
        if self.eng == "pool":
            self.busy += 0.2 + n / 500.0
        else:
            self.busy += 0.12 + n / 900.0
        return self

    def tensor_tensor(self, out=None, **kw):
        return self._dve(out)

    def tensor_scalar(self, out=None, **kw):
        return self._dve(out)

    def tensor_copy(self, out=None, **kw):
        return self._dve(out)

    def scalar_tensor_tensor(self, out=None, **kw):
        return self._dve(out)

    def tensor_single_scalar(self, out=None, **kw):
        return self._dve(out)

    def reciprocal(self, out=None, **kw):
        self.busy += 0.1 + _nfree(out) / 150.0
        return self

    def memset(self, ap, *a, **kw):
        return self._dve(ap)

    def reduce_sum(self, out=None, in_=None, **kw):
        return self._dve(in_)

    def then_inc(self, *a, **kw):
        return self


class Sched:
    def __init__(self, n_dma_sems=12):
        self.ops = {e: [] for e in ENGS}
        self.order = []
        self.lastw = {}
        self.readers = {}
        self.n_dma_sems = n_dma_sems
        self.dma_ops = {e: [] for e in ENGS}
        self.dma_since_bar = []

    def add(self, eng, emit, reads=(), writes=(), dma=False, deps=()):
        op = Op()
        op.eng = eng
        op.emit = _freeze(emit)
        op.is_dma = dma
        op.marked = False
        op.count = 0
        op.idx = len(self.ops[eng])
        d = set(deps)
        for k in reads:
            w = self.lastw.get(k)
            if w is not None:
                d.add(w)
        for k in writes:
            w = self.lastw.get(k)
            if w is not None:
                d.add(w)
            for r in self.readers.get(k, ()):
                d.add(r)
        for k in reads:
            self.readers.setdefault(k, []).append(op)
        for k in writes:
            self.lastw[k] = op
            self.readers[k] = []
        d.discard(op)
        op.deps = d
        op.is_bar = False
        op.seq = len(self.order)
        self.ops[eng].append(op)
        self.order.append(op)
        return op

    def barrier(self):
        for e in ENGS:
            self.add(e, lambda g: None).is_bar = True

    def _list_schedule(self, seg):
        import heapq
        segset = set(seg)
        succ = {o: [] for o in seg}
        indeg = {}
        for o in seg:
            fk = _Fake(o.eng)
            o.emit(fk)
            o.busy = fk.busy
            o.lat = fk.lat if fk.lat is not None else fk.busy + 0.06
            k = 0
            for d in o.deps:
                if d in segset:
                    succ[d].append(o)
                    k += 1
            indeg[o] = k
        bl = {}
        for o in reversed(seg):
            m = 0.0
            for s_ in succ[o]:
                if bl[s_] > m:
                    m = bl[s_]
            bl[o] = o.lat + m
        free = {e: 0.0 for e in ENGS}
        avail = {e: [] for e in ENGS}
        future = {e: [] for e in ENGS}
        rtime = {o: 0.0 for o in seg}
        for o in seg:
            if indeg[o] == 0:
                heapq.heappush(future[o.eng], (0.0, o.seq, o))
        out = []
        n = len(seg)
        while len(out) < n:
            best_e, best_t = None, None
            for e in ENGS:
                fu, av = future[e], avail[e]
                while fu and fu[0][0] <= free[e]:
                    _, sq, o = heapq.heappop(fu)
                    heapq.heappush(av, (-bl[o], sq, o))
                if av:
                    t = free[e]
                elif fu:
                    t = fu[0][0]
                else:
                    continue
                if best_t is None or t < best_t:
                    best_e, best_t = e, t
            e = best_e
            if not avail[e]:
                free[e] = best_t
                fu, av = future[e], avail[e]
                while fu and fu[0][0] <= free[e]:
                    _, sq, o = heapq.heappop(fu)
                    heapq.heappush(av, (-bl[o], sq, o))
            _, sq, o = heapq.heappop(avail[e])
            st = free[e]
            o.st = st
            free[e] = st + o.busy
            fin = st + o.lat
            out.append(o)
            for s_ in succ[o]:
                if fin > rtime[s_]:
                    rtime[s_] = fin
                indeg[s_] -= 1
                if indeg[s_] == 0:
                    heapq.heappush(future[s_.eng], (rtime[s_], s_.seq, s_))
        return out

    def schedule(self, reorder=True):
        segs, cur = [], []
        for o in self.order:
            if o.is_bar:
                if cur:
                    segs.append(("seg", cur))
                    cur = []
                if segs and segs[-1][0] == "bar":
                    segs[-1][1].append(o)
                else:
                    segs.append(("bar", [o]))
            else:
                cur.append(o)
        if cur:
            segs.append(("seg", cur))
        new = []
        last_seg = []
        for kind, lst in segs:
            if kind == "seg":
                lst2 = self._list_schedule(lst) if reorder else lst
                new += lst2
                last_seg = lst2
            else:
                deps = [o for o in last_seg if o.is_dma]
                for e in ENGS:
                    for o in reversed(last_seg):
                        if o.eng == e and not o.is_dma:
                            deps.append(o)
                            break
                for o in lst:
                    o.deps = set(deps)
                new += lst
        self.order = new
        self.ops = {e: [] for e in ENGS}
        self.dma_ops = {e: [] for e in ENGS}
        for o in new:
            o.idx = len(self.ops[o.eng])
            self.ops[o.eng].append(o)
            if o.is_dma:
                o.dma_i = len(self.dma_ops[o.eng])
                if o.dma_i >= self.n_dma_sems:
                    o.deps.add(self.dma_ops[o.eng][o.dma_i - self.n_dma_sems])
                self.dma_ops[o.eng].append(o)

    def resolve(self):
        known = {e: {f: -1 for f in ENGS} for e in ENGS}
        known_dma = {e: set() for e in ENGS}
        for op in self.order:
            e = op.eng
            kn = known[e]
            waits = []
            for d in sorted(op.deps, key=lambda o: -o.idx):
                if d.is_dma:
                    if d in known_dma[e]:
                        continue
                    known_dma[e].add(d)
                    waits.append(d)
                else:
                    if d.eng == "pe" and e == "pe":
                        continue
                    if kn[d.eng] >= d.idx:
                        continue
                    d.marked = True
                    waits.append(d)
                ck = d.clock
                for f in ENGS:
                    if ck[f] > kn[f]:
                        kn[f] = ck[f]
            op.waits = waits
            ck = dict(kn)
            if not op.is_dma:
                ck[e] = max(ck[e], op.idx)
            op.clock = ck
        for e in ENGS:
            c = 0
            for op in self.ops[e]:
                if op.marked:
                    c += 1
                    op.count = c

    def emit_all(self, block, esem, dsem):
        self.schedule(reorder=REORDER)
        self.resolve()
        n = self.n_dma_sems

        def run(e, engobj):
            for op in self.ops[e]:
                for d in op.waits:
                    if d.is_dma:
                        engobj.wait_ge(dsem[d.eng][d.dma_i % n], 16 * (d.dma_i // n + 1))
                    else:
                        engobj.wait_ge(esem[d.eng], d.count)
                ins = op.emit(engobj)
                if op.is_dma:
                    ins.then_inc(dsem[e][op.dma_i % n], 16)
                elif op.marked:
                    if ins is None:
                        ins = engobj.nop()
                    ins.then_inc(esem[e], 1)

        block.tensor(lambda t: run("pe", t))
        block.scalar(lambda t: run("act", t))
        block.vector(lambda t: run("dve", t))
        block.gpsimd(lambda t: run("pool", t))
        block.sync(lambda t: run("sp", t))


class Arena:
    def __init__(self, ap, ncols):
        self.ap = ap
        self.n = ncols
        self.off = 0

    def alloc(self, cols, dt=F32):
        nb = cols * (4 if dt in (F32, I32) else 2)
        n32 = ((nb + 31) // 32) * 8
        assert self.off + n32 <= self.n, ("arena overflow", self.off, n32, self.n)
        v = self.ap[:, self.off:self.off + n32]
        self.off += n32
        if dt != F32:
            v = v.bitcast(dt)
        return v[:, 0:cols]

    def reset(self):
        self.off = 0

    def alloc_at(self, off32, cols, dt=F32):
        save = self.off
        self.off = off32
        v = self.alloc(cols, dt)
        end = self.off
        self.off = save
        return v, end


def build_nc():
    nc = bass.Bass("TRN2", target_bir_lowering=False)

    def DI(name, shape, dt=F32):
        return nc.dram_tensor(name, shape, dt, kind="ExternalInput").ap()

    x_d = DI("x", [T, D])
    c_d = DI("cT", [128, 8])
    pos_d = DI("pos", [1, T], I32)
    wada_d = DI("w_ada", [D, 6 * D])
    bada_d = DI("b_ada", [1, 6 * D])
    gpre1_d = DI("gpre1", [128, 8])
    gpre2_d = DI("gpre2", [128, 8])
    gpost1_d = DI("gpost1", [1, D])
    gpost2_d = DI("gpost2", [1, D])
    qg_d = DI("qg", [128, 3])
    kvg_d = DI("kvg", [128, 2])
    og_d = DI("og", [128, 8])
    w1_d = DI("w1", [D, NC1])
    wq_d = DI("wq", [384, 1024])
    wkv_d = DI("wkv", [256, 1536])
    wout_d = DI("wout", [D, D])
    wg_d = DI("wg", [D, DFF])
    wu_d = DI("wu", [D, DFF])
    wd_d = DI("wd", [DFF, D])
    ident_d = DI("ident", [128, 128])
    tri_d = DI("tri", [128, 128])
    dt_d = DI("dtc", [128, 512])
    wqc_d = DI("wqc", [128, 1024])
    wkc_d = DI("wkc", [128, 256])
    dec_d = DI("decc", [128, 2])
    inv_d = DI("invc", [128, 3])
    ph_d = DI("phc", [128, 3])
    out_d = nc.dram_tensor("out", [T, D], F32, kind="ExternalOutput").ap()
    yret_d = nc.dram_tensor("yret_scr", [4, 128, T], BF16).ap()

    S = Sched(n_dma_sems=12)
    A = S.add

    with contextlib.ExitStack() as ctx:
        def sbt(name, cols, dt=F32, parts=128):
            return ctx.enter_context(nc.sbuf_tensor(name, [parts, cols], dt))

        identb = sbt("identb", 128, BF16)
        trib = sbt("trib", 128, BF16)
        onesb = sbt("onesb", 128, BF16)
        onesf = sbt("onesf", 128)
        epst = sbt("epst", 1)
        DECc = sbt("DECc", 2)
        INVc = sbt("INVc", 3)
        PHc = sbt("PHc", 3)
        modc = sbt("modc", 32)
        qg = sbt("qg_sb", 3)
        kvg = sbt("kvg_sb", 2)
        a1 = sbt("a1", 8)
        a2 = sbt("a2", 8)
        G1b = sbt("G1b", D)
        G2b = sbt("G2b", D)
        ARN = 50000
        arena_t = sbt("arena", ARN)
        AR = Arena(arena_t, ARN)
        P = [ctx.enter_context(nc.psum_tensor(f"bank{i}", [128, 512], F32)) for i in range(8)]
        Pb = [p[:, :].bitcast(BF16) for p in P]

        esem = {e: ctx.enter_context(nc.semaphore("es_" + e)) for e in ENGS}
        dsem = {e: [ctx.enter_context(nc.semaphore(f"ds_{e}{i}")) for i in range(12)] for e in ("sp", "pool")}

        def dma(q, out, in_, r=(), w=()):
            return A(q, lambda g: g.dma_start(out=out, in_=in_), reads=r, writes=w, dma=True)

        def tap(name, ap, keys):
            if name not in TAPS:
                return
            shp = list(ap.shape)
            dd = nc.dram_tensor("dbg_" + name, shp, ap.dtype, kind="ExternalOutput").ap()
            dma("sp", dd, ap, r=keys)

        def rsqrt_ops(dst, src, scale, rk, wk):
            A("act", lambda g: g.activation(out=dst, in_=src, func=AF.Sqrt, scale=scale, bias=epst[0:dst.shape[0], :]),
              reads=list(rk) + ["epst"], writes=[wk])
            A("dve", lambda g: g.reciprocal(out=dst, in_=dst), reads=[wk], writes=[wk])

        _phase = [0]
        try:
            dma("pool", identb[:, :], ident_d, w=["identb"])
            dma("pool", trib[:, :], tri_d, w=["trib"])
            A("pool", lambda g: g.memset(onesb[:, :], 1.0), writes=["onesb"])
            A("pool", lambda g: g.memset(onesf[:, :], 1.0), writes=["onesf"])
            A("pool", lambda g: g.memset(epst[:, :], EPS), writes=["epst"])
            for t_, d_, k_ in ((DECc, dec_d, "DECc"), (INVc, inv_d, "INVc"), (PHc, ph_d, "PHc")):
                dma("sp", t_[:, :], d_, w=[k_])
            cT = AR.alloc(8)
            gp1 = AR.alloc(8)
            gp2 = AR.alloc(8)
            scb = AR.alloc(8, BF16)
            gpo1 = AR.alloc(D)
            gpo2 = AR.alloc(D)
            bada = AR.alloc(6 * D)
            modrow = AR.alloc(6 * D)
            grow1 = AR.alloc(D)
            grow2 = AR.alloc(D)
            wa = [AR.alloc(8 * 1024, BF16), AR.alloc(8 * 1024, BF16)]
            dma("sp", cT, c_d, w=["cT"])
            dma("sp", qg[:, :], qg_d, w=["qg"])
            dma("sp", kvg[:, :], kvg_d, w=["kvg"])
            dma("sp", gp1, gpre1_d, w=["gp1"])
            dma("sp", gp2, gpre2_d, w=["gp2"])
            dma("sp", gpo1[0:1, :], gpost1_d, w=["gpo1"])
            dma("sp", gpo2[0:1, :], gpost2_d, w=["gpo2"])
            dma("sp", bada[0:1, :], bada_d, w=["bada"])
            A("act", lambda g: g.activation(out=scb, in_=cT, func=AF.Silu), reads=["cT"], writes=["scb"])
            for gd in range(6):
                wb = wa[gd % 2]
                dma("pool", wb.rearrange("p (c n) -> p c n", c=8),
                    wada_d[:, gd * 1024:(gd + 1) * 1024].rearrange("(c p) n -> p c n", p=128), w=[f"wa{gd % 2}"])
                for g2_ in range(2):
                    gi = gd * 2 + g2_

                    def mm_ada(g, gi=gi, wb=wb, g2_=g2_):
                        for k in range(8):
                            r = g.matmul(P[gi % 2][0:1, :], lhsT=scb[:, k:k + 1], rhs=wb[:, k * 1024 + g2_ * 512:k * 1024 + (g2_ + 1) * 512],
                                         start=(k == 0), stop=(k == 7))
                        return r
                    A("pe", mm_ada, reads=["scb", f"wa{gd % 2}"], writes=[f"P{gi % 2}"])
                    A("dve", lambda g, gi=gi: g.tensor_tensor(out=modrow[0:1, gi * 512:(gi + 1) * 512], in0=P[gi % 2][0:1, :],
                                                             in1=bada[0:1, gi * 512:(gi + 1) * 512], op=ALU.add),
                      reads=[f"P{gi % 2}", "bada"], writes=["modrow"])
            col_offs = [0 * D, 1 * D, 3 * D, 4 * D]

            def mm_cols(g):
                for vi, off in enumerate(col_offs):
                    for c in range(8):
                        r = g.matmul(P[2][:, vi * 8 + c:vi * 8 + c + 1], lhsT=modrow[0:1, off + c * 128:off + (c + 1) * 128],
                                     rhs=onesf[0:1, 0:1], start=True, stop=True)
                return r
            A("pe", mm_cols, reads=["modrow", "onesf"], writes=["P2"])
            A("dve", lambda g: g.tensor_copy(out=modc[:, :], in_=P[2][:, 0:32]), reads=["P2"], writes=["modc"])
            A("dve", lambda g: g.scalar_tensor_tensor(out=a1[:, :], in0=modc[:, 8:16], scalar=1.0, in1=gp1, op0=ALU.add, op1=ALU.mult),
              reads=["modc", "gp1"], writes=["a1"])
            A("dve", lambda g: g.scalar_tensor_tensor(out=a2[:, :], in0=modc[:, 24:32], scalar=1.0, in1=gp2, op0=ALU.add, op1=ALU.mult),
              reads=["modc", "gp2"], writes=["a2"])
            sh1 = modc[:, 0:8]
            sh2 = modc[:, 16:24]
            A("dve", lambda g: g.tensor_tensor(out=grow1[0:1, :], in0=modrow[0:1, 2 * D:3 * D], in1=gpo1[0:1, :], op=ALU.mult),
              reads=["modrow", "gpo1"], writes=["grow1"])
            A("dve", lambda g: g.tensor_tensor(out=grow2[0:1, :], in0=modrow[0:1, 5 * D:6 * D], in1=gpo2[0:1, :], op=ALU.mult),
              reads=["modrow", "gpo2"], writes=["grow2"])
            for gi, (grow, Gb, gk) in enumerate(((grow1, G1b, "G1b"), (grow2, G2b, "G2b"))):
                for hf in range(2):
                    bk = 3 + hf
                    A("pe", lambda g, grow=grow, hf=hf, bk=bk: g.matmul(P[bk][:, :], lhsT=onesf[0:1, 0:128],
                                                                        rhs=grow[0:1, hf * 512:(hf + 1) * 512], start=True, stop=True),
                      reads=[f"grow{gi + 1}", "onesf"], writes=[f"P{bk}"])
                    A("act", lambda g, Gb=Gb, hf=hf, bk=bk: g.activation(out=Gb[:, hf * 512:(hf + 1) * 512], in_=P[bk][:, :], func=AF.Copy),
                      reads=[f"P{bk}"], writes=[gk])
            S.barrier()
            _phase[0] += 1
            if _phase[0] > STOP:
                raise _Stop()
            AR.reset()

            cqnT = AR.alloc(3 * T, BF16)
            ckvnT = AR.alloc(2 * T, BF16)
            TABm = AR.alloc(T)
            kpeT = TABm[64:96, 0:2048].bitcast(BF16)
            P12 = AR.off
            DTc = AR.alloc(512)
            WQc = AR.alloc(1024)
            WKc = AR.alloc(256)
            dma("sp", DTc, dt_d, w=["DTc"])
            dma("sp", WQc, wqc_d, w=["WQc"])
            dma("sp", WKc, wkc_d, w=["WKc"])
            W1 = AR.alloc(8 * NC1, BF16)
            xt = [AR.alloc(D), AR.alloc(D)]
            xn = [AR.alloc(D, BF16), AR.alloc(D, BF16)]
            junk = AR.alloc(D, BF16)
            ssq = [AR.alloc(1), AR.alloc(1)]
            hT = AR.alloc(8 * 512, BF16)
            cqraw = AR.alloc(3 * 512)
            ckvraw = AR.alloc(2 * 512)
            sq = AR.alloc(3 * 512, BF16)
            sq2 = AR.alloc(2 * 512, BF16)
            Rq = AR.alloc(512)
            Rkv = Rq
            posi = AR.alloc(512, I32)
            posf = AR.alloc(512)
            ang = AR.alloc(512)
            ni = posi
            nf = AR.alloc(512)
            msk = nf
            Cr = AR.alloc(512)
            Sr = AR.alloc(512)
            t1 = [AR.alloc(512), AR.alloc(512)]
            t2 = [AR.alloc(512), AR.alloc(512)]
            rqT = AR.alloc(2 * 512, BF16)
            rkT = AR.alloc(2 * 512, BF16)
            qwT = AR.alloc(2 * 512, BF16)
            rqm = AR.alloc(2 * 512, BF16)
            qwm = AR.alloc(2 * 512, BF16)
            A("pool", lambda g: g.memset(rqm[64:128, :], 0.0), writes=["rqm"])
            A("pool", lambda g: g.memset(qwm[64:128, :], 0.0), writes=["qwm"])
            vtok = AR.alloc(4 * 512, BF16)
            sg = AR.alloc(4 * 512, BF16)
            kwtok = AR.alloc(256, BF16)
            scTm = AR.alloc(512, BF16)
            osb = AR.alloc(512)
            ynorm = AR.alloc(512)
            osq = ynorm
            ytok = AR.alloc(512, BF16)
            ysT = AR.alloc(4 * 512, BF16)
            Sf = AR.alloc(256)
            Sbf = AR.alloc(256, BF16)
            st = {k: AR.alloc(4) for k in ("osum", "osqs", "mean", "msq", "var", "rgn")}

            for hf in range(2):
                dma("pool", W1.rearrange("p (c n) -> p c n", c=8)[:, hf * 4:(hf + 1) * 4, :],
                    w1_d.rearrange("(c p) n -> p c n", p=128)[:, hf * 4:(hf + 1) * 4, :], w=[f"W1{hf}"])
            A("pool", lambda g: g.memset(Sf, 0.0), writes=["Sf"])
            A("pool", lambda g: g.memset(Sbf, 0.0), writes=["Sbf"])

            def ck(n):
                if SUB == n:
                    raise _Stop()
            ck(0)

            def w1s(c, off, n):
                return W1[:, c * NC1 + off:c * NC1 + off + n]

            def table(dst, dk, col, sbi):
                A("dve", lambda g: g.tensor_scalar(out=ang, in0=posf, scalar1=INVc[:, col:col + 1], scalar2=PHc[:, col:col + 1],
                                                   op0=ALU.mult, op1=ALU.add), reads=["posf", "INVc", "PHc"], writes=["ang"])
                A("dve", lambda g: g.tensor_scalar(out=ni, in0=ang, scalar1=float(1.0 / (2 * PI)), scalar2=None, op0=ALU.mult),
                  reads=["ang"], writes=["ibuf"])
                A("dve", lambda g: g.tensor_copy(out=nf, in_=ni), reads=["ibuf"], writes=["nf"])
                A("dve", lambda g: g.scalar_tensor_tensor(out=ang, in0=nf, scalar=-2 * PI, in1=ang, op0=ALU.mult, op1=ALU.add),
                  reads=["nf", "ang"], writes=["ang"])
                A("dve", lambda g: g.tensor_single_scalar(out=msk, in_=ang, scalar=PI, op=ALU.is_gt), reads=["ang", "nf"], writes=["nf"])
                A("dve", lambda g: g.scalar_tensor_tensor(out=ang, in0=msk, scalar=-2 * PI, in1=ang, op0=ALU.mult, op1=ALU.add),
                  reads=["nf", "ang"], writes=["ang"])
                A("dve", lambda g: g.tensor_scalar(out=ang, in0=ang, scalar1=-3.14159, scalar2=3.14159, op0=ALU.max, op1=ALU.min),
                  reads=["ang"], writes=["ang"])
                np_ = dst.shape[0]
                A("act", lambda g: g.activation(out=dst, in_=ang[0:np_, :], func=AF.Sin), reads=["ang"], writes=[dk])

            mtiles = [(0, 128, "cq", 0), (128, 128, "cq", 1), (256, 128, "cq", 2), (384, 128, "ckv", 0), (512, 128, "ckv", 1),
                      (640, 64, "kpe", 0)]
            o_ = 704
            for nm in ("rq", "rk"):
                for i in range(2):
                    mtiles.append((o_, 128, nm + "n", i))
                    mtiles.append((o_ + 128, 128, nm + "s", i))
                    o_ += 256
            RV = 1728
            RG = 2240

            for sbi in range(NSB):
                sc0 = sbi * 512
                dma("sp", posi, bass.AP(pos_d.tensor, sc0, [[0, 128], [1, 512]]), w=["ibuf"])
                A("dve", lambda g: g.tensor_copy(out=posf, in_=posi), reads=["ibuf"], writes=["posf"])
                table(TABm[0:64, sc0:sc0 + 512], "TABm", 0, sbi)
                table(Cr, "Cr", 1, sbi)
                table(Sr, "Sr", 2, sbi)
                ck(1)
                for j in range(4):
                    tb = sbi * 4 + j
                    b2 = tb % 2
                    dma("sp", xt[b2], x_d[tb * 128:(tb + 1) * 128, :], w=[f"xt{b2}"])
                    A("act", lambda g, b2=b2: g.activation(out=junk, in_=xt[b2], func=AF.Square, accum_out=ssq[b2]),
                      reads=[f"xt{b2}"], writes=["junk", f"ssq{b2}"])
                    rsqrt_ops(ssq[b2], ssq[b2], 1.0 / D, [f"ssq{b2}"], f"ssq{b2}")
                    A("act", lambda g, b2=b2: g.activation(out=xn[b2], in_=xt[b2], func=AF.Copy, scale=ssq[b2]),
                      reads=[f"xt{b2}", f"ssq{b2}"], writes=[f"xn{b2}"])

                    def tr8(g, b2=b2):
                        for c in range(8):
                            r = g.transpose(out=Pb[b2][:, c * 128:(c + 1) * 128], in_=xn[b2][:, c * 128:(c + 1) * 128], identity=identb[:, :])
                        return r
                    A("pe", tr8, reads=[f"xn{b2}", "identb"], writes=[f"P{b2}"])
                    for c in range(8):
                        dst = hT[:, c * 512 + j * 128:c * 512 + (j + 1) * 128]
                        if c % 2 == 0:
                            A("dve", lambda g, c=c, dst=dst, b2=b2: g.tensor_scalar(out=dst, in0=Pb[b2][:, c * 128:(c + 1) * 128],
                                                                                   scalar1=a1[:, c:c + 1], scalar2=sh1[:, c:c + 1],
                                                                                   op0=ALU.mult, op1=ALU.add),
                              reads=[f"P{b2}", "a1", "modc"], writes=["hT"])
                        else:
                            A("act", lambda g, c=c, dst=dst, b2=b2: g.activation(out=dst, in_=Pb[b2][:, c * 128:(c + 1) * 128], func=AF.Identity,
                                                                                scale=a1[:, c:c + 1], bias=sh1[:, c:c + 1]),
                              reads=[f"P{b2}", "a1", "modc"], writes=["hT"])
                ck(2)
                for mi, (off, M, kind, i) in enumerate(mtiles):
                    bk = 2 + mi % 2
                    pk = f"P{bk}"

                    def mmz(g, off=off, M=M, bk=bk):
                        for c in range(8):
                            r = g.matmul(P[bk][0:M, :], lhsT=w1s(c, off, M), rhs=hT[:, c * 512:(c + 1) * 512], start=(c == 0), stop=(c == 7))
                        return r
                    A("pe", mmz, reads=["W10", "W11", "hT"], writes=[pk])
                    if kind in ("cq", "ckv"):
                        raw, sqt, nt, Rt, bank, scl, dstT, rk_ = ((cqraw, sq, 3, Rq, 4, 1.0 / 384, cqnT, "Rq") if kind == "cq"
                                                                  else (ckvraw, sq2, 2, Rkv, 5, 1.0 / 256, ckvnT, "Rq"))
                        gcol = qg if kind == "cq" else kvg
                        A("act", lambda g, raw=raw, i=i, bk=bk, gcol=gcol: g.activation(out=raw[:, i * 512:(i + 1) * 512], in_=P[bk][:, :], func=AF.Copy,
                                                                                      scale=gcol[:, i:i + 1]),
                          reads=[pk, "qg", "kvg"], writes=[f"{kind}raw{i}"])
                        A("act", lambda g, sqt=sqt, i=i, bk=bk: g.activation(out=sqt[:, i * 512:(i + 1) * 512], in_=P[bk][:, :], func=AF.Square),
                          reads=[pk], writes=[f"{kind}sq{i}"])
                        if i == nt - 1:
                            def mmst(g, sqt=sqt, nt=nt, bank=bank):
                                for q in range(nt):
                                    r = g.matmul(P[bank][:, :], lhsT=onesb[:, :], rhs=sqt[:, q * 512:(q + 1) * 512], start=(q == 0), stop=(q == nt - 1))
                                return r
                            A("pe", mmst, reads=[f"{kind}sq{q}" for q in range(nt)] + ["onesb"], writes=[f"P{bank}"])
                            rsqrt_ops(Rt, P[bank][:, :], scl, [f"P{bank}"], rk_)
                            for q in range(nt):
                                A("pool", lambda g, raw=raw, Rt=Rt, q=q, dstT=dstT: g.tensor_tensor(
                                    out=dstT[:, q * T + sc0:q * T + sc0 + 512], in0=raw[:, q * 512:(q + 1) * 512], in1=Rt, op=ALU.mult),
                                  reads=[f"{kind}raw{q}", rk_], writes=[f"{kind}nT"])
                    elif kind == "kpe":
                        A("dve", lambda g, bk=bk: g.tensor_tensor(out=t1[0][0:32, :], in0=P[bk][0:32, :], in1=TABm[0:32, sc0:sc0 + 512], op=ALU.mult),
                          reads=[pk, "TABm"], writes=["t1_0"])
                        A("dve", lambda g, bk=bk: g.tensor_tensor(out=t2[0][0:32, :], in0=P[bk][32:64, :], in1=TABm[32:64, sc0:sc0 + 512], op=ALU.mult),
                          reads=[pk, "TABm"], writes=["t2_0"])
                        A("dve", lambda g: g.tensor_tensor(out=kpeT[:, sc0:sc0 + 512], in0=t1[0][0:32, :], in1=t2[0][0:32, :], op=ALU.add),
                          reads=["t1_0", "t2_0"], writes=["kpeT"])
                    else:
                        nm = kind[:2]
                        if kind[2] == "n":
                            A("dve", lambda g, bk=bk, i=i: g.tensor_tensor(out=t1[i], in0=P[bk][:, :], in1=Cr, op=ALU.mult),
                              reads=[pk, "Cr"], writes=[f"t1_{i}"])
                        else:
                            A("dve", lambda g, bk=bk, i=i: g.tensor_tensor(out=t2[i], in0=P[bk][:, :], in1=Sr, op=ALU.mult),
                              reads=[pk, "Sr"], writes=[f"t2_{i}"])
                            dstq = rqT if nm == "rq" else rkT
                            A("pool", lambda g, i=i, dstq=dstq: g.tensor_tensor(out=dstq[:, i * 512:(i + 1) * 512], in0=t1[i], in1=t2[i], op=ALU.add),
                              reads=[f"t1_{i}", f"t2_{i}"], writes=[nm + "T"])
                            if nm == "rq":
                                A("pool", lambda g, i=i: g.tensor_tensor(out=rqm[0:64, i * 512:(i + 1) * 512], in0=t1[i][0:64, :], in1=t2[i][0:64, :],
                                                                        op=ALU.add),
                                  reads=[f"t1_{i}", f"t2_{i}"], writes=["rqm"])
                                A("pool", lambda g, i=i: g.tensor_tensor(out=qwT[:, i * 512:(i + 1) * 512], in0=rqT[:, i * 512:(i + 1) * 512],
                                                                        in1=WQc[:, i * 512:(i + 1) * 512], op=ALU.mult),
                                  reads=["rqT", "WQc"], writes=["qwT"])
                                A("pool", lambda g, i=i: g.tensor_tensor(out=qwm[0:64, i * 512:(i + 1) * 512], in0=rqT[0:64, i * 512:(i + 1) * 512],
                                                                        in1=WQc[0:64, i * 512:(i + 1) * 512], op=ALU.mult),
                                  reads=["rqT", "WQc"], writes=["qwm"])
                ck(3)
                for j in range(4):
                    for which, off, bank in (("v", RV, 4), ("g", RG, 5)):
                        def mmt(g, j=j, off=off, bank=bank):
                            for c in range(8):
                                r = g.matmul(P[bank][:, :], lhsT=hT[:, c * 512 + j * 128:c * 512 + (j + 1) * 128], rhs=w1s(c, off, 512),
                                             start=(c == 0), stop=(c == 7))
                            return r
                        A("pe", mmt, reads=["W10", "W11", "hT"], writes=[f"P{bank}"])
                        if which == "v":
                            A("act", lambda g, j=j: g.activation(out=vtok[:, j * 512:(j + 1) * 512], in_=P[4][:, :], func=AF.Copy),
                              reads=["P4"], writes=["vtok"])
                        else:
                            A("act", lambda g, j=j: g.activation(out=sg[:, j * 512:(j + 1) * 512], in_=P[5][:, :], func=AF.Silu),
                              reads=["P5"], writes=["sg"])
                ck(4)
                for j in range(4):
                    jc = slice(j * 128, (j + 1) * 128)

                    def trk(g, j=j):
                        for i in range(2):
                            r = g.transpose(out=Pb[5][:, i * 128:(i + 1) * 128], in_=rkT[:, i * 512 + j * 128:i * 512 + (j + 1) * 128], identity=identb[:, :])
                        return r
                    A("pe", trk, reads=["rkT", "identb"], writes=["P5"])
                    A("dve", lambda g: g.tensor_tensor(out=kwtok, in0=Pb[5][:, 0:256], in1=WKc[:, :], op=ALU.mult),
                      reads=["P5", "WKc"], writes=["kwtok"])

                    ck(6)

                    def mmsc(g, j=j):
                        for h in range(4):
                            i, r0 = h // 2, 64 * (h % 2)
                            cs = slice(i * 512 + j * 128, i * 512 + (j + 1) * 128)
                            if r0 == 0:
                                r = g.matmul(P[6][:, h * 128:(h + 1) * 128], lhsT=rkT[:, cs], rhs=rqm[:, cs], start=True, stop=True)
                            else:
                                r = g.matmul(P[6][:, h * 128:(h + 1) * 128], lhsT=rkT[64:128, cs], rhs=rqT[64:128, cs], start=True, stop=True,
                                             tile_position=(64, 0))
                        return r
                    A("pe", mmsc, reads=["rkT", "rqT", "rqm"], writes=["P6"])
                    A("dve", lambda g: g.tensor_tensor(out=scTm, in0=P[6][:, :], in1=DTc[:, :], op=ALU.mult), reads=["P6", "DTc"], writes=["scTm"])

                    ck(7)

                    def mmo(g, j=j):
                        for h in range(4):
                            i, r0 = h // 2, 64 * (h % 2)
                            cs = slice(i * 512 + j * 128, i * 512 + (j + 1) * 128)
                            g.matmul(P[7][:, h * 128:(h + 1) * 128], lhsT=scTm[:, h * 128:(h + 1) * 128],
                                     rhs=vtok[:, j * 512 + h * 128:j * 512 + (h + 1) * 128], start=True, stop=False)
                            if r0 == 0:
                                r = g.matmul(P[7][:, h * 128:(h + 1) * 128], lhsT=qwm[:, cs], rhs=Sbf[:, i * 128:(i + 1) * 128], start=False, stop=True)
                            else:
                                r = g.matmul(P[7][:, h * 128:(h + 1) * 128], lhsT=qwT[64:128, cs], rhs=Sbf[64:128, i * 128:(i + 1) * 128],
                                             start=False, stop=True, tile_position=(64, 0))
                        return r
                    A("pe", mmo, reads=["scTm", "vtok", "qwT", "qwm", "Sbf"], writes=["P7"])
                    A("act", lambda g: g.activation(out=osb, in_=P[7][:, :], func=AF.Copy), reads=["P7"], writes=["osb"])

                    ck(8)

                    def mmu(g, j=j):
                        for h in range(4):
                            i, r0 = h // 2, 64 * (h % 2)
                            kw = dict(tile_position=(0, 64)) if r0 else {}
                            r = g.matmul(P[5][r0:r0 + 64, 256 + i * 128:256 + (i + 1) * 128], lhsT=kwtok[:, h * 64:(h + 1) * 64],
                                         rhs=vtok[:, j * 512 + h * 128:j * 512 + (h + 1) * 128], start=True, stop=True, **kw)
                        return r
                    A("pe", mmu, reads=["kwtok", "vtok"], writes=["P5"])
                    for i in range(2):
                        A("dve", lambda g, i=i: g.scalar_tensor_tensor(out=Sf[:, i * 128:(i + 1) * 128], in0=Sf[:, i * 128:(i + 1) * 128],
                                                                      scalar=DECc[:, i:i + 1], in1=P[5][:, 256 + i * 128:256 + (i + 1) * 128],
                                                                      op0=ALU.mult, op1=ALU.add),
                          reads=["P5", "DECc", "Sf"], writes=["Sf"])
                    A("pool", lambda g: g.tensor_copy(out=Sbf, in_=Sf), reads=["Sf"], writes=["Sbf"])
                    ck(9)
                    o3 = osb.rearrange("p (h v) -> p h v", h=4)
                    A("dve", lambda g, o3=o3: g.reduce_sum(out=st["osum"], in_=o3, axis=AX.X), reads=["osb"], writes=["osum"])
                    A("pool", lambda g: g.tensor_tensor(out=osq, in0=osb, in1=osb, op=ALU.mult), reads=["osb"], writes=["ynorm"])
                    A("dve", lambda g: g.reduce_sum(out=st["osqs"], in_=osq.rearrange("p (h v) -> p h v", h=4), axis=AX.X),
                      reads=["ynorm"], writes=["osqs"])
                    A("dve", lambda g: g.tensor_scalar(out=st["mean"], in0=st["osum"], scalar1=1.0 / 128, scalar2=None, op0=ALU.mult),
                      reads=["osum"], writes=["mean"])
                    A("dve", lambda g: g.tensor_tensor(out=st["msq"], in0=st["mean"], in1=st["mean"], op=ALU.mult), reads=["mean"], writes=["msq"])
                    A("dve", lambda g: g.scalar_tensor_tensor(out=st["var"], in0=st["osqs"], scalar=1.0 / 128, in1=st["msq"], op0=ALU.mult, op1=ALU.subtract),
                      reads=["osqs", "msq"], writes=["var"])
                    rsqrt_ops(st["rgn"], st["var"], 1.0, ["var"], "rgn")
                    for h in range(4):
                        A("dve", lambda g, h=h: g.tensor_scalar(out=ynorm[:, h * 128:(h + 1) * 128], in0=osb[:, h * 128:(h + 1) * 128],
                                                                scalar1=st["mean"][:, h:h + 1], scalar2=st["rgn"][:, h:h + 1],
                                                                op0=ALU.subtract, op1=ALU.mult),
                          reads=["osb", "mean", "rgn"], writes=["ynorm"])
                    A("pool", lambda g, j=j: g.tensor_tensor(out=ytok, in0=ynorm, in1=sg[:, j * 512:(j + 1) * 512], op=ALU.mult),
                      reads=["ynorm", "sg"], writes=["ytok"])

                    ck(10)

                    def try_(g):
                        for t in range(4):
                            r = g.transpose(out=Pb[4][:, t * 128:(t + 1) * 128], in_=ytok[:, t * 128:(t + 1) * 128], identity=identb[:, :])
                        return r
                    A("pe", try_, reads=["ytok", "identb"], writes=["P4"])
                    A("act", lambda g, j=j: g.activation(out=ysT.rearrange("p (t n) -> p t n", t=4)[:, :, j * 128:(j + 1) * 128],
                                                         in_=Pb[4][:, 0:512].rearrange("p (t n) -> p t n", t=4), func=AF.Copy),
                      reads=["P4"], writes=["ysT"])
                ck(5)
                dma("sp", yret_d[:, :, sc0:sc0 + 512].rearrange("t p n -> p t n"), ysT.rearrange("p (t n) -> p t n", t=4),
                    r=["ysT"], w=["yret_d"])
            tap("cqnT", cqnT, ["cqnT"])
            tap("ckvnT", ckvnT, ["ckvnT"])
            tap("kpeT", kpeT, ["kpeT"])
            tap("TABm", TABm, ["TABm"])
            tap("yret", yret_d, ["yret_d"])
            tap("hT", hT, ["hT"])
            tap("cqraw", cqraw, ["cqraw0", "cqraw1", "cqraw2"])
            tap("Rq", Rq, ["Rq"])
            tap("sq", sq, ["cqsq0", "cqsq1", "cqsq2"])
            tap("rqT", rqT, ["rqT"])
            tap("rkT", rkT, ["rkT"])
            tap("osb", osb, ["osb"])
            tap("ytok", ytok, ["ytok"])
            tap("Sf", Sf, ["Sf"])
            S.barrier()
            _phase[0] += 1
            if _phase[0] > STOP:
                raise _Stop()
            AR.off = P12

            ymlaT = AR.alloc(4 * T, BF16)
            P23 = AR.off
            Wq = AR.alloc(3 * 1024, BF16)
            Wkv = AR.alloc(2 * 1536, BF16)
            dma("pool", Wq.rearrange("p (c n) -> p c n", c=3), wq_d.rearrange("(c p) n -> p c n", p=128), w=["Wq"])
            dma("pool", Wkv.rearrange("p (c n) -> p c n", c=2), wkv_d.rearrange("(c p) n -> p c n", p=128), w=["Wkv"])
            KT = [AR.alloc(T, BF16), AR.alloc(T, BF16)]
            QT = [AR.alloc(T, BF16), AR.alloc(T, BF16)]
            Vg = [AR.alloc(32 * 128, BF16), AR.alloc(32 * 128, BF16)]
            PT = [AR.alloc(GS * 512, BF16) for _ in range(NPT)]
            NSLOT = 4 // GS
            it_i = 0
            u1 = AR.alloc(512)
            u2 = AR.alloc(512)
            rec = AR.alloc(512)
            for b in range(2):
                A("dve", lambda g, b=b: g.memset(KT[b][0:64, :], 0.0), writes=[f"KT{b}"])
                A("pool", lambda g, b=b: g.memset(QT[b][0:64, :], 0.0), writes=[f"QT{b}"])
                A("dve", lambda g, b=b: g.memset(Vg[b].rearrange("p (k v) -> p k v", v=128)[:, :, 64:128], 1.0), writes=[f"Vg{b}"])
            SCALE = float(96 ** -0.5)
            pt_i = 0
            for h in range(8):
                hb = h % 2
                kk, qk, vk = f"KT{hb}", f"QT{hb}", f"Vg{hb}"
                A("dve", lambda g, hb=hb: g.tensor_copy(out=KT[hb][0:32, :], in_=kpeT[:, :]), reads=["kpeT"], writes=[kk])
                for sbi in range(NSB):
                    sc0 = sbi * 512

                    def mmk(g, h=h, sc0=sc0):
                        for c in range(2):
                            r = g.matmul(P[6][:, :], lhsT=Wkv[:, c * 1536 + h * 128:c * 1536 + (h + 1) * 128], rhs=ckvnT[:, c * T + sc0:c * T + sc0 + 512],
                                         start=(c == 0), stop=(c == 1))
                        return r
                    A("pe", mmk, reads=["Wkv", "ckvnT"], writes=["P6"])
                    A("dve", lambda g, hb=hb, sc0=sc0: g.tensor_copy(out=KT[hb][64:128, sc0:sc0 + 512], in_=P[6][64:128, :]),
                      reads=["P6"], writes=[kk])

                    def mmq(g, h=h, sc0=sc0):
                        for c in range(3):
                            r = g.matmul(P[7][:, :], lhsT=Wq[:, c * 1024 + h * 128:c * 1024 + (h + 1) * 128], rhs=cqnT[:, c * T + sc0:c * T + sc0 + 512],
                                         start=(c == 0), stop=(c == 2))
                        return r
                    A("pe", mmq, reads=["Wq", "cqnT"], writes=["P7"])
                    A("dve", lambda g, sc0=sc0: g.tensor_tensor(out=u1[0:32, :], in0=P[7][0:32, :], in1=TABm[0:32, sc0:sc0 + 512], op=ALU.mult),
                      reads=["P7", "TABm"], writes=["u1"])
                    A("dve", lambda g, sc0=sc0: g.tensor_tensor(out=u2[0:32, :], in0=P[7][32:64, :], in1=TABm[32:64, sc0:sc0 + 512], op=ALU.mult),
                      reads=["P7", "TABm"], writes=["u2"])
                    A("pool", lambda g, hb=hb, sc0=sc0: g.tensor_tensor(out=QT[hb][0:32, sc0:sc0 + 512], in0=u1[0:32, :], in1=u2[0:32, :], op=ALU.add),
                      reads=["u1", "u2"], writes=[qk])
                    A("dve", lambda g, hb=hb, sc0=sc0: g.tensor_copy(out=QT[hb][64:128, sc0:sc0 + 512], in_=P[7][64:128, :]),
                      reads=["P7"], writes=[qk])
                for k8 in range(4):
                    def mmv(g, h=h, k8=k8):
                        for q in range(8):
                            kb = k8 * 8 + q
                            for c in range(2):
                                r = g.matmul(P[6][:, q * 64:(q + 1) * 64], lhsT=ckvnT[:, c * T + kb * 128:c * T + (kb + 1) * 128],
                                             rhs=Wkv[:, c * 1536 + 1024 + h * 64:c * 1536 + 1024 + (h + 1) * 64], start=(c == 0), stop=(c == 1))
                        return r
                    A("pe", mmv, reads=["Wkv", "ckvnT"], writes=["P6"])
                    A("dve", lambda g, hb=hb, k8=k8: g.tensor_copy(
                        out=Vg[hb].rearrange("p (k v) -> p k v", v=128)[:, k8 * 8:(k8 + 1) * 8, 0:64],
                        in_=P[6][:, :].rearrange("p (k v) -> p k v", v=64)), reads=["P6"], writes=[vk])
                for qi, qs in enumerate(QS_ORDER):
                    q0 = qs * 512
                    acc = 4 + qi % 2
                    ak = f"P{acc}"
                    nfull = 4 * qs
                    groups = [(kb, min(kb + GS, nfull)) for kb in range(0, nfull, GS)]
                    items = [("full", a, b) for a, b in groups] + [("diag", 4 * qs + d, d) for d in range(4)]
                    last_kb = 4 * qs + 3
                    for gi, it in enumerate(items):
                        slot = it_i % NSLOT
                        it_i += 1
                        sbank = GS * slot
                        sk = f"PS{slot}"
                        pt = PT[pt_i % NPT]
                        ptk = f"PT{pt_i % NPT}"
                        pt_i += 1
                        if it[0] == "full":
                            kbs = list(range(it[1], it[2]))

                            def mms(g, hb=hb, kbs=kbs, sbank=sbank, q0=q0):
                                for n_, kb in enumerate(kbs):
                                    r = g.matmul(P[sbank + n_][:, :], lhsT=KT[hb][:, kb * 128:(kb + 1) * 128], rhs=QT[hb][:, q0:q0 + 512],
                                                 start=True, stop=True)
                                return r
                            A("pe", mms, reads=[kk, qk], writes=[sk])
                            for n_ in range(len(kbs)):
                                A("act", lambda g, pt=pt, sbank=sbank, n_=n_: g.activation(out=pt[:, n_ * 512:(n_ + 1) * 512], in_=P[sbank + n_][:, :],
                                                                                            func=AF.Exp, scale=SCALE),
                                  reads=[sk], writes=[ptk])

                            def mmpv(g, hb=hb, kbs=kbs, pt=pt, acc=acc, last_kb=last_kb):
                                for n_, kb in enumerate(kbs):
                                    r = g.matmul(P[acc][:, :], lhsT=Vg[hb][:, kb * 128:(kb + 1) * 128], rhs=pt[:, n_ * 512:(n_ + 1) * 512],
                                                 start=(kb == 0), stop=(kb == last_kb))
                                return r
                            A("pe", mmpv, reads=[vk, ptk], writes=[ak])
                        else:
                            kb, d = it[1], it[2]
                            c0 = d * 128
                            A("pe", lambda g, hb=hb, kb=kb, c0=c0, sbank=sbank, q0=q0: g.matmul(
                                P[sbank][:, c0:512], lhsT=KT[hb][:, kb * 128:(kb + 1) * 128], rhs=QT[hb][:, q0 + c0:q0 + 512], start=True, stop=True),
                              reads=[kk, qk], writes=[sk])
                            A("act", lambda g, pt=pt, sbank=sbank, c0=c0: g.activation(out=pt[:, c0:512], in_=P[sbank][:, c0:512], func=AF.Exp, scale=SCALE),
                              reads=[sk], writes=[ptk])
                            A("pool", lambda g, pt=pt, c0=c0: g.tensor_tensor(out=pt[:, c0:c0 + 128], in0=pt[:, c0:c0 + 128], in1=trib[:, :], op=ALU.mult),
                              reads=[ptk, "trib"], writes=[ptk])
                            A("pe", lambda g, hb=hb, kb=kb, c0=c0, pt=pt, acc=acc, last_kb=last_kb: g.matmul(
                                P[acc][:, c0:512], lhsT=Vg[hb][:, kb * 128:(kb + 1) * 128], rhs=pt[:, c0:512], start=(kb == 0), stop=(kb == last_kb)),
                              reads=[vk, ptk], writes=[ak])
                    A("dve", lambda g, acc=acc: g.reciprocal(out=rec[0:64, :], in_=P[acc][64:128, :]), reads=[ak], writes=["rec"])
                    r0 = 64 * (h % 2)
                    A("dve", lambda g, acc=acc, r0=r0, h=h, q0=q0: g.tensor_tensor(
                        out=ymlaT[r0:r0 + 64, (h // 2) * T + q0:(h // 2) * T + q0 + 512], in0=P[acc][0:64, :], in1=rec[0:64, :], op=ALU.mult),
                      reads=[ak, "rec"], writes=["ymlaT"])
            S.barrier()
            _phase[0] += 1
            if _phase[0] > STOP:
                raise _Stop()

            AR.off = P23
            WU_OFF = ARN - (8 * DFF * 2) // 4
            Wg, wg_end = AR.alloc_at(0, 8 * DFF, BF16)
            Wu, _ = AR.alloc_at(WU_OFF, 8 * DFF, BF16)
            assert wg_end <= P12
            for hf in range(2):
                dma("pool", Wg.rearrange("p (c n) -> p c n", c=8)[:, hf * 4:(hf + 1) * 4, :],
                    wg_d.rearrange("(c p) n -> p c n", p=128)[:, hf * 4:(hf + 1) * 4, :], w=[f"Wg{hf}"])
                dma("pool", Wu.rearrange("p (c n) -> p c n", c=8)[:, hf * 4:(hf + 1) * 4, :],
                    wu_d.rearrange("(c p) n -> p c n", p=128)[:, hf * 4:(hf + 1) * 4, :], w=[f"Wu{hf}"])
            Wo = AR.alloc(8 * D, BF16)
            og = AR.alloc(8)
            yrTs = [AR.alloc(4 * 512, BF16), AR.alloc(4 * 512, BF16)]
            ysq = [AR.alloc(4 * 128, BF16), AR.alloc(4 * 128, BF16)]
            xt3 = [AR.alloc(D), AR.alloc(D)]
            x1 = [AR.alloc(D), AR.alloc(D)]
            mB = [AR.alloc(D)]
            mixs = [AR.alloc(D)]
            tt = [AR.alloc(D)]
            rm = [AR.alloc(1), AR.alloc(1)]
            r2 = [AR.alloc(1), AR.alloc(1)]
            hole = wg_end
            for lst in (mB, mixs, tt):
                v_, hole = AR.alloc_at(hole, D)
                lst.append(v_)
            assert hole <= P12
            dma("sp", og, og_d, w=["og"])
            for hf in range(2):
                dma("pool", Wo.rearrange("p (c n) -> p c n", c=8)[:, hf * 4:(hf + 1) * 4, :],
                    wout_d.rearrange("(c p) n -> p c n", p=128)[:, hf * 4:(hf + 1) * 4, :], w=[f"Wo{hf}"])
            for c in range(8):
                A("dve", lambda g, c=c: g.tensor_scalar(out=Wo[:, c * D:(c + 1) * D], in0=Wo[:, c * D:(c + 1) * D], scalar1=og[:, c:c + 1],
                                                         scalar2=None, op0=ALU.mult), reads=[f"Wo{c // 4}", "og"], writes=[f"Wo{c // 4}"])
            for tb in range(32):
                b2 = tb % 2
                tc0 = tb * 128
                sbi, j = tb // 4, tb % 4
                yb = yrTs[sbi % 2]
                ybk = f"yrT{sbi % 2}"
                pa = (0, 1) if b2 == 0 else (4, 5)
                pak = [f"P{pa[0]}", f"P{pa[1]}"]
                stb = 6 + b2
                if j == 0:
                    dma("sp", yb.rearrange("p (t n) -> p t n", t=4), yret_d[:, :, sbi * 512:(sbi + 1) * 512].rearrange("t p n -> p t n"),
                        r=["yret_d"], w=[ybk])
                dma("sp", xt3[b2], x_d[tc0:tc0 + 128, :], w=[f"xt3{b2}"])
                A("pool", lambda g, tc0=tc0, b2=b2: g.tensor_tensor(out=ysq[b2].rearrange("p (c n) -> p c n", c=4),
                                                                     in0=ymlaT.rearrange("p (c n) -> p c n", c=4)[:, :, tc0:tc0 + 128],
                                                                     in1=ymlaT.rearrange("p (c n) -> p c n", c=4)[:, :, tc0:tc0 + 128], op=ALU.mult),
                  reads=["ymlaT"], writes=[f"ysq{b2}"])

                def mmss(g, b2=b2, stb=stb):
                    for c in range(4):
                        r = g.matmul(P[stb][:, 0:1], lhsT=ysq[b2][:, c * 128:(c + 1) * 128], rhs=onesb[:, 0:1], start=(c == 0), stop=(c == 3))
                    return r
                A("pe", mmss, reads=[f"ysq{b2}", "onesb"], writes=[f"P{stb}"])
                rsqrt_ops(rm[b2], P[stb][:, 0:1], 1.0 / 512, [f"P{stb}"], f"rm{b2}")

                def mmA(g, tc0=tc0, pa=pa):
                    for hf in range(2):
                        for c in range(4):
                            r = g.matmul(P[pa[hf]][:, :], lhsT=ymlaT[:, c * T + tc0:c * T + tc0 + 128], rhs=Wo[:, c * D + hf * 512:c * D + (hf + 1) * 512],
                                         start=(c == 0), stop=(c == 3))
                    return r
                A("pe", mmA, reads=["ymlaT", "Wo0"], writes=pak)

                def mmB(g, yb=yb, j=j):
                    for hf in range(2):
                        for c in range(4):
                            r = g.matmul(P[2 + hf][:, :], lhsT=yb[:, c * 512 + j * 128:c * 512 + (j + 1) * 128],
                                         rhs=Wo[:, (4 + c) * D + hf * 512:(4 + c) * D + (hf + 1) * 512], start=(c == 0), stop=(c == 3))
                    return r
                A("pe", mmB, reads=[ybk, "Wo1"], writes=["PB"])
                for hf in range(2):
                    A("act", lambda g, hf=hf, b2=b2: g.activation(out=mB[b2][:, hf * 512:(hf + 1) * 512], in_=P[2 + hf][:, :], func=AF.Copy),
                      reads=["PB"], writes=[f"mB{b2}"])
                    A("dve", lambda g, hf=hf, b2=b2, pa=pa: g.scalar_tensor_tensor(out=mixs[b2][:, hf * 512:(hf + 1) * 512], in0=P[pa[hf]][:, :],
                                                                                scalar=rm[b2][:, 0:1], in1=mB[b2][:, hf * 512:(hf + 1) * 512],
                                                                                op0=ALU.mult, op1=ALU.add),
                      reads=[pak[hf], f"rm{b2}", f"mB{b2}"], writes=[f"mixs{b2}"])
                A("act", lambda g, b2=b2: g.activation(out=tt[b2].bitcast(BF16)[:, 0:D], in_=mixs[b2], func=AF.Square, accum_out=r2[b2]),
                  reads=[f"mixs{b2}"], writes=[f"tt{b2}", f"r2{b2}"])
                rsqrt_ops(r2[b2], r2[b2], 1.0 / D, [f"r2{b2}"], f"r2{b2}")
                A("dve", lambda g, b2=b2: g.scalar_tensor_tensor(out=tt[b2], in0=mixs[b2], scalar=r2[b2][:, 0:1], in1=G1b[:, :], op0=ALU.mult, op1=ALU.mult),
                  reads=[f"mixs{b2}", f"r2{b2}", "G1b"], writes=[f"tt{b2}"])
                A("pool", lambda g, b2=b2: g.tensor_tensor(out=x1[b2], in0=xt3[b2], in1=tt[b2], op=ALU.add), reads=[f"xt3{b2}", f"tt{b2}"], writes=[f"x1{b2}"])
                dma("sp", out_d[tc0:tc0 + 128, :], x1[b2], r=[f"x1{b2}"], w=[f"out{tb}"])
            assert AR.off <= WU_OFF, (AR.off, WU_OFF)
            S.barrier()
            _phase[0] += 1
            if _phase[0] > STOP:
                raise _Stop()

            Wd, wd_end = AR.alloc_at(wg_end, NJ * D, BF16)
            assert wd_end <= P23
            AR.off = P23
            xa = [AR.alloc(D), AR.alloc(D)]
            xb = [AR.alloc(D), AR.alloc(D)]
            xn4_ = AR.alloc(D, BF16)
            xn4 = [xn4_, xn4_]
            junk4 = AR.alloc(D, BF16)
            junk5 = junk4
            s4 = [AR.alloc(1), AR.alloc(1)]
            h2T = AR.alloc(8 * 512, BF16)
            h1T = AR.alloc(NJ * 512, BF16)
            sgt = [AR.alloc(512), AR.alloc(512)]
            t4 = [AR.alloc(D), AR.alloc(D)]
            r3 = [AR.alloc(1), AR.alloc(1)]
            assert AR.off <= WU_OFF, (AR.off, WU_OFF)
            for hf in range(2):
                dma("pool", Wd.rearrange("p (c n) -> p c n", c=NJ)[:, hf * 11:(hf + 1) * 11, :],
                    wd_d.rearrange("(c p) n -> p c n", p=128)[:, hf * 11:(hf + 1) * 11, :], w=[f"Wd{hf}"])
            fin = []
            for sbi in range(NSB):
                for j in range(4):
                    tb = sbi * 4 + j
                    b2 = tb % 2
                    xv = xa[b2]
                    dma("sp", xv, out_d[tb * 128:(tb + 1) * 128, :], r=[f"out{tb}"], w=[f"xa{b2}"])
                    A("act", lambda g, xv=xv, b2=b2: g.activation(out=xn4[b2], in_=xv, func=AF.Square, accum_out=s4[b2]),
                      reads=[f"xa{b2}"], writes=["xn4", f"s4{b2}"])
                    rsqrt_ops(s4[b2], s4[b2], 1.0 / D, [f"s4{b2}"], f"s4{b2}")
                    A("act", lambda g, xv=xv, b2=b2: g.activation(out=xn4[b2], in_=xv, func=AF.Copy, scale=s4[b2]),
                      reads=[f"xa{b2}", f"s4{b2}"], writes=["xn4"])

                    def tr8b(g, b2=b2):
                        for c in range(8):
                            r = g.transpose(out=Pb[b2][:, c * 128:(c + 1) * 128], in_=xn4[b2][:, c * 128:(c + 1) * 128], identity=identb[:, :])
                        return r
                    A("pe", tr8b, reads=["xn4", "identb"], writes=[f"P{b2}"])
                    for c in range(8):
                        dst = h2T[:, c * 512 + j * 128:c * 512 + (j + 1) * 128]
                        if c % 2 == 0:
                            A("dve", lambda g, c=c, dst=dst, b2=b2: g.tensor_scalar(out=dst, in0=Pb[b2][:, c * 128:(c + 1) * 128],
                                                                                   scalar1=a2[:, c:c + 1], scalar2=sh2[:, c:c + 1],
                                                                                   op0=ALU.mult, op1=ALU.add),
                              reads=[f"P{b2}", "a2", "modc"], writes=["h2T"])
                        else:
                            A("act", lambda g, c=c, dst=dst, b2=b2: g.activation(out=dst, in_=Pb[b2][:, c * 128:(c + 1) * 128], func=AF.Identity,
                                                                                scale=a2[:, c:c + 1], bias=sh2[:, c:c + 1]),
                              reads=[f"P{b2}", "a2", "modc"], writes=["h2T"])
                for jj in range(NJ):
                    gb = 2 + jj % 2
                    ub = 4 + jj % 2

                    def mmg(g, jj=jj, gb=gb, ub=ub):
                        for c in range(8):
                            g.matmul(P[gb][:, :], lhsT=Wg[:, c * DFF + jj * 128:c * DFF + (jj + 1) * 128], rhs=h2T[:, c * 512:(c + 1) * 512],
                                     start=(c == 0), stop=(c == 7))
                        for c in range(8):
                            r = g.matmul(P[ub][:, :], lhsT=Wu[:, c * DFF + jj * 128:c * DFF + (jj + 1) * 128], rhs=h2T[:, c * 512:(c + 1) * 512],
                                         start=(c == 0), stop=(c == 7))
                        return r
                    A("pe", mmg, reads=["Wg0", "Wg1", "Wu0", "Wu1", "h2T"], writes=[f"P{gb}", f"P{ub}"])
                    A("act", lambda g, jj=jj, gb=gb: g.activation(out=sgt[jj % 2], in_=P[gb][:, :], func=AF.Silu), reads=[f"P{gb}"], writes=[f"sgt{jj % 2}"])
                    A("dve", lambda g, jj=jj, ub=ub: g.tensor_tensor(out=h1T[:, jj * 512:(jj + 1) * 512], in0=P[ub][:, :], in1=sgt[jj % 2], op=ALU.mult),
                      reads=[f"P{ub}", f"sgt{jj % 2}"], writes=["h1T"])
                for j in range(4):
                    tb = sbi * 4 + j
                    b2 = tb % 2
                    xv = xb[b2]
                    tv = t4[b2]
                    dma("sp", xv, out_d[tb * 128:(tb + 1) * 128, :], r=[f"out{tb}"], w=[f"xb{b2}"])

                    fb = (6, 7) if j % 2 == 0 else (3, 5)
                    fk_ = [f"P{fb[0]}", f"P{fb[1]}"]

                    def mmd(g, j=j, fb=fb):
                        for hf in range(2):
                            for jj in range(NJ):
                                r = g.matmul(P[fb[hf]][:, :], lhsT=h1T[:, jj * 512 + j * 128:jj * 512 + (j + 1) * 128],
                                             rhs=Wd[:, jj * D + hf * 512:jj * D + (hf + 1) * 512], start=(jj == 0), stop=(jj == NJ - 1))
                        return r
                    A("pe", mmd, reads=["h1T", "Wd0", "Wd1"], writes=fk_)
                    A("act", lambda g, tv=tv, fb=fb: g.activation(out=tv[:, 0:512], in_=P[fb[0]][:, :], func=AF.Copy), reads=[fk_[0]], writes=[f"t4{b2}"])
                    A("dve", lambda g, tv=tv, fb=fb: g.tensor_copy(out=tv[:, 512:1024], in_=P[fb[1]][:, :]), reads=[fk_[1]], writes=[f"t4{b2}"])
                    A("act", lambda g, tv=tv, b2=b2: g.activation(out=junk5, in_=tv, func=AF.Square, accum_out=r3[b2]), reads=[f"t4{b2}"], writes=["junk4", f"r3{b2}"])
                    rsqrt_ops(r3[b2], r3[b2], 1.0 / D, [f"r3{b2}"], f"r3{b2}")
                    A("dve", lambda g, tv=tv, b2=b2: g.scalar_tensor_tensor(out=tv, in0=tv, scalar=r3[b2][:, 0:1], in1=G2b[:, :], op0=ALU.mult, op1=ALU.mult),
                      reads=[f"t4{b2}", f"r3{b2}", "G2b"], writes=[f"t4{b2}"])
                    A("pool", lambda g, xv=xv, tv=tv: g.tensor_tensor(out=xv, in0=xv, in1=tv, op=ALU.add), reads=[f"xb{b2}", f"t4{b2}"], writes=[f"xb{b2}"])
                    fin.append(dma("sp", out_d[tb * 128:(tb + 1) * 128, :], xv, r=[f"xb{b2}"], w=[f"out{tb}"]))
            A("sp", lambda g: None, deps=fin)

        except _Stop:
            pass
        with nc.Block() as block:
            S.emit_all(block, esem, dsem)
    return nc


def _consts():
    f = np.float32
    gam = 1.0 - 2.0 ** (-5.0 - np.arange(4, dtype=np.float64))
    idx = np.arange(128)
    ident = np.eye(128, dtype=f)
    tri = (idx[None, :] >= idx[:, None]).astype(f)
    dtc = np.zeros((128, 4, 128), np.float64)
    rel = idx[None, :] - idx[:, None]
    for h in range(4):
        dtc[:, h, :] = np.where(rel >= 0, gam[h] ** np.maximum(rel, 0), 0.0) * 0.125
    wqc = np.zeros((128, 2, 512), np.float64)
    wkc = np.zeros((128, 2, 128), np.float64)
    decc = np.zeros((128, 2), np.float64)
    for i in range(2):
        for r in range(128):
            h = 2 * i + r // 64
            wqc[r, i, :] = np.tile(gam[h] ** (idx + 1.0), 4)
            decc[r, i] = gam[h] ** 128
        for ft in range(128):
            h = 2 * i + ft // 64
            wkc[:, i, ft] = gam[h] ** (127.0 - idx) * 0.125
    inv_m = 10000.0 ** (-np.arange(16, dtype=np.float64) / 16.0)
    inv_r = 10000.0 ** (-np.arange(32, dtype=np.float64) / 32.0)
    invc = np.zeros((128, 3), np.float64)
    phc = np.zeros((128, 3), np.float64)
    for r in range(64):
        invc[r, 0] = inv_m[r % 16]
        phc[r, 0] = np.pi / 2 if r < 32 else (np.pi if r < 48 else 0.0)
    for r in range(128):
        invc[r, 1] = inv_r[r % 32]
        invc[r, 2] = inv_r[r % 32]
        phc[r, 1] = np.pi / 2
        phc[r, 2] = np.pi if (r % 64) < 32 else 0.0
    return dict(ident=ident, tri=tri, dtc=dtc.reshape(128, 512).astype(f), wqc=wqc.reshape(128, 1024).astype(f),
                wkc=wkc.reshape(128, 256).astype(f), decc=decc.astype(f), invc=invc.astype(f), phc=phc.astype(f))


def _colmajor(v, n):
    return np.ascontiguousarray(np.asarray(v, np.float32).reshape(n, 128).T)


def _prep_shared(inp):
    f = np.float32
    w_in = np.asarray(inp["w_in"], f)[0]
    cols = list(range(0, 640))
    cols += list(range(640, 672)) + [640 + k for k in list(range(16, 32)) + list(range(0, 16))]
    for base in (672, 928):
        for i in range(2):
            nat, sw = [], []
            for hh in (2 * i, 2 * i + 1):
                b = base + hh * 64
                nat += list(range(b, b + 64))
                sw += list(range(b + 32, b + 64)) + list(range(b, b + 32))
            cols += nat + sw
    cols += list(range(1184, 2208))
    w1 = np.ascontiguousarray(w_in[:, cols])
    assert w1.shape[1] == NC1
    wqb = np.asarray(inp["w_q_b"], f)[0]
    qc = []
    for h in range(8):
        b = h * 96
        qc += list(range(b + 64, b + 96)) + [b + 64 + k for k in list(range(16, 32)) + list(range(0, 16))] + list(range(b, b + 64))
    wq = np.ascontiguousarray(wqb[:, qc])
    wkvb = np.asarray(inp["w_kv_b"], f)[0]
    wkv = np.zeros((256, 1536), f)
    for h in range(8):
        wkv[:, h * 128 + 64:h * 128 + 128] = wkvb[:, h * 128:h * 128 + 64]
        wkv[:, 1024 + h * 64:1024 + (h + 1) * 64] = wkvb[:, h * 128 + 64:h * 128 + 128]
    sh = dict(
        w_ada=np.ascontiguousarray(np.asarray(inp["w_ada"], f)[0]),
        b_ada=np.ascontiguousarray(np.asarray(inp["b_ada"], f)[0][None, :]),
        gpre1=_colmajor(inp["pre_norm_mix"][0], 8), gpre2=_colmajor(inp["pre_norm_ffn"][0], 8),
        gpost1=np.ascontiguousarray(np.asarray(inp["post_norm_mix"], f)[0][None, :]),
        gpost2=np.ascontiguousarray(np.asarray(inp["post_norm_ffn"], f)[0][None, :]),
        qg=_colmajor(inp["q_a_norm"][0], 3), kvg=_colmajor(inp["kv_a_norm"][0], 2),
        og=_colmajor(np.concatenate([np.asarray(inp["mla_out_norm"], f)[0], np.asarray(inp["ret_gn_gain"], f)[0]]), 8),
        w1=w1, wq=wq, wkv=wkv,
        wout=np.ascontiguousarray(np.asarray(inp["w_out"], f)[0]),
        wg=np.ascontiguousarray(np.asarray(inp["w_gate"], f)[0]),
        wu=np.ascontiguousarray(np.asarray(inp["w_up"], f)[0]),
        wd=np.ascontiguousarray(np.asarray(inp["w_down"], f)[0]),
    )
    sh.update(_consts())
    return sh


def make_in_maps(inp, cores):
    sh = _prep_shared(inp)
    x = np.asarray(inp["x"], np.float32)
    c = np.asarray(inp["c"], np.float32)
    pos = np.asarray(inp["positions"], np.int32)
    maps = []
    for b in cores:
        m = dict(sh)
        m["x"] = np.ascontiguousarray(x[b])
        m["cT"] = _colmajor(c[b], 8)
        m["pos"] = np.ascontiguousarray(pos[b][None, :])
        maps.append(m)
    return maps


_NC = None


def kernel(**inputs):
    global _NC
    if _NC is None:
        _NC = build_nc()
    maps = make_in_maps(inputs, list(range(8)))
    res = run_bass_kernel_spmd(_NC, maps, core_ids=list(range(8)))
    return np.stack([np.asarray(r["out"], np.float32) for r in res.results], axis=0)
```

```python
import contextlib
import types
import numpy as np
import concourse.bass as bass
import concourse.mybir as mybir
from concourse.bass_utils import run_bass_kernel_spmd

F32 = mybir.dt.float32
BF16 = mybir.dt.bfloat16
I32 = mybir.dt.int32
AF = mybir.ActivationFunctionType
ALU = mybir.AluOpType
AX = mybir.AxisListType

ENGS = ("pe", "act", "dve", "pool", "sp")
STOP = 99
SUB = 99
HSEL = (0, 1, 2, 3)
TAPS = ()
REORDER = True
QS_ORDER = (0, 7, 1, 6, 2, 5, 3, 4)
GS = 1
NPT = 6


class _Stop(Exception):
    pass

T = 4096
D = 1024
NSB = 8
DFF = 2816
NJ = 22
NC1 = 2752
EPS = 1e-6
PI = float(np.pi)


def _freeze(fn):
    if fn.__closure__ is None:
        return fn
    cells = []
    for c in fn.__closure__:
        try:
            cells.append(types.CellType(c.cell_contents))
        except ValueError:
            cells.append(c)
    return types.FunctionType(fn.__code__, fn.__globals__, fn.__name__, fn.__defaults__, tuple(cells))


class Op:
    __slots__ = ("eng", "idx", "emit", "deps", "is_dma", "dma_i", "marked", "count", "clock", "waits", "is_bar", "busy", "lat", "seq", "st")


def _nfree(ap):
    n = 1
    for d in ap.shape[1:]:
        n *= int(d)
    return n


class _Fake:
    def __init__(self, eng):
        self.eng = eng
        self.busy = 0.0
        self.lat = None

    def matmul(self, out, lhsT=None, rhs=None, **kw):
        n = max(_nfree(rhs), 64)
        f = 4.0 if rhs.dtype == F32 else 1.0
        self.busy += f * n / 2370.0 + (0.004 if n >= 512 else 0.06)
        return self

    def transpose(self, out=None, in_=None, identity=None, **kw):
        self.busy += 0.12
        return self

    def activation(self, out=None, in_=None, **kw):
        n = _nfree(in_)
        self.busy += (0.07 if n >= 512 else 0.25) + n / 1200.0
        return self

    def dma_start(self, out=None, in_=None, **kw):
        nb = _nfree(out) * int(out.shape[0]) * (4 if out.dtype in (F32, I32) else 2)
        self.busy += 0.15 if self.eng == "sp" else 1.2
        self.lat = 2.5 + nb / 150e3
        return self

    def _dve(self, out, **kw):
        n = _nfree(out)
        if self.eng == "pool":
            self.busy += 0.2 + n / 500.0
        else:
            self.busy += 0.12 + n / 900.0
        return self

    def tensor_tensor(self, out=None, **kw):
        return self._dve(out)

    def tensor_scalar(self, out=None, **kw):
        return self._dve(out)

    def tensor_copy(self, out=None, **kw):
        return self._dve(out)

    def scalar_tensor_tensor(self, out=None, **kw):
        return self._dve(out)

    def tensor_single_scalar(self, out=None, **kw):
        return self._dve(out)

    def reciprocal(self, out=None, **kw):
        self.busy += 0.1 + _nfree(out) / 150.0
        return self

    def memset(self, ap, *a, **kw):
        return self._dve(ap)

    def reduce_sum(self, out=None, in_=None, **kw):
        return self._dve(in_)

    def then_inc(self, *a, **kw):
        return self


class Sched:
    def __init__(self, n_dma_sems=12):
        self.ops = {e: [] for e in ENGS}
        self.order = []
        self.lastw = {}
        self.readers = {}
        self.n_dma_sems = n_dma_sems
        self.dma_ops = {e: [] for e in ENGS}
        self.dma_since_bar = []

    def add(self, eng, emit, reads=(), writes=(), dma=False, deps=()):
        op = Op()
        op.eng = eng
        op.emit = _freeze(emit)
        op.is_dma = dma
        op.marked = False
        op.count = 0
        op.idx = len(self.ops[eng])
        d = set(deps)
        for k in reads:
            w = self.lastw.get(k)
            if w is not None:
                d.add(w)
        for k in writes:
            w = self.lastw.get(k)
            if w is not None:
                d.add(w)
            for r in self.readers.get(k, ()):
                d.add(r)
        for k in reads:
            self.readers.setdefault(k, []).append(op)
        for k in writes:
            self.lastw[k] = op
            self.readers[k] = []
        d.discard(op)
        op.deps = d
        op.is_bar = False
        op.seq = len(self.order)
        self.ops[eng].append(op)
        self.order.append(op)
        return op

    def barrier(self):
        for e in ENGS:
            self.add(e, lambda g: None).is_bar = True

    def _list_schedule(self, seg):
        import heapq
        segset = set(seg)
        succ = {o: [] for o in seg}
        indeg = {}
        for o in seg:
            fk = _Fake(o.eng)
            o.emit(fk)
            o.busy = fk.busy
            o.lat = fk.lat if fk.lat is not None else fk.busy + 0.06
            k = 0
            for d in o.deps:
                if d in segset:
                    succ[d].append(o)
                    k += 1
            indeg[o] = k
        bl = {}
        for o in reversed(seg):
            m = 0.0
            for s_ in succ[o]:
                if bl[s_] > m:
                    m = bl[s_]
            bl[o] = o.lat + m
        free = {e: 0.0 for e in ENGS}
        avail = {e: [] for e in ENGS}
        future = {e: [] for e in ENGS}
        rtime = {o: 0.0 for o in seg}
        for o in seg:
            if indeg[o] == 0:
                heapq.heappush(future[o.eng], (0.0, o.seq, o))
        out = []
        n = len(seg)
        while len(out) < n:
            best_e, best_t = None, None
            for e in ENGS:
                fu, av = future[e], avail[e]
                while fu and fu[0][0] <= free[e]:
                    _, sq, o = heapq.heappop(fu)
                    heapq.heappush(av, (-bl[o], sq, o))
                if av:
                    t = free[e]
                elif fu:
                    t = fu[0][0]
                else:
                    continue
                if best_t is None or t < best_t:
                    best_e, best_t = e, t
            e = best_e
            if not avail[e]:
                free[e] = best_t
                fu, av = future[e], avail[e]
                while fu and fu[0][0] <= free[e]:
                    _, sq, o = heapq.heappop(fu)
                    heapq.heappush(av, (-bl[o], sq, o))
            _, sq, o = heapq.heappop(avail[e])
            st = free[e]
            o.st = st
            free[e] = st + o.busy
            fin = st + o.lat
            out.append(o)
            for s_ in succ[o]:
                if fin > rtime[s_]:
                    rtime[s_] = fin
                indeg[s_] -= 1
                if indeg[s_] == 0:
                    heapq.heappush(future[s_.eng], (rtime[s_], s_.seq, s_))
        return out

    def schedule(self, reorder=True):
        segs, cur = [], []
        for o in self.order:
            if o.is_bar:
                if cur:
                    segs.append(("seg", cur))
                    cur = []
                if segs and segs[-1][0] == "bar":
                    segs[-1][1].append(o)
                else:
                    segs.append(("bar", [o]))
            else:
                cur.append(o)
        if cur:
            segs.append(("seg", cur))
        new = []
        last_seg = []
        for kind, lst in segs:
            if kind == "seg":
                lst2 = self._list_schedule(lst) if reorder else lst
                new += lst2
                last_seg = lst2
            else:
                deps = [o for o in last_seg if o.is_dma]
                for e in ENGS:
                    for o in reversed(last_seg):
                        if o.eng == e and not o.is_dma:
                            deps.append(o)
                            break
                for o in lst:
                    o.deps = set(deps)
                new += lst
        self.order = new
        self.ops = {e: [] for e in ENGS}
        self.dma_ops = {e: [] for e in ENGS}
        for o in new:
            o.idx = len(self.ops[o.eng])
            self.ops[o.eng].append(o)
            if o.is_dma:
                o.dma_i = len(self.dma_ops[o.eng])
                if o.dma_i >= self.n_dma_sems:
                    o.deps.add(self.dma_ops[o.eng][o.dma_i - self.n_dma_sems])
                self.dma_ops[o.eng].append(o)

    def resolve(self):
        known = {e: {f: -1 for f in ENGS} for e in ENGS}
        known_dma = {e: set() for e in ENGS}
        for op in self.order:
            e = op.eng
            kn = known[e]
            waits = []
            for d in sorted(op.deps, key=lambda o: -o.idx):
                if d.is_dma:
                    if d in known_dma[e]:
                        continue
                    known_dma[e].add(d)
                    waits.append(d)
                else:
                    if d.eng == "pe" and e == "pe":
                        continue
                    if kn[d.eng] >= d.idx:
                        continue
                    d.marked = True
                    waits.append(d)
                ck = d.clock
                for f in ENGS:
                    if ck[f] > kn[f]:
                        kn[f] = ck[f]
            op.waits = waits
            ck = dict(kn)
            if not op.is_dma:
                ck[e] = max(ck[e], op.idx)
            op.clock = ck
        for e in ENGS:
            c = 0
            for op in self.ops[e]:
                if op.marked:
                    c += 1
                    op.count = c

    def emit_all(self, block, esem, dsem):
        self.schedule(reorder=REORDER)
        self.resolve()
        n = self.n_dma_sems

        def run(e, engobj):
            for op in self.ops[e]:
                for d in op.waits:
                    if d.is_dma:
                        engobj.wait_ge(dsem[d.eng][d.dma_i % n], 16 * (d.dma_i // n + 1))
                    else:
                        engobj.wait_ge(esem[d.eng], d.count)
                ins = op.emit(engobj)
                if op.is_dma:
                    ins.then_inc(dsem[e][op.dma_i % n], 16)
                elif op.marked:
                    if ins is None:
                        ins = engobj.nop()
                    ins.then_inc(esem[e], 1)

        block.tensor(lambda t: run("pe", t))
        block.scalar(lambda t: run("act", t))
        block.vector(lambda t: run("dve", t))
        block.gpsimd(lambda t: run("pool", t))
        block.sync(lambda t: run("sp", t))


class Arena:
    def __init__(self, ap, ncols):
        self.ap = ap
        self.n = ncols
        self.off = 0

    def alloc(self, cols, dt=F32):
        nb = cols * (4 if dt in (F32, I32) else 2)
        n32 = ((nb + 31) // 32) * 8
        assert self.off + n32 <= self.n, ("arena overflow", self.off, n32, self.n)
        v = self.ap[:, self.off:self.off + n32]
        self.off += n32
        if dt != F32:
            v = v.bitcast(dt)
        return v[:, 0:cols]

    def reset(self):
        self.off = 0

    def alloc_at(self, off32, cols, dt=F32):
        save = self.off
        self.off = off32
        v = self.alloc(cols, dt)
        end = self.off
        self.off = save
        return v, end


def build_nc():
    nc = bass.Bass("TRN2", target_bir_lowering=False)

    def DI(name, shape, dt=F32):
        return nc.dram_tensor(name, shape, dt, kind="ExternalInput").ap()

    x_d = DI("x", [T, D])
    c_d = DI("cT", [128, 8])
    pos_d = DI("pos", [1, T], I32)
    wada_d = DI("w_ada", [D, 6 * D])
    bada_d = DI("b_ada", [1, 6 * D])
    gpre1_d = DI("gpre1", [128, 8])
    gpre2_d = DI("gpre2", [128, 8])
    gpost1_d = DI("gpost1", [1, D])
    gpost2_d = DI("gpost2", [1, D])
    qg_d = DI("qg", [128, 3])
    kvg_d = DI("kvg", [128, 2])
    og_d = DI("og", [128, 8])
    w1_d = DI("w1", [D, NC1])
    wq_d = DI("wq", [384, 1024])
    wkv_d = DI("wkv", [256, 1536])
    wout_d = DI("wout", [D, D])
    wg_d = DI("wg", [D, DFF])
    wu_d = DI("wu", [D, DFF])
    wd_d = DI("wd", [DFF, D])
    ident_d = DI("ident", [128, 128])
    tri_d = DI("tri", [128, 128])
    dt_d = DI("dtc", [128, 512])
    wqc_d = DI("wqc", [128, 1024])
    wkc_d = DI("wkc", [128, 256])
    dec_d = DI("decc", [128, 2])
    inv_d = DI("invc", [128, 3])
    ph_d = DI("phc", [128, 3])
    out_d = nc.dram_tensor("out", [T, D], F32, kind="ExternalOutput").ap()
    yret_d = nc.dram_tensor("yret_scr", [4, 128, T], BF16).ap()

    S = Sched(n_dma_sems=12)
    A = S.add

    with contextlib.ExitStack() as ctx:
        def sbt(name, cols, dt=F32, parts=128):
            return ctx.enter_context(nc.sbuf_tensor(name, [parts, cols], dt))

        identb = sbt("identb", 128, BF16)
        trib = sbt("trib", 128, BF16)
        onesb = sbt("onesb", 128, BF16)
        onesf = sbt("onesf", 128)
        epst = sbt("epst", 1)
        DECc = sbt("DECc", 2)
        INVc = sbt("INVc", 3)
        PHc = sbt("PHc", 3)
        modc = sbt("modc", 32)
        qg = sbt("qg_sb", 3)
        kvg = sbt("kvg_sb", 2)
        a1 = sbt("a1", 8)
        a2 = sbt("a2", 8)
        G1b = sbt("G1b", D)
        G2b = sbt("G2b", D)
        ARN = 50000
        arena_t = sbt("arena", ARN)
        AR = Arena(arena_t, ARN)
        P = [ctx.enter_context(nc.psum_tensor(f"bank{i}", [128, 512], F32)) for i in range(8)]
        Pb = [p[:, :].bitcast(BF16) for p in P]

        esem = {e: ctx.enter_context(nc.semaphore("es_" + e)) for e in ENGS}
        dsem = {e: [ctx.enter_context(nc.semaphore(f"ds_{e}{i}")) for i in range(12)] for e in ("sp", "pool")}

        def dma(q, out, in_, r=(), w=()):
            return A(q, lambda g: g.dma_start(out=out, in_=in_), reads=r, writes=w, dma=True)

        def tap(name, ap, keys):
            if name not in TAPS:
                return
            shp = list(ap.shape)
            dd = nc.dram_tensor("dbg_" + name, shp, ap.dtype, kind="ExternalOutput").ap()
            dma("sp", dd, ap, r=keys)

        def rsqrt_ops(dst, src, scale, rk, wk):
            A("act", lambda g: g.activation(out=dst, in_=src, func=AF.Sqrt, scale=scale, bias=epst[0:dst.shape[0], :]),
              reads=list(rk) + ["epst"], writes=[wk])
            A("dve", lambda g: g.reciprocal(out=dst, in_=dst), reads=[wk], writes=[wk])

        _phase = [0]
        try:
            dma("pool", identb[:, :], ident_d, w=["identb"])
            dma("pool", trib[:, :], tri_d, w=["trib"])
            A("pool", lambda g: g.memset(onesb[:, :], 1.0), writes=["onesb"])
            A("pool", lambda g: g.memset(onesf[:, :], 1.0), writes=["onesf"])
            A("pool", lambda g: g.memset(epst[:, :], EPS), writes=["epst"])
            for t_, d_, k_ in ((DECc, dec_d, "DECc"), (INVc, inv_d, "INVc"), (PHc, ph_d, "PHc")):
                dma("sp", t_[:, :], d_, w=[k_])
            cT = AR.alloc(8)
            gp1 = AR.alloc(8)
            gp2 = AR.alloc(8)
            scb = AR.alloc(8, BF16)
            gpo1 = AR.alloc(D)
            gpo2 = AR.alloc(D)
            bada = AR.alloc(6 * D)
            modrow = AR.alloc(6 * D)
            grow1 = AR.alloc(D)
            grow2 = AR.alloc(D)
            wa = [AR.alloc(8 * 1024, BF16), AR.alloc(8 * 1024, BF16)]
            dma("sp", cT, c_d, w=["cT"])
            dma("sp", qg[:, :], qg_d, w=["qg"])
            dma("sp", kvg[:, :], kvg_d, w=["kvg"])
            dma("sp", gp1, gpre1_d, w=["gp1"])
            dma("sp", gp2, gpre2_d, w=["gp2"])
            dma("sp", gpo1[0:1, :], gpost1_d, w=["gpo1"])
            dma("sp", gpo2[0:1, :], gpost2_d, w=["gpo2"])
            dma("sp", bada[0:1, :], bada_d, w=["bada"])
            A("act", lambda g: g.activation(out=scb, in_=cT, func=AF.Silu), reads=["cT"], writes=["scb"])
            for gd in range(6):
                wb = wa[gd % 2]
                dma("pool", wb.rearrange("p (c n) -> p c n", c=8),
                    wada_d[:, gd * 1024:(gd + 1) * 1024].rearrange("(c p) n -> p c n", p=128), w=[f"wa{gd % 2}"])
                for g2_ in range(2):
                    gi = gd * 2 + g2_

                    def mm_ada(g, gi=gi, wb=wb, g2_=g2_):
                        for k in range(8):
                            r = g.matmul(P[gi % 2][0:1, :], lhsT=scb[:, k:k + 1], rhs=wb[:, k * 1024 + g2_ * 512:k * 1024 + (g2_ + 1) * 512],
                                         start=(k == 0), stop=(k == 7))
                        return r
                    A("pe", mm_ada, reads=["scb", f"wa{gd % 2}"], writes=[f"P{gi % 2}"])
                    A("dve", lambda g, gi=gi: g.tensor_tensor(out=modrow[0:1, gi * 512:(gi + 1) * 512], in0=P[gi % 2][0:1, :],
                                                             in1=bada[0:1, gi * 512:(gi + 1) * 512], op=ALU.add),
                      reads=[f"P{gi % 2}", "bada"], writes=["modrow"])
            col_offs = [0 * D, 1 * D, 3 * D, 4 * D]

            def mm_cols(g):
                for vi, off in enumerate(col_offs):
                    for c in range(8):
                        r = g.matmul(P[2][:, vi * 8 + c:vi * 8 + c + 1], lhsT=modrow[0:1, off + c * 128:off + (c + 1) * 128],
                                     rhs=onesf[0:1, 0:1], start=True, stop=True)
                return r
            A("pe", mm_cols, reads=["modrow", "onesf"], writes=["P2"])
            A("dve", lambda g: g.tensor_copy(out=modc[:, :], in_=P[2][:, 0:32]), reads=["P2"], writes=["modc"])
            A("dve", lambda g: g.scalar_tensor_tensor(out=a1[:, :], in0=modc[:, 8:16], scalar=1.0, in1=gp1, op0=ALU.add, op1=ALU.mult),
              reads=["modc", "gp1"], writes=["a1"])
            A("dve", lambda g: g.scalar_tensor_tensor(out=a2[:, :], in0=modc[:, 24:32], scalar=1.0, in1=gp2, op0=ALU.add, op1=ALU.mult),
              reads=["modc", "gp2"], writes=["a2"])
            sh1 = modc[:, 0:8]
            sh2 = modc[:, 16:24]
            A("dve", lambda g: g.tensor_tensor(out=grow1[0:1, :], in0=modrow[0:1, 2 * D:3 * D], in1=gpo1[0:1, :], op=ALU.mult),
              reads=["modrow", "gpo1"], writes=["grow1"])
            A("dve", lambda g: g.tensor_tensor(out=grow2[0:1, :], in0=modrow[0:1, 5 * D:6 * D], in1=gpo2[0:1, :], op=ALU.mult),
              reads=["modrow", "gpo2"], writes=["grow2"])
            for gi, (grow, Gb, gk) in enumerate(((grow1, G1b, "G1b"), (grow2, G2b, "G2b"))):
                for hf in range(2):
                    bk = 3 + hf
                    A("pe", lambda g, grow=grow, hf=hf, bk=bk: g.matmul(P[bk][:, :], lhsT=onesf[0:1, 0:128],
                                                                        rhs=grow[0:1, hf * 512:(hf + 1) * 512], start=True, stop=True),
                      reads=[f"grow{gi + 1}", "onesf"], writes=[f"P{bk}"])
                    A("act", lambda g, Gb=Gb, hf=hf, bk=bk: g.activation(out=Gb[:, hf * 512:(hf + 1) * 512], in_=P[bk][:, :], func=AF.Copy),
                      reads=[f"P{bk}"], writes=[gk])
            S.barrier()
            _phase[0] += 1
            if _phase[0] > STOP:
                raise _Stop()
            AR.reset()

            cqnT = AR.alloc(3 * T, BF16)
            ckvnT = AR.alloc(2 * T, BF16)
            TABm = AR.alloc(T)
            kpeT = TABm[64:96, 0:2048].bitcast(BF16)
            P12 = AR.off
            DTc = AR.alloc(512)
            WQc = AR.alloc(1024)
            WKc = AR.alloc(256)
            dma("sp", DTc, dt_d, w=["DTc"])
            dma("sp", WQc, wqc_d, w=["WQc"])
            dma("sp", WKc, wkc_d, w=["WKc"])
            W1 = AR.alloc(8 * NC1, BF16)
            xt = [AR.alloc(D), AR.alloc(D)]
            xn = [AR.alloc(D, BF16), AR.alloc(D, BF16)]
            junk = AR.alloc(D, BF16)
            ssq = [AR.alloc(1), AR.alloc(1)]
            hT = AR.alloc(8 * 512, BF16)
            cqraw = AR.alloc(3 * 512)
            ckvraw = AR.alloc(2 * 512)
            sq = AR.alloc(3 * 512, BF16)
            sq2 = AR.alloc(2 * 512, BF16)
            Rq = AR.alloc(512)
            Rkv = Rq
            posi = AR.alloc(512, I32)
            posf = AR.alloc(512)
            ang = AR.alloc(512)
            ni = posi
            nf = AR.alloc(512)
            msk = nf
            Cr = AR.alloc(512)
            Sr = AR.alloc(512)
            t1 = [AR.alloc(512), AR.alloc(512)]
            t2 = [AR.alloc(512), AR.alloc(512)]
            rqT = AR.alloc(2 * 512, BF16)
            rkT = AR.alloc(2 * 512, BF16)
            qwT = AR.alloc(2 * 512, BF16)
            rqm = AR.alloc(2 * 512, BF16)
            qwm = AR.alloc(2 * 512, BF16)
            A("pool", lambda g: g.memset(rqm[64:128, :], 0.0), writes=["rqm"])
            A("pool", lambda g: g.memset(qwm[64:128, :], 0.0), writes=["qwm"])
            vtok = AR.alloc(4 * 512, BF16)
            sg = AR.alloc(4 * 512, BF16)
            kwtok = AR.alloc(256, BF16)
            scTm = AR.alloc(512, BF16)
            osb = AR.alloc(512)
            ynorm = AR.alloc(512)
            osq = ynorm
            ytok = AR.alloc(512, BF16)
            ysT = AR.alloc(4 * 512, BF16)
            Sf = AR.alloc(256)
            Sbf = AR.alloc(256, BF16)
            st = {k: AR.alloc(4) for k in ("osum", "osqs", "mean", "msq", "var", "rgn")}

            for hf in range(2):
                dma("pool", W1.rearrange("p (c n) -> p c n", c=8)[:, hf * 4:(hf + 1) * 4, :],
                    w1_d.rearrange("(c p) n -> p c n", p=128)[:, hf * 4:(hf + 1) * 4, :], w=[f"W1{hf}"])
            A("pool", lambda g: g.memset(Sf, 0.0), writes=["Sf"])
            A("pool", lambda g: g.memset(Sbf, 0.0), writes=["Sbf"])

            def ck(n):
                if SUB == n:
                    raise _Stop()
            ck(0)

            def w1s(c, off, n):
                return W1[:, c * NC1 + off:c * NC1 + off + n]

            def table(dst, dk, col, sbi):
                A("dve", lambda g: g.tensor_scalar(out=ang, in0=posf, scalar1=INVc[:, col:col + 1], scalar2=PHc[:, col:col + 1],
                                                   op0=ALU.mult, op1=ALU.add), reads=["posf", "INVc", "PHc"], writes=["ang"])
                A("dve", lambda g: g.tensor_scalar(out=ni, in0=ang, scalar1=float(1.0 / (2 * PI)), scalar2=None, op0=ALU.mult),
                  reads=["ang"], writes=["ibuf"])
                A("dve", lambda g: g.tensor_copy(out=nf, in_=ni), reads=["ibuf"], writes=["nf"])
                A("dve", lambda g: g.scalar_tensor_tensor(out=ang, in0=nf, scalar=-2 * PI, in1=ang, op0=ALU.mult, op1=ALU.add),
                  reads=["nf", "ang"], writes=["ang"])
                A("dve", lambda g: g.tensor_single_scalar(out=msk, in_=ang, scalar=PI, op=ALU.is_gt), reads=["ang", "nf"], writes=["nf"])
                A("dve", lambda g: g.scalar_tensor_tensor(out=ang, in0=msk, scalar=-2 * PI, in1=ang, op0=ALU.mult, op1=ALU.add),
                  reads=["nf", "ang"], writes=["ang"])
                A("dve", lambda g: g.tensor_scalar(out=ang, in0=ang, scalar1=-3.14159, scalar2=3.14159, op0=ALU.max, op1=ALU.min),
                  reads=["ang"], writes=["ang"])
                np_ = dst.shape[0]
                A("act", lambda g: g.activation(out=dst, in_=ang[0:np_, :], func=AF.Sin), reads=["ang"], writes=[dk])

            mtiles = [(0, 128, "cq", 0), (128, 128, "cq", 1), (256, 128, "cq", 2), (384, 128, "ckv", 0), (512, 128, "ckv", 1),
                      (640, 64, "kpe", 0)]
            o_ = 704
            for nm in ("rq", "rk"):
                for i in range(2):
                    mtiles.append((o_, 128, nm + "n", i))
                    mtiles.append((o_ + 128, 128, nm + "s", i))
                    o_ += 256
            RV = 1728
            RG = 2240

            for sbi in range(NSB):
                sc0 = sbi * 512
                dma("sp", posi, bass.AP(pos_d.tensor, sc0, [[0, 128], [1, 512]]), w=["ibuf"])
                A("dve", lambda g: g.tensor_copy(out=posf, in_=posi), reads=["ibuf"], writes=["posf"])
                table(TABm[0:64, sc0:sc0 + 512], "TABm", 0, sbi)
                table(Cr, "Cr", 1, sbi)
                table(Sr, "Sr", 2, sbi)
                ck(1)
                for j in range(4):
                    tb = sbi * 4 + j
                    b2 = tb % 2
                    dma("sp", xt[b2], x_d[tb * 128:(tb + 1) * 128, :], w=[f"xt{b2}"])
                    A("act", lambda g, b2=b2: g.activation(out=junk, in_=xt[b2], func=AF.Square, accum_out=ssq[b2]),
                      reads=[f"xt{b2}"], writes=["junk", f"ssq{b2}"])
                    rsqrt_ops(ssq[b2], ssq[b2], 1.0 / D, [f"ssq{b2}"], f"ssq{b2}")
                    A("act", lambda g, b2=b2: g.activation(out=xn[b2], in_=xt[b2], func=AF.Copy, scale=ssq[b2]),
                      reads=[f"xt{b2}", f"ssq{b2}"], writes=[f"xn{b2}"])

                    def tr8(g, b2=b2):
                        for c in range(8):
                            r = g.transpose(out=Pb[b2][:, c * 128:(c + 1) * 128], in_=xn[b2][:, c * 128:(c + 1) * 128], identity=identb[:, :])
                        return r
                    A("pe", tr8, reads=[f"xn{b2}", "identb"], writes=[f"P{b2}"])
                    for c in range(8):
                        dst = hT[:, c * 512 + j * 128:c * 512 + (j + 1) * 128]
                        if b2 == 0:
                            A("dve", lambda g, c=c, dst=dst, b2=b2: g.tensor_scalar(out=dst, in0=Pb[b2][:, c * 128:(c + 1) * 128],
                                                                                   scalar1=a1[:, c:c + 1], scalar2=sh1[:, c:c + 1],
                                                                                   op0=ALU.mult, op1=ALU.add),
                              reads=[f"P{b2}", "a1", "modc"], writes=["hT"])
                        else:
                            A("act", lambda g, c=c, dst=dst, b2=b2: g.activation(out=dst, in_=Pb[b2][:, c * 128:(c + 1) * 128], func=AF.Identity,
                                                                                scale=a1[:, c:c + 1], bias=sh1[:, c:c + 1]),
                              reads=[f"P{b2}", "a1", "modc"], writes=["hT"])
                ck(2)
                for mi, (off, M, kind, i) in enumerate(mtiles):
                    bk = 2 + mi % 2
                    pk = f"P{bk}"

                    def mmz(g, off=off, M=M, bk=bk):
                        for c in range(8):
                            r = g.matmul(P[bk][0:M, :], lhsT=w1s(c, off, M), rhs=hT[:, c * 512:(c + 1) * 512], start=(c == 0), stop=(c == 7))
                        return r
                    A("pe", mmz, reads=["W10", "W11", "hT"], writes=[pk])
                    if kind in ("cq", "ckv"):
                        raw, sqt, nt, Rt, bank, scl, dstT, rk_ = ((cqraw, sq, 3, Rq, 4, 1.0 / 384, cqnT, "Rq") if kind == "cq"
                                                                  else (ckvraw, sq2, 2, Rkv, 5, 1.0 / 256, ckvnT, "Rq"))
                        gcol = qg if kind == "cq" else kvg
                        A("act", lambda g, raw=raw, i=i, bk=bk, gcol=gcol: g.activation(out=raw[:, i * 512:(i + 1) * 512], in_=P[bk][:, :], func=AF.Copy,
                                                                                      scale=gcol[:, i:i + 1]),
                          reads=[pk, "qg", "kvg"], writes=[f"{kind}raw{i}"])
                        A("act", lambda g, sqt=sqt, i=i, bk=bk: g.activation(out=sqt[:, i * 512:(i + 1) * 512], in_=P[bk][:, :], func=AF.Square),
                          reads=[pk], writes=[f"{kind}sq{i}"])
                        if i == nt - 1:
                            def mmst(g, sqt=sqt, nt=nt, bank=bank):
                                for q in range(nt):
                                    r = g.matmul(P[bank][:, :], lhsT=onesb[:, :], rhs=sqt[:, q * 512:(q + 1) * 512], start=(q == 0), stop=(q == nt - 1))
                                return r
                            A("pe", mmst, reads=[f"{kind}sq{q}" for q in range(nt)] + ["onesb"], writes=[f"P{bank}"])
                            rsqrt_ops(Rt, P[bank][:, :], scl, [f"P{bank}"], rk_)
                            for q in range(nt):
                                A("pool", lambda g, raw=raw, Rt=Rt, q=q, dstT=dstT: g.tensor_tensor(
                                    out=dstT[:, q * T + sc0:q * T + sc0 + 512], in0=raw[:, q * 512:(q + 1) * 512], in1=Rt, op=ALU.mult),
                                  reads=[f"{kind}raw{q}", rk_], writes=[f"{kind}nT"])
                    elif kind == "kpe":
                        A("dve", lambda g, bk=bk: g.tensor_tensor(out=t1[0][0:32, :], in0=P[bk][0:32, :], in1=TABm[0:32, sc0:sc0 + 512], op=ALU.mult),
                          reads=[pk, "TABm"], writes=["t1_0"])
                        A("dve", lambda g, bk=bk: g.tensor_tensor(out=t2[0][0:32, :], in0=P[bk][32:64, :], in1=TABm[32:64, sc0:sc0 + 512], op=ALU.mult),
                          reads=[pk, "TABm"], writes=["t2_0"])
                        A("dve", lambda g: g.tensor_tensor(out=kpeT[:, sc0:sc0 + 512], in0=t1[0][0:32, :], in1=t2[0][0:32, :], op=ALU.add),
                          reads=["t1_0", "t2_0"], writes=["kpeT"])
                    else:
                        nm = kind[:2]
                        if kind[2] == "n":
                            A("dve", lambda g, bk=bk, i=i: g.tensor_tensor(out=t1[i], in0=P[bk][:, :], in1=Cr, op=ALU.mult),
                              reads=[pk, "Cr"], writes=[f"t1_{i}"])
                        else:
                            A("dve", lambda g, bk=bk, i=i: g.tensor_tensor(out=t2[i], in0=P[bk][:, :], in1=Sr, op=ALU.mult),
                              reads=[pk, "Sr"], writes=[f"t2_{i}"])
                            dstq = rqT if nm == "rq" else rkT
                            A("pool", lambda g, i=i, dstq=dstq: g.tensor_tensor(out=dstq[:, i * 512:(i + 1) * 512], in0=t1[i], in1=t2[i], op=ALU.add),
                              reads=[f"t1_{i}", f"t2_{i}"], writes=[nm + "T"])
                            if nm == "rq":
                                A("pool", lambda g, i=i: g.tensor_tensor(out=rqm[0:64, i * 512:(i + 1) * 512], in0=t1[i][0:64, :], in1=t2[i][0:64, :],
                                                                        op=ALU.add),
                                  reads=[f"t1_{i}", f"t2_{i}"], writes=["rqm"])
                                A("pool", lambda g, i=i: g.tensor_tensor(out=qwT[:, i * 512:(i + 1) * 512], in0=rqT[:, i * 512:(i + 1) * 512],
                                                                        in1=WQc[:, i * 512:(i + 1) * 512], op=ALU.mult),
                                  reads=["rqT", "WQc"], writes=["qwT"])
                                A("pool", lambda g, i=i: g.tensor_tensor(out=qwm[0:64, i * 512:(i + 1) * 512], in0=rqT[0:64, i * 512:(i + 1) * 512],
                                                                        in1=WQc[0:64, i * 512:(i + 1) * 512], op=ALU.mult),
                                  reads=["rqT", "WQc"], writes=["qwm"])
                ck(3)
                for j in range(4):
                    for which, off, bank in (("v", RV, 4), ("g", RG, 5)):
                        def mmt(g, j=j, off=off, bank=bank):
                            for c in range(8):
                                r = g.matmul(P[bank][:, :], lhsT=hT[:, c * 512 + j * 128:c * 512 + (j + 1) * 128], rhs=w1s(c, off, 512),
                                             start=(c == 0), stop=(c == 7))
                            return r
                        A("pe", mmt, reads=["W10", "W11", "hT"], writes=[f"P{bank}"])
                        if which == "v":
                            A("act", lambda g, j=j: g.activation(out=vtok[:, j * 512:(j + 1) * 512], in_=P[4][:, :], func=AF.Copy),
                              reads=["P4"], writes=["vtok"])
                        else:
                            A("act", lambda g, j=j: g.activation(out=sg[:, j * 512:(j + 1) * 512], in_=P[5][:, :], func=AF.Silu),
                              reads=["P5"], writes=["sg"])
                ck(4)
                for j in range(4):
                    jc = slice(j * 128, (j + 1) * 128)

                    def trk(g, j=j):
                        for i in range(2):
                            r = g.transpose(out=Pb[5][:, i * 128:(i + 1) * 128], in_=rkT[:, i * 512 + j * 128:i * 512 + (j + 1) * 128], identity=identb[:, :])
                        return r
                    A("pe", trk, reads=["rkT", "identb"], writes=["P5"])
                    A("dve", lambda g: g.tensor_tensor(out=kwtok, in0=Pb[5][:, 0:256], in1=WKc[:, :], op=ALU.mult),
                      reads=["P5", "WKc"], writes=["kwtok"])

                    ck(6)

                    def mmsc(g, j=j):
                        for h in range(4):
                            i, r0 = h // 2, 64 * (h % 2)
                            cs = slice(i * 512 + j * 128, i * 512 + (j + 1) * 128)
                            if r0 == 0:
                                r = g.matmul(P[6][:, h * 128:(h + 1) * 128], lhsT=rkT[:, cs], rhs=rqm[:, cs], start=True, stop=True)
                            else:
                                r = g.matmul(P[6][:, h * 128:(h + 1) * 128], lhsT=rkT[64:128, cs], rhs=rqT[64:128, cs], start=True, stop=True,
                                             tile_position=(64, 0))
                        return r
                    A("pe", mmsc, reads=["rkT", "rqT", "rqm"], writes=["P6"])
                    A("dve", lambda g: g.tensor_tensor(out=scTm, in0=P[6][:, :], in1=DTc[:, :], op=ALU.mult), reads=["P6", "DTc"], writes=["scTm"])

                    ck(7)

                    def mmo(g, j=j):
                        for h in range(4):
                            i, r0 = h // 2, 64 * (h % 2)
                            cs = slice(i * 512 + j * 128, i * 512 + (j + 1) * 128)
                            g.matmul(P[7][:, h * 128:(h + 1) * 128], lhsT=scTm[:, h * 128:(h + 1) * 128],
                                     rhs=vtok[:, j * 512 + h * 128:j * 512 + (h + 1) * 128], start=True, stop=False)
                            if r0 == 0:
                                r = g.matmul(P[7][:, h * 128:(h + 1) * 128], lhsT=qwm[:, cs], rhs=Sbf[:, i * 128:(i + 1) * 128], start=False, stop=True)
                            else:
                                r = g.matmul(P[7][:, h * 128:(h + 1) * 128], lhsT=qwT[64:128, cs], rhs=Sbf[64:128, i * 128:(i + 1) * 128],
                                             start=False, stop=True, tile_position=(64, 0))
                        return r
                    A("pe", mmo, reads=["scTm", "vtok", "qwT", "qwm", "Sbf"], writes=["P7"])
                    A("act", lambda g: g.activation(out=osb, in_=P[7][:, :], func=AF.Copy), reads=["P7"], writes=["osb"])

                    ck(8)

                    def mmu(g, j=j):
                        for h in range(4):
                            i, r0 = h // 2, 64 * (h % 2)
                            kw = dict(tile_position=(0, 64)) if r0 else {}
                            r = g.matmul(P[5][r0:r0 + 64, 256 + i * 128:256 + (i + 1) * 128], lhsT=kwtok[:, h * 64:(h + 1) * 64],
                                         rhs=vtok[:, j * 512 + h * 128:j * 512 + (h + 1) * 128], start=True, stop=True, **kw)
                        return r
                    A("pe", mmu, reads=["kwtok", "vtok"], writes=["P5"])
                    for i in range(2):
                        A("dve", lambda g, i=i: g.scalar_tensor_tensor(out=Sf[:, i * 128:(i + 1) * 128], in0=Sf[:, i * 128:(i + 1) * 128],
                                                                      scalar=DECc[:, i:i + 1], in1=P[5][:, 256 + i * 128:256 + (i + 1) * 128],
                                                                      op0=ALU.mult, op1=ALU.add),
                          reads=["P5", "DECc", "Sf"], writes=["Sf"])
                    A("pool", lambda g: g.tensor_copy(out=Sbf, in_=Sf), reads=["Sf"], writes=["Sbf"])
                    ck(9)
                    o3 = osb.rearrange("p (h v) -> p h v", h=4)
                    A("dve", lambda g, o3=o3: g.reduce_sum(out=st["osum"], in_=o3, axis=AX.X), reads=["osb"], writes=["osum"])
                    A("pool", lambda g: g.tensor_tensor(out=osq, in0=osb, in1=osb, op=ALU.mult), reads=["osb"], writes=["ynorm"])
                    A("dve", lambda g: g.reduce_sum(out=st["osqs"], in_=osq.rearrange("p (h v) -> p h v", h=4), axis=AX.X),
                      reads=["ynorm"], writes=["osqs"])
                    A("dve", lambda g: g.tensor_scalar(out=st["mean"], in0=st["osum"], scalar1=1.0 / 128, scalar2=None, op0=ALU.mult),
                      reads=["osum"], writes=["mean"])
                    A("dve", lambda g: g.tensor_tensor(out=st["msq"], in0=st["mean"], in1=st["mean"], op=ALU.mult), reads=["mean"], writes=["msq"])
                    A("dve", lambda g: g.scalar_tensor_tensor(out=st["var"], in0=st["osqs"], scalar=1.0 / 128, in1=st["msq"], op0=ALU.mult, op1=ALU.subtract),
                      reads=["osqs", "msq"], writes=["var"])
                    rsqrt_ops(st["rgn"], st["var"], 1.0, ["var"], "rgn")
                    for h in range(4):
                        A("dve", lambda g, h=h: g.tensor_scalar(out=ynorm[:, h * 128:(h + 1) * 128], in0=osb[:, h * 128:(h + 1) * 128],
                                                                scalar1=st["mean"][:, h:h + 1], scalar2=st["rgn"][:, h:h + 1],
                                                                op0=ALU.subtract, op1=ALU.mult),
                          reads=["osb", "mean", "rgn"], writes=["ynorm"])
                    A("pool", lambda g, j=j: g.tensor_tensor(out=ytok, in0=ynorm, in1=sg[:, j * 512:(j + 1) * 512], op=ALU.mult),
                      reads=["ynorm", "sg"], writes=["ytok"])

                    ck(10)

                    def try_(g):
                        for t in range(4):
                            r = g.transpose(out=Pb[4][:, t * 128:(t + 1) * 128], in_=ytok[:, t * 128:(t + 1) * 128], identity=identb[:, :])
                        return r
                    A("pe", try_, reads=["ytok", "identb"], writes=["P4"])
                    A("act", lambda g, j=j: g.activation(out=ysT.rearrange("p (t n) -> p t n", t=4)[:, :, j * 128:(j + 1) * 128],
                                                         in_=Pb[4][:, 0:512].rearrange("p (t n) -> p t n", t=4), func=AF.Copy),
                      reads=["P4"], writes=["ysT"])
                ck(5)
                dma("sp", yret_d[:, :, sc0:sc0 + 512].rearrange("t p n -> p t n"), ysT.rearrange("p (t n) -> p t n", t=4),
                    r=["ysT"], w=["yret_d"])
            tap("cqnT", cqnT, ["cqnT"])
            tap("ckvnT", ckvnT, ["ckvnT"])
            tap("kpeT", kpeT, ["kpeT"])
            tap("TABm", TABm, ["TABm"])
            tap("yret", yret_d, ["yret_d"])
            tap("hT", hT, ["hT"])
            tap("cqraw", cqraw, ["cqraw0", "cqraw1", "cqraw2"])
            tap("Rq", Rq, ["Rq"])
            tap("sq", sq, ["cqsq0", "cqsq1", "cqsq2"])
            tap("rqT", rqT, ["rqT"])
            tap("rkT", rkT, ["rkT"])
            tap("osb", osb, ["osb"])
            tap("ytok", ytok, ["ytok"])
            tap("Sf", Sf, ["Sf"])
            S.barrier()
            _phase[0] += 1
            if _phase[0] > STOP:
                raise _Stop()
            AR.off = P12

            ymlaT = AR.alloc(4 * T, BF16)
            P23 = AR.off
            Wq = AR.alloc(3 * 1024, BF16)
            Wkv = AR.alloc(2 * 1536, BF16)
            dma("pool", Wq.rearrange("p (c n) -> p c n", c=3), wq_d.rearrange("(c p) n -> p c n", p=128), w=["Wq"])
            dma("pool", Wkv.rearrange("p (c n) -> p c n", c=2), wkv_d.rearrange("(c p) n -> p c n", p=128), w=["Wkv"])
            KT = [AR.alloc(T, BF16), AR.alloc(T, BF16)]
            QT = [AR.alloc(T, BF16), AR.alloc(T, BF16)]
            Vg = [AR.alloc(32 * 128, BF16), AR.alloc(32 * 128, BF16)]
            PT = [AR.alloc(GS * 512, BF16) for _ in range(NPT)]
            NSLOT = 4 // GS
            it_i = 0
            u1 = AR.alloc(512)
            u2 = AR.alloc(512)
            rec = AR.alloc(512)
            for b in range(2):
                A("dve", lambda g, b=b: g.memset(KT[b][0:64, :], 0.0), writes=[f"KT{b}"])
                A("pool", lambda g, b=b: g.memset(QT[b][0:64, :], 0.0), writes=[f"QT{b}"])
                A("dve", lambda g, b=b: g.memset(Vg[b].rearrange("p (k v) -> p k v", v=128)[:, :, 64:128], 1.0), writes=[f"Vg{b}"])
            SCALE = float(96 ** -0.5)
            pt_i = 0
            for h in range(8):
                hb = h % 2
                kk, qk, vk = f"KT{hb}", f"QT{hb}", f"Vg{hb}"
                A("dve", lambda g, hb=hb: g.tensor_copy(out=KT[hb][0:32, :], in_=kpeT[:, :]), reads=["kpeT"], writes=[kk])
                for sbi in range(NSB):
                    sc0 = sbi * 512

                    def mmk(g, h=h, sc0=sc0):
                        for c in range(2):
                            r = g.matmul(P[6][:, :], lhsT=Wkv[:, c * 1536 + h * 128:c * 1536 + (h + 1) * 128], rhs=ckvnT[:, c * T + sc0:c * T + sc0 + 512],
                                         start=(c == 0), stop=(c == 1))
                        return r
                    A("pe", mmk, reads=["Wkv", "ckvnT"], writes=["P6"])
                    A("dve", lambda g, hb=hb, sc0=sc0: g.tensor_copy(out=KT[hb][64:128, sc0:sc0 + 512], in_=P[6][64:128, :]),
                      reads=["P6"], writes=[kk])

                    def mmq(g, h=h, sc0=sc0):
                        for c in range(3):
                            r = g.matmul(P[7][:, :], lhsT=Wq[:, c * 1024 + h * 128:c * 1024 + (h + 1) * 128], rhs=cqnT[:, c * T + sc0:c * T + sc0 + 512],
                                         start=(c == 0), stop=(c == 2))
                        return r
                    A("pe", mmq, reads=["Wq", "cqnT"], writes=["P7"])
                    A("dve", lambda g, sc0=sc0: g.tensor_tensor(out=u1[0:32, :], in0=P[7][0:32, :], in1=TABm[0:32, sc0:sc0 + 512], op=ALU.mult),
                      reads=["P7", "TABm"], writes=["u1"])
                    A("dve", lambda g, sc0=sc0: g.tensor_tensor(out=u2[0:32, :], in0=P[7][32:64, :], in1=TABm[32:64, sc0:sc0 + 512], op=ALU.mult),
                      reads=["P7", "TABm"], writes=["u2"])
                    A("pool", lambda g, hb=hb, sc0=sc0: g.tensor_tensor(out=QT[hb][0:32, sc0:sc0 + 512], in0=u1[0:32, :], in1=u2[0:32, :], op=ALU.add),
                      reads=["u1", "u2"], writes=[qk])
                    A("dve", lambda g, hb=hb, sc0=sc0: g.tensor_copy(out=QT[hb][64:128, sc0:sc0 + 512], in_=P[7][64:128, :]),
                      reads=["P7"], writes=[qk])
                for k8 in range(4):
                    def mmv(g, h=h, k8=k8):
                        for q in range(8):
                            kb = k8 * 8 + q
                            for c in range(2):
                                r = g.matmul(P[6][:, q * 64:(q + 1) * 64], lhsT=ckvnT[:, c * T + kb * 128:c * T + (kb + 1) * 128],
                                             rhs=Wkv[:, c * 1536 + 1024 + h * 64:c * 1536 + 1024 + (h + 1) * 64], start=(c == 0), stop=(c == 1))
                        return r
                    A("pe", mmv, reads=["Wkv", "ckvnT"], writes=["P6"])
                    A("dve", lambda g, hb=hb, k8=k8: g.tensor_copy(
                        out=Vg[hb].rearrange("p (k v) -> p k v", v=128)[:, k8 * 8:(k8 + 1) * 8, 0:64],
                        in_=P[6][:, :].rearrange("p (k v) -> p k v", v=64)), reads=["P6"], writes=[vk])
                for qi, qs in enumerate(QS_ORDER):
                    q0 = qs * 512
                    acc = 4 + qi % 2
                    ak = f"P{acc}"
                    nfull = 4 * qs
                    groups = [(kb, min(kb + GS, nfull)) for kb in range(0, nfull, GS)]
                    items = [("full", a, b) for a, b in groups] + [("diag", 4 * qs + d, d) for d in range(4)]
                    last_kb = 4 * qs + 3
                    for gi, it in enumerate(items):
                        slot = it_i % NSLOT
                        it_i += 1
                        sbank = GS * slot
                        sk = f"PS{slot}"
                        pt = PT[pt_i % NPT]
                        ptk = f"PT{pt_i % NPT}"
                        pt_i += 1
                        if it[0] == "full":
                            kbs = list(range(it[1], it[2]))

                            def mms(g, hb=hb, kbs=kbs, sbank=sbank, q0=q0):
                                for n_, kb in enumerate(kbs):
                                    r = g.matmul(P[sbank + n_][:, :], lhsT=KT[hb][:, kb * 128:(kb + 1) * 128], rhs=QT[hb][:, q0:q0 + 512],
                                                 start=True, stop=True)
                                return r
                            A("pe", mms, reads=[kk, qk], writes=[sk])
                            for n_ in range(len(kbs)):
                                A("act", lambda g, pt=pt, sbank=sbank, n_=n_: g.activation(out=pt[:, n_ * 512:(n_ + 1) * 512], in_=P[sbank + n_][:, :],
                                                                                            func=AF.Exp, scale=SCALE),
                                  reads=[sk], writes=[ptk])

                            def mmpv(g, hb=hb, kbs=kbs, pt=pt, acc=acc, last_kb=last_kb):
                                for n_, kb in enumerate(kbs):
                                    r = g.matmul(P[acc][:, :], lhsT=Vg[hb][:, kb * 128:(kb + 1) * 128], rhs=pt[:, n_ * 512:(n_ + 1) * 512],
                                                 start=(kb == 0), stop=(kb == last_kb))
                                return r
                            A("pe", mmpv, reads=[vk, ptk], writes=[ak])
                        else:
                            kb, d = it[1], it[2]
                            c0 = d * 128
                            A("pe", lambda g, hb=hb, kb=kb, c0=c0, sbank=sbank, q0=q0: g.matmul(
                                P[sbank][:, c0:512], lhsT=KT[hb][:, kb * 128:(kb + 1) * 128], rhs=QT[hb][:, q0 + c0:q0 + 512], start=True, stop=True),
                              reads=[kk, qk], writes=[sk])
                            A("act", lambda g, pt=pt, sbank=sbank, c0=c0: g.activation(out=pt[:, c0:512], in_=P[sbank][:, c0:512], func=AF.Exp, scale=SCALE),
                              reads=[sk], writes=[ptk])
                            A("pool", lambda g, pt=pt, c0=c0: g.tensor_tensor(out=pt[:, c0:c0 + 128], in0=pt[:, c0:c0 + 128], in1=trib[:, :], op=ALU.mult),
                              reads=[ptk, "trib"], writes=[ptk])
                            A("pe", lambda g, hb=hb, kb=kb, c0=c0, pt=pt, acc=acc, last_kb=last_kb: g.matmul(
                                P[acc][:, c0:512], lhsT=Vg[hb][:, kb * 128:(kb + 1) * 128], rhs=pt[:, c0:512], start=(kb == 0), stop=(kb == last_kb)),
                              reads=[vk, ptk], writes=[ak])
                    A("dve", lambda g, acc=acc: g.reciprocal(out=rec[0:64, :], in_=P[acc][64:128, :]), reads=[ak], writes=["rec"])
                    r0 = 64 * (h % 2)
                    A("dve", lambda g, acc=acc, r0=r0, h=h, q0=q0: g.tensor_tensor(
                        out=ymlaT[r0:r0 + 64, (h // 2) * T + q0:(h // 2) * T + q0 + 512], in0=P[acc][0:64, :], in1=rec[0:64, :], op=ALU.mult),
                      reads=[ak, "rec"], writes=["ymlaT"])
            S.barrier()
            _phase[0] += 1
            if _phase[0] > STOP:
                raise _Stop()

            AR.off = P23
            WU_OFF = ARN - (8 * DFF * 2) // 4
            Wg, wg_end = AR.alloc_at(0, 8 * DFF, BF16)
            Wu, _ = AR.alloc_at(WU_OFF, 8 * DFF, BF16)
            assert wg_end <= P12
            for hf in range(2):
                dma("pool", Wg.rearrange("p (c n) -> p c n", c=8)[:, hf * 4:(hf + 1) * 4, :],
                    wg_d.rearrange("(c p) n -> p c n", p=128)[:, hf * 4:(hf + 1) * 4, :], w=[f"Wg{hf}"])
                dma("pool", Wu.rearrange("p (c n) -> p c n", c=8)[:, hf * 4:(hf + 1) * 4, :],
                    wu_d.rearrange("(c p) n -> p c n", p=128)[:, hf * 4:(hf + 1) * 4, :], w=[f"Wu{hf}"])
            Wo = AR.alloc(8 * D, BF16)
            og = AR.alloc(8)
            yrTs = [AR.alloc(4 * 512, BF16), AR.alloc(4 * 512, BF16)]
            ysq = [AR.alloc(4 * 128, BF16), AR.alloc(4 * 128, BF16)]
            xt3 = [AR.alloc(D), AR.alloc(D)]
            x1 = [AR.alloc(D), AR.alloc(D)]
            mB = [AR.alloc(D)]
            mixs = [AR.alloc(D)]
            tt = [AR.alloc(D)]
            rm = [AR.alloc(1), AR.alloc(1)]
            r2 = [AR.alloc(1), AR.alloc(1)]
            hole = wg_end
            for lst in (mB, mixs, tt):
                v_, hole = AR.alloc_at(hole, D)
                lst.append(v_)
            assert hole <= P12
            dma("sp", og, og_d, w=["og"])
            for hf in range(2):
                dma("pool", Wo.rearrange("p (c n) -> p c n", c=8)[:, hf * 4:(hf + 1) * 4, :],
                    wout_d.rearrange("(c p) n -> p c n", p=128)[:, hf * 4:(hf + 1) * 4, :], w=[f"Wo{hf}"])
            for c in range(8):
                A("dve", lambda g, c=c: g.tensor_scalar(out=Wo[:, c * D:(c + 1) * D], in0=Wo[:, c * D:(c + 1) * D], scalar1=og[:, c:c + 1],
                                                         scalar2=None, op0=ALU.mult), reads=[f"Wo{c // 4}", "og"], writes=[f"Wo{c // 4}"])
            for tb in range(32):
                b2 = tb % 2
                tc0 = tb * 128
                sbi, j = tb // 4, tb % 4
                yb = yrTs[sbi % 2]
                ybk = f"yrT{sbi % 2}"
                pa = (0, 1) if b2 == 0 else (4, 5)
                pak = [f"P{pa[0]}", f"P{pa[1]}"]
                stb = 6 + b2
                if j == 0:
                    dma("sp", yb.rearrange("p (t n) -> p t n", t=4), yret_d[:, :, sbi * 512:(sbi + 1) * 512].rearrange("t p n -> p t n"),
                        r=["yret_d"], w=[ybk])
                dma("sp", xt3[b2], x_d[tc0:tc0 + 128, :], w=[f"xt3{b2}"])
                A("pool", lambda g, tc0=tc0, b2=b2: g.tensor_tensor(out=ysq[b2].rearrange("p (c n) -> p c n", c=4),
                                                                     in0=ymlaT.rearrange("p (c n) -> p c n", c=4)[:, :, tc0:tc0 + 128],
                                                                     in1=ymlaT.rearrange("p (c n) -> p c n", c=4)[:, :, tc0:tc0 + 128], op=ALU.mult),
                  reads=["ymlaT"], writes=[f"ysq{b2}"])

                def mmss(g, b2=b2, stb=stb):
                    for c in range(4):
                        r = g.matmul(P[stb][:, 0:1], lhsT=ysq[b2][:, c * 128:(c + 1) * 128], rhs=onesb[:, 0:1], start=(c == 0), stop=(c == 3))
                    return r
                A("pe", mmss, reads=[f"ysq{b2}", "onesb"], writes=[f"P{stb}"])
                rsqrt_ops(rm[b2], P[stb][:, 0:1], 1.0 / 512, [f"P{stb}"], f"rm{b2}")

                def mmA(g, tc0=tc0, pa=pa):
                    for hf in range(2):
                        for c in range(4):
                            r = g.matmul(P[pa[hf]][:, :], lhsT=ymlaT[:, c * T + tc0:c * T + tc0 + 128], rhs=Wo[:, c * D + hf * 512:c * D + (hf + 1) * 512],
                                         start=(c == 0), stop=(c == 3))
                    return r
                A("pe", mmA, reads=["ymlaT", "Wo0"], writes=pak)

                def mmB(g, yb=yb, j=j):
                    for hf in range(2):
                        for c in range(4):
                            r = g.matmul(P[2 + hf][:, :], lhsT=yb[:, c * 512 + j * 128:c * 512 + (j + 1) * 128],
                                         rhs=Wo[:, (4 + c) * D + hf * 512:(4 + c) * D + (hf + 1) * 512], start=(c == 0), stop=(c == 3))
                    return r
                A("pe", mmB, reads=[ybk, "Wo1"], writes=["PB"])
                for hf in range(2):
                    A("act", lambda g, hf=hf, b2=b2: g.activation(out=mB[b2][:, hf * 512:(hf + 1) * 512], in_=P[2 + hf][:, :], func=AF.Copy),
                      reads=["PB"], writes=[f"mB{b2}"])
                    A("dve", lambda g, hf=hf, b2=b2, pa=pa: g.scalar_tensor_tensor(out=mixs[b2][:, hf * 512:(hf + 1) * 512], in0=P[pa[hf]][:, :],
                                                                                scalar=rm[b2][:, 0:1], in1=mB[b2][:, hf * 512:(hf + 1) * 512],
                                                                                op0=ALU.mult, op1=ALU.add),
                      reads=[pak[hf], f"rm{b2}", f"mB{b2}"], writes=[f"mixs{b2}"])
                A("act", lambda g, b2=b2: g.activation(out=tt[b2].bitcast(BF16)[:, 0:D], in_=mixs[b2], func=AF.Square, accum_out=r2[b2]),
                  reads=[f"mixs{b2}"], writes=[f"tt{b2}", f"r2{b2}"])
                rsqrt_ops(r2[b2], r2[b2], 1.0 / D, [f"r2{b2}"], f"r2{b2}")
                A("dve", lambda g, b2=b2: g.scalar_tensor_tensor(out=tt[b2], in0=mixs[b2], scalar=r2[b2][:, 0:1], in1=G1b[:, :], op0=ALU.mult, op1=ALU.mult),
                  reads=[f"mixs{b2}", f"r2{b2}", "G1b"], writes=[f"tt{b2}"])
                A("pool", lambda g, b2=b2: g.tensor_tensor(out=x1[b2], in0=xt3[b2], in1=tt[b2], op=ALU.add), reads=[f"xt3{b2}", f"tt{b2}"], writes=[f"x1{b2}"])
                dma("sp", out_d[tc0:tc0 + 128, :], x1[b2], r=[f"x1{b2}"], w=[f"out{tb}"])
            assert AR.off <= WU_OFF, (AR.off, WU_OFF)
            S.barrier()
            _phase[0] += 1
            if _phase[0] > STOP:
                raise _Stop()

            Wd, wd_end = AR.alloc_at(wg_end, NJ * D, BF16)
            assert wd_end <= P23
            AR.off = P23
            xa = [AR.alloc(D), AR.alloc(D)]
            xb = [AR.alloc(D), AR.alloc(D)]
            xn4 = [AR.alloc(D, BF16), AR.alloc(D, BF16)]
            junk4 = AR.alloc(D, BF16)
            junk5 = junk4
            s4 = [AR.alloc(1), AR.alloc(1)]
            h2T = AR.alloc(8 * 512, BF16)
            h1T = AR.alloc(NJ * 512, BF16)
            sgt = [AR.alloc(512, BF16), AR.alloc(512, BF16)]
            t4 = [AR.alloc(D), AR.alloc(D)]
            r3 = [AR.alloc(1), AR.alloc(1)]
            assert AR.off <= WU_OFF, (AR.off, WU_OFF)
            for hf in range(2):
                dma("pool", Wd.rearrange("p (c n) -> p c n", c=NJ)[:, hf * 11:(hf + 1) * 11, :],
                    wd_d.rearrange("(c p) n -> p c n", p=128)[:, hf * 11:(hf + 1) * 11, :], w=[f"Wd{hf}"])
            fin = []
            for sbi in range(NSB):
                for j in range(4):
                    tb = sbi * 4 + j
                    b2 = tb % 2
                    xv = xa[b2]
                    dma("sp", xv, out_d[tb * 128:(tb + 1) * 128, :], r=[f"out{tb}"], w=[f"xa{b2}"])
                    A("act", lambda g, xv=xv, b2=b2: g.activation(out=xn4[b2], in_=xv, func=AF.Square, accum_out=s4[b2]),
                      reads=[f"xa{b2}"], writes=[f"xn4{b2}", f"s4{b2}"])
                    rsqrt_ops(s4[b2], s4[b2], 1.0 / D, [f"s4{b2}"], f"s4{b2}")
                    A("act", lambda g, xv=xv, b2=b2: g.activation(out=xn4[b2], in_=xv, func=AF.Copy, scale=s4[b2]),
                      reads=[f"xa{b2}", f"s4{b2}"], writes=[f"xn4{b2}"])

                    def tr8b(g, b2=b2):
                        for c in range(8):
                            r = g.transpose(out=Pb[b2][:, c * 128:(c + 1) * 128], in_=xn4[b2][:, c * 128:(c + 1) * 128], identity=identb[:, :])
                        return r
                    A("pe", tr8b, reads=[f"xn4{b2}", "identb"], writes=[f"P{b2}"])
                    for c in range(8):
                        dst = h2T[:, c * 512 + j * 128:c * 512 + (j + 1) * 128]
                        if b2 == 0:
                            A("dve", lambda g, c=c, dst=dst, b2=b2: g.tensor_scalar(out=dst, in0=Pb[b2][:, c * 128:(c + 1) * 128],
                                                                                   scalar1=a2[:, c:c + 1], scalar2=sh2[:, c:c + 1],
                                                                                   op0=ALU.mult, op1=ALU.add),
                              reads=[f"P{b2}", "a2", "modc"], writes=["h2T"])
                        else:
                            A("act", lambda g, c=c, dst=dst, b2=b2: g.activation(out=dst, in_=Pb[b2][:, c * 128:(c + 1) * 128], func=AF.Identity,
                                                                                scale=a2[:, c:c + 1], bias=sh2[:, c:c + 1]),
                              reads=[f"P{b2}", "a2", "modc"], writes=["h2T"])
                for jj in range(NJ):
                    gb = 2 + jj % 2
                    ub = 4 + jj % 2

                    def mmg(g, jj=jj, gb=gb, ub=ub):
                        for c in range(8):
                            g.matmul(P[gb][:, :], lhsT=Wg[:, c * DFF + jj * 128:c * DFF + (jj + 1) * 128], rhs=h2T[:, c * 512:(c + 1) * 512],
                                     start=(c == 0), stop=(c == 7))
                        for c in range(8):
                            r = g.matmul(P[ub][:, :], lhsT=Wu[:, c * DFF + jj * 128:c * DFF + (jj + 1) * 128], rhs=h2T[:, c * 512:(c + 1) * 512],
                                         start=(c == 0), stop=(c == 7))
                        return r
                    A("pe", mmg, reads=["Wg0", "Wg1", "Wu0", "Wu1", "h2T"], writes=[f"P{gb}", f"P{ub}"])
                    A("act", lambda g, jj=jj, gb=gb: g.activation(out=sgt[jj % 2], in_=P[gb][:, :], func=AF.Silu), reads=[f"P{gb}"], writes=[f"sgt{jj % 2}"])
                    A("dve", lambda g, jj=jj, ub=ub: g.tensor_tensor(out=h1T[:, jj * 512:(jj + 1) * 512], in0=P[ub][:, :], in1=sgt[jj % 2], op=ALU.mult),
                      reads=[f"P{ub}", f"sgt{jj % 2}"], writes=["h1T"])
                for j in range(4):
                    tb = sbi * 4 + j
                    b2 = tb % 2
                    xv = xb[b2]
                    tv = t4[b2]
                    dma("sp", xv, out_d[tb * 128:(tb + 1) * 128, :], r=[f"out{tb}"], w=[f"xb{b2}"])

                    fb = (6, 7) if j % 2 == 0 else (3, 5)
                    fk_ = [f"P{fb[0]}", f"P{fb[1]}"]

                    def mmd(g, j=j, fb=fb):
                        for hf in range(2):
                            for jj in range(NJ):
                                r = g.matmul(P[fb[hf]][:, :], lhsT=h1T[:, jj * 512 + j * 128:jj * 512 + (j + 1) * 128],
                                             rhs=Wd[:, jj * D + hf * 512:jj * D + (hf + 1) * 512], start=(jj == 0), stop=(jj == NJ - 1))
                        return r
                    A("pe", mmd, reads=["h1T", "Wd0", "Wd1"], writes=fk_)
                    A("act", lambda g, tv=tv, fb=fb: g.activation(out=tv[:, 0:512], in_=P[fb[0]][:, :], func=AF.Copy), reads=[fk_[0]], writes=[f"t4{b2}"])
                    A("dve", lambda g, tv=tv, fb=fb: g.tensor_copy(out=tv[:, 512:1024], in_=P[fb[1]][:, :]), reads=[fk_[1]], writes=[f"t4{b2}"])
                    A("act", lambda g, tv=tv, b2=b2: g.activation(out=junk5, in_=tv, func=AF.Square, accum_out=r3[b2]), reads=[f"t4{b2}"], writes=["junk4", f"r3{b2}"])
                    rsqrt_ops(r3[b2], r3[b2], 1.0 / D, [f"r3{b2}"], f"r3{b2}")
                    A("dve", lambda g, tv=tv, b2=b2: g.scalar_tensor_tensor(out=tv, in0=tv, scalar=r3[b2][:, 0:1], in1=G2b[:, :], op0=ALU.mult, op1=ALU.mult),
                      reads=[f"t4{b2}", f"r3{b2}", "G2b"], writes=[f"t4{b2}"])
                    A("pool", lambda g, xv=xv, tv=tv: g.tensor_tensor(out=xv, in0=xv, in1=tv, op=ALU.add), reads=[f"xb{b2}", f"t4{b2}"], writes=[f"xb{b2}"])
                    fin.append(dma("sp", out_d[tb * 128:(tb + 1) * 128, :], xv, r=[f"xb{b2}"], w=[f"out{tb}"]))
            A("sp", lambda g: None, deps=fin)

        except _Stop:
            pass
        with nc.Block() as block:
            S.emit_all(block, esem, dsem)
    return nc


def _consts():
    f = np.float32
    gam = 1.0 - 2.0 ** (-5.0 - np.arange(4, dtype=np.float64))
    idx = np.arange(128)
    ident = np.eye(128, dtype=f)
    tri = (idx[None, :] >= idx[:, None]).astype(f)
    dtc = np.zeros((128, 4, 128), np.float64)
    rel = idx[None, :] - idx[:, None]
    for h in range(4):
        dtc[:, h, :] = np.where(rel >= 0, gam[h] ** np.maximum(rel, 0), 0.0) * 0.125
    wqc = np.zeros((128, 2, 512), np.float64)
    wkc = np.zeros((128, 2, 128), np.float64)
    decc = np.zeros((128, 2), np.float64)
    for i in range(2):
        for r in range(128):
            h = 2 * i + r // 64
            wqc[r, i, :] = np.tile(gam[h] ** (idx + 1.0), 4)
            decc[r, i] = gam[h] ** 128
        for ft in range(128):
            h = 2 * i + ft // 64
            wkc[:, i, ft] = gam[h] ** (127.0 - idx) * 0.125
    inv_m = 10000.0 ** (-np.arange(16, dtype=np.float64) / 16.0)
    inv_r = 10000.0 ** (-np.arange(32, dtype=np.float64) / 32.0)
    invc = np.zeros((128, 3), np.float64)
    phc = np.zeros((128, 3), np.float64)
    for r in range(64):
        invc[r, 0] = inv_m[r % 16]
        phc[r, 0] = np.pi / 2 if r < 32 else (np.pi if r < 48 else 0.0)
    for r in range(128):
        invc[r, 1] = inv_r[r % 32]
        invc[r, 2] = inv_r[r % 32]
        phc[r, 1] = np.pi / 2
        phc[r, 2] = np.pi if (r % 64) < 32 else 0.0
    return dict(ident=ident, tri=tri, dtc=dtc.reshape(128, 512).astype(f), wqc=wqc.reshape(128, 1024).astype(f),
                wkc=wkc.reshape(128, 256).astype(f), decc=decc.astype(f), invc=invc.astype(f), phc=phc.astype(f))


def _colmajor(v, n):
    return np.ascontiguousarray(np.asarray(v, np.float32).reshape(n, 128).T)


def _prep_shared(inp):
    f = np.float32
    w_in = np.asarray(inp["w_in"], f)[0]
    cols = list(range(0, 640))
    cols += list(range(640, 672)) + [640 + k for k in list(range(16, 32)) + list(range(0, 16))]
    for base in (672, 928):
        for i in range(2):
            nat, sw = [], []
            for hh in (2 * i, 2 * i + 1):
                b = base + hh * 64
                nat += list(range(b, b + 64))
                sw += list(range(b + 32, b + 64)) + list(range(b, b + 32))
            cols += nat + sw
    cols += list(range(1184, 2208))
    w1 = np.ascontiguousarray(w_in[:, cols])
    assert w1.shape[1] == NC1
    wqb = np.asarray(inp["w_q_b"], f)[0]
    qc = []
    for h in range(8):
        b = h * 96
        qc += list(range(b + 64, b + 96)) + [b + 64 + k for k in list(range(16, 32)) + list(range(0, 16))] + list(range(b, b + 64))
    wq = np.ascontiguousarray(wqb[:, qc])
    wkvb = np.asarray(inp["w_kv_b"], f)[0]
    wkv = np.zeros((256, 1536), f)
    for h in range(8):
        wkv[:, h * 128 + 64:h * 128 + 128] = wkvb[:, h * 128:h * 128 + 64]
        wkv[:, 1024 + h * 64:1024 + (h + 1) * 64] = wkvb[:, h * 128 + 64:h * 128 + 128]
    sh = dict(
        w_ada=np.ascontiguousarray(np.asarray(inp["w_ada"], f)[0]),
        b_ada=np.ascontiguousarray(np.asarray(inp["b_ada"], f)[0][None, :]),
        gpre1=_colmajor(inp["pre_norm_mix"][0], 8), gpre2=_colmajor(inp["pre_norm_ffn"][0], 8),
        gpost1=np.ascontiguousarray(np.asarray(inp["post_norm_mix"], f)[0][None, :]),
        gpost2=np.ascontiguousarray(np.asarray(inp["post_norm_ffn"], f)[0][None, :]),
        qg=_colmajor(inp["q_a_norm"][0], 3), kvg=_colmajor(inp["kv_a_norm"][0], 2),
        og=_colmajor(np.concatenate([np.asarray(inp["mla_out_norm"], f)[0], np.asarray(inp["ret_gn_gain"], f)[0]]), 8),
        w1=w1, wq=wq, wkv=wkv,
        wout=np.ascontiguousarray(np.asarray(inp["w_out"], f)[0]),
        wg=np.ascontiguousarray(np.asarray(inp["w_gate"], f)[0]),
        wu=np.ascontiguousarray(np.asarray(inp["w_up"], f)[0]),
        wd=np.ascontiguousarray(np.asarray(inp["w_down"], f)[0]),
    )
    sh.update(_consts())
    return sh


def make_in_maps(inp, cores):
    sh = _prep_shared(inp)
    x = np.asarray(inp["x"], np.float32)
    c = np.asarray(inp["c"], np.float32)
    pos = np.asarray(inp["positions"], np.int32)
    maps = []
    for b in cores:
        m = dict(sh)
        m["x"] = np.ascontiguousarray(x[b])
        m["cT"] = _colmajor(c[b], 8)
        m["pos"] = np.ascontiguousarray(pos[b][None, :])
        maps.append(m)
    return maps


_NC = None


def kernel(**inputs):
    global _NC
    if _NC is None:
        _NC = build_nc()
    maps = make_in_maps(inputs, list(range(8)))
    res = run_bass_kernel_spmd(_NC, maps, core_ids=list(range(8)))
    return np.stack([np.asarray(r["out"], np.float32) for r in res.results], axis=0)
```

```python
import contextlib
import types
import numpy as np
import concourse.bass as bass
import concourse.mybir as mybir
from concourse.bass_utils import run_bass_kernel_spmd

F32 = mybir.dt.float32
BF16 = mybir.dt.bfloat16
I32 = mybir.dt.int32
AF = mybir.ActivationFunctionType
ALU = mybir.AluOpType
AX = mybir.AxisListType

ENGS = ("pe", "act", "dve", "pool", "sp")
STOP = 99
SUB = 99
HSEL = (0, 1, 2, 3)
TAPS = ()
REORDER = True
SEM_LAT = 0.3
QS_ORDER = (0, 7, 1, 6, 2, 5, 3, 4)
GS = 1
NPT = 6


class _Stop(Exception):
    pass

T = 4096
D = 1024
NSB = 8
DFF = 2816
NJ = 22
NC1 = 2752
EPS = 1e-6
PI = float(np.pi)


def _freeze(fn):
    if fn.__closure__ is None:
        return fn
    cells = []
    for c in fn.__closure__:
        try:
            cells.append(types.CellType(c.cell_contents))
        except ValueError:
            cells.append(c)
    return types.FunctionType(fn.__code__, fn.__globals__, fn.__name__, fn.__defaults__, tuple(cells))


class Op:
    __slots__ = ("eng", "idx", "emit", "deps", "is_dma", "dma_i", "marked", "count", "clock", "waits", "is_bar", "busy", "lat", "seq", "st")


def _nfree(ap):
    n = 1
    for d in ap.shape[1:]:
        n *= int(d)
    return n


class _Fake:
    def __init__(self, eng):
        self.eng = eng
        self.busy = 0.0
        self.lat = None

    def matmul(self, out, lhsT=None, rhs=None, **kw):
        n = max(_nfree(rhs), 64)
        f = 4.0 if rhs.dtype == F32 else 1.0
        self.busy += f * n / 2370.0 + (0.004 if n >= 512 else 0.06)
        return self

    def transpose(self, out=None, in_=None, identity=None, **kw):
        self.busy += 0.12
        return self

    def activation(self, out=None, in_=None, **kw):
        n = _nfree(in_)
        self.busy += (0.07 if n >= 512 else 0.25) + n / 1200.0
        return self

    def dma_start(self, out=None, in_=None, **kw):
        nb = _nfree(out) * int(out.shape[0]) * (4 if out.dtype in (F32, I32) else 2)
        self.busy += 0.15 if self.eng == "sp" else 1.2
        self.lat = 2.5 + nb / 150e3
        return self

    def _dve(self, out, **kw):
        n = _nfree(out)
        if self.eng == "pool":
            self.busy += 0.2 + n / 500.0
        else:
            self.busy += 0.12 + n / 900.0
        return self

    def tensor_tensor(self, out=None, **kw):
        return self._dve(out)

    def tensor_scalar(self, out=None, **kw):
        return self._dve(out)

    def tensor_copy(self, out=None, **kw):
        return self._dve(out)

    def scalar_tensor_tensor(self, out=None, **kw):
        return self._dve(out)

    def tensor_single_scalar(self, out=None, **kw):
        return self._dve(out)

    def reciprocal(self, out=None, **kw):
        self.busy += 0.1 + _nfree(out) / 150.0
        return self

    def memset(self, ap, *a, **kw):
        return self._dve(ap)

    def reduce_sum(self, out=None, in_=None, **kw):
        return self._dve(in_)

    def then_inc(self, *a, **kw):
        return self


class Sched:
    def __init__(self, n_dma_sems=12):
        self.ops = {e: [] for e in ENGS}
        self.order = []
        self.lastw = {}
        self.readers = {}
        self.n_dma_sems = n_dma_sems
        self.dma_ops = {e: [] for e in ENGS}
        self.dma_since_bar = []

    def add(self, eng, emit, reads=(), writes=(), dma=False, deps=()):
        op = Op()
        op.eng = eng
        op.emit = _freeze(emit)
        op.is_dma = dma
        op.marked = False
        op.count = 0
        op.idx = len(self.ops[eng])
        d = set(deps)
        for k in reads:
            w = self.lastw.get(k)
            if w is not None:
                d.add(w)
        for k in writes:
            w = self.lastw.get(k)
            if w is not None:
                d.add(w)
            for r in self.readers.get(k, ()):
                d.add(r)
        for k in reads:
            self.readers.setdefault(k, []).append(op)
        for k in writes:
            self.lastw[k] = op
            self.readers[k] = []
        d.discard(op)
        op.deps = d
        op.is_bar = False
        op.seq = len(self.order)
        self.ops[eng].append(op)
        self.order.append(op)
        return op

    def barrier(self):
        for e in ENGS:
            self.add(e, lambda g: None).is_bar = True

    def _list_schedule(self, seg):
        import heapq
        segset = set(seg)
        succ = {o: [] for o in seg}
        indeg = {}
        for o in seg:
            fk = _Fake(o.eng)
            o.emit(fk)
            o.busy = fk.busy
            o.lat = fk.lat if fk.lat is not None else fk.busy + SEM_LAT
            k = 0
            for d in o.deps:
                if d in segset:
                    succ[d].append(o)
                    k += 1
            indeg[o] = k
        bl = {}
        for o in reversed(seg):
            m = 0.0
            for s_ in succ[o]:
                if bl[s_] > m:
                    m = bl[s_]
            bl[o] = o.lat + m
        free = {e: 0.0 for e in ENGS}
        avail = {e: [] for e in ENGS}
        future = {e: [] for e in ENGS}
        rtime = {o: 0.0 for o in seg}
        for o in seg:
            if indeg[o] == 0:
                heapq.heappush(future[o.eng], (0.0, o.seq, o))
        out = []
        n = len(seg)
        while len(out) < n:
            best_e, best_t = None, None
            for e in ENGS:
                fu, av = future[e], avail[e]
                while fu and fu[0][0] <= free[e]:
                    _, sq, o = heapq.heappop(fu)
                    heapq.heappush(av, (-bl[o], sq, o))
                if av:
                    t = free[e]
                elif fu:
                    t = fu[0][0]
                else:
                    continue
                if best_t is None or t < best_t:
                    best_e, best_t = e, t
            e = best_e
            if not avail[e]:
                free[e] = best_t
                fu, av = future[e], avail[e]
                while fu and fu[0][0] <= free[e]:
                    _, sq, o = heapq.heappop(fu)
                    heapq.heappush(av, (-bl[o], sq, o))
            _, sq, o = heapq.heappop(avail[e])
            st = free[e]
            o.st = st
            free[e] = st + o.busy
            fin = st + o.lat
            out.append(o)
            for s_ in succ[o]:
                if fin > rtime[s_]:
                    rtime[s_] = fin
                indeg[s_] -= 1
                if indeg[s_] == 0:
                    heapq.heappush(future[s_.eng], (rtime[s_], s_.seq, s_))
        return out

    def schedule(self, reorder=True):
        segs, cur = [], []
        for o in self.order:
            if o.is_bar:
                if cur:
                    segs.append(("seg", cur))
                    cur = []
                if segs and segs[-1][0] == "bar":
                    segs[-1][1].append(o)
                else:
                    segs.append(("bar", [o]))
            else:
                cur.append(o)
        if cur:
            segs.append(("seg", cur))
        new = []
        last_seg = []
        for kind, lst in segs:
            if kind == "seg":
                lst2 = self._list_schedule(lst) if reorder else lst
                new += lst2
                last_seg = lst2
            else:
                deps = [o for o in last_seg if o.is_dma]
                for e in ENGS:
                    for o in reversed(last_seg):
                        if o.eng == e and not o.is_dma:
                            deps.append(o)
                            break
                for o in lst:
                    o.deps = set(deps)
                new += lst
        self.order = new
        self.ops = {e: [] for e in ENGS}
        self.dma_ops = {e: [] for e in ENGS}
        for o in new:
            o.idx = len(self.ops[o.eng])
            self.ops[o.eng].append(o)
            if o.is_dma:
                o.dma_i = len(self.dma_ops[o.eng])
                if o.dma_i >= self.n_dma_sems:
                    o.deps.add(self.dma_ops[o.eng][o.dma_i - self.n_dma_sems])
                self.dma_ops[o.eng].append(o)

    def resolve(self):
        known = {e: {f: -1 for f in ENGS} for e in ENGS}
        known_dma = {e: set() for e in ENGS}
        for op in self.order:
            e = op.eng
            kn = known[e]
            waits = []
            for d in sorted(op.deps, key=lambda o: -o.idx):
                if d.is_dma:
                    if d in known_dma[e]:
                        continue
                    known_dma[e].add(d)
                    waits.append(d)
                else:
                    if d.eng == "pe" and e == "pe":
                        continue
                    if kn[d.eng] >= d.idx:
                        continue
                    d.marked = True
                    waits.append(d)
                ck = d.clock
                for f in ENGS:
                    if ck[f] > kn[f]:
                        kn[f] = ck[f]
            op.waits = waits
            ck = dict(kn)
            if not op.is_dma:
                ck[e] = max(ck[e], op.idx)
            op.clock = ck
        for e in ENGS:
            c = 0
            for op in self.ops[e]:
                if op.marked:
                    c += 1
                    op.count = c

    def emit_all(self, block, esem, dsem):
        self.schedule(reorder=REORDER)
        self.resolve()
        n = self.n_dma_sems

        def run(e, engobj):
            for op in self.ops[e]:
                for d in op.waits:
                    if d.is_dma:
                        engobj.wait_ge(dsem[d.eng][d.dma_i % n], 16 * (d.dma_i // n + 1))
                    else:
                        engobj.wait_ge(esem[d.eng], d.count)
                ins = op.emit(engobj)
                if op.is_dma:
                    ins.then_inc(dsem[e][op.dma_i % n], 16)
                elif op.marked:
                    if ins is None:
                        ins = engobj.nop()
                    ins.then_inc(esem[e], 1)

        block.tensor(lambda t: run("pe", t))
        block.scalar(lambda t: run("act", t))
        block.vector(lambda t: run("dve", t))
        block.gpsimd(lambda t: run("pool", t))
        block.sync(lambda t: run("sp", t))


class Arena:
    def __init__(self, ap, ncols):
        self.ap = ap
        self.n = ncols
        self.off = 0

    def alloc(self, cols, dt=F32):
        nb = cols * (4 if dt in (F32, I32) else 2)
        n32 = ((nb + 31) // 32) * 8
        assert self.off + n32 <= self.n, ("arena overflow", self.off, n32, self.n)
        v = self.ap[:, self.off:self.off + n32]
        self.off += n32
        if dt != F32:
            v = v.bitcast(dt)
        return v[:, 0:cols]

    def reset(self):
        self.off = 0

    def alloc_at(self, off32, cols, dt=F32):
        save = self.off
        self.off = off32
        v = self.alloc(cols, dt)
        end = self.off
        self.off = save
        return v, end


def build_nc():
    nc = bass.Bass("TRN2", target_bir_lowering=False)

    def DI(name, shape, dt=F32):
        return nc.dram_tensor(name, shape, dt, kind="ExternalInput").ap()

    x_d = DI("x", [T, D])
    c_d = DI("cT", [128, 8])
    pos_d = DI("pos", [1, T], I32)
    wada_d = DI("w_ada", [D, 6 * D])
    bada_d = DI("b_ada", [1, 6 * D])
    gpre1_d = DI("gpre1", [128, 8])
    gpre2_d = DI("gpre2", [128, 8])
    gpost1_d = DI("gpost1", [1, D])
    gpost2_d = DI("gpost2", [1, D])
    qg_d = DI("qg", [128, 3])
    kvg_d = DI("kvg", [128, 2])
    og_d = DI("og", [128, 8])
    w1_d = DI("w1", [D, NC1])
    wq_d = DI("wq", [384, 1024])
    wkv_d = DI("wkv", [256, 1536])
    wout_d = DI("wout", [D, D])
    wg_d = DI("wg", [D, DFF])
    wu_d = DI("wu", [D, DFF])
    wd_d = DI("wd", [DFF, D])
    ident_d = DI("ident", [128, 128])
    tri_d = DI("tri", [128, 128])
    dt_d = DI("dtc", [128, 512])
    wqc_d = DI("wqc", [128, 1024])
    wkc_d = DI("wkc", [128, 256])
    dec_d = DI("decc", [128, 2])
    inv_d = DI("invc", [128, 3])
    ph_d = DI("phc", [128, 3])
    out_d = nc.dram_tensor("out", [T, D], F32, kind="ExternalOutput").ap()
    yret_d = nc.dram_tensor("yret_scr", [4, 128, T], BF16).ap()

    S = Sched(n_dma_sems=12)
    A = S.add

    with contextlib.ExitStack() as ctx:
        def sbt(name, cols, dt=F32, parts=128):
            return ctx.enter_context(nc.sbuf_tensor(name, [parts, cols], dt))

        identb = sbt("identb", 128, BF16)
        trib = sbt("trib", 128, BF16)
        onesb = sbt("onesb", 128, BF16)
        onesf = sbt("onesf", 128)
        epst = sbt("epst", 1)
        DECc = sbt("DECc", 2)
        INVc = sbt("INVc", 3)
        PHc = sbt("PHc", 3)
        modc = sbt("modc", 32)
        qg = sbt("qg_sb", 3)
        kvg = sbt("kvg_sb", 2)
        a1 = sbt("a1", 8)
        a2 = sbt("a2", 8)
        G1b = sbt("G1b", D)
        G2b = sbt("G2b", D)
        ARN = 50000
        arena_t = sbt("arena", ARN)
        AR = Arena(arena_t, ARN)
        P = [ctx.enter_context(nc.psum_tensor(f"bank{i}", [128, 512], F32)) for i in range(8)]
        Pb = [p[:, :].bitcast(BF16) for p in P]

        esem = {e: ctx.enter_context(nc.semaphore("es_" + e)) for e in ENGS}
        dsem = {e: [ctx.enter_context(nc.semaphore(f"ds_{e}{i}")) for i in range(12)] for e in ("sp", "pool")}

        def dma(q, out, in_, r=(), w=()):
            return A(q, lambda g: g.dma_start(out=out, in_=in_), reads=r, writes=w, dma=True)

        def tap(name, ap, keys):
            if name not in TAPS:
                return
            shp = list(ap.shape)
            dd = nc.dram_tensor("dbg_" + name, shp, ap.dtype, kind="ExternalOutput").ap()
            dma("sp", dd, ap, r=keys)

        def rsqrt_ops(dst, src, scale, rk, wk):
            A("act", lambda g: g.activation(out=dst, in_=src, func=AF.Sqrt, scale=scale, bias=epst[0:dst.shape[0], :]),
              reads=list(rk) + ["epst"], writes=[wk])
            A("dve", lambda g: g.reciprocal(out=dst, in_=dst), reads=[wk], writes=[wk])

        _phase = [0]
        try:
            dma("pool", identb[:, :], ident_d, w=["identb"])
            dma("pool", trib[:, :], tri_d, w=["trib"])
            A("pool", lambda g: g.memset(onesb[:, :], 1.0), writes=["onesb"])
            A("pool", lambda g: g.memset(onesf[:, :], 1.0), writes=["onesf"])
            A("pool", lambda g: g.memset(epst[:, :], EPS), writes=["epst"])
            for t_, d_, k_ in ((DECc, dec_d, "DECc"), (INVc, inv_d, "INVc"), (PHc, ph_d, "PHc")):
                dma("sp", t_[:, :], d_, w=[k_])
            cT = AR.alloc(8)
            gp1 = AR.alloc(8)
            gp2 = AR.alloc(8)
            scb = AR.alloc(8, BF16)
            gpo1 = AR.alloc(D)
            gpo2 = AR.alloc(D)
            bada = AR.alloc(6 * D)
            modrow = AR.alloc(6 * D)
            grow1 = AR.alloc(D)
            grow2 = AR.alloc(D)
            wa = [AR.alloc(8 * 1024, BF16), AR.alloc(8 * 1024, BF16)]
            dma("sp", cT, c_d, w=["cT"])
            dma("sp", qg[:, :], qg_d, w=["qg"])
            dma("sp", kvg[:, :], kvg_d, w=["kvg"])
            dma("sp", gp1, gpre1_d, w=["gp1"])
            dma("sp", gp2, gpre2_d, w=["gp2"])
            dma("sp", gpo1[0:1, :], gpost1_d, w=["gpo1"])
            dma("sp", gpo2[0:1, :], gpost2_d, w=["gpo2"])
            dma("sp", bada[0:1, :], bada_d, w=["bada"])
            A("act", lambda g: g.activation(out=scb, in_=cT, func=AF.Silu), reads=["cT"], writes=["scb"])
            for gd in range(6):
                wb = wa[gd % 2]
                dma("pool", wb.rearrange("p (c n) -> p c n", c=8),
                    wada_d[:, gd * 1024:(gd + 1) * 1024].rearrange("(c p) n -> p c n", p=128), w=[f"wa{gd % 2}"])
                for g2_ in range(2):
                    gi = gd * 2 + g2_

                    def mm_ada(g, gi=gi, wb=wb, g2_=g2_):
                        for k in range(8):
                            r = g.matmul(P[gi % 2][0:1, :], lhsT=scb[:, k:k + 1], rhs=wb[:, k * 1024 + g2_ * 512:k * 1024 + (g2_ + 1) * 512],
                                         start=(k == 0), stop=(k == 7))
                        return r
                    A("pe", mm_ada, reads=["scb", f"wa{gd % 2}"], writes=[f"P{gi % 2}"])
                    A("dve", lambda g, gi=gi: g.tensor_tensor(out=modrow[0:1, gi * 512:(gi + 1) * 512], in0=P[gi % 2][0:1, :],
                                                             in1=bada[0:1, gi * 512:(gi + 1) * 512], op=ALU.add),
                      reads=[f"P{gi % 2}", "bada"], writes=["modrow"])
            col_offs = [0 * D, 1 * D, 3 * D, 4 * D]

            def mm_cols(g):
                for vi, off in enumerate(col_offs):
                    for c in range(8):
                        r = g.matmul(P[2][:, vi * 8 + c:vi * 8 + c + 1], lhsT=modrow[0:1, off + c * 128:off + (c + 1) * 128],
                                     rhs=onesf[0:1, 0:1], start=True, stop=True)
                return r
            A("pe", mm_cols, reads=["modrow", "onesf"], writes=["P2"])
            A("dve", lambda g: g.tensor_copy(out=modc[:, :], in_=P[2][:, 0:32]), reads=["P2"], writes=["modc"])
            A("dve", lambda g: g.scalar_tensor_tensor(out=a1[:, :], in0=modc[:, 8:16], scalar=1.0, in1=gp1, op0=ALU.add, op1=ALU.mult),
              reads=["modc", "gp1"], writes=["a1"])
            A("dve", lambda g: g.scalar_tensor_tensor(out=a2[:, :], in0=modc[:, 24:32], scalar=1.0, in1=gp2, op0=ALU.add, op1=ALU.mult),
              reads=["modc", "gp2"], writes=["a2"])
            sh1 = modc[:, 0:8]
            sh2 = modc[:, 16:24]
            A("dve", lambda g: g.tensor_tensor(out=grow1[0:1, :], in0=modrow[0:1, 2 * D:3 * D], in1=gpo1[0:1, :], op=ALU.mult),
              reads=["modrow", "gpo1"], writes=["grow1"])
            A("dve", lambda g: g.tensor_tensor(out=grow2[0:1, :], in0=modrow[0:1, 5 * D:6 * D], in1=gpo2[0:1, :], op=ALU.mult),
              reads=["modrow", "gpo2"], writes=["grow2"])
            for gi, (grow, Gb, gk) in enumerate(((grow1, G1b, "G1b"), (grow2, G2b, "G2b"))):
                for hf in range(2):
                    bk = 3 + hf
                    A("pe", lambda g, grow=grow, hf=hf, bk=bk: g.matmul(P[bk][:, :], lhsT=onesf[0:1, 0:128],
                                                                        rhs=grow[0:1, hf * 512:(hf + 1) * 512], start=True, stop=True),
                      reads=[f"grow{gi + 1}", "onesf"], writes=[f"P{bk}"])
                    A("act", lambda g, Gb=Gb, hf=hf, bk=bk: g.activation(out=Gb[:, hf * 512:(hf + 1) * 512], in_=P[bk][:, :], func=AF.Copy),
                      reads=[f"P{bk}"], writes=[gk])
            S.barrier()
            _phase[0] += 1
            if _phase[0] > STOP:
                raise _Stop()
            AR.reset()

            cqnT = AR.alloc(3 * T, BF16)
            ckvnT = AR.alloc(2 * T, BF16)
            TABm = AR.alloc(T)
            kpeT = TABm[64:96, 0:2048].bitcast(BF16)
            P12 = AR.off
            DTc = AR.alloc(512)
            WQc = AR.alloc(1024)
            WKc = AR.alloc(256)
            dma("sp", DTc, dt_d, w=["DTc"])
            dma("sp", WQc, wqc_d, w=["WQc"])
            dma("sp", WKc, wkc_d, w=["WKc"])
            W1 = AR.alloc(8 * NC1, BF16)
            xt = [AR.alloc(D), AR.alloc(D)]
            xn = [AR.alloc(D, BF16), AR.alloc(D, BF16)]
            junk = AR.alloc(D, BF16)
            ssq = [AR.alloc(1), AR.alloc(1)]
            hT = AR.alloc(8 * 512, BF16)
            cqraw = AR.alloc(3 * 512)
            ckvraw = AR.alloc(2 * 512)
            sq = AR.alloc(3 * 512, BF16)
            sq2 = AR.alloc(2 * 512, BF16)
            Rq = AR.alloc(512)
            Rkv = Rq
            posi = AR.alloc(512, I32)
            posf = AR.alloc(512)
            ang = AR.alloc(512)
            ni = posi
            nf = AR.alloc(512)
            msk = nf
            Cr = AR.alloc(512)
            Sr = AR.alloc(512)
            t1 = [AR.alloc(512), AR.alloc(512)]
            t2 = [AR.alloc(512), AR.alloc(512)]
            rqT = AR.alloc(2 * 512, BF16)
            rkT = AR.alloc(2 * 512, BF16)
            qwT = AR.alloc(2 * 512, BF16)
            rqm = AR.alloc(2 * 512, BF16)
            qwm = AR.alloc(2 * 512, BF16)
            A("pool", lambda g: g.memset(rqm[64:128, :], 0.0), writes=["rqm"])
            A("pool", lambda g: g.memset(qwm[64:128, :], 0.0), writes=["qwm"])
            vtok = AR.alloc(4 * 512, BF16)
            sg = AR.alloc(4 * 512, BF16)
            kwtok = AR.alloc(256, BF16)
            scTm = AR.alloc(512, BF16)
            osb = AR.alloc(512)
            ynorm = AR.alloc(512)
            osq = ynorm
            ytok = AR.alloc(512, BF16)
            ysT = AR.alloc(4 * 512, BF16)
            Sf = AR.alloc(256)
            Sbf = AR.alloc(256, BF16)
            st = {k: AR.alloc(4) for k in ("osum", "osqs", "mean", "msq", "var", "rgn")}

            for hf in range(2):
                dma("pool", W1.rearrange("p (c n) -> p c n", c=8)[:, hf * 4:(hf + 1) * 4, :],
                    w1_d.rearrange("(c p) n -> p c n", p=128)[:, hf * 4:(hf + 1) * 4, :], w=[f"W1{hf}"])
            A("pool", lambda g: g.memset(Sf, 0.0), writes=["Sf"])
            A("pool", lambda g: g.memset(Sbf, 0.0), writes=["Sbf"])

            def ck(n):
                if SUB == n:
                    raise _Stop()
            ck(0)

            def w1s(c, off, n):
                return W1[:, c * NC1 + off:c * NC1 + off + n]

            def table(dst, dk, col, sbi):
                A("dve", lambda g: g.tensor_scalar(out=ang, in0=posf, scalar1=INVc[:, col:col + 1], scalar2=PHc[:, col:col + 1],
                                                   op0=ALU.mult, op1=ALU.add), reads=["posf", "INVc", "PHc"], writes=["ang"])
                A("dve", lambda g: g.tensor_scalar(out=ni, in0=ang, scalar1=float(1.0 / (2 * PI)), scalar2=None, op0=ALU.mult),
                  reads=["ang"], writes=["ibuf"])
                A("dve", lambda g: g.tensor_copy(out=nf, in_=ni), reads=["ibuf"], writes=["nf"])
                A("dve", lambda g: g.scalar_tensor_tensor(out=ang, in0=nf, scalar=-2 * PI, in1=ang, op0=ALU.mult, op1=ALU.add),
                  reads=["nf", "ang"], writes=["ang"])
                A("dve", lambda g: g.tensor_single_scalar(out=msk, in_=ang, scalar=PI, op=ALU.is_gt), reads=["ang", "nf"], writes=["nf"])
                A("dve", lambda g: g.scalar_tensor_tensor(out=ang, in0=msk, scalar=-2 * PI, in1=ang, op0=ALU.mult, op1=ALU.add),
                  reads=["nf", "ang"], writes=["ang"])
                A("dve", lambda g: g.tensor_scalar(out=ang, in0=ang, scalar1=-3.14159, scalar2=3.14159, op0=ALU.max, op1=ALU.min),
                  reads=["ang"], writes=["ang"])
                np_ = dst.shape[0]
                A("act", lambda g: g.activation(out=dst, in_=ang[0:np_, :], func=AF.Sin), reads=["ang"], writes=[dk])

            mtiles = [(0, 128, "cq", 0), (128, 128, "cq", 1), (256, 128, "cq", 2), (384, 128, "ckv", 0), (512, 128, "ckv", 1),
                      (640, 64, "kpe", 0)]
            o_ = 704
            for nm in ("rq", "rk"):
                for i in range(2):
                    mtiles.append((o_, 128, nm + "n", i))
                    mtiles.append((o_ + 128, 128, nm + "s", i))
                    o_ += 256
            RV = 1728
            RG = 2240

            for sbi in range(NSB):
                sc0 = sbi * 512
                dma("sp", posi, bass.AP(pos_d.tensor, sc0, [[0, 128], [1, 512]]), w=["ibuf"])
                A("dve", lambda g: g.tensor_copy(out=posf, in_=posi), reads=["ibuf"], writes=["posf"])
                table(TABm[0:64, sc0:sc0 + 512], "TABm", 0, sbi)
                table(Cr, "Cr", 1, sbi)
                table(Sr, "Sr", 2, sbi)
                ck(1)
                for j in range(4):
                    tb = sbi * 4 + j
                    b2 = tb % 2
                    dma("sp", xt[b2], x_d[tb * 128:(tb + 1) * 128, :], w=[f"xt{b2}"])
                    A("act", lambda g, b2=b2: g.activation(out=junk, in_=xt[b2], func=AF.Square, accum_out=ssq[b2]),
                      reads=[f"xt{b2}"], writes=["junk", f"ssq{b2}"])
                    rsqrt_ops(ssq[b2], ssq[b2], 1.0 / D, [f"ssq{b2}"], f"ssq{b2}")
                    A("act", lambda g, b2=b2: g.activation(out=xn[b2], in_=xt[b2], func=AF.Copy, scale=ssq[b2]),
                      reads=[f"xt{b2}", f"ssq{b2}"], writes=[f"xn{b2}"])

                    def tr8(g, b2=b2):
                        for c in range(8):
                            r = g.transpose(out=Pb[b2][:, c * 128:(c + 1) * 128], in_=xn[b2][:, c * 128:(c + 1) * 128], identity=identb[:, :])
                        return r
                    A("pe", tr8, reads=[f"xn{b2}", "identb"], writes=[f"P{b2}"])
                    for c in range(8):
                        dst = hT[:, c * 512 + j * 128:c * 512 + (j + 1) * 128]
                        if b2 == 0:
                            A("dve", lambda g, c=c, dst=dst, b2=b2: g.tensor_scalar(out=dst, in0=Pb[b2][:, c * 128:(c + 1) * 128],
                                                                                   scalar1=a1[:, c:c + 1], scalar2=sh1[:, c:c + 1],
                                                                                   op0=ALU.mult, op1=ALU.add),
                              reads=[f"P{b2}", "a1", "modc"], writes=["hT"])
                        else:
                            A("act", lambda g, c=c, dst=dst, b2=b2: g.activation(out=dst, in_=Pb[b2][:, c * 128:(c + 1) * 128], func=AF.Identity,
                                                                                scale=a1[:, c:c + 1], bias=sh1[:, c:c + 1]),
                              reads=[f"P{b2}", "a1", "modc"], writes=["hT"])
                ck(2)
                for mi, (off, M, kind, i) in enumerate(mtiles):
                    bk = 2 + mi % 2
                    pk = f"P{bk}"

                    def mmz(g, off=off, M=M, bk=bk):
                        for c in range(8):
                            r = g.matmul(P[bk][0:M, :], lhsT=w1s(c, off, M), rhs=hT[:, c * 512:(c + 1) * 512], start=(c == 0), stop=(c == 7))
                        return r
                    A("pe", mmz, reads=["W10", "W11", "hT"], writes=[pk])
                    if kind in ("cq", "ckv"):
                        raw, sqt, nt, Rt, bank, scl, dstT, rk_ = ((cqraw, sq, 3, Rq, 4, 1.0 / 384, cqnT, "Rq") if kind == "cq"
                                                                  else (ckvraw, sq2, 2, Rkv, 5, 1.0 / 256, ckvnT, "Rq"))
                        gcol = qg if kind == "cq" else kvg
                        A("act", lambda g, raw=raw, i=i, bk=bk, gcol=gcol: g.activation(out=raw[:, i * 512:(i + 1) * 512], in_=P[bk][:, :], func=AF.Copy,
                                                                                      scale=gcol[:, i:i + 1]),
                          reads=[pk, "qg", "kvg"], writes=[f"{kind}raw{i}"])
                        A("act", lambda g, sqt=sqt, i=i, bk=bk: g.activation(out=sqt[:, i * 512:(i + 1) * 512], in_=P[bk][:, :], func=AF.Square),
                          reads=[pk], writes=[f"{kind}sq{i}"])
                        if i == nt - 1:
                            def mmst(g, sqt=sqt, nt=nt, bank=bank):
                                for q in range(nt):
                                    r = g.matmul(P[bank][:, :], lhsT=onesb[:, :], rhs=sqt[:, q * 512:(q + 1) * 512], start=(q == 0), stop=(q == nt - 1))
                                return r
                            A("pe", mmst, reads=[f"{kind}sq{q}" for q in range(nt)] + ["onesb"], writes=[f"P{bank}"])
                            rsqrt_ops(Rt, P[bank][:, :], scl, [f"P{bank}"], rk_)
                            for q in range(nt):
                                A("pool", lambda g, raw=raw, Rt=Rt, q=q, dstT=dstT: g.tensor_tensor(
                                    out=dstT[:, q * T + sc0:q * T + sc0 + 512], in0=raw[:, q * 512:(q + 1) * 512], in1=Rt, op=ALU.mult),
                                  reads=[f"{kind}raw{q}", rk_], writes=[f"{kind}nT"])
                    elif kind == "kpe":
                        A("dve", lambda g, bk=bk: g.tensor_tensor(out=t1[0][0:32, :], in0=P[bk][0:32, :], in1=TABm[0:32, sc0:sc0 + 512], op=ALU.mult),
                          reads=[pk, "TABm"], writes=["t1_0"])
                        A("dve", lambda g, bk=bk: g.tensor_tensor(out=t2[0][0:32, :], in0=P[bk][32:64, :], in1=TABm[32:64, sc0:sc0 + 512], op=ALU.mult),
                          reads=[pk, "TABm"], writes=["t2_0"])
                        A("dve", lambda g: g.tensor_tensor(out=kpeT[:, sc0:sc0 + 512], in0=t1[0][0:32, :], in1=t2[0][0:32, :], op=ALU.add),
                          reads=["t1_0", "t2_0"], writes=["kpeT"])
                    else:
                        nm = kind[:2]
                        if kind[2] == "n":
                            A("dve", lambda g, bk=bk, i=i: g.tensor_tensor(out=t1[i], in0=P[bk][:, :], in1=Cr, op=ALU.mult),
                              reads=[pk, "Cr"], writes=[f"t1_{i}"])
                        else:
                            A("dve", lambda g, bk=bk, i=i: g.tensor_tensor(out=t2[i], in0=P[bk][:, :], in1=Sr, op=ALU.mult),
                              reads=[pk, "Sr"], writes=[f"t2_{i}"])
                            dstq = rqT if nm == "rq" else rkT
                            A("pool", lambda g, i=i, dstq=dstq: g.tensor_tensor(out=dstq[:, i * 512:(i + 1) * 512], in0=t1[i], in1=t2[i], op=ALU.add),
                              reads=[f"t1_{i}", f"t2_{i}"], writes=[nm + "T"])
                            if nm == "rq":
                                A("pool", lambda g, i=i: g.tensor_tensor(out=rqm[0:64, i * 512:(i + 1) * 512], in0=t1[i][0:64, :], in1=t2[i][0:64, :],
                                                                        op=ALU.add),
                                  reads=[f"t1_{i}", f"t2_{i}"], writes=["rqm"])
                                A("pool", lambda g, i=i: g.tensor_tensor(out=qwT[:, i * 512:(i + 1) * 512], in0=rqT[:, i * 512:(i + 1) * 512],
                                                                        in1=WQc[:, i * 512:(i + 1) * 512], op=ALU.mult),
                                  reads=["rqT", "WQc"], writes=["qwT"])
                                A("pool", lambda g, i=i: g.tensor_tensor(out=qwm[0:64, i * 512:(i + 1) * 512], in0=rqT[0:64, i * 512:(i + 1) * 512],
                                                                        in1=WQc[0:64, i * 512:(i + 1) * 512], op=ALU.mult),
                                  reads=["rqT", "WQc"], writes=["qwm"])
                ck(3)
                for j in range(4):
                    for which, off, bank in (("v", RV, 4), ("g", RG, 5)):
                        def mmt(g, j=j, off=off, bank=bank):
                            for c in range(8):
                                r = g.matmul(P[bank][:, :], lhsT=hT[:, c * 512 + j * 128:c * 512 + (j + 1) * 128], rhs=w1s(c, off, 512),
                                             start=(c == 0), stop=(c == 7))
                            return r
                        A("pe", mmt, reads=["W10", "W11", "hT"], writes=[f"P{bank}"])
                        if which == "v":
                            A("act", lambda g, j=j: g.activation(out=vtok[:, j * 512:(j + 1) * 512], in_=P[4][:, :], func=AF.Copy),
                              reads=["P4"], writes=["vtok"])
                        else:
                            A("act", lambda g, j=j: g.activation(out=sg[:, j * 512:(j + 1) * 512], in_=P[5][:, :], func=AF.Silu),
                              reads=["P5"], writes=["sg"])
                ck(4)
                for j in range(4):
                    jc = slice(j * 128, (j + 1) * 128)

                    def trk(g, j=j):
                        for i in range(2):
                            r = g.transpose(out=Pb[5][:, i * 128:(i + 1) * 128], in_=rkT[:, i * 512 + j * 128:i * 512 + (j + 1) * 128], identity=identb[:, :])
                        return r
                    A("pe", trk, reads=["rkT", "identb"], writes=["P5"])
                    A("dve", lambda g: g.tensor_tensor(out=kwtok, in0=Pb[5][:, 0:256], in1=WKc[:, :], op=ALU.mult),
                      reads=["P5", "WKc"], writes=["kwtok"])

                    ck(6)

                    def mmsc(g, j=j):
                        for h in range(4):
                            i, r0 = h // 2, 64 * (h % 2)
                            cs = slice(i * 512 + j * 128, i * 512 + (j + 1) * 128)
                            if r0 == 0:
                                r = g.matmul(P[6][:, h * 128:(h + 1) * 128], lhsT=rkT[:, cs], rhs=rqm[:, cs], start=True, stop=True)
                            else:
                                r = g.matmul(P[6][:, h * 128:(h + 1) * 128], lhsT=rkT[64:128, cs], rhs=rqT[64:128, cs], start=True, stop=True,
                                             tile_position=(64, 0))
                        return r
                    A("pe", mmsc, reads=["rkT", "rqT", "rqm"], writes=["P6"])
                    A("dve", lambda g: g.tensor_tensor(out=scTm, in0=P[6][:, :], in1=DTc[:, :], op=ALU.mult), reads=["P6", "DTc"], writes=["scTm"])

                    ck(7)

                    def mmo(g, j=j):
                        for h in range(4):
                            i, r0 = h // 2, 64 * (h % 2)
                            cs = slice(i * 512 + j * 128, i * 512 + (j + 1) * 128)
                            g.matmul(P[7][:, h * 128:(h + 1) * 128], lhsT=scTm[:, h * 128:(h + 1) * 128],
                                     rhs=vtok[:, j * 512 + h * 128:j * 512 + (h + 1) * 128], start=True, stop=False)
                            if r0 == 0:
                                r = g.matmul(P[7][:, h * 128:(h + 1) * 128], lhsT=qwm[:, cs], rhs=Sbf[:, i * 128:(i + 1) * 128], start=False, stop=True)
                            else:
                                r = g.matmul(P[7][:, h * 128:(h + 1) * 128], lhsT=qwT[64:128, cs], rhs=Sbf[64:128, i * 128:(i + 1) * 128],
                                             start=False, stop=True, tile_position=(64, 0))
                        return r
                    A("pe", mmo, reads=["scTm", "vtok", "qwT", "qwm", "Sbf"], writes=["P7"])
                    A("act", lambda g: g.activation(out=osb, in_=P[7][:, :], func=AF.Copy), reads=["P7"], writes=["osb"])

                    ck(8)

                    def mmu(g, j=j):
                        for h in range(4):
                            i, r0 = h // 2, 64 * (h % 2)
                            kw = dict(tile_position=(0, 64)) if r0 else {}
                            r = g.matmul(P[5][r0:r0 + 64, 256 + i * 128:256 + (i + 1) * 128], lhsT=kwtok[:, h * 64:(h + 1) * 64],
                                         rhs=vtok[:, j * 512 + h * 128:j * 512 + (h + 1) * 128], start=True, stop=True, **kw)
                        return r
                    A("pe", mmu, reads=["kwtok", "vtok"], writes=["P5"])
                    for i in range(2):
                        A("dve", lambda g, i=i: g.scalar_tensor_tensor(out=Sf[:, i * 128:(i + 1) * 128], in0=Sf[:, i * 128:(i + 1) * 128],
                                                                      scalar=DECc[:, i:i + 1], in1=P[5][:, 256 + i * 128:256 + (i + 1) * 128],
                                                                      op0=ALU.mult, op1=ALU.add),
                          reads=["P5", "DECc", "Sf"], writes=["Sf"])
                    A("pool", lambda g: g.tensor_copy(out=Sbf, in_=Sf), reads=["Sf"], writes=["Sbf"])
                    ck(9)
                    o3 = osb.rearrange("p (h v) -> p h v", h=4)
                    A("dve", lambda g, o3=o3: g.reduce_sum(out=st["osum"], in_=o3, axis=AX.X), reads=["osb"], writes=["osum"])
                    A("pool", lambda g: g.tensor_tensor(out=osq, in0=osb, in1=osb, op=ALU.mult), reads=["osb"], writes=["ynorm"])
                    A("dve", lambda g: g.reduce_sum(out=st["osqs"], in_=osq.rearrange("p (h v) -> p h v", h=4), axis=AX.X),
                      reads=["ynorm"], writes=["osqs"])
                    A("dve", lambda g: g.tensor_scalar(out=st["mean"], in0=st["osum"], scalar1=1.0 / 128, scalar2=None, op0=ALU.mult),
                      reads=["osum"], writes=["mean"])
                    A("dve", lambda g: g.tensor_tensor(out=st["msq"], in0=st["mean"], in1=st["mean"], op=ALU.mult), reads=["mean"], writes=["msq"])
                    A("dve", lambda g: g.scalar_tensor_tensor(out=st["var"], in0=st["osqs"], scalar=1.0 / 128, in1=st["msq"], op0=ALU.mult, op1=ALU.subtract),
                      reads=["osqs", "msq"], writes=["var"])
                    rsqrt_ops(st["rgn"], st["var"], 1.0, ["var"], "rgn")
                    for h in range(4):
                        A("dve", lambda g, h=h: g.tensor_scalar(out=ynorm[:, h * 128:(h + 1) * 128], in0=osb[:, h * 128:(h + 1) * 128],
                                                                scalar1=st["mean"][:, h:h + 1], scalar2=st["rgn"][:, h:h + 1],
                                                                op0=ALU.subtract, op1=ALU.mult),
                          reads=["osb", "mean", "rgn"], writes=["ynorm"])
                    A("pool", lambda g, j=j: g.tensor_tensor(out=ytok, in0=ynorm, in1=sg[:, j * 512:(j + 1) * 512], op=ALU.mult),
                      reads=["ynorm", "sg"], writes=["ytok"])

                    ck(10)

                    def try_(g):
                        for t in range(4):
                            r = g.transpose(out=Pb[4][:, t * 128:(t + 1) * 128], in_=ytok[:, t * 128:(t + 1) * 128], identity=identb[:, :])
                        return r
                    A("pe", try_, reads=["ytok", "identb"], writes=["P4"])
                    A("act", lambda g, j=j: g.activation(out=ysT.rearrange("p (t n) -> p t n", t=4)[:, :, j * 128:(j + 1) * 128],
                                                         in_=Pb[4][:, 0:512].rearrange("p (t n) -> p t n", t=4), func=AF.Copy),
                      reads=["P4"], writes=["ysT"])
                ck(5)
                dma("sp", yret_d[:, :, sc0:sc0 + 512].rearrange("t p n -> p t n"), ysT.rearrange("p (t n) -> p t n", t=4),
                    r=["ysT"], w=["yret_d"])
            tap("cqnT", cqnT, ["cqnT"])
            tap("ckvnT", ckvnT, ["ckvnT"])
            tap("kpeT", kpeT, ["kpeT"])
            tap("TABm", TABm, ["TABm"])
            tap("yret", yret_d, ["yret_d"])
            tap("hT", hT, ["hT"])
            tap("cqraw", cqraw, ["cqraw0", "cqraw1", "cqraw2"])
            tap("Rq", Rq, ["Rq"])
            tap("sq", sq, ["cqsq0", "cqsq1", "cqsq2"])
            tap("rqT", rqT, ["rqT"])
            tap("rkT", rkT, ["rkT"])
            tap("osb", osb, ["osb"])
            tap("ytok", ytok, ["ytok"])
            tap("Sf", Sf, ["Sf"])
            S.barrier()
            _phase[0] += 1
            if _phase[0] > STOP:
                raise _Stop()
            AR.off = P12

            ymlaT = AR.alloc(4 * T, BF16)
            P23 = AR.off
            Wq = AR.alloc(3 * 1024, BF16)
            Wkv = AR.alloc(2 * 1536, BF16)
            dma("pool", Wq.rearrange("p (c n) -> p c n", c=3), wq_d.rearrange("(c p) n -> p c n", p=128), w=["Wq"])
            dma("pool", Wkv.rearrange("p (c n) -> p c n", c=2), wkv_d.rearrange("(c p) n -> p c n", p=128), w=["Wkv"])
            KT = [AR.alloc(T, BF16), AR.alloc(T, BF16)]
            QT = [AR.alloc(T, BF16), AR.alloc(T, BF16)]
            Vg = [AR.alloc(32 * 128, BF16), AR.alloc(32 * 128, BF16)]
            PT = [AR.alloc(GS * 512, BF16) for _ in range(NPT)]
            NSLOT = 4 // GS
            it_i = 0
            u1 = AR.alloc(512)
            u2 = AR.alloc(512)
            rec = AR.alloc(512)
            for b in range(2):
                A("dve", lambda g, b=b: g.memset(KT[b][0:64, :], 0.0), writes=[f"KT{b}"])
                A("pool", lambda g, b=b: g.memset(QT[b][0:64, :], 0.0), writes=[f"QT{b}"])
                A("dve", lambda g, b=b: g.memset(Vg[b].rearrange("p (k v) -> p k v", v=128)[:, :, 64:128], 1.0), writes=[f"Vg{b}"])
            SCALE = float(96 ** -0.5)
            pt_i = 0
            for h in range(8):
                hb = h % 2
                kk, qk, vk = f"KT{hb}", f"QT{hb}", f"Vg{hb}"
                A("dve", lambda g, hb=hb: g.tensor_copy(out=KT[hb][0:32, :], in_=kpeT[:, :]), reads=["kpeT"], writes=[kk])
                for sbi in range(NSB):
                    sc0 = sbi * 512

                    def mmk(g, h=h, sc0=sc0):
                        for c in range(2):
                            r = g.matmul(P[6][:, :], lhsT=Wkv[:, c * 1536 + h * 128:c * 1536 + (h + 1) * 128], rhs=ckvnT[:, c * T + sc0:c * T + sc0 + 512],
                                         start=(c == 0), stop=(c == 1))
                        return r
                    A("pe", mmk, reads=["Wkv", "ckvnT"], writes=["P6"])
                    A("dve", lambda g, hb=hb, sc0=sc0: g.tensor_copy(out=KT[hb][64:128, sc0:sc0 + 512], in_=P[6][64:128, :]),
                      reads=["P6"], writes=[kk])

                    def mmq(g, h=h, sc0=sc0):
                        for c in range(3):
                            r = g.matmul(P[7][:, :], lhsT=Wq[:, c * 1024 + h * 128:c * 1024 + (h + 1) * 128], rhs=cqnT[:, c * T + sc0:c * T + sc0 + 512],
                                         start=(c == 0), stop=(c == 2))
                        return r
                    A("pe", mmq, reads=["Wq", "cqnT"], writes=["P7"])
                    A("dve", lambda g, sc0=sc0: g.tensor_tensor(out=u1[0:32, :], in0=P[7][0:32, :], in1=TABm[0:32, sc0:sc0 + 512], op=ALU.mult),
                      reads=["P7", "TABm"], writes=["u1"])
                    A("dve", lambda g, sc0=sc0: g.tensor_tensor(out=u2[0:32, :], in0=P[7][32:64, :], in1=TABm[32:64, sc0:sc0 + 512], op=ALU.mult),
                      reads=["P7", "TABm"], writes=["u2"])
                    A("pool", lambda g, hb=hb, sc0=sc0: g.tensor_tensor(out=QT[hb][0:32, sc0:sc0 + 512], in0=u1[0:32, :], in1=u2[0:32, :], op=ALU.add),
                      reads=["u1", "u2"], writes=[qk])
                    A("dve", lambda g, hb=hb, sc0=sc0: g.tensor_copy(out=QT[hb][64:128, sc0:sc0 + 512], in_=P[7][64:128, :]),
                      reads=["P7"], writes=[qk])
                for k8 in range(4):
                    def mmv(g, h=h, k8=k8):
                        for q in range(8):
                            kb = k8 * 8 + q
                            for c in range(2):
                                r = g.matmul(P[6][:, q * 64:(q + 1) * 64], lhsT=ckvnT[:, c * T + kb * 128:c * T + (kb + 1) * 128],
                                             rhs=Wkv[:, c * 1536 + 1024 + h * 64:c * 1536 + 1024 + (h + 1) * 64], start=(c == 0), stop=(c == 1))
                        return r
                    A("pe", mmv, reads=["Wkv", "ckvnT"], writes=["P6"])
                    A("dve", lambda g, hb=hb, k8=k8: g.tensor_copy(
                        out=Vg[hb].rearrange("p (k v) -> p k v", v=128)[:, k8 * 8:(k8 + 1) * 8, 0:64],
                        in_=P[6][:, :].rearrange("p (k v) -> p k v", v=64)), reads=["P6"], writes=[vk])
                for qi, qs in enumerate(QS_ORDER):
                    q0 = qs * 512
                    acc = 4 + qi % 2
                    ak = f"P{acc}"
                    nfull = 4 * qs
                    groups = [(kb, min(kb + GS, nfull)) for kb in range(0, nfull, GS)]
                    items = [("full", a, b) for a, b in groups] + [("diag", 4 * qs + d, d) for d in range(4)]
                    last_kb = 4 * qs + 3
                    for gi, it in enumerate(items):
                        slot = it_i % NSLOT
                        it_i += 1
                        sbank = GS * slot
                        sk = f"PS{slot}"
                        pt = PT[pt_i % NPT]
                        ptk = f"PT{pt_i % NPT}"
                        pt_i += 1
                        if it[0] == "full":
                            kbs = list(range(it[1], it[2]))

                            def mms(g, hb=hb, kbs=kbs, sbank=sbank, q0=q0):
                                for n_, kb in enumerate(kbs):
                                    r = g.matmul(P[sbank + n_][:, :], lhsT=KT[hb][:, kb * 128:(kb + 1) * 128], rhs=QT[hb][:, q0:q0 + 512],
                                                 start=True, stop=True)
                                return r
                            A("pe", mms, reads=[kk, qk], writes=[sk])
                            for n_ in range(len(kbs)):
                                A("act", lambda g, pt=pt, sbank=sbank, n_=n_: g.activation(out=pt[:, n_ * 512:(n_ + 1) * 512], in_=P[sbank + n_][:, :],
                                                                                            func=AF.Exp, scale=SCALE),
                                  reads=[sk], writes=[ptk])

                            def mmpv(g, hb=hb, kbs=kbs, pt=pt, acc=acc, last_kb=last_kb):
                                for n_, kb in enumerate(kbs):
                                    r = g.matmul(P[acc][:, :], lhsT=Vg[hb][:, kb * 128:(kb + 1) * 128], rhs=pt[:, n_ * 512:(n_ + 1) * 512],
                                                 start=(kb == 0), stop=(kb == last_kb))
                                return r
                            A("pe", mmpv, reads=[vk, ptk], writes=[ak])
                        else:
                            kb, d = it[1], it[2]
                            c0 = d * 128
                            A("pe", lambda g, hb=hb, kb=kb, c0=c0, sbank=sbank, q0=q0: g.matmul(
                                P[sbank][:, c0:512], lhsT=KT[hb][:, kb * 128:(kb + 1) * 128], rhs=QT[hb][:, q0 + c0:q0 + 512], start=True, stop=True),
                              reads=[kk, qk], writes=[sk])
                            A("act", lambda g, pt=pt, sbank=sbank, c0=c0: g.activation(out=pt[:, c0:512], in_=P[sbank][:, c0:512], func=AF.Exp, scale=SCALE),
                              reads=[sk], writes=[ptk])
                            A("pool", lambda g, pt=pt, c0=c0: g.tensor_tensor(out=pt[:, c0:c0 + 128], in0=pt[:, c0:c0 + 128], in1=trib[:, :], op=ALU.mult),
                              reads=[ptk, "trib"], writes=[ptk])
                            A("pe", lambda g, hb=hb, kb=kb, c0=c0, pt=pt, acc=acc, last_kb=last_kb: g.matmul(
                                P[acc][:, c0:512], lhsT=Vg[hb][:, kb * 128:(kb + 1) * 128], rhs=pt[:, c0:512], start=(kb == 0), stop=(kb == last_kb)),
                              reads=[vk, ptk], writes=[ak])
                    A("dve", lambda g, acc=acc: g.reciprocal(out=rec[0:64, :], in_=P[acc][64:128, :]), reads=[ak], writes=["rec"])
                    r0 = 64 * (h % 2)
                    A("dve", lambda g, acc=acc, r0=r0, h=h, q0=q0: g.tensor_tensor(
                        out=ymlaT[r0:r0 + 64, (h // 2) * T + q0:(h // 2) * T + q0 + 512], in0=P[acc][0:64, :], in1=rec[0:64, :], op=ALU.mult),
                      reads=[ak, "rec"], writes=["ymlaT"])
            S.barrier()
            _phase[0] += 1
            if _phase[0] > STOP:
                raise _Stop()

            AR.off = P23
            WU_OFF = ARN - (8 * DFF * 2) // 4
            Wg, wg_end = AR.alloc_at(0, 8 * DFF, BF16)
            Wu, _ = AR.alloc_at(WU_OFF, 8 * DFF, BF16)
            assert wg_end <= P12
            for hf in range(2):
                dma("pool", Wg.rearrange("p (c n) -> p c n", c=8)[:, hf * 4:(hf + 1) * 4, :],
                    wg_d.rearrange("(c p) n -> p c n", p=128)[:, hf * 4:(hf + 1) * 4, :], w=[f"Wg{hf}"])
                dma("pool", Wu.rearrange("p (c n) -> p c n", c=8)[:, hf * 4:(hf + 1) * 4, :],
                    wu_d.rearrange("(c p) n -> p c n", p=128)[:, hf * 4:(hf + 1) * 4, :], w=[f"Wu{hf}"])
            Wo = AR.alloc(8 * D, BF16)
            og = AR.alloc(8)
            yrTs = [AR.alloc(4 * 512, BF16), AR.alloc(4 * 512, BF16)]
            ysq = [AR.alloc(4 * 128, BF16), AR.alloc(4 * 128, BF16)]
            xt3 = [AR.alloc(D), AR.alloc(D)]
            x1 = [AR.alloc(D), AR.alloc(D)]
            mB = [AR.alloc(D)]
            mixs = [AR.alloc(D)]
            tt = [AR.alloc(D)]
            rm = [AR.alloc(1), AR.alloc(1)]
            r2 = [AR.alloc(1), AR.alloc(1)]
            hole = wg_end
            for lst in (mB, mixs, tt):
                v_, hole = AR.alloc_at(hole, D)
                lst.append(v_)
            assert hole <= P12
            dma("sp", og, og_d, w=["og"])
            for hf in range(2):
                dma("pool", Wo.rearrange("p (c n) -> p c n", c=8)[:, hf * 4:(hf + 1) * 4, :],
                    wout_d.rearrange("(c p) n -> p c n", p=128)[:, hf * 4:(hf + 1) * 4, :], w=[f"Wo{hf}"])
            for c in range(8):
                A("dve", lambda g, c=c: g.tensor_scalar(out=Wo[:, c * D:(c + 1) * D], in0=Wo[:, c * D:(c + 1) * D], scalar1=og[:, c:c + 1],
                                                         scalar2=None, op0=ALU.mult), reads=[f"Wo{c // 4}", "og"], writes=[f"Wo{c // 4}"])
            for tb in range(32):
                b2 = tb % 2
                tc0 = tb * 128
                sbi, j = tb // 4, tb % 4
                yb = yrTs[sbi % 2]
                ybk = f"yrT{sbi % 2}"
                pa = (0, 1) if b2 == 0 else (4, 5)
                pak = [f"P{pa[0]}", f"P{pa[1]}"]
                stb = 6 + b2
                if j == 0:
                    dma("sp", yb.rearrange("p (t n) -> p t n", t=4), yret_d[:, :, sbi * 512:(sbi + 1) * 512].rearrange("t p n -> p t n"),
                        r=["yret_d"], w=[ybk])
                dma("sp", xt3[b2], x_d[tc0:tc0 + 128, :], w=[f"xt3{b2}"])
                A("pool", lambda g, tc0=tc0, b2=b2: g.tensor_tensor(out=ysq[b2].rearrange("p (c n) -> p c n", c=4),
                                                                     in0=ymlaT.rearrange("p (c n) -> p c n", c=4)[:, :, tc0:tc0 + 128],
                                                                     in1=ymlaT.rearrange("p (c n) -> p c n", c=4)[:, :, tc0:tc0 + 128], op=ALU.mult),
                  reads=["ymlaT"], writes=[f"ysq{b2}"])

                def mmss(g, b2=b2, stb=stb):
                    for c in range(4):
                        r = g.matmul(P[stb][:, 0:1], lhsT=ysq[b2][:, c * 128:(c + 1) * 128], rhs=onesb[:, 0:1], start=(c == 0), stop=(c == 3))
                    return r
                A("pe", mmss, reads=[f"ysq{b2}", "onesb"], writes=[f"P{stb}"])
                rsqrt_ops(rm[b2], P[stb][:, 0:1], 1.0 / 512, [f"P{stb}"], f"rm{b2}")

                def mmA(g, tc0=tc0, pa=pa):
                    for hf in range(2):
                        for c in range(4):
                            r = g.matmul(P[pa[hf]][:, :], lhsT=ymlaT[:, c * T + tc0:c * T + tc0 + 128], rhs=Wo[:, c * D + hf * 512:c * D + (hf + 1) * 512],
                                         start=(c == 0), stop=(c == 3))
                    return r
                A("pe", mmA, reads=["ymlaT", "Wo0"], writes=pak)

                def mmB(g, yb=yb, j=j):
                    for hf in range(2):
                        for c in range(4):
                            r = g.matmul(P[2 + hf][:, :], lhsT=yb[:, c * 512 + j * 128:c * 512 + (j + 1) * 128],
                                         rhs=Wo[:, (4 + c) * D + hf * 512:(4 + c) * D + (hf + 1) * 512], start=(c == 0), stop=(c == 3))
                    return r
                A("pe", mmB, reads=[ybk, "Wo1"], writes=["PB"])
                for hf in range(2):
                    A("act", lambda g, hf=hf, b2=b2: g.activation(out=mB[b2][:, hf * 512:(hf + 1) * 512], in_=P[2 + hf][:, :], func=AF.Copy),
                      reads=["PB"], writes=[f"mB{b2}"])
                    A("dve", lambda g, hf=hf, b2=b2, pa=pa: g.scalar_tensor_tensor(out=mixs[b2][:, hf * 512:(hf + 1) * 512], in0=P[pa[hf]][:, :],
                                                                                scalar=rm[b2][:, 0:1], in1=mB[b2][:, hf * 512:(hf + 1) * 512],
                                                                                op0=ALU.mult, op1=ALU.add),
                      reads=[pak[hf], f"rm{b2}", f"mB{b2}"], writes=[f"mixs{b2}"])
                A("act", lambda g, b2=b2: g.activation(out=tt[b2].bitcast(BF16)[:, 0:D], in_=mixs[b2], func=AF.Square, accum_out=r2[b2]),
                  reads=[f"mixs{b2}"], writes=[f"tt{b2}", f"r2{b2}"])
                rsqrt_ops(r2[b2], r2[b2], 1.0 / D, [f"r2{b2}"], f"r2{b2}")
                A("dve", lambda g, b2=b2: g.scalar_tensor_tensor(out=tt[b2], in0=mixs[b2], scalar=r2[b2][:, 0:1], in1=G1b[:, :], op0=ALU.mult, op1=ALU.mult),
                  reads=[f"mixs{b2}", f"r2{b2}", "G1b"], writes=[f"tt{b2}"])
                A("pool", lambda g, b2=b2: g.tensor_tensor(out=x1[b2], in0=xt3[b2], in1=tt[b2], op=ALU.add), reads=[f"xt3{b2}", f"tt{b2}"], writes=[f"x1{b2}"])
                dma("sp", out_d[tc0:tc0 + 128, :], x1[b2], r=[f"x1{b2}"], w=[f"out{tb}"])
            assert AR.off <= WU_OFF, (AR.off, WU_OFF)
            S.barrier()
            _phase[0] += 1
            if _phase[0] > STOP:
                raise _Stop()

            Wd, wd_end = AR.alloc_at(wg_end, NJ * D, BF16)
            assert wd_end <= P23
            AR.off = P23
            xa = [AR.alloc(D), AR.alloc(D)]
            xb = [AR.alloc(D), AR.alloc(D)]
            xn4 = [AR.alloc(D, BF16), AR.alloc(D, BF16)]
            junk4 = AR.alloc(D, BF16)
            junk5 = junk4
            s4 = [AR.alloc(1), AR.alloc(1)]
            h2T = AR.alloc(8 * 512, BF16)
            h1T = AR.alloc(NJ * 512, BF16)
            sgt = [AR.alloc(512, BF16), AR.alloc(512, BF16)]
            t4 = [AR.alloc(D), AR.alloc(D)]
            r3 = [AR.alloc(1), AR.alloc(1)]
            assert AR.off <= WU_OFF, (AR.off, WU_OFF)
            for hf in range(2):
                dma("pool", Wd.rearrange("p (c n) -> p c n", c=NJ)[:, hf * 11:(hf + 1) * 11, :],
                    wd_d.rearrange("(c p) n -> p c n", p=128)[:, hf * 11:(hf + 1) * 11, :], w=[f"Wd{hf}"])
            fin = []
            for sbi in range(NSB):
                for j in range(4):
                    tb = sbi * 4 + j
                    b2 = tb % 2
                    xv = xa[b2]
                    dma("sp", xv, out_d[tb * 128:(tb + 1) * 128, :], r=[f"out{tb}"], w=[f"xa{b2}"])
                    A("act", lambda g, xv=xv, b2=b2: g.activation(out=xn4[b2], in_=xv, func=AF.Square, accum_out=s4[b2]),
                      reads=[f"xa{b2}"], writes=[f"xn4{b2}", f"s4{b2}"])
                    rsqrt_ops(s4[b2], s4[b2], 1.0 / D, [f"s4{b2}"], f"s4{b2}")
                    A("act", lambda g, xv=xv, b2=b2: g.activation(out=xn4[b2], in_=xv, func=AF.Copy, scale=s4[b2]),
                      reads=[f"xa{b2}", f"s4{b2}"], writes=[f"xn4{b2}"])

                    def tr8b(g, b2=b2):
                        for c in range(8):
                            r = g.transpose(out=Pb[b2][:, c * 128:(c + 1) * 128], in_=xn4[b2][:, c * 128:(c + 1) * 128], identity=identb[:, :])
                        return r
                    A("pe", tr8b, reads=[f"xn4{b2}", "identb"], writes=[f"P{b2}"])
                    for c in range(8):
                        dst = h2T[:, c * 512 + j * 128:c * 512 + (j + 1) * 128]
                        if b2 == 0:
                            A("dve", lambda g, c=c, dst=dst, b2=b2: g.tensor_scalar(out=dst, in0=Pb[b2][:, c * 128:(c + 1) * 128],
                                                                                   scalar1=a2[:, c:c + 1], scalar2=sh2[:, c:c + 1],
                                                                                   op0=ALU.mult, op1=ALU.add),
                              reads=[f"P{b2}", "a2", "modc"], writes=["h2T"])
                        else:
                            A("act", lambda g, c=c, dst=dst, b2=b2: g.activation(out=dst, in_=Pb[b2][:, c * 128:(c + 1) * 128], func=AF.Identity,
                                                                                scale=a2[:, c:c + 1], bias=sh2[:, c:c + 1]),
                              reads=[f"P{b2}", "a2", "modc"], writes=["h2T"])
                for jj in range(NJ):
                    gb = 2 + jj % 2
                    ub = 4 + jj % 2

                    def mmg(g, jj=jj, gb=gb, ub=ub):
                        for c in range(8):
                            g.matmul(P[gb][:, :], lhsT=Wg[:, c * DFF + jj * 128:c * DFF + (jj + 1) * 128], rhs=h2T[:, c * 512:(c + 1) * 512],
                                     start=(c == 0), stop=(c == 7))
                        for c in range(8):
                            r = g.matmul(P[ub][:, :], lhsT=Wu[:, c * DFF + jj * 128:c * DFF + (jj + 1) * 128], rhs=h2T[:, c * 512:(c + 1) * 512],
                                         start=(c == 0), stop=(c == 7))
                        return r
                    A("pe", mmg, reads=["Wg0", "Wg1", "Wu0", "Wu1", "h2T"], writes=[f"P{gb}", f"P{ub}"])
                    A("act", lambda g, jj=jj, gb=gb: g.activation(out=sgt[jj % 2], in_=P[gb][:, :], func=AF.Silu), reads=[f"P{gb}"], writes=[f"sgt{jj % 2}"])
                    A("dve", lambda g, jj=jj, ub=ub: g.tensor_tensor(out=h1T[:, jj * 512:(jj + 1) * 512], in0=P[ub][:, :], in1=sgt[jj % 2], op=ALU.mult),
                      reads=[f"P{ub}", f"sgt{jj % 2}"], writes=["h1T"])
                for j in range(4):
                    tb = sbi * 4 + j
                    b2 = tb % 2
                    xv = xb[b2]
                    tv = t4[b2]
                    dma("sp", xv, out_d[tb * 128:(tb + 1) * 128, :], r=[f"out{tb}"], w=[f"xb{b2}"])

                    fb = (6, 7) if j % 2 == 0 else (3, 5)
                    fk_ = [f"P{fb[0]}", f"P{fb[1]}"]

                    def mmd(g, j=j, fb=fb):
                        for hf in range(2):
                            for jj in range(NJ):
                                r = g.matmul(P[fb[hf]][:, :], lhsT=h1T[:, jj * 512 + j * 128:jj * 512 + (j + 1) * 128],
                                             rhs=Wd[:, jj * D + hf * 512:jj * D + (hf + 1) * 512], start=(jj == 0), stop=(jj == NJ - 1))
                        return r
                    A("pe", mmd, reads=["h1T", "Wd0", "Wd1"], writes=fk_)
                    A("act", lambda g, tv=tv, fb=fb: g.activation(out=tv[:, 0:512], in_=P[fb[0]][:, :], func=AF.Copy), reads=[fk_[0]], writes=[f"t4{b2}"])
                    A("dve", lambda g, tv=tv, fb=fb: g.tensor_copy(out=tv[:, 512:1024], in_=P[fb[1]][:, :]), reads=[fk_[1]], writes=[f"t4{b2}"])
                    A("act", lambda g, tv=tv, b2=b2: g.activation(out=junk5, in_=tv, func=AF.Square, accum_out=r3[b2]), reads=[f"t4{b2}"], writes=["junk4", f"r3{b2}"])
                    rsqrt_ops(r3[b2], r3[b2], 1.0 / D, [f"r3{b2}"], f"r3{b2}")
                    A("dve", lambda g, tv=tv, b2=b2: g.scalar_tensor_tensor(out=tv, in0=tv, scalar=r3[b2][:, 0:1], in1=G2b[:, :], op0=ALU.mult, op1=ALU.mult),
                      reads=[f"t4{b2}", f"r3{b2}", "G2b"], writes=[f"t4{b2}"])
                    A("pool", lambda g, xv=xv, tv=tv: g.tensor_tensor(out=xv, in0=xv, in1=tv, op=ALU.add), reads=[f"xb{b2}", f"t4{b2}"], writes=[f"xb{b2}"])
                    fin.append(dma("sp", out_d[tb * 128:(tb + 1) * 128, :], xv, r=[f"xb{b2}"], w=[f"out{tb}"]))
            A("sp", lambda g: None, deps=fin)

        except _Stop:
            pass
        with nc.Block() as block:
            S.emit_all(block, esem, dsem)
    return nc


def _consts():
    f = np.float32
    gam = 1.0 - 2.0 ** (-5.0 - np.arange(4, dtype=np.float64))
    idx = np.arange(128)
    ident = np.eye(128, dtype=f)
    tri = (idx[None, :] >= idx[:, None]).astype(f)
    dtc = np.zeros((128, 4, 128), np.float64)
    rel = idx[None, :] - idx[:, None]
    for h in range(4):
        dtc[:, h, :] = np.where(rel >= 0, gam[h] ** np.maximum(rel, 0), 0.0) * 0.125
    wqc = np.zeros((128, 2, 512), np.float64)
    wkc = np.zeros((128, 2, 128), np.float64)
    decc = np.zeros((128, 2), np.float64)
    for i in range(2):
        for r in range(128):
            h = 2 * i + r // 64
            wqc[r, i, :] = np.tile(gam[h] ** (idx + 1.0), 4)
            decc[r, i] = gam[h] ** 128
        for ft in range(128):
            h = 2 * i + ft // 64
            wkc[:, i, ft] = gam[h] ** (127.0 - idx) * 0.125
    inv_m = 10000.0 ** (-np.arange(16, dtype=np.float64) / 16.0)
    inv_r = 10000.0 ** (-np.arange(32, dtype=np.float64) / 32.0)
    invc = np.zeros((128, 3), np.float64)
    phc = np.zeros((128, 3), np.float64)
    for r in range(64):
        invc[r, 0] = inv_m[r % 16]
        phc[r, 0] = np.pi / 2 if r < 32 else (np.pi if r < 48 else 0.0)
    for r in range(128):
        invc[r, 1] = inv_r[r % 32]
        invc[r, 2] = inv_r[r % 32]
        phc[r, 1] = np.pi / 2
        phc[r, 2] = np.pi if (r % 64) < 32 else 0.0
    return dict(ident=ident, tri=tri, dtc=dtc.reshape(128, 512).astype(f), wqc=wqc.reshape(128, 1024).astype(f),
                wkc=wkc.reshape(128, 256).astype(f), decc=decc.astype(f), invc=invc.astype(f), phc=phc.astype(f))


def _colmajor(v, n):
    return np.ascontiguousarray(np.asarray(v, np.float32).reshape(n, 128).T)


def _prep_shared(inp):
    f = np.float32
    w_in = np.asarray(inp["w_in"], f)[0]
    cols = list(range(0, 640))
    cols += list(range(640, 672)) + [640 + k for k in list(range(16, 32)) + list(range(0, 16))]
    for base in (672, 928):
        for i in range(2):
            nat, sw = [], []
            for hh in (2 * i, 2 * i + 1):
                b = base + hh * 64
                nat += list(range(b, b + 64))
                sw += list(range(b + 32, b + 64)) + list(range(b, b + 32))
            cols += nat + sw
    cols += list(range(1184, 2208))
    w1 = np.ascontiguousarray(w_in[:, cols])
    assert w1.shape[1] == NC1
    wqb = np.asarray(inp["w_q_b"], f)[0]
    qc = []
    for h in range(8):
        b = h * 96
        qc += list(range(b + 64, b + 96)) + [b + 64 + k for k in list(range(16, 32)) + list(range(0, 16))] + list(range(b, b + 64))
    wq = np.ascontiguousarray(wqb[:, qc])
    wkvb = np.asarray(inp["w_kv_b"], f)[0]
    wkv = np.zeros((256, 1536), f)
    for h in range(8):
        wkv[:, h * 128 + 64:h * 128 + 128] = wkvb[:, h * 128:h * 128 + 64]
        wkv[:, 1024 + h * 64:1024 + (h + 1) * 64] = wkvb[:, h * 128 + 64:h * 128 + 128]
    sh = dict(
        w_ada=np.ascontiguousarray(np.asarray(inp["w_ada"], f)[0]),
        b_ada=np.ascontiguousarray(np.asarray(inp["b_ada"], f)[0][None, :]),
        gpre1=_colmajor(inp["pre_norm_mix"][0], 8), gpre2=_colmajor(inp["pre_norm_ffn"][0], 8),
        gpost1=np.ascontiguousarray(np.asarray(inp["post_norm_mix"], f)[0][None, :]),
        gpost2=np.ascontiguousarray(np.asarray(inp["post_norm_ffn"], f)[0][None, :]),
        qg=_colmajor(inp["q_a_norm"][0], 3), kvg=_colmajor(inp["kv_a_norm"][0], 2),
        og=_colmajor(np.concatenate([np.asarray(inp["mla_out_norm"], f)[0], np.asarray(inp["ret_gn_gain"], f)[0]]), 8),
        w1=w1, wq=wq, wkv=wkv,
        wout=np.ascontiguousarray(np.asarray(inp["w_out"], f)[0]),
        wg=np.ascontiguousarray(np.asarray(inp["w_gate"], f)[0]),
        wu=np.ascontiguousarray(np.asarray(inp["w_up"], f)[0]),
        wd=np.ascontiguousarray(np.asarray(inp["w_down"], f)[0]),
    )
    sh.update(_consts())
    return sh


def make_in_maps(inp, cores):
    sh = _prep_shared(inp)
    x = np.asarray(inp["x"], np.float32)
    c = np.asarray(inp["c"], np.float32)
    pos = np.asarray(inp["positions"], np.int32)
    maps = []
    for b in cores:
        m = dict(sh)
        m["x"] = np.ascontiguousarray(x[b])
        m["cT"] = _colmajor(c[b], 8)
        m["pos"] = np.ascontiguousarray(pos[b][None, :])
        maps.append(m)
    return maps


_NC = None


def kernel(**inputs):
    global _NC
    if _NC is None:
        _NC = build_nc()
    maps = make_in_maps(inputs, list(range(8)))
    res = run_bass_kernel_spmd(_NC, maps, core_ids=list(range(8)))
    return np.stack([np.asarray(r["out"], np.float32) for r in res.results], axis=0)
```

```python
import contextlib
import types
import numpy as np
import concourse.bass as bass
import concourse.mybir as mybir
from concourse.bass_utils import run_bass_kernel_spmd

F32 = mybir.dt.float32
BF16 = mybir.dt.bfloat16
I32 = mybir.dt.int32
AF = mybir.ActivationFunctionType
ALU = mybir.AluOpType
AX = mybir.AxisListType

ENGS = ("pe", "act", "dve", "pool", "sp")
STOP = 99
SUB = 99
HSEL = (0, 1, 2, 3)
TAPS = ()
REORDER = True
SEM_LAT = 0.2
QS_ORDER = (0, 7, 1, 6, 2, 5, 3, 4)
GS = 1
NPT = 6


class _Stop(Exception):
    pass

T = 4096
D = 1024
NSB = 8
DFF = 2816
NJ = 22
NC1 = 2752
EPS = 1e-6
PI = float(np.pi)


def _freeze(fn):
    if fn.__closure__ is None:
        return fn
    cells = []
    for c in fn.__closure__:
        try:
            cells.append(types.CellType(c.cell_contents))
        except ValueError:
            cells.append(c)
    return types.FunctionType(fn.__code__, fn.__globals__, fn.__name__, fn.__defaults__, tuple(cells))


class Op:
    __slots__ = ("eng", "idx", "emit", "deps", "is_dma", "dma_i", "marked", "count", "clock", "waits", "is_bar", "busy", "lat", "seq", "st", "nobar")


def _nfree(ap):
    n = 1
    for d in ap.shape[1:]:
        n *= int(d)
    return n


class _Fake:
    def __init__(self, eng):
        self.eng = eng
        self.busy = 0.0
        self.lat = None

    def matmul(self, out, lhsT=None, rhs=None, **kw):
        n = max(_nfree(rhs), 64)
        f = 4.0 if rhs.dtype == F32 else 1.0
        self.busy += f * n / 2370.0 + (0.004 if n >= 512 else 0.06)
        return self

    def transpose(self, out=None, in_=None, identity=None, **kw):
        self.busy += 0.12
        return self

    def activation(self, out=None, in_=None, **kw):
        n = _nfree(in_)
        self.busy += (0.07 if n >= 512 else 0.25) + n / 1200.0
        return self

    def dma_start(self, out=None, in_=None, **kw):
        nb = _nfree(out) * int(out.shape[0]) * (4 if out.dtype in (F32, I32) else 2)
        self.busy += 0.15 if self.eng == "sp" else 1.2
        self.lat = 2.5 + nb / 150e3
        return self

    def _dve(self, out, **kw):
        n = _nfree(out)
        if self.eng == "pool":
            self.busy += 0.2 + n / 500.0
        else:
            self.busy += 0.12 + n / 900.0
        return self

    def tensor_tensor(self, out=None, **kw):
        return self._dve(out)

    def tensor_scalar(self, out=None, **kw):
        return self._dve(out)

    def tensor_copy(self, out=None, **kw):
        return self._dve(out)

    def scalar_tensor_tensor(self, out=None, **kw):
        return self._dve(out)

    def tensor_single_scalar(self, out=None, **kw):
        return self._dve(out)

    def reciprocal(self, out=None, **kw):
        self.busy += 0.1 + _nfree(out) / 150.0
        return self

    def memset(self, ap, *a, **kw):
        return self._dve(ap)

    def reduce_sum(self, out=None, in_=None, **kw):
        return self._dve(in_)

    def then_inc(self, *a, **kw):
        return self


class Sched:
    def __init__(self, n_dma_sems=12):
        self.ops = {e: [] for e in ENGS}
        self.order = []
        self.lastw = {}
        self.readers = {}
        self.n_dma_sems = n_dma_sems
        self.dma_ops = {e: [] for e in ENGS}
        self.dma_since_bar = []

    def add(self, eng, emit, reads=(), writes=(), dma=False, deps=()):
        op = Op()
        op.eng = eng
        op.emit = _freeze(emit)
        op.is_dma = dma
        op.marked = False
        op.count = 0
        op.idx = len(self.ops[eng])
        d = set(deps)
        for k in reads:
            w = self.lastw.get(k)
            if w is not None:
                d.add(w)
        for k in writes:
            w = self.lastw.get(k)
            if w is not None:
                d.add(w)
            for r in self.readers.get(k, ()):
                d.add(r)
        for k in reads:
            self.readers.setdefault(k, []).append(op)
        for k in writes:
            self.lastw[k] = op
            self.readers[k] = []
        d.discard(op)
        op.deps = d
        op.is_bar = False
        op.nobar = False
        op.seq = len(self.order)
        self.ops[eng].append(op)
        self.order.append(op)
        return op

    def barrier(self):
        for e in ENGS:
            self.add(e, lambda g: None).is_bar = True

    def _list_schedule(self, seg):
        import heapq
        segset = set(seg)
        succ = {o: [] for o in seg}
        indeg = {}
        for o in seg:
            fk = _Fake(o.eng)
            o.emit(fk)
            o.busy = fk.busy
            o.lat = fk.lat if fk.lat is not None else fk.busy + SEM_LAT
            k = 0
            for d in o.deps:
                if d in segset:
                    succ[d].append(o)
                    k += 1
            indeg[o] = k
        bl = {}
        for o in reversed(seg):
            m = 0.0
            for s_ in succ[o]:
                if bl[s_] > m:
                    m = bl[s_]
            bl[o] = o.lat + m
        free = {e: 0.0 for e in ENGS}
        avail = {e: [] for e in ENGS}
        future = {e: [] for e in ENGS}
        rtime = {o: 0.0 for o in seg}
        for o in seg:
            if indeg[o] == 0:
                heapq.heappush(future[o.eng], (0.0, o.seq, o))
        out = []
        n = len(seg)
        while len(out) < n:
            best_e, best_t = None, None
            for e in ENGS:
                fu, av = future[e], avail[e]
                while fu and fu[0][0] <= free[e]:
                    _, sq, o = heapq.heappop(fu)
                    heapq.heappush(av, (-bl[o], sq, o))
                if av:
                    t = free[e]
                elif fu:
                    t = fu[0][0]
                else:
                    continue
                if best_t is None or t < best_t:
                    best_e, best_t = e, t
            e = best_e
            if not avail[e]:
                free[e] = best_t
                fu, av = future[e], avail[e]
                while fu and fu[0][0] <= free[e]:
                    _, sq, o = heapq.heappop(fu)
                    heapq.heappush(av, (-bl[o], sq, o))
            _, sq, o = heapq.heappop(avail[e])
            st = free[e]
            o.st = st
            free[e] = st + o.busy
            fin = st + o.lat
            out.append(o)
            for s_ in succ[o]:
                if fin > rtime[s_]:
                    rtime[s_] = fin
                indeg[s_] -= 1
                if indeg[s_] == 0:
                    heapq.heappush(future[s_.eng], (rtime[s_], s_.seq, s_))
        return out

    def schedule(self, reorder=True):
        segs, cur = [], []
        for o in self.order:
            if o.is_bar:
                if cur:
                    segs.append(("seg", cur))
                    cur = []
                if segs and segs[-1][0] == "bar":
                    segs[-1][1].append(o)
                else:
                    segs.append(("bar", [o]))
            else:
                cur.append(o)
        if cur:
            segs.append(("seg", cur))
        new = []
        last_seg = []
        for kind, lst in segs:
            if kind == "seg":
                lst2 = self._list_schedule(lst) if reorder else lst
                new += lst2
                last_seg = lst2
            else:
                deps = [o for o in last_seg if o.is_dma and not o.nobar]
                for e in ENGS:
                    for o in reversed(last_seg):
                        if o.eng == e and not o.is_dma:
                            deps.append(o)
                            break
                for o in lst:
                    o.deps = set(deps)
                new += lst
        self.order = new
        self.ops = {e: [] for e in ENGS}
        self.dma_ops = {e: [] for e in ENGS}
        for o in new:
            o.idx = len(self.ops[o.eng])
            self.ops[o.eng].append(o)
            if o.is_dma:
                o.dma_i = len(self.dma_ops[o.eng])
                if o.dma_i >= self.n_dma_sems:
                    o.deps.add(self.dma_ops[o.eng][o.dma_i - self.n_dma_sems])
                self.dma_ops[o.eng].append(o)

    def resolve(self):
        known = {e: {f: -1 for f in ENGS} for e in ENGS}
        known_dma = {e: set() for e in ENGS}
        for op in self.order:
            e = op.eng
            kn = known[e]
            waits = []
            for d in sorted(op.deps, key=lambda o: -o.idx):
                if d.is_dma:
                    if d in known_dma[e]:
                        continue
                    known_dma[e].add(d)
                    waits.append(d)
                else:
                    if d.eng == "pe" and e == "pe":
                        continue
                    if kn[d.eng] >= d.idx:
                        continue
                    d.marked = True
                    waits.append(d)
                ck = d.clock
                for f in ENGS:
                    if ck[f] > kn[f]:
                        kn[f] = ck[f]
            op.waits = waits
            ck = dict(kn)
            if not op.is_dma:
                ck[e] = max(ck[e], op.idx)
            op.clock = ck
        for e in ENGS:
            c = 0
            for op in self.ops[e]:
                if op.marked:
                    c += 1
                    op.count = c

    def emit_all(self, block, esem, dsem):
        self.schedule(reorder=REORDER)
        self.resolve()
        n = self.n_dma_sems

        def run(e, engobj):
            for op in self.ops[e]:
                for d in op.waits:
                    if d.is_dma:
                        engobj.wait_ge(dsem[d.eng][d.dma_i % n], 16 * (d.dma_i // n + 1))
                    else:
                        engobj.wait_ge(esem[d.eng], d.count)
                ins = op.emit(engobj)
                if op.is_dma:
                    ins.then_inc(dsem[e][op.dma_i % n], 16)
                elif op.marked:
                    if ins is None:
                        ins = engobj.nop()
                    ins.then_inc(esem[e], 1)

        block.tensor(lambda t: run("pe", t))
        block.scalar(lambda t: run("act", t))
        block.vector(lambda t: run("dve", t))
        block.gpsimd(lambda t: run("pool", t))
        block.sync(lambda t: run("sp", t))


class Arena:
    def __init__(self, ap, ncols):
        self.ap = ap
        self.n = ncols
        self.off = 0

    def alloc(self, cols, dt=F32):
        nb = cols * (4 if dt in (F32, I32) else 2)
        n32 = ((nb + 31) // 32) * 8
        assert self.off + n32 <= self.n, ("arena overflow", self.off, n32, self.n)
        v = self.ap[:, self.off:self.off + n32]
        self.off += n32
        if dt != F32:
            v = v.bitcast(dt)
        return v[:, 0:cols]

    def reset(self):
        self.off = 0

    def alloc_at(self, off32, cols, dt=F32):
        save = self.off
        self.off = off32
        v = self.alloc(cols, dt)
        end = self.off
        self.off = save
        return v, end


def build_nc():
    nc = bass.Bass("TRN2", target_bir_lowering=False)

    def DI(name, shape, dt=F32):
        return nc.dram_tensor(name, shape, dt, kind="ExternalInput").ap()

    x_d = DI("x", [T, D])
    c_d = DI("cT", [128, 8])
    pos_d = DI("pos", [1, T], I32)
    wada_d = DI("w_ada", [D, 6 * D])
    bada_d = DI("b_ada", [1, 6 * D])
    gpre1_d = DI("gpre1", [128, 8])
    gpre2_d = DI("gpre2", [128, 8])
    gpost1_d = DI("gpost1", [1, D])
    gpost2_d = DI("gpost2", [1, D])
    qg_d = DI("qg", [128, 3])
    kvg_d = DI("kvg", [128, 2])
    og_d = DI("og", [128, 8])
    w1_d = DI("w1", [D, NC1])
    wq_d = DI("wq", [384, 1024])
    wkv_d = DI("wkv", [256, 1536])
    wout_d = DI("wout", [D, D])
    wg_d = DI("wg", [D, DFF])
    wu_d = DI("wu", [D, DFF])
    wd_d = DI("wd", [DFF, D])
    ident_d = DI("ident", [128, 128])
    tri_d = DI("tri", [128, 128])
    dt_d = DI("dtc", [128, 512])
    wqc_d = DI("wqc", [128, 1024])
    wkc_d = DI("wkc", [128, 256])
    dec_d = DI("decc", [128, 2])
    inv_d = DI("invc", [128, 3])
    ph_d = DI("phc", [128, 3])
    out_d = nc.dram_tensor("out", [T, D], F32, kind="ExternalOutput").ap()
    yret_d = nc.dram_tensor("yret_scr", [4, 128, T], BF16).ap()

    S = Sched(n_dma_sems=12)
    A = S.add

    with contextlib.ExitStack() as ctx:
        def sbt(name, cols, dt=F32, parts=128):
            return ctx.enter_context(nc.sbuf_tensor(name, [parts, cols], dt))

        identb = sbt("identb", 128, BF16)
        trib = sbt("trib", 128, BF16)
        onesb = sbt("onesb", 128, BF16)
        onesf = sbt("onesf", 128)
        epst = sbt("epst", 1)
        DECc = sbt("DECc", 2)
        INVc = sbt("INVc", 3)
        PHc = sbt("PHc", 3)
        modc = sbt("modc", 32)
        qg = sbt("qg_sb", 3)
        kvg = sbt("kvg_sb", 2)
        a1 = sbt("a1", 8)
        a2 = sbt("a2", 8)
        G1b = sbt("G1b", D)
        G2b = sbt("G2b", D)
        ARN = 50000
        arena_t = sbt("arena", ARN)
        AR = Arena(arena_t, ARN)
        P = [ctx.enter_context(nc.psum_tensor(f"bank{i}", [128, 512], F32)) for i in range(8)]
        Pb = [p[:, :].bitcast(BF16) for p in P]

        esem = {e: ctx.enter_context(nc.semaphore("es_" + e)) for e in ENGS}
        dsem = {e: [ctx.enter_context(nc.semaphore(f"ds_{e}{i}")) for i in range(12)] for e in ("sp", "pool")}

        def dma(q, out, in_, r=(), w=(), nobar=False):
            o = A(q, lambda g: g.dma_start(out=out, in_=in_), reads=r, writes=w, dma=True)
            o.nobar = nobar
            return o

        def tap(name, ap, keys):
            if name not in TAPS:
                return
            shp = list(ap.shape)
            dd = nc.dram_tensor("dbg_" + name, shp, ap.dtype, kind="ExternalOutput").ap()
            dma("sp", dd, ap, r=keys)

        def rsqrt_ops(dst, src, scale, rk, wk):
            A("act", lambda g: g.activation(out=dst, in_=src, func=AF.Sqrt, scale=scale, bias=epst[0:dst.shape[0], :]),
              reads=list(rk) + ["epst"], writes=[wk])
            A("dve", lambda g: g.reciprocal(out=dst, in_=dst), reads=[wk], writes=[wk])

        _phase = [0]
        try:
            dma("pool", identb[:, :], ident_d, w=["identb"])
            dma("pool", trib[:, :], tri_d, w=["trib"])
            A("pool", lambda g: g.memset(onesb[:, :], 1.0), writes=["onesb"])
            A("pool", lambda g: g.memset(onesf[:, :], 1.0), writes=["onesf"])
            A("pool", lambda g: g.memset(epst[:, :], EPS), writes=["epst"])
            for t_, d_, k_ in ((DECc, dec_d, "DECc"), (INVc, inv_d, "INVc"), (PHc, ph_d, "PHc")):
                dma("sp", t_[:, :], d_, w=[k_])
            cT = AR.alloc(8)
            gp1 = AR.alloc(8)
            gp2 = AR.alloc(8)
            scb = AR.alloc(8, BF16)
            gpo1 = AR.alloc(D)
            gpo2 = AR.alloc(D)
            bada = AR.alloc(6 * D)
            modrow = AR.alloc(6 * D)
            grow1 = AR.alloc(D)
            grow2 = AR.alloc(D)
            wa = [AR.alloc(8 * 1024, BF16), AR.alloc(8 * 1024, BF16)]
            dma("sp", cT, c_d, w=["cT"])
            dma("sp", qg[:, :], qg_d, w=["qg"])
            dma("sp", kvg[:, :], kvg_d, w=["kvg"])
            dma("sp", gp1, gpre1_d, w=["gp1"])
            dma("sp", gp2, gpre2_d, w=["gp2"])
            dma("sp", gpo1[0:1, :], gpost1_d, w=["gpo1"])
            dma("sp", gpo2[0:1, :], gpost2_d, w=["gpo2"])
            dma("sp", bada[0:1, :], bada_d, w=["bada"])
            A("act", lambda g: g.activation(out=scb, in_=cT, func=AF.Silu), reads=["cT"], writes=["scb"])
            for gd in range(6):
                wb = wa[gd % 2]
                dma("pool", wb.rearrange("p (c n) -> p c n", c=8),
                    wada_d[:, gd * 1024:(gd + 1) * 1024].rearrange("(c p) n -> p c n", p=128), w=[f"wa{gd % 2}"])
                for g2_ in range(2):
                    gi = gd * 2 + g2_

                    def mm_ada(g, gi=gi, wb=wb, g2_=g2_):
                        for k in range(8):
                            r = g.matmul(P[gi % 2][0:1, :], lhsT=scb[:, k:k + 1], rhs=wb[:, k * 1024 + g2_ * 512:k * 1024 + (g2_ + 1) * 512],
                                         start=(k == 0), stop=(k == 7))
                        return r
                    A("pe", mm_ada, reads=["scb", f"wa{gd % 2}"], writes=[f"P{gi % 2}"])
                    A("dve", lambda g, gi=gi: g.tensor_tensor(out=modrow[0:1, gi * 512:(gi + 1) * 512], in0=P[gi % 2][0:1, :],
                                                             in1=bada[0:1, gi * 512:(gi + 1) * 512], op=ALU.add),
                      reads=[f"P{gi % 2}", "bada"], writes=["modrow"])
            col_offs = [0 * D, 1 * D, 3 * D, 4 * D]

            def mm_cols(g):
                for vi, off in enumerate(col_offs):
                    for c in range(8):
                        r = g.matmul(P[2][:, vi * 8 + c:vi * 8 + c + 1], lhsT=modrow[0:1, off + c * 128:off + (c + 1) * 128],
                                     rhs=onesf[0:1, 0:1], start=True, stop=True)
                return r
            A("pe", mm_cols, reads=["modrow", "onesf"], writes=["P2"])
            A("dve", lambda g: g.tensor_copy(out=modc[:, :], in_=P[2][:, 0:32]), reads=["P2"], writes=["modc"])
            A("dve", lambda g: g.scalar_tensor_tensor(out=a1[:, :], in0=modc[:, 8:16], scalar=1.0, in1=gp1, op0=ALU.add, op1=ALU.mult),
              reads=["modc", "gp1"], writes=["a1"])
            A("dve", lambda g: g.scalar_tensor_tensor(out=a2[:, :], in0=modc[:, 24:32], scalar=1.0, in1=gp2, op0=ALU.add, op1=ALU.mult),
              reads=["modc", "gp2"], writes=["a2"])
            sh1 = modc[:, 0:8]
            sh2 = modc[:, 16:24]
            A("dve", lambda g: g.tensor_tensor(out=grow1[0:1, :], in0=modrow[0:1, 2 * D:3 * D], in1=gpo1[0:1, :], op=ALU.mult),
              reads=["modrow", "gpo1"], writes=["grow1"])
            A("dve", lambda g: g.tensor_tensor(out=grow2[0:1, :], in0=modrow[0:1, 5 * D:6 * D], in1=gpo2[0:1, :], op=ALU.mult),
              reads=["modrow", "gpo2"], writes=["grow2"])
            for gi, (grow, Gb, gk) in enumerate(((grow1, G1b, "G1b"), (grow2, G2b, "G2b"))):
                for hf in range(2):
                    bk = 3 + hf
                    A("pe", lambda g, grow=grow, hf=hf, bk=bk: g.matmul(P[bk][:, :], lhsT=onesf[0:1, 0:128],
                                                                        rhs=grow[0:1, hf * 512:(hf + 1) * 512], start=True, stop=True),
                      reads=[f"grow{gi + 1}", "onesf"], writes=[f"P{bk}"])
                    A("act", lambda g, Gb=Gb, hf=hf, bk=bk: g.activation(out=Gb[:, hf * 512:(hf + 1) * 512], in_=P[bk][:, :], func=AF.Copy),
                      reads=[f"P{bk}"], writes=[gk])
            S.barrier()
            _phase[0] += 1
            if _phase[0] > STOP:
                raise _Stop()
            AR.reset()

            cqnT = AR.alloc(3 * T, BF16)
            ckvnT = AR.alloc(2 * T, BF16)
            TABm = AR.alloc(T)
            kpeT = TABm[64:96, 0:2048].bitcast(BF16)
            P12 = AR.off
            DTc = AR.alloc(512)
            WQc = AR.alloc(1024)
            WKc = AR.alloc(256)
            dma("sp", DTc, dt_d, w=["DTc"])
            dma("sp", WQc, wqc_d, w=["WQc"])
            dma("sp", WKc, wkc_d, w=["WKc"])
            W1 = AR.alloc(8 * NC1, BF16)
            xt = [AR.alloc(D), AR.alloc(D)]
            xn = [AR.alloc(D, BF16), AR.alloc(D, BF16)]
            junk = AR.alloc(D, BF16)
            ssq = [AR.alloc(1), AR.alloc(1)]
            hT = AR.alloc(8 * 512, BF16)
            cqraw = AR.alloc(3 * 512)
            ckvraw = AR.alloc(2 * 512)
            sq = AR.alloc(3 * 512, BF16)
            sq2 = AR.alloc(2 * 512, BF16)
            Rq = AR.alloc(512)
            Rkv = Rq
            posi = AR.alloc(512, I32)
            posf = AR.alloc(512)
            ang = AR.alloc(512)
            ni = posi
            nf = AR.alloc(512)
            msk = nf
            Cr = AR.alloc(512)
            Sr = AR.alloc(512)
            t1 = [AR.alloc(512), AR.alloc(512)]
            t2 = [AR.alloc(512), AR.alloc(512)]
            rqT = AR.alloc(2 * 512, BF16)
            rkT = AR.alloc(2 * 512, BF16)
            qwT = AR.alloc(2 * 512, BF16)
            rqm = AR.alloc(2 * 512, BF16)
            qwm = AR.alloc(2 * 512, BF16)
            A("pool", lambda g: g.memset(rqm[64:128, :], 0.0), writes=["rqm"])
            A("pool", lambda g: g.memset(qwm[64:128, :], 0.0), writes=["qwm"])
            vtok = AR.alloc(4 * 512, BF16)
            sg = AR.alloc(4 * 512, BF16)
            kwtok = AR.alloc(256, BF16)
            scTm = AR.alloc(512, BF16)
            osb = AR.alloc(512)
            ynorm = AR.alloc(512)
            osq = ynorm
            ytok = AR.alloc(512, BF16)
            ysT = AR.alloc(4 * 512, BF16)
            Sf = AR.alloc(256)
            Sbf = AR.alloc(256, BF16)
            st = {k: AR.alloc(4) for k in ("osum", "osqs", "mean", "msq", "var", "rgn")}

            for hf in range(2):
                dma("pool", W1.rearrange("p (c n) -> p c n", c=8)[:, hf * 4:(hf + 1) * 4, :],
                    w1_d.rearrange("(c p) n -> p c n", p=128)[:, hf * 4:(hf + 1) * 4, :], w=[f"W1{hf}"])
            A("pool", lambda g: g.memset(Sf, 0.0), writes=["Sf"])
            A("pool", lambda g: g.memset(Sbf, 0.0), writes=["Sbf"])

            def ck(n):
                if SUB == n:
                    raise _Stop()
            ck(0)

            def w1s(c, off, n):
                return W1[:, c * NC1 + off:c * NC1 + off + n]

            def table(dst, dk, col, sbi):
                A("dve", lambda g: g.tensor_scalar(out=ang, in0=posf, scalar1=INVc[:, col:col + 1], scalar2=PHc[:, col:col + 1],
                                                   op0=ALU.mult, op1=ALU.add), reads=["posf", "INVc", "PHc"], writes=["ang"])
                A("dve", lambda g: g.tensor_scalar(out=ni, in0=ang, scalar1=float(1.0 / (2 * PI)), scalar2=None, op0=ALU.mult),
                  reads=["ang"], writes=["ibuf"])
                A("dve", lambda g: g.tensor_copy(out=nf, in_=ni), reads=["ibuf"], writes=["nf"])
                A("dve", lambda g: g.scalar_tensor_tensor(out=ang, in0=nf, scalar=-2 * PI, in1=ang, op0=ALU.mult, op1=ALU.add),
                  reads=["nf", "ang"], writes=["ang"])
                A("dve", lambda g: g.tensor_single_scalar(out=msk, in_=ang, scalar=PI, op=ALU.is_gt), reads=["ang", "nf"], writes=["nf"])
                A("dve", lambda g: g.scalar_tensor_tensor(out=ang, in0=msk, scalar=-2 * PI, in1=ang, op0=ALU.mult, op1=ALU.add),
                  reads=["nf", "ang"], writes=["ang"])
                A("dve", lambda g: g.tensor_scalar(out=ang, in0=ang, scalar1=-3.14159, scalar2=3.14159, op0=ALU.max, op1=ALU.min),
                  reads=["ang"], writes=["ang"])
                np_ = dst.shape[0]
                A("act", lambda g: g.activation(out=dst, in_=ang[0:np_, :], func=AF.Sin), reads=["ang"], writes=[dk])

            mtiles = [(0, 128, "cq", 0), (128, 128, "cq", 1), (256, 128, "cq", 2), (384, 128, "ckv", 0), (512, 128, "ckv", 1),
                      (640, 64, "kpe", 0)]
            o_ = 704
            for nm in ("rq", "rk"):
                for i in range(2):
                    mtiles.append((o_, 128, nm + "n", i))
                    mtiles.append((o_ + 128, 128, nm + "s", i))
                    o_ += 256
            RV = 1728
            RG = 2240

            for sbi in range(NSB):
                sc0 = sbi * 512
                dma("sp", posi, bass.AP(pos_d.tensor, sc0, [[0, 128], [1, 512]]), w=["ibuf"])
                A("dve", lambda g: g.tensor_copy(out=posf, in_=posi), reads=["ibuf"], writes=["posf"])
                table(TABm[0:64, sc0:sc0 + 512], "TABm", 0, sbi)
                table(Cr, "Cr", 1, sbi)
                table(Sr, "Sr", 2, sbi)
                ck(1)
                for j in range(4):
                    tb = sbi * 4 + j
                    b2 = tb % 2
                    dma("sp", xt[b2], x_d[tb * 128:(tb + 1) * 128, :], w=[f"xt{b2}"])
                    A("act", lambda g, b2=b2: g.activation(out=junk, in_=xt[b2], func=AF.Square, accum_out=ssq[b2]),
                      reads=[f"xt{b2}"], writes=["junk", f"ssq{b2}"])
                    rsqrt_ops(ssq[b2], ssq[b2], 1.0 / D, [f"ssq{b2}"], f"ssq{b2}")
                    A("act", lambda g, b2=b2: g.activation(out=xn[b2], in_=xt[b2], func=AF.Copy, scale=ssq[b2]),
                      reads=[f"xt{b2}", f"ssq{b2}"], writes=[f"xn{b2}"])

                    def tr8(g, b2=b2):
                        for c in range(8):
                            r = g.transpose(out=Pb[b2][:, c * 128:(c + 1) * 128], in_=xn[b2][:, c * 128:(c + 1) * 128], identity=identb[:, :])
                        return r
                    A("pe", tr8, reads=[f"xn{b2}", "identb"], writes=[f"P{b2}"])
                    for c in range(8):
                        dst = hT[:, c * 512 + j * 128:c * 512 + (j + 1) * 128]
                        if b2 == 0:
                            A("dve", lambda g, c=c, dst=dst, b2=b2: g.tensor_scalar(out=dst, in0=Pb[b2][:, c * 128:(c + 1) * 128],
                                                                                   scalar1=a1[:, c:c + 1], scalar2=sh1[:, c:c + 1],
                                                                                   op0=ALU.mult, op1=ALU.add),
                              reads=[f"P{b2}", "a1", "modc"], writes=["hT"])
                        else:
                            A("act", lambda g, c=c, dst=dst, b2=b2: g.activation(out=dst, in_=Pb[b2][:, c * 128:(c + 1) * 128], func=AF.Identity,
                                                                                scale=a1[:, c:c + 1], bias=sh1[:, c:c + 1]),
                              reads=[f"P{b2}", "a1", "modc"], writes=["hT"])
                ck(2)
                for mi, (off, M, kind, i) in enumerate(mtiles):
                    bk = 2 + mi % 2
                    pk = f"P{bk}"

                    def mmz(g, off=off, M=M, bk=bk):
                        for c in range(8):
                            r = g.matmul(P[bk][0:M, :], lhsT=w1s(c, off, M), rhs=hT[:, c * 512:(c + 1) * 512], start=(c == 0), stop=(c == 7))
                        return r
                    A("pe", mmz, reads=["W10", "W11", "hT"], writes=[pk])
                    if kind in ("cq", "ckv"):
                        raw, sqt, nt, Rt, bank, scl, dstT, rk_ = ((cqraw, sq, 3, Rq, 4, 1.0 / 384, cqnT, "Rq") if kind == "cq"
                                                                  else (ckvraw, sq2, 2, Rkv, 5, 1.0 / 256, ckvnT, "Rq"))
                        gcol = qg if kind == "cq" else kvg
                        A("act", lambda g, raw=raw, i=i, bk=bk, gcol=gcol: g.activation(out=raw[:, i * 512:(i + 1) * 512], in_=P[bk][:, :], func=AF.Copy,
                                                                                      scale=gcol[:, i:i + 1]),
                          reads=[pk, "qg", "kvg"], writes=[f"{kind}raw{i}"])
                        A("act", lambda g, sqt=sqt, i=i, bk=bk: g.activation(out=sqt[:, i * 512:(i + 1) * 512], in_=P[bk][:, :], func=AF.Square),
                          reads=[pk], writes=[f"{kind}sq{i}"])
                        if i == nt - 1:
                            def mmst(g, sqt=sqt, nt=nt, bank=bank):
                                for q in range(nt):
                                    r = g.matmul(P[bank][:, :], lhsT=onesb[:, :], rhs=sqt[:, q * 512:(q + 1) * 512], start=(q == 0), stop=(q == nt - 1))
                                return r
                            A("pe", mmst, reads=[f"{kind}sq{q}" for q in range(nt)] + ["onesb"], writes=[f"P{bank}"])
                            rsqrt_ops(Rt, P[bank][:, :], scl, [f"P{bank}"], rk_)
                            for q in range(nt):
                                A("pool", lambda g, raw=raw, Rt=Rt, q=q, dstT=dstT: g.tensor_tensor(
                                    out=dstT[:, q * T + sc0:q * T + sc0 + 512], in0=raw[:, q * 512:(q + 1) * 512], in1=Rt, op=ALU.mult),
                                  reads=[f"{kind}raw{q}", rk_], writes=[f"{kind}nT"])
                    elif kind == "kpe":
                        A("dve", lambda g, bk=bk: g.tensor_tensor(out=t1[0][0:32, :], in0=P[bk][0:32, :], in1=TABm[0:32, sc0:sc0 + 512], op=ALU.mult),
                          reads=[pk, "TABm"], writes=["t1_0"])
                        A("dve", lambda g, bk=bk: g.tensor_tensor(out=t2[0][0:32, :], in0=P[bk][32:64, :], in1=TABm[32:64, sc0:sc0 + 512], op=ALU.mult),
                          reads=[pk, "TABm"], writes=["t2_0"])
                        A("dve", lambda g: g.tensor_tensor(out=kpeT[:, sc0:sc0 + 512], in0=t1[0][0:32, :], in1=t2[0][0:32, :], op=ALU.add),
                          reads=["t1_0", "t2_0"], writes=["kpeT"])
                    else:
                        nm = kind[:2]
                        if kind[2] == "n":
                            A("dve", lambda g, bk=bk, i=i: g.tensor_tensor(out=t1[i], in0=P[bk][:, :], in1=Cr, op=ALU.mult),
                              reads=[pk, "Cr"], writes=[f"t1_{i}"])
                        else:
                            A("dve", lambda g, bk=bk, i=i: g.tensor_tensor(out=t2[i], in0=P[bk][:, :], in1=Sr, op=ALU.mult),
                              reads=[pk, "Sr"], writes=[f"t2_{i}"])
                            dstq = rqT if nm == "rq" else rkT
                            A("pool", lambda g, i=i, dstq=dstq: g.tensor_tensor(out=dstq[:, i * 512:(i + 1) * 512], in0=t1[i], in1=t2[i], op=ALU.add),
                              reads=[f"t1_{i}", f"t2_{i}"], writes=[nm + "T"])
                            if nm == "rq":
                                A("pool", lambda g, i=i: g.tensor_tensor(out=rqm[0:64, i * 512:(i + 1) * 512], in0=t1[i][0:64, :], in1=t2[i][0:64, :],
                                                                        op=ALU.add),
                                  reads=[f"t1_{i}", f"t2_{i}"], writes=["rqm"])
                                A("pool", lambda g, i=i: g.tensor_tensor(out=qwT[:, i * 512:(i + 1) * 512], in0=rqT[:, i * 512:(i + 1) * 512],
                                                                        in1=WQc[:, i * 512:(i + 1) * 512], op=ALU.mult),
                                  reads=["rqT", "WQc"], writes=["qwT"])
                                A("pool", lambda g, i=i: g.tensor_tensor(out=qwm[0:64, i * 512:(i + 1) * 512], in0=rqT[0:64, i * 512:(i + 1) * 512],
                                                                        in1=WQc[0:64, i * 512:(i + 1) * 512], op=ALU.mult),
                                  reads=["rqT", "WQc"], writes=["qwm"])
                ck(3)
                for j in range(4):
                    for which, off, bank in (("v", RV, 4), ("g", RG, 5)):
                        def mmt(g, j=j, off=off, bank=bank):
                            for c in range(8):
                                r = g.matmul(P[bank][:, :], lhsT=hT[:, c * 512 + j * 128:c * 512 + (j + 1) * 128], rhs=w1s(c, off, 512),
                                             start=(c == 0), stop=(c == 7))
                            return r
                        A("pe", mmt, reads=["W10", "W11", "hT"], writes=[f"P{bank}"])
                        if which == "v":
                            A("act", lambda g, j=j: g.activation(out=vtok[:, j * 512:(j + 1) * 512], in_=P[4][:, :], func=AF.Copy),
                              reads=["P4"], writes=["vtok"])
                        else:
                            A("act", lambda g, j=j: g.activation(out=sg[:, j * 512:(j + 1) * 512], in_=P[5][:, :], func=AF.Silu),
                              reads=["P5"], writes=["sg"])
                ck(4)
                for j in range(4):
                    jc = slice(j * 128, (j + 1) * 128)

                    def trk(g, j=j):
                        for i in range(2):
                            r = g.transpose(out=Pb[5][:, i * 128:(i + 1) * 128], in_=rkT[:, i * 512 + j * 128:i * 512 + (j + 1) * 128], identity=identb[:, :])
                        return r
                    A("pe", trk, reads=["rkT", "identb"], writes=["P5"])
                    A("dve", lambda g: g.tensor_tensor(out=kwtok, in0=Pb[5][:, 0:256], in1=WKc[:, :], op=ALU.mult),
                      reads=["P5", "WKc"], writes=["kwtok"])

                    ck(6)

                    def mmsc(g, j=j):
                        for h in range(4):
                            i, r0 = h // 2, 64 * (h % 2)
                            cs = slice(i * 512 + j * 128, i * 512 + (j + 1) * 128)
                            if r0 == 0:
                                r = g.matmul(P[6][:, h * 128:(h + 1) * 128], lhsT=rkT[:, cs], rhs=rqm[:, cs], start=True, stop=True)
                            else:
                                r = g.matmul(P[6][:, h * 128:(h + 1) * 128], lhsT=rkT[64:128, cs], rhs=rqT[64:128, cs], start=True, stop=True,
                                             tile_position=(64, 0))
                        return r
                    A("pe", mmsc, reads=["rkT", "rqT", "rqm"], writes=["P6"])
                    A("dve", lambda g: g.tensor_tensor(out=scTm, in0=P[6][:, :], in1=DTc[:, :], op=ALU.mult), reads=["P6", "DTc"], writes=["scTm"])

                    ck(7)

                    def mmo(g, j=j):
                        for h in range(4):
                            i, r0 = h // 2, 64 * (h % 2)
                            cs = slice(i * 512 + j * 128, i * 512 + (j + 1) * 128)
                            g.matmul(P[7][:, h * 128:(h + 1) * 128], lhsT=scTm[:, h * 128:(h + 1) * 128],
                                     rhs=vtok[:, j * 512 + h * 128:j * 512 + (h + 1) * 128], start=True, stop=False)
                            if r0 == 0:
                                r = g.matmul(P[7][:, h * 128:(h + 1) * 128], lhsT=qwm[:, cs], rhs=Sbf[:, i * 128:(i + 1) * 128], start=False, stop=True)
                            else:
                                r = g.matmul(P[7][:, h * 128:(h + 1) * 128], lhsT=qwT[64:128, cs], rhs=Sbf[64:128, i * 128:(i + 1) * 128],
                                             start=False, stop=True, tile_position=(64, 0))
                        return r
                    A("pe", mmo, reads=["scTm", "vtok", "qwT", "qwm", "Sbf"], writes=["P7"])
                    A("act", lambda g: g.activation(out=osb, in_=P[7][:, :], func=AF.Copy), reads=["P7"], writes=["osb"])

                    ck(8)

                    def mmu(g, j=j):
                        for h in range(4):
                            i, r0 = h // 2, 64 * (h % 2)
                            kw = dict(tile_position=(0, 64)) if r0 else {}
                            r = g.matmul(P[5][r0:r0 + 64, 256 + i * 128:256 + (i + 1) * 128], lhsT=kwtok[:, h * 64:(h + 1) * 64],
                                         rhs=vtok[:, j * 512 + h * 128:j * 512 + (h + 1) * 128], start=True, stop=True, **kw)
                        return r
                    A("pe", mmu, reads=["kwtok", "vtok"], writes=["P5"])
                    for i in range(2):
                        A("dve", lambda g, i=i: g.scalar_tensor_tensor(out=Sf[:, i * 128:(i + 1) * 128], in0=Sf[:, i * 128:(i + 1) * 128],
                                                                      scalar=DECc[:, i:i + 1], in1=P[5][:, 256 + i * 128:256 + (i + 1) * 128],
                                                                      op0=ALU.mult, op1=ALU.add),
                          reads=["P5", "DECc", "Sf"], writes=["Sf"])
                    A("pool", lambda g: g.tensor_copy(out=Sbf, in_=Sf), reads=["Sf"], writes=["Sbf"])
                    ck(9)
                    o3 = osb.rearrange("p (h v) -> p h v", h=4)
                    A("dve", lambda g, o3=o3: g.reduce_sum(out=st["osum"], in_=o3, axis=AX.X), reads=["osb"], writes=["osum"])
                    A("pool", lambda g: g.tensor_tensor(out=osq, in0=osb, in1=osb, op=ALU.mult), reads=["osb"], writes=["ynorm"])
                    A("dve", lambda g: g.reduce_sum(out=st["osqs"], in_=osq.rearrange("p (h v) -> p h v", h=4), axis=AX.X),
                      reads=["ynorm"], writes=["osqs"])
                    A("dve", lambda g: g.tensor_scalar(out=st["mean"], in0=st["osum"], scalar1=1.0 / 128, scalar2=None, op0=ALU.mult),
                      reads=["osum"], writes=["mean"])
                    A("dve", lambda g: g.tensor_tensor(out=st["msq"], in0=st["mean"], in1=st["mean"], op=ALU.mult), reads=["mean"], writes=["msq"])
                    A("dve", lambda g: g.scalar_tensor_tensor(out=st["var"], in0=st["osqs"], scalar=1.0 / 128, in1=st["msq"], op0=ALU.mult, op1=ALU.subtract),
                      reads=["osqs", "msq"], writes=["var"])
                    rsqrt_ops(st["rgn"], st["var"], 1.0, ["var"], "rgn")
                    for h in range(4):
                        A("dve", lambda g, h=h: g.tensor_scalar(out=ynorm[:, h * 128:(h + 1) * 128], in0=osb[:, h * 128:(h + 1) * 128],
                                                                scalar1=st["mean"][:, h:h + 1], scalar2=st["rgn"][:, h:h + 1],
                                                                op0=ALU.subtract, op1=ALU.mult),
                          reads=["osb", "mean", "rgn"], writes=["ynorm"])
                    A("pool", lambda g, j=j: g.tensor_tensor(out=ytok, in0=ynorm, in1=sg[:, j * 512:(j + 1) * 512], op=ALU.mult),
                      reads=["ynorm", "sg"], writes=["ytok"])

                    ck(10)

                    def try_(g):
                        for t in range(4):
                            r = g.transpose(out=Pb[4][:, t * 128:(t + 1) * 128], in_=ytok[:, t * 128:(t + 1) * 128], identity=identb[:, :])
                        return r
                    A("pe", try_, reads=["ytok", "identb"], writes=["P4"])
                    A("act", lambda g, j=j: g.activation(out=ysT.rearrange("p (t n) -> p t n", t=4)[:, :, j * 128:(j + 1) * 128],
                                                         in_=Pb[4][:, 0:512].rearrange("p (t n) -> p t n", t=4), func=AF.Copy),
                      reads=["P4"], writes=["ysT"])
                ck(5)
                dma("sp", yret_d[:, :, sc0:sc0 + 512].rearrange("t p n -> p t n"), ysT.rearrange("p (t n) -> p t n", t=4),
                    r=["ysT"], w=["yret_d"])
            tap("cqnT", cqnT, ["cqnT"])
            tap("ckvnT", ckvnT, ["ckvnT"])
            tap("kpeT", kpeT, ["kpeT"])
            tap("TABm", TABm, ["TABm"])
            tap("yret", yret_d, ["yret_d"])
            tap("hT", hT, ["hT"])
            tap("cqraw", cqraw, ["cqraw0", "cqraw1", "cqraw2"])
            tap("Rq", Rq, ["Rq"])
            tap("sq", sq, ["cqsq0", "cqsq1", "cqsq2"])
            tap("rqT", rqT, ["rqT"])
            tap("rkT", rkT, ["rkT"])
            tap("osb", osb, ["osb"])
            tap("ytok", ytok, ["ytok"])
            tap("Sf", Sf, ["Sf"])
            S.barrier()
            _phase[0] += 1
            if _phase[0] > STOP:
                raise _Stop()
            AR.off = P12

            ymlaT = AR.alloc(4 * T, BF16)
            P23 = AR.off
            Wq = AR.alloc(3 * 1024, BF16)
            Wkv = AR.alloc(2 * 1536, BF16)
            WU_OFF = ARN - (8 * DFF * 2) // 4
            Wu, _ = AR.alloc_at(WU_OFF, 8 * DFF, BF16)
            WU_PRE = 2
            dma("pool", Wq.rearrange("p (c n) -> p c n", c=3), wq_d.rearrange("(c p) n -> p c n", p=128), w=["Wq"])
            dma("pool", Wkv.rearrange("p (c n) -> p c n", c=2), wkv_d.rearrange("(c p) n -> p c n", p=128), w=["Wkv"])
            KT = [AR.alloc(T, BF16), AR.alloc(T, BF16)]
            QT = [AR.alloc(T, BF16), AR.alloc(T, BF16)]
            Vg = [AR.alloc(32 * 128, BF16), AR.alloc(32 * 128, BF16)]
            PT = [AR.alloc(GS * 512, BF16) for _ in range(NPT)]
            NSLOT = 4 // GS
            it_i = 0
            u1 = AR.alloc(512)
            u2 = AR.alloc(512)
            rec = AR.alloc(512)
            assert AR.off <= WU_OFF + (WU_PRE * DFF * 2) // 4, (AR.off, WU_OFF)
            dma("pool", Wu.rearrange("p (c n) -> p c n", c=8)[:, WU_PRE:8, :],
                wu_d.rearrange("(c p) n -> p c n", p=128)[:, WU_PRE:8, :], w=["Wu1"], nobar=True)
            for b in range(2):
                A("dve", lambda g, b=b: g.memset(KT[b][0:64, :], 0.0), writes=[f"KT{b}"])
                A("pool", lambda g, b=b: g.memset(QT[b][0:64, :], 0.0), writes=[f"QT{b}"])
                A("dve", lambda g, b=b: g.memset(Vg[b].rearrange("p (k v) -> p k v", v=128)[:, :, 64:128], 1.0), writes=[f"Vg{b}"])
            SCALE = float(96 ** -0.5)
            pt_i = 0
            for h in range(8):
                hb = h % 2
                kk, qk, vk = f"KT{hb}", f"QT{hb}", f"Vg{hb}"
                A("dve", lambda g, hb=hb: g.tensor_copy(out=KT[hb][0:32, :], in_=kpeT[:, :]), reads=["kpeT"], writes=[kk])
                for sbi in range(NSB):
                    sc0 = sbi * 512

                    def mmk(g, h=h, sc0=sc0):
                        for c in range(2):
                            r = g.matmul(P[6][:, :], lhsT=Wkv[:, c * 1536 + h * 128:c * 1536 + (h + 1) * 128], rhs=ckvnT[:, c * T + sc0:c * T + sc0 + 512],
                                         start=(c == 0), stop=(c == 1))
                        return r
                    A("pe", mmk, reads=["Wkv", "ckvnT"], writes=["P6"])
                    A("dve", lambda g, hb=hb, sc0=sc0: g.tensor_copy(out=KT[hb][64:128, sc0:sc0 + 512], in_=P[6][64:128, :]),
                      reads=["P6"], writes=[kk])

                    def mmq(g, h=h, sc0=sc0):
                        for c in range(3):
                            r = g.matmul(P[7][:, :], lhsT=Wq[:, c * 1024 + h * 128:c * 1024 + (h + 1) * 128], rhs=cqnT[:, c * T + sc0:c * T + sc0 + 512],
                                         start=(c == 0), stop=(c == 2))
                        return r
                    A("pe", mmq, reads=["Wq", "cqnT"], writes=["P7"])
                    A("dve", lambda g, sc0=sc0: g.tensor_tensor(out=u1[0:32, :], in0=P[7][0:32, :], in1=TABm[0:32, sc0:sc0 + 512], op=ALU.mult),
                      reads=["P7", "TABm"], writes=["u1"])
                    A("dve", lambda g, sc0=sc0: g.tensor_tensor(out=u2[0:32, :], in0=P[7][32:64, :], in1=TABm[32:64, sc0:sc0 + 512], op=ALU.mult),
                      reads=["P7", "TABm"], writes=["u2"])
                    A("pool", lambda g, hb=hb, sc0=sc0: g.tensor_tensor(out=QT[hb][0:32, sc0:sc0 + 512], in0=u1[0:32, :], in1=u2[0:32, :], op=ALU.add),
                      reads=["u1", "u2"], writes=[qk])
                    A("dve", lambda g, hb=hb, sc0=sc0: g.tensor_copy(out=QT[hb][64:128, sc0:sc0 + 512], in_=P[7][64:128, :]),
                      reads=["P7"], writes=[qk])
                for k8 in range(4):
                    def mmv(g, h=h, k8=k8):
                        for q in range(8):
                            kb = k8 * 8 + q
                            for c in range(2):
                                r = g.matmul(P[6][:, q * 64:(q + 1) * 64], lhsT=ckvnT[:, c * T + kb * 128:c * T + (kb + 1) * 128],
                                             rhs=Wkv[:, c * 1536 + 1024 + h * 64:c * 1536 + 1024 + (h + 1) * 64], start=(c == 0), stop=(c == 1))
                        return r
                    A("pe", mmv, reads=["Wkv", "ckvnT"], writes=["P6"])
                    A("dve", lambda g, hb=hb, k8=k8: g.tensor_copy(
                        out=Vg[hb].rearrange("p (k v) -> p k v", v=128)[:, k8 * 8:(k8 + 1) * 8, 0:64],
                        in_=P[6][:, :].rearrange("p (k v) -> p k v", v=64)), reads=["P6"], writes=[vk])
                for qi, qs in enumerate(QS_ORDER):
                    q0 = qs * 512
                    acc = 4 + qi % 2
                    ak = f"P{acc}"
                    nfull = 4 * qs
                    groups = [(kb, min(kb + GS, nfull)) for kb in range(0, nfull, GS)]
                    items = [("full", a, b) for a, b in groups] + [("diag", 4 * qs + d, d) for d in range(4)]
                    last_kb = 4 * qs + 3
                    for gi, it in enumerate(items):
                        slot = it_i % NSLOT
                        it_i += 1
                        sbank = GS * slot
                        sk = f"PS{slot}"
                        pt = PT[pt_i % NPT]
                        ptk = f"PT{pt_i % NPT}"
                        pt_i += 1
                        if it[0] == "full":
                            kbs = list(range(it[1], it[2]))

                            def mms(g, hb=hb, kbs=kbs, sbank=sbank, q0=q0):
                                for n_, kb in enumerate(kbs):
                                    r = g.matmul(P[sbank + n_][:, :], lhsT=KT[hb][:, kb * 128:(kb + 1) * 128], rhs=QT[hb][:, q0:q0 + 512],
                                                 start=True, stop=True)
                                return r
                            A("pe", mms, reads=[kk, qk], writes=[sk])
                            for n_ in range(len(kbs)):
                                A("act", lambda g, pt=pt, sbank=sbank, n_=n_: g.activation(out=pt[:, n_ * 512:(n_ + 1) * 512], in_=P[sbank + n_][:, :],
                                                                                            func=AF.Exp, scale=SCALE),
                                  reads=[sk], writes=[ptk])

                            def mmpv(g, hb=hb, kbs=kbs, pt=pt, acc=acc, last_kb=last_kb):
                                for n_, kb in enumerate(kbs):
                                    r = g.matmul(P[acc][:, :], lhsT=Vg[hb][:, kb * 128:(kb + 1) * 128], rhs=pt[:, n_ * 512:(n_ + 1) * 512],
                                                 start=(kb == 0), stop=(kb == last_kb))
                                return r
                            A("pe", mmpv, reads=[vk, ptk], writes=[ak])
                        else:
                            kb, d = it[1], it[2]
                            c0 = d * 128
                            A("pe", lambda g, hb=hb, kb=kb, c0=c0, sbank=sbank, q0=q0: g.matmul(
                                P[sbank][:, c0:512], lhsT=KT[hb][:, kb * 128:(kb + 1) * 128], rhs=QT[hb][:, q0 + c0:q0 + 512], start=True, stop=True),
                              reads=[kk, qk], writes=[sk])
                            A("act", lambda g, pt=pt, sbank=sbank, c0=c0: g.activation(out=pt[:, c0:512], in_=P[sbank][:, c0:512], func=AF.Exp, scale=SCALE),
                              reads=[sk], writes=[ptk])
                            A("pool", lambda g, pt=pt, c0=c0: g.tensor_tensor(out=pt[:, c0:c0 + 128], in0=pt[:, c0:c0 + 128], in1=trib[:, :], op=ALU.mult),
                              reads=[ptk, "trib"], writes=[ptk])
                            A("pe", lambda g, hb=hb, kb=kb, c0=c0, pt=pt, acc=acc, last_kb=last_kb: g.matmul(
                                P[acc][:, c0:512], lhsT=Vg[hb][:, kb * 128:(kb + 1) * 128], rhs=pt[:, c0:512], start=(kb == 0), stop=(kb == last_kb)),
                              reads=[vk, ptk], writes=[ak])
                    A("dve", lambda g, acc=acc: g.reciprocal(out=rec[0:64, :], in_=P[acc][64:128, :]), reads=[ak], writes=["rec"])
                    r0 = 64 * (h % 2)
                    A("dve", lambda g, acc=acc, r0=r0, h=h, q0=q0: g.tensor_tensor(
                        out=ymlaT[r0:r0 + 64, (h // 2) * T + q0:(h // 2) * T + q0 + 512], in0=P[acc][0:64, :], in1=rec[0:64, :], op=ALU.mult),
                      reads=[ak, "rec"], writes=["ymlaT"])
            S.barrier()
            _phase[0] += 1
            if _phase[0] > STOP:
                raise _Stop()

            AR.off = P23
            Wg, wg_end = AR.alloc_at(0, 8 * DFF, BF16)
            assert wg_end <= P12
            dma("pool", Wu.rearrange("p (c n) -> p c n", c=8)[:, 0:WU_PRE, :],
                wu_d.rearrange("(c p) n -> p c n", p=128)[:, 0:WU_PRE, :], w=["Wu0"], nobar=True)
            for hf in range(2):
                dma("pool", Wg.rearrange("p (c n) -> p c n", c=8)[:, hf * 4:(hf + 1) * 4, :],
                    wg_d.rearrange("(c p) n -> p c n", p=128)[:, hf * 4:(hf + 1) * 4, :], w=[f"Wg{hf}"], nobar=True)
            Wo = AR.alloc(8 * D, BF16)
            og = AR.alloc(8)
            yrTs = [AR.alloc(4 * 512, BF16), AR.alloc(4 * 512, BF16)]
            ysq = [AR.alloc(4 * 128, BF16), AR.alloc(4 * 128, BF16)]
            xt3 = [AR.alloc(D), AR.alloc(D)]
            x1 = [AR.alloc(D), AR.alloc(D)]
            mB = [AR.alloc(D)]
            mixs = [AR.alloc(D)]
            tt = [AR.alloc(D)]
            rm = [AR.alloc(1), AR.alloc(1)]
            r2 = [AR.alloc(1), AR.alloc(1)]
            hole = wg_end
            for lst in (mB, mixs, tt):
                v_, hole = AR.alloc_at(hole, D)
                lst.append(v_)
            assert hole <= P12
            dma("sp", og, og_d, w=["og"])
            for hf in range(2):
                dma("pool", Wo.rearrange("p (c n) -> p c n", c=8)[:, hf * 4:(hf + 1) * 4, :],
                    wout_d.rearrange("(c p) n -> p c n", p=128)[:, hf * 4:(hf + 1) * 4, :], w=[f"Wo{hf}"])
            for c in range(8):
                A("dve", lambda g, c=c: g.tensor_scalar(out=Wo[:, c * D:(c + 1) * D], in0=Wo[:, c * D:(c + 1) * D], scalar1=og[:, c:c + 1],
                                                         scalar2=None, op0=ALU.mult), reads=[f"Wo{c // 4}", "og"], writes=[f"Wo{c // 4}"])
            for tb in range(32):
                b2 = tb % 2
                tc0 = tb * 128
                sbi, j = tb // 4, tb % 4
                yb = yrTs[sbi % 2]
                ybk = f"yrT{sbi % 2}"
                pa = (0, 1) if b2 == 0 else (4, 5)
                pak = [f"P{pa[0]}", f"P{pa[1]}"]
                stb = 6 + b2
                if j == 0:
                    dma("sp", yb.rearrange("p (t n) -> p t n", t=4), yret_d[:, :, sbi * 512:(sbi + 1) * 512].rearrange("t p n -> p t n"),
                        r=["yret_d"], w=[ybk])
                dma("sp", xt3[b2], x_d[tc0:tc0 + 128, :], w=[f"xt3{b2}"])
                A("pool", lambda g, tc0=tc0, b2=b2: g.tensor_tensor(out=ysq[b2].rearrange("p (c n) -> p c n", c=4),
                                                                     in0=ymlaT.rearrange("p (c n) -> p c n", c=4)[:, :, tc0:tc0 + 128],
                                                                     in1=ymlaT.rearrange("p (c n) -> p c n", c=4)[:, :, tc0:tc0 + 128], op=ALU.mult),
                  reads=["ymlaT"], writes=[f"ysq{b2}"])

                def mmss(g, b2=b2, stb=stb):
                    for c in range(4):
                        r = g.matmul(P[stb][:, 0:1], lhsT=ysq[b2][:, c * 128:(c + 1) * 128], rhs=onesb[:, 0:1], start=(c == 0), stop=(c == 3))
                    return r
                A("pe", mmss, reads=[f"ysq{b2}", "onesb"], writes=[f"P{stb}"])
                rsqrt_ops(rm[b2], P[stb][:, 0:1], 1.0 / 512, [f"P{stb}"], f"rm{b2}")

                def mmA(g, tc0=tc0, pa=pa):
                    for hf in range(2):
                        for c in range(4):
                            r = g.matmul(P[pa[hf]][:, :], lhsT=ymlaT[:, c * T + tc0:c * T + tc0 + 128], rhs=Wo[:, c * D + hf * 512:c * D + (hf + 1) * 512],
                                         start=(c == 0), stop=(c == 3))
                    return r
                A("pe", mmA, reads=["ymlaT", "Wo0"], writes=pak)

                def mmB(g, yb=yb, j=j):
                    for hf in range(2):
                        for c in range(4):
                            r = g.matmul(P[2 + hf][:, :], lhsT=yb[:, c * 512 + j * 128:c * 512 + (j + 1) * 128],
                                         rhs=Wo[:, (4 + c) * D + hf * 512:(4 + c) * D + (hf + 1) * 512], start=(c == 0), stop=(c == 3))
                    return r
                A("pe", mmB, reads=[ybk, "Wo1"], writes=["PB"])
                for hf in range(2):
                    A("act", lambda g, hf=hf, b2=b2: g.activation(out=mB[b2][:, hf * 512:(hf + 1) * 512], in_=P[2 + hf][:, :], func=AF.Copy),
                      reads=["PB"], writes=[f"mB{b2}"])
                    A("dve", lambda g, hf=hf, b2=b2, pa=pa: g.scalar_tensor_tensor(out=mixs[b2][:, hf * 512:(hf + 1) * 512], in0=P[pa[hf]][:, :],
                                                                                scalar=rm[b2][:, 0:1], in1=mB[b2][:, hf * 512:(hf + 1) * 512],
                                                                                op0=ALU.mult, op1=ALU.add),
                      reads=[pak[hf], f"rm{b2}", f"mB{b2}"], writes=[f"mixs{b2}"])
                A("act", lambda g, b2=b2: g.activation(out=tt[b2].bitcast(BF16)[:, 0:D], in_=mixs[b2], func=AF.Square, accum_out=r2[b2]),
                  reads=[f"mixs{b2}"], writes=[f"tt{b2}", f"r2{b2}"])
                rsqrt_ops(r2[b2], r2[b2], 1.0 / D, [f"r2{b2}"], f"r2{b2}")
                A("dve", lambda g, b2=b2: g.scalar_tensor_tensor(out=tt[b2], in0=mixs[b2], scalar=r2[b2][:, 0:1], in1=G1b[:, :], op0=ALU.mult, op1=ALU.mult),
                  reads=[f"mixs{b2}", f"r2{b2}", "G1b"], writes=[f"tt{b2}"])
                A("pool", lambda g, b2=b2: g.tensor_tensor(out=x1[b2], in0=xt3[b2], in1=tt[b2], op=ALU.add), reads=[f"xt3{b2}", f"tt{b2}"], writes=[f"x1{b2}"])
                dma("sp", out_d[tc0:tc0 + 128, :], x1[b2], r=[f"x1{b2}"], w=[f"out{tb}"])
            assert AR.off <= WU_OFF, (AR.off, WU_OFF)
            S.barrier()
            _phase[0] += 1
            if _phase[0] > STOP:
                raise _Stop()

            Wd, wd_end = AR.alloc_at(wg_end, NJ * D, BF16)
            assert wd_end <= P23
            AR.off = P23
            xa = [AR.alloc(D), AR.alloc(D)]
            xb = [AR.alloc(D), AR.alloc(D)]
            xn4 = [AR.alloc(D, BF16), AR.alloc(D, BF16)]
            junk4 = AR.alloc(D, BF16)
            junk5 = junk4
            s4 = [AR.alloc(1), AR.alloc(1)]
            h2T = AR.alloc(8 * 512, BF16)
            h1T = AR.alloc(NJ * 512, BF16)
            sgt = [AR.alloc(512, BF16), AR.alloc(512, BF16)]
            t4 = [AR.alloc(D), AR.alloc(D)]
            r3 = [AR.alloc(1), AR.alloc(1)]
            assert AR.off <= WU_OFF, (AR.off, WU_OFF)
            for hf in range(2):
                dma("pool", Wd.rearrange("p (c n) -> p c n", c=NJ)[:, hf * 11:(hf + 1) * 11, :],
                    wd_d.rearrange("(c p) n -> p c n", p=128)[:, hf * 11:(hf + 1) * 11, :], w=[f"Wd{hf}"])
            fin = []
            for sbi in range(NSB):
                for j in range(4):
                    tb = sbi * 4 + j
                    b2 = tb % 2
                    xv = xa[b2]
                    dma("sp", xv, out_d[tb * 128:(tb + 1) * 128, :], r=[f"out{tb}"], w=[f"xa{b2}"])
                    A("act", lambda g, xv=xv, b2=b2: g.activation(out=xn4[b2], in_=xv, func=AF.Square, accum_out=s4[b2]),
                      reads=[f"xa{b2}"], writes=[f"xn4{b2}", f"s4{b2}"])
                    rsqrt_ops(s4[b2], s4[b2], 1.0 / D, [f"s4{b2}"], f"s4{b2}")
                    A("act", lambda g, xv=xv, b2=b2: g.activation(out=xn4[b2], in_=xv, func=AF.Copy, scale=s4[b2]),
                      reads=[f"xa{b2}", f"s4{b2}"], writes=[f"xn4{b2}"])

                    def tr8b(g, b2=b2):
                        for c in range(8):
                            r = g.transpose(out=Pb[b2][:, c * 128:(c + 1) * 128], in_=xn4[b2][:, c * 128:(c + 1) * 128], identity=identb[:, :])
                        return r
                    A("pe", tr8b, reads=[f"xn4{b2}", "identb"], writes=[f"P{b2}"])
                    for c in range(8):
                        dst = h2T[:, c * 512 + j * 128:c * 512 + (j + 1) * 128]
                        if b2 == 0:
                            A("dve", lambda g, c=c, dst=dst, b2=b2: g.tensor_scalar(out=dst, in0=Pb[b2][:, c * 128:(c + 1) * 128],
                                                                                   scalar1=a2[:, c:c + 1], scalar2=sh2[:, c:c + 1],
                                                                                   op0=ALU.mult, op1=ALU.add),
                              reads=[f"P{b2}", "a2", "modc"], writes=["h2T"])
                        else:
                            A("act", lambda g, c=c, dst=dst, b2=b2: g.activation(out=dst, in_=Pb[b2][:, c * 128:(c + 1) * 128], func=AF.Identity,
                                                                                scale=a2[:, c:c + 1], bias=sh2[:, c:c + 1]),
                              reads=[f"P{b2}", "a2", "modc"], writes=["h2T"])
                for jj in range(NJ):
                    gb = 2 + jj % 2
                    ub = 4 + jj % 2

                    def mmg(g, jj=jj, gb=gb, ub=ub):
                        for c in range(8):
                            g.matmul(P[gb][:, :], lhsT=Wg[:, c * DFF + jj * 128:c * DFF + (jj + 1) * 128], rhs=h2T[:, c * 512:(c + 1) * 512],
                                     start=(c == 0), stop=(c == 7))
                        for c in range(8):
                            r = g.matmul(P[ub][:, :], lhsT=Wu[:, c * DFF + jj * 128:c * DFF + (jj + 1) * 128], rhs=h2T[:, c * 512:(c + 1) * 512],
                                         start=(c == 0), stop=(c == 7))
                        return r
                    A("pe", mmg, reads=["Wg0", "Wg1", "Wu0", "Wu1", "h2T"], writes=[f"P{gb}", f"P{ub}"])
                    A("act", lambda g, jj=jj, gb=gb: g.activation(out=sgt[jj % 2], in_=P[gb][:, :], func=AF.Silu), reads=[f"P{gb}"], writes=[f"sgt{jj % 2}"])
                    A("dve", lambda g, jj=jj, ub=ub: g.tensor_tensor(out=h1T[:, jj * 512:(jj + 1) * 512], in0=P[ub][:, :], in1=sgt[jj % 2], op=ALU.mult),
                      reads=[f"P{ub}", f"sgt{jj % 2}"], writes=["h1T"])
                for j in range(4):
                    tb = sbi * 4 + j
                    b2 = tb % 2
                    xv = xb[b2]
                    tv = t4[b2]
                    dma("sp", xv, out_d[tb * 128:(tb + 1) * 128, :], r=[f"out{tb}"], w=[f"xb{b2}"])

                    fb = (6, 7) if j % 2 == 0 else (3, 5)
                    fk_ = [f"P{fb[0]}", f"P{fb[1]}"]

                    def mmd(g, j=j, fb=fb):
                        for hf in range(2):
                            for jj in range(NJ):
                                r = g.matmul(P[fb[hf]][:, :], lhsT=h1T[:, jj * 512 + j * 128:jj * 512 + (j + 1) * 128],
                                             rhs=Wd[:, jj * D + hf * 512:jj * D + (hf + 1) * 512], start=(jj == 0), stop=(jj == NJ - 1))
                        return r
                    A("pe", mmd, reads=["h1T", "Wd0", "Wd1"], writes=fk_)
                    A("act", lambda g, tv=tv, fb=fb: g.activation(out=tv[:, 0:512], in_=P[fb[0]][:, :], func=AF.Copy), reads=[fk_[0]], writes=[f"t4{b2}"])
                    A("dve", lambda g, tv=tv, fb=fb: g.tensor_copy(out=tv[:, 512:1024], in_=P[fb[1]][:, :]), reads=[fk_[1]], writes=[f"t4{b2}"])
                    A("act", lambda g, tv=tv, b2=b2: g.activation(out=junk5, in_=tv, func=AF.Square, accum_out=r3[b2]), reads=[f"t4{b2}"], writes=["junk4", f"r3{b2}"])
                    rsqrt_ops(r3[b2], r3[b2], 1.0 / D, [f"r3{b2}"], f"r3{b2}")
                    A("dve", lambda g, tv=tv, b2=b2: g.scalar_tensor_tensor(out=tv, in0=tv, scalar=r3[b2][:, 0:1], in1=G2b[:, :], op0=ALU.mult, op1=ALU.mult),
                      reads=[f"t4{b2}", f"r3{b2}", "G2b"], writes=[f"t4{b2}"])
                    A("pool", lambda g, xv=xv, tv=tv: g.tensor_tensor(out=xv, in0=xv, in1=tv, op=ALU.add), reads=[f"xb{b2}", f"t4{b2}"], writes=[f"xb{b2}"])
                    fin.append(dma("sp", out_d[tb * 128:(tb + 1) * 128, :], xv, r=[f"xb{b2}"], w=[f"out{tb}"]))
            A("sp", lambda g: None, deps=fin)

        except _Stop:
            pass
        with nc.Block() as block:
            S.emit_all(block, esem, dsem)
    return nc


def _consts():
    f = np.float32
    gam = 1.0 - 2.0 ** (-5.0 - np.arange(4, dtype=np.float64))
    idx = np.arange(128)
    ident = np.eye(128, dtype=f)
    tri = (idx[None, :] >= idx[:, None]).astype(f)
    dtc = np.zeros((128, 4, 128), np.float64)
    rel = idx[None, :] - idx[:, None]
    for h in range(4):
        dtc[:, h, :] = np.where(rel >= 0, gam[h] ** np.maximum(rel, 0), 0.0) * 0.125
    wqc = np.zeros((128, 2, 512), np.float64)
    wkc = np.zeros((128, 2, 128), np.float64)
    decc = np.zeros((128, 2), np.float64)
    for i in range(2):
        for r in range(128):
            h = 2 * i + r // 64
            wqc[r, i, :] = np.tile(gam[h] ** (idx + 1.0), 4)
            decc[r, i] = gam[h] ** 128
        for ft in range(128):
            h = 2 * i + ft // 64
            wkc[:, i, ft] = gam[h] ** (127.0 - idx) * 0.125
    inv_m = 10000.0 ** (-np.arange(16, dtype=np.float64) / 16.0)
    inv_r = 10000.0 ** (-np.arange(32, dtype=np.float64) / 32.0)
    invc = np.zeros((128, 3), np.float64)
    phc = np.zeros((128, 3), np.float64)
    for r in range(64):
        invc[r, 0] = inv_m[r % 16]
        phc[r, 0] = np.pi / 2 if r < 32 else (np.pi if r < 48 else 0.0)
    for r in range(128):
        invc[r, 1] = inv_r[r % 32]
        invc[r, 2] = inv_r[r % 32]
        phc[r, 1] = np.pi / 2
        phc[r, 2] = np.pi if (r % 64) < 32 else 0.0
    return dict(ident=ident, tri=tri, dtc=dtc.reshape(128, 512).astype(f), wqc=wqc.reshape(128, 1024).astype(f),
                wkc=wkc.reshape(128, 256).astype(f), decc=decc.astype(f), invc=invc.astype(f), phc=phc.astype(f))


def _colmajor(v, n):
    return np.ascontiguousarray(np.asarray(v, np.float32).reshape(n, 128).T)


def _prep_shared(inp):
    f = np.float32
    w_in = np.asarray(inp["w_in"], f)[0]
    cols = list(range(0, 640))
    cols += list(range(640, 672)) + [640 + k for k in list(range(16, 32)) + list(range(0, 16))]
    for base in (672, 928):
        for i in range(2):
            nat, sw = [], []
            for hh in (2 * i, 2 * i + 1):
                b = base + hh * 64
                nat += list(range(b, b + 64))
                sw += list(range(b + 32, b + 64)) + list(range(b, b + 32))
            cols += nat + sw
    cols += list(range(1184, 2208))
    w1 = np.ascontiguousarray(w_in[:, cols])
    assert w1.shape[1] == NC1
    wqb = np.asarray(inp["w_q_b"], f)[0]
    qc = []
    for h in range(8):
        b = h * 96
        qc += list(range(b + 64, b + 96)) + [b + 64 + k for k in list(range(16, 32)) + list(range(0, 16))] + list(range(b, b + 64))
    wq = np.ascontiguousarray(wqb[:, qc])
    wkvb = np.asarray(inp["w_kv_b"], f)[0]
    wkv = np.zeros((256, 1536), f)
    for h in range(8):
        wkv[:, h * 128 + 64:h * 128 + 128] = wkvb[:, h * 128:h * 128 + 64]
        wkv[:, 1024 + h * 64:1024 + (h + 1) * 64] = wkvb[:, h * 128 + 64:h * 128 + 128]
    sh = dict(
        w_ada=np.ascontiguousarray(np.asarray(inp["w_ada"], f)[0]),
        b_ada=np.ascontiguousarray(np.asarray(inp["b_ada"], f)[0][None, :]),
        gpre1=_colmajor(inp["pre_norm_mix"][0], 8), gpre2=_colmajor(inp["pre_norm_ffn"][0], 8),
        gpost1=np.ascontiguousarray(np.asarray(inp["post_norm_mix"], f)[0][None, :]),
        gpost2=np.ascontiguousarray(np.asarray(inp["post_norm_ffn"], f)[0][None, :]),
        qg=_colmajor(inp["q_a_norm"][0], 3), kvg=_colmajor(inp["kv_a_norm"][0], 2),
        og=_colmajor(np.concatenate([np.asarray(inp["mla_out_norm"], f)[0], np.asarray(inp["ret_gn_gain"], f)[0]]), 8),
        w1=w1, wq=wq, wkv=wkv,
        wout=np.ascontiguousarray(np.asarray(inp["w_out"], f)[0]),
        wg=np.ascontiguousarray(np.asarray(inp["w_gate"], f)[0]),
        wu=np.ascontiguousarray(np.asarray(inp["w_up"], f)[0]),
        wd=np.ascontiguousarray(np.asarray(inp["w_down"], f)[0]),
    )
    sh.update(_consts())
    return sh


def make_in_maps(inp, cores):
    sh = _prep_shared(inp)
    x = np.asarray(inp["x"], np.float32)
    c = np.asarray(inp["c"], np.float32)
    pos = np.asarray(inp["positions"], np.int32)
    maps = []
    for b in cores:
        m = dict(sh)
        m["x"] = np.ascontiguousarray(x[b])
        m["cT"] = _colmajor(c[b], 8)
        m["pos"] = np.ascontiguousarray(pos[b][None, :])
        maps.append(m)
    return maps


_NC = None


def kernel(**inputs):
    global _NC
    if _NC is None:
        _NC = build_nc()
    maps = make_in_maps(inputs, list(range(8)))
    res = run_bass_kernel_spmd(_NC, maps, core_ids=list(range(8)))
    return np.stack([np.asarray(r["out"], np.float32) for r in res.results], axis=0)
```

```python
import contextlib
import types
import numpy as np
import concourse.bass as bass
import concourse.mybir as mybir
from concourse.bass_utils import run_bass_kernel_spmd

F32 = mybir.dt.float32
BF16 = mybir.dt.bfloat16
I32 = mybir.dt.int32
AF = mybir.ActivationFunctionType
ALU = mybir.AluOpType
AX = mybir.AxisListType

ENGS = ("pe", "act", "dve", "pool", "sp")
STOP = 99
SUB = 99
HSEL = (0, 1, 2, 3)
TAPS = ()
REORDER = True
SEM_LAT = 0.2
QS_ORDER = (0, 7, 1, 6, 2, 5, 3, 4)
GS = 1
NPT = 6


class _Stop(Exception):
    pass

T = 4096
D = 1024
NSB = 8
DFF = 2816
NJ = 22
NC1 = 2752
EPS = 1e-6
PI = float(np.pi)


def _freeze(fn):
    if fn.__closure__ is None:
        return fn
    cells = []
    for c in fn.__closure__:
        try:
            cells.append(types.CellType(c.cell_contents))
        except ValueError:
            cells.append(c)
    return types.FunctionType(fn.__code__, fn.__globals__, fn.__name__, fn.__defaults__, tuple(cells))


class Op:
    __slots__ = ("eng", "idx", "emit", "deps", "is_dma", "dma_i", "marked", "count", "clock", "waits", "is_bar", "busy", "lat", "seq", "st", "nobar")


def _nfree(ap):
    n = 1
    for d in ap.shape[1:]:
        n *= int(d)
    return n


class _Fake:
    def __init__(self, eng):
        self.eng = eng
        self.busy = 0.0
        self.lat = None

    def matmul(self, out, lhsT=None, rhs=None, **kw):
        n = max(_nfree(rhs), 64)
        f = 4.0 if rhs.dtype == F32 else 1.0
        self.busy += f * n / 2370.0 + (0.004 if n >= 512 else 0.06)
        return self

    def transpose(self, out=None, in_=None, identity=None, **kw):
        self.busy += 0.12
        return self

    def activation(self, out=None, in_=None, **kw):
        n = _nfree(in_)
        self.busy += (0.07 if n >= 512 else 0.25) + n / 1200.0
        return self

    def dma_start(self, out=None, in_=None, **kw):
        nb = _nfree(out) * int(out.shape[0]) * (4 if out.dtype in (F32, I32) else 2)
        self.busy += 0.15 if self.eng == "sp" else 1.2
        self.lat = 2.5 + nb / 150e3
        return self

    def _dve(self, out, **kw):
        n = _nfree(out)
        if self.eng == "pool":
            self.busy += 0.2 + n / 500.0
        else:
            self.busy += 0.12 + n / 900.0
        return self

    def tensor_tensor(self, out=None, **kw):
        return self._dve(out)

    def tensor_scalar(self, out=None, **kw):
        return self._dve(out)

    def tensor_copy(self, out=None, **kw):
        return self._dve(out)

    def scalar_tensor_tensor(self, out=None, **kw):
        return self._dve(out)

    def tensor_single_scalar(self, out=None, **kw):
        return self._dve(out)

    def reciprocal(self, out=None, **kw):
        self.busy += 0.1 + _nfree(out) / 150.0
        return self

    def memset(self, ap, *a, **kw):
        return self._dve(ap)

    def reduce_sum(self, out=None, in_=None, **kw):
        return self._dve(in_)

    def then_inc(self, *a, **kw):
        return self


class Sched:
    def __init__(self, n_dma_sems=12):
        self.ops = {e: [] for e in ENGS}
        self.order = []
        self.lastw = {}
        self.readers = {}
        self.n_dma_sems = n_dma_sems
        self.dma_ops = {e: [] for e in ENGS}
        self.dma_since_bar = []

    def add(self, eng, emit, reads=(), writes=(), dma=False, deps=()):
        op = Op()
        op.eng = eng
        op.emit = _freeze(emit)
        op.is_dma = dma
        op.marked = False
        op.count = 0
        op.idx = len(self.ops[eng])
        d = set(deps)
        for k in reads:
            w = self.lastw.get(k)
            if w is not None:
                d.add(w)
        for k in writes:
            w = self.lastw.get(k)
            if w is not None:
                d.add(w)
            for r in self.readers.get(k, ()):
                d.add(r)
        for k in reads:
            self.readers.setdefault(k, []).append(op)
        for k in writes:
            self.lastw[k] = op
            self.readers[k] = []
        d.discard(op)
        op.deps = d
        op.is_bar = False
        op.nobar = False
        op.seq = len(self.order)
        self.ops[eng].append(op)
        self.order.append(op)
        return op

    def barrier(self):
        for e in ENGS:
            self.add(e, lambda g: None).is_bar = True

    def _list_schedule(self, seg):
        import heapq
        segset = set(seg)
        succ = {o: [] for o in seg}
        indeg = {}
        for o in seg:
            fk = _Fake(o.eng)
            o.emit(fk)
            o.busy = fk.busy
            o.lat = fk.lat if fk.lat is not None else fk.busy + SEM_LAT
            k = 0
            for d in o.deps:
                if d in segset:
                    succ[d].append(o)
                    k += 1
            indeg[o] = k
        bl = {}
        for o in reversed(seg):
            m = 0.0
            for s_ in succ[o]:
                if bl[s_] > m:
                    m = bl[s_]
            bl[o] = o.lat + m
        free = {e: 0.0 for e in ENGS}
        avail = {e: [] for e in ENGS}
        future = {e: [] for e in ENGS}
        rtime = {o: 0.0 for o in seg}
        for o in seg:
            if indeg[o] == 0:
                heapq.heappush(future[o.eng], (0.0, o.seq, o))
        out = []
        n = len(seg)
        while len(out) < n:
            best_e, best_t = None, None
            for e in ENGS:
                fu, av = future[e], avail[e]
                while fu and fu[0][0] <= free[e]:
                    _, sq, o = heapq.heappop(fu)
                    heapq.heappush(av, (-bl[o], sq, o))
                if av:
                    t = free[e]
                elif fu:
                    t = fu[0][0]
                else:
                    continue
                if best_t is None or t < best_t:
                    best_e, best_t = e, t
            e = best_e
            if not avail[e]:
                free[e] = best_t
                fu, av = future[e], avail[e]
                while fu and fu[0][0] <= free[e]:
                    _, sq, o = heapq.heappop(fu)
                    heapq.heappush(av, (-bl[o], sq, o))
            _, sq, o = heapq.heappop(avail[e])
            st = free[e]
            o.st = st
            free[e] = st + o.busy
            fin = st + o.lat
            out.append(o)
            for s_ in succ[o]:
                if fin > rtime[s_]:
                    rtime[s_] = fin
                indeg[s_] -= 1
                if indeg[s_] == 0:
                    heapq.heappush(future[s_.eng], (rtime[s_], s_.seq, s_))
        return out

    def schedule(self, reorder=True):
        segs, cur = [], []
        for o in self.order:
            if o.is_bar:
                if cur:
                    segs.append(("seg", cur))
                    cur = []
                if segs and segs[-1][0] == "bar":
                    segs[-1][1].append(o)
                else:
                    segs.append(("bar", [o]))
            else:
                cur.append(o)
        if cur:
            segs.append(("seg", cur))
        new = []
        last_seg = []
        for kind, lst in segs:
            if kind == "seg":
                lst2 = self._list_schedule(lst) if reorder else lst
                new += lst2
                last_seg = lst2
            else:
                deps = [o for o in last_seg if o.is_dma and not o.nobar]
                for e in ENGS:
                    for o in reversed(last_seg):
                        if o.eng == e and not o.is_dma:
                            deps.append(o)
                            break
                for o in lst:
                    o.deps = set(deps)
                new += lst
        self.order = new
        self.ops = {e: [] for e in ENGS}
        self.dma_ops = {e: [] for e in ENGS}
        for o in new:
            o.idx = len(self.ops[o.eng])
            self.ops[o.eng].append(o)
            if o.is_dma:
                o.dma_i = len(self.dma_ops[o.eng])
                if o.dma_i >= self.n_dma_sems:
                    o.deps.add(self.dma_ops[o.eng][o.dma_i - self.n_dma_sems])
                self.dma_ops[o.eng].append(o)

    def resolve(self):
        known = {e: {f: -1 for f in ENGS} for e in ENGS}
        known_dma = {e: set() for e in ENGS}
        for op in self.order:
            e = op.eng
            kn = known[e]
            waits = []
            for d in sorted(op.deps, key=lambda o: -o.idx):
                if d.is_dma:
                    if d in known_dma[e]:
                        continue
                    known_dma[e].add(d)
                    waits.append(d)
                else:
                    if d.eng == "pe" and e == "pe":
                        continue
                    if kn[d.eng] >= d.idx:
                        continue
                    d.marked = True
                    waits.append(d)
                ck = d.clock
                for f in ENGS:
                    if ck[f] > kn[f]:
                        kn[f] = ck[f]
            op.waits = waits
            ck = dict(kn)
            if not op.is_dma:
                ck[e] = max(ck[e], op.idx)
            op.clock = ck
        for e in ENGS:
            c = 0
            for op in self.ops[e]:
                if op.marked:
                    c += 1
                    op.count = c

    def emit_all(self, block, esem, dsem):
        self.schedule(reorder=REORDER)
        self.resolve()
        n = self.n_dma_sems

        def run(e, engobj):
            for op in self.ops[e]:
                for d in op.waits:
                    if d.is_dma:
                        engobj.wait_ge(dsem[d.eng][d.dma_i % n], 16 * (d.dma_i // n + 1))
                    else:
                        engobj.wait_ge(esem[d.eng], d.count)
                ins = op.emit(engobj)
                if op.is_dma:
                    ins.then_inc(dsem[e][op.dma_i % n], 16)
                elif op.marked:
                    if ins is None:
                        ins = engobj.nop()
                    ins.then_inc(esem[e], 1)

        block.tensor(lambda t: run("pe", t))
        block.scalar(lambda t: run("act", t))
        block.vector(lambda t: run("dve", t))
        block.gpsimd(lambda t: run("pool", t))
        block.sync(lambda t: run("sp", t))


class Arena:
    def __init__(self, ap, ncols):
        self.ap = ap
        self.n = ncols
        self.off = 0

    def alloc(self, cols, dt=F32):
        nb = cols * (4 if dt in (F32, I32) else 2)
        n32 = ((nb + 31) // 32) * 8
        assert self.off + n32 <= self.n, ("arena overflow", self.off, n32, self.n)
        v = self.ap[:, self.off:self.off + n32]
        self.off += n32
        if dt != F32:
            v = v.bitcast(dt)
        return v[:, 0:cols]

    def reset(self):
        self.off = 0

    def alloc_at(self, off32, cols, dt=F32):
        save = self.off
        self.off = off32
        v = self.alloc(cols, dt)
        end = self.off
        self.off = save
        return v, end


def build_nc():
    nc = bass.Bass("TRN2", target_bir_lowering=False)

    def DI(name, shape, dt=F32):
        return nc.dram_tensor(name, shape, dt, kind="ExternalInput").ap()

    x_d = DI("x", [T, D])
    c_d = DI("cT", [128, 8])
    pos_d = DI("pos", [1, T], I32)
    wada_d = DI("w_ada", [D, 6 * D])
    bada_d = DI("b_ada", [1, 6 * D])
    gpre1_d = DI("gpre1", [128, 8])
    gpre2_d = DI("gpre2", [128, 8])
    gpost1_d = DI("gpost1", [1, D])
    gpost2_d = DI("gpost2", [1, D])
    qg_d = DI("qg", [128, 3])
    kvg_d = DI("kvg", [128, 2])
    og_d = DI("og", [128, 8])
    w1_d = DI("w1", [D, NC1])
    wq_d = DI("wq", [384, 1024])
    wkv_d = DI("wkv", [256, 1536])
    wout_d = DI("wout", [D, D])
    wg_d = DI("wg", [D, DFF])
    wu_d = DI("wu", [D, DFF])
    wd_d = DI("wd", [DFF, D])
    ident_d = DI("ident", [128, 128])
    tri_d = DI("tri", [128, 128])
    dt_d = DI("dtc", [128, 512])
    wqc_d = DI("wqc", [128, 1024])
    wkc_d = DI("wkc", [128, 256])
    dec_d = DI("decc", [128, 2])
    inv_d = DI("invc", [128, 3])
    ph_d = DI("phc", [128, 3])
    out_d = nc.dram_tensor("out", [T, D], F32, kind="ExternalOutput").ap()
    yret_d = nc.dram_tensor("yret_scr", [4, 128, T], BF16).ap()

    S = Sched(n_dma_sems=12)
    A = S.add

    with contextlib.ExitStack() as ctx:
        def sbt(name, cols, dt=F32, parts=128):
            return ctx.enter_context(nc.sbuf_tensor(name, [parts, cols], dt))

        identb = sbt("identb", 128, BF16)
        trib = sbt("trib", 128, BF16)
        onesb = sbt("onesb", 128, BF16)
        onesf = sbt("onesf", 128)
        epst = sbt("epst", 1)
        DECc = sbt("DECc", 2)
        INVc = sbt("INVc", 3)
        PHc = sbt("PHc", 3)
        modc = sbt("modc", 32)
        qg = sbt("qg_sb", 3)
        kvg = sbt("kvg_sb", 2)
        a1 = sbt("a1", 8)
        a2 = sbt("a2", 8)
        G1b = sbt("G1b", D)
        G2b = sbt("G2b", D)
        ARN = 50000
        arena_t = sbt("arena", ARN)
        AR = Arena(arena_t, ARN)
        P = [ctx.enter_context(nc.psum_tensor(f"bank{i}", [128, 512], F32)) for i in range(8)]
        Pb = [p[:, :].bitcast(BF16) for p in P]

        esem = {e: ctx.enter_context(nc.semaphore("es_" + e)) for e in ENGS}
        dsem = {e: [ctx.enter_context(nc.semaphore(f"ds_{e}{i}")) for i in range(12)] for e in ("sp", "pool")}

        def dma(q, out, in_, r=(), w=(), nobar=False):
            o = A(q, lambda g: g.dma_start(out=out, in_=in_), reads=r, writes=w, dma=True)
            o.nobar = nobar
            return o

        def tap(name, ap, keys):
            if name not in TAPS:
                return
            shp = list(ap.shape)
            dd = nc.dram_tensor("dbg_" + name, shp, ap.dtype, kind="ExternalOutput").ap()
            dma("sp", dd, ap, r=keys)

        def rsqrt_ops(dst, src, scale, rk, wk):
            A("act", lambda g: g.activation(out=dst, in_=src, func=AF.Sqrt, scale=scale, bias=epst[0:dst.shape[0], :]),
              reads=list(rk) + ["epst"], writes=[wk])
            A("dve", lambda g: g.reciprocal(out=dst, in_=dst), reads=[wk], writes=[wk])

        _phase = [0]
        try:
            dma("pool", identb[:, :], ident_d, w=["identb"])
            dma("pool", trib[:, :], tri_d, w=["trib"])
            A("pool", lambda g: g.memset(onesb[:, :], 1.0), writes=["onesb"])
            A("pool", lambda g: g.memset(onesf[:, :], 1.0), writes=["onesf"])
            A("pool", lambda g: g.memset(epst[:, :], EPS), writes=["epst"])
            for t_, d_, k_ in ((DECc, dec_d, "DECc"), (INVc, inv_d, "INVc"), (PHc, ph_d, "PHc")):
                dma("sp", t_[:, :], d_, w=[k_])
            cT = AR.alloc(8)
            gp1 = AR.alloc(8)
            gp2 = AR.alloc(8)
            scb = AR.alloc(8, BF16)
            gpo1 = AR.alloc(D)
            gpo2 = AR.alloc(D)
            bada = AR.alloc(6 * D)
            modrow = AR.alloc(6 * D)
            grow1 = AR.alloc(D)
            grow2 = AR.alloc(D)
            wa = [AR.alloc(8 * 1024, BF16), AR.alloc(8 * 1024, BF16)]
            dma("sp", cT, c_d, w=["cT"])
            dma("sp", qg[:, :], qg_d, w=["qg"])
            dma("sp", kvg[:, :], kvg_d, w=["kvg"])
            dma("sp", gp1, gpre1_d, w=["gp1"])
            dma("sp", gp2, gpre2_d, w=["gp2"])
            dma("sp", gpo1[0:1, :], gpost1_d, w=["gpo1"])
            dma("sp", gpo2[0:1, :], gpost2_d, w=["gpo2"])
            dma("sp", bada[0:1, :], bada_d, w=["bada"])
            A("act", lambda g: g.activation(out=scb, in_=cT, func=AF.Silu), reads=["cT"], writes=["scb"])
            for gd in range(6):
                wb = wa[gd % 2]
                dma("pool", wb.rearrange("p (c n) -> p c n", c=8),
                    wada_d[:, gd * 1024:(gd + 1) * 1024].rearrange("(c p) n -> p c n", p=128), w=[f"wa{gd % 2}"])
                for g2_ in range(2):
                    gi = gd * 2 + g2_

                    def mm_ada(g, gi=gi, wb=wb, g2_=g2_):
                        for k in range(8):
                            r = g.matmul(P[gi % 2][0:1, :], lhsT=scb[:, k:k + 1], rhs=wb[:, k * 1024 + g2_ * 512:k * 1024 + (g2_ + 1) * 512],
                                         start=(k == 0), stop=(k == 7))
                        return r
                    A("pe", mm_ada, reads=["scb", f"wa{gd % 2}"], writes=[f"P{gi % 2}"])
                    A("dve", lambda g, gi=gi: g.tensor_tensor(out=modrow[0:1, gi * 512:(gi + 1) * 512], in0=P[gi % 2][0:1, :],
                                                             in1=bada[0:1, gi * 512:(gi + 1) * 512], op=ALU.add),
                      reads=[f"P{gi % 2}", "bada"], writes=["modrow"])
            col_offs = [0 * D, 1 * D, 3 * D, 4 * D]

            def mm_cols(g):
                for vi, off in enumerate(col_offs):
                    for c in range(8):
                        r = g.matmul(P[2][:, vi * 8 + c:vi * 8 + c + 1], lhsT=modrow[0:1, off + c * 128:off + (c + 1) * 128],
                                     rhs=onesf[0:1, 0:1], start=True, stop=True)
                return r
            A("pe", mm_cols, reads=["modrow", "onesf"], writes=["P2"])
            A("dve", lambda g: g.tensor_copy(out=modc[:, :], in_=P[2][:, 0:32]), reads=["P2"], writes=["modc"])
            A("dve", lambda g: g.scalar_tensor_tensor(out=a1[:, :], in0=modc[:, 8:16], scalar=1.0, in1=gp1, op0=ALU.add, op1=ALU.mult),
              reads=["modc", "gp1"], writes=["a1"])
            A("dve", lambda g: g.scalar_tensor_tensor(out=a2[:, :], in0=modc[:, 24:32], scalar=1.0, in1=gp2, op0=ALU.add, op1=ALU.mult),
              reads=["modc", "gp2"], writes=["a2"])
            sh1 = modc[:, 0:8]
            sh2 = modc[:, 16:24]
            A("dve", lambda g: g.tensor_tensor(out=grow1[0:1, :], in0=modrow[0:1, 2 * D:3 * D], in1=gpo1[0:1, :], op=ALU.mult),
              reads=["modrow", "gpo1"], writes=["grow1"])
            A("dve", lambda g: g.tensor_tensor(out=grow2[0:1, :], in0=modrow[0:1, 5 * D:6 * D], in1=gpo2[0:1, :], op=ALU.mult),
              reads=["modrow", "gpo2"], writes=["grow2"])
            for gi, (grow, Gb, gk) in enumerate(((grow1, G1b, "G1b"), (grow2, G2b, "G2b"))):
                for hf in range(2):
                    bk = 3 + hf
                    A("pe", lambda g, grow=grow, hf=hf, bk=bk: g.matmul(P[bk][:, :], lhsT=onesf[0:1, 0:128],
                                                                        rhs=grow[0:1, hf * 512:(hf + 1) * 512], start=True, stop=True),
                      reads=[f"grow{gi + 1}", "onesf"], writes=[f"P{bk}"])
                    A("act", lambda g, Gb=Gb, hf=hf, bk=bk: g.activation(out=Gb[:, hf * 512:(hf + 1) * 512], in_=P[bk][:, :], func=AF.Copy),
                      reads=[f"P{bk}"], writes=[gk])
            S.barrier()
            _phase[0] += 1
            if _phase[0] > STOP:
                raise _Stop()
            AR.reset()

            cqnT = AR.alloc(3 * T, BF16)
            ckvnT = AR.alloc(2 * T, BF16)
            TABm = AR.alloc(T)
            kpeT = TABm[64:96, 0:2048].bitcast(BF16)
            P12 = AR.off
            DTc = AR.alloc(512)
            WQc = AR.alloc(1024)
            WKc = AR.alloc(256)
            dma("sp", DTc, dt_d, w=["DTc"])
            dma("sp", WQc, wqc_d, w=["WQc"])
            dma("sp", WKc, wkc_d, w=["WKc"])
            W1 = AR.alloc(8 * NC1, BF16)
            xt = [AR.alloc(D), AR.alloc(D)]
            xn = [AR.alloc(D, BF16), AR.alloc(D, BF16)]
            junk = AR.alloc(D, BF16)
            ssq = [AR.alloc(1), AR.alloc(1)]
            hT = AR.alloc(8 * 512, BF16)
            cqraw = AR.alloc(3 * 512)
            ckvraw = AR.alloc(2 * 512)
            sq = AR.alloc(3 * 512, BF16)
            sq2 = AR.alloc(2 * 512, BF16)
            Rq = AR.alloc(512)
            Rkv = Rq
            posi = AR.alloc(512, I32)
            posf = AR.alloc(512)
            ang = AR.alloc(512)
            ni = posi
            nf = AR.alloc(512)
            msk = nf
            Cr = AR.alloc(512)
            Sr = AR.alloc(512)
            t1 = [AR.alloc(512), AR.alloc(512)]
            t2 = [AR.alloc(512), AR.alloc(512)]
            rqT = AR.alloc(2 * 512, BF16)
            rkT = AR.alloc(2 * 512, BF16)
            qwT = AR.alloc(2 * 512, BF16)
            rqm = AR.alloc(2 * 512, BF16)
            qwm = AR.alloc(2 * 512, BF16)
            A("pool", lambda g: g.memset(rqm[64:128, :], 0.0), writes=["rqm"])
            A("pool", lambda g: g.memset(qwm[64:128, :], 0.0), writes=["qwm"])
            vtok = AR.alloc(4 * 512, BF16)
            sg = AR.alloc(4 * 512, BF16)
            kwtok = AR.alloc(256, BF16)
            scTm = AR.alloc(512, BF16)
            osb = AR.alloc(512)
            ynorm = AR.alloc(512)
            osq = ynorm
            ytok = AR.alloc(512, BF16)
            ysT = AR.alloc(4 * 512, BF16)
            Sf = AR.alloc(256)
            Sbf = AR.alloc(256, BF16)
            st = {k: AR.alloc(4) for k in ("osum", "osqs", "mean", "msq", "var", "rgn")}

            for hf in range(2):
                dma("pool", W1.rearrange("p (c n) -> p c n", c=8)[:, hf * 4:(hf + 1) * 4, :],
                    w1_d.rearrange("(c p) n -> p c n", p=128)[:, hf * 4:(hf + 1) * 4, :], w=[f"W1{hf}"])
            A("pool", lambda g: g.memset(Sf, 0.0), writes=["Sf"])
            A("pool", lambda g: g.memset(Sbf, 0.0), writes=["Sbf"])

            def ck(n):
                if SUB == n:
                    raise _Stop()
            ck(0)

            def w1s(c, off, n):
                return W1[:, c * NC1 + off:c * NC1 + off + n]

            def table(dst, dk, col, sbi):
                A("dve", lambda g: g.tensor_scalar(out=ang, in0=posf, scalar1=INVc[:, col:col + 1], scalar2=PHc[:, col:col + 1],
                                                   op0=ALU.mult, op1=ALU.add), reads=["posf", "INVc", "PHc"], writes=["ang"])
                A("dve", lambda g: g.tensor_scalar(out=ni, in0=ang, scalar1=float(1.0 / (2 * PI)), scalar2=None, op0=ALU.mult),
                  reads=["ang"], writes=["ibuf"])
                A("dve", lambda g: g.tensor_copy(out=nf, in_=ni), reads=["ibuf"], writes=["nf"])
                A("dve", lambda g: g.scalar_tensor_tensor(out=ang, in0=nf, scalar=-6.28125, in1=ang, op0=ALU.mult, op1=ALU.add),
                  reads=["nf", "ang"], writes=["ang"])
                A("dve", lambda g: g.scalar_tensor_tensor(out=ang, in0=nf, scalar=-(2 * PI - 6.28125), in1=ang, op0=ALU.mult, op1=ALU.add),
                  reads=["nf", "ang"], writes=["ang"])
                A("dve", lambda g: g.tensor_single_scalar(out=msk, in_=ang, scalar=PI, op=ALU.is_gt), reads=["ang", "nf"], writes=["nf"])
                A("dve", lambda g: g.scalar_tensor_tensor(out=ang, in0=msk, scalar=-2 * PI, in1=ang, op0=ALU.mult, op1=ALU.add),
                  reads=["nf", "ang"], writes=["ang"])
                A("dve", lambda g: g.tensor_scalar(out=ang, in0=ang, scalar1=-3.14159, scalar2=3.14159, op0=ALU.max, op1=ALU.min),
                  reads=["ang"], writes=["ang"])
                np_ = dst.shape[0]
                A("act", lambda g: g.activation(out=dst, in_=ang[0:np_, :], func=AF.Sin), reads=["ang"], writes=[dk])

            mtiles = [(0, 128, "cq", 0), (128, 128, "cq", 1), (256, 128, "cq", 2), (384, 128, "ckv", 0), (512, 128, "ckv", 1),
                      (640, 64, "kpe", 0)]
            o_ = 704
            for nm in ("rq", "rk"):
                for i in range(2):
                    mtiles.append((o_, 128, nm + "n", i))
                    mtiles.append((o_ + 128, 128, nm + "s", i))
                    o_ += 256
            RV = 1728
            RG = 2240

            for sbi in range(NSB):
                sc0 = sbi * 512
                dma("sp", posi, bass.AP(pos_d.tensor, sc0, [[0, 128], [1, 512]]), w=["ibuf"])
                A("dve", lambda g: g.tensor_copy(out=posf, in_=posi), reads=["ibuf"], writes=["posf"])
                table(TABm[0:64, sc0:sc0 + 512], "TABm", 0, sbi)
                table(Cr, "Cr", 1, sbi)
                table(Sr, "Sr", 2, sbi)
                ck(1)
                for j in range(4):
                    tb = sbi * 4 + j
                    b2 = tb % 2
                    dma("sp", xt[b2], x_d[tb * 128:(tb + 1) * 128, :], w=[f"xt{b2}"])
                    A("act", lambda g, b2=b2: g.activation(out=junk, in_=xt[b2], func=AF.Square, accum_out=ssq[b2]),
                      reads=[f"xt{b2}"], writes=["junk", f"ssq{b2}"])
                    rsqrt_ops(ssq[b2], ssq[b2], 1.0 / D, [f"ssq{b2}"], f"ssq{b2}")
                    A("act", lambda g, b2=b2: g.activation(out=xn[b2], in_=xt[b2], func=AF.Copy, scale=ssq[b2]),
                      reads=[f"xt{b2}", f"ssq{b2}"], writes=[f"xn{b2}"])

                    def tr8(g, b2=b2):
                        for c in range(8):
                            r = g.transpose(out=Pb[b2][:, c * 128:(c + 1) * 128], in_=xn[b2][:, c * 128:(c + 1) * 128], identity=identb[:, :])
                        return r
                    A("pe", tr8, reads=[f"xn{b2}", "identb"], writes=[f"P{b2}"])
                    for c in range(8):
                        dst = hT[:, c * 512 + j * 128:c * 512 + (j + 1) * 128]
                        if b2 == 0:
                            A("dve", lambda g, c=c, dst=dst, b2=b2: g.tensor_scalar(out=dst, in0=Pb[b2][:, c * 128:(c + 1) * 128],
                                                                                   scalar1=a1[:, c:c + 1], scalar2=sh1[:, c:c + 1],
                                                                                   op0=ALU.mult, op1=ALU.add),
                              reads=[f"P{b2}", "a1", "modc"], writes=["hT"])
                        else:
                            A("act", lambda g, c=c, dst=dst, b2=b2: g.activation(out=dst, in_=Pb[b2][:, c * 128:(c + 1) * 128], func=AF.Identity,
                                                                                scale=a1[:, c:c + 1], bias=sh1[:, c:c + 1]),
                              reads=[f"P{b2}", "a1", "modc"], writes=["hT"])
                ck(2)
                for mi, (off, M, kind, i) in enumerate(mtiles):
                    bk = 2 + mi % 2
                    pk = f"P{bk}"

                    def mmz(g, off=off, M=M, bk=bk):
                        for c in range(8):
                            r = g.matmul(P[bk][0:M, :], lhsT=w1s(c, off, M), rhs=hT[:, c * 512:(c + 1) * 512], start=(c == 0), stop=(c == 7))
                        return r
                    A("pe", mmz, reads=["W10", "W11", "hT"], writes=[pk])
                    if kind in ("cq", "ckv"):
                        raw, sqt, nt, Rt, bank, scl, dstT, rk_ = ((cqraw, sq, 3, Rq, 4, 1.0 / 384, cqnT, "Rq") if kind == "cq"
                                                                  else (ckvraw, sq2, 2, Rkv, 5, 1.0 / 256, ckvnT, "Rq"))
                        gcol = qg if kind == "cq" else kvg
                        A("act", lambda g, raw=raw, i=i, bk=bk, gcol=gcol: g.activation(out=raw[:, i * 512:(i + 1) * 512], in_=P[bk][:, :], func=AF.Copy,
                                                                                      scale=gcol[:, i:i + 1]),
                          reads=[pk, "qg", "kvg"], writes=[f"{kind}raw{i}"])
                        A("act", lambda g, sqt=sqt, i=i, bk=bk: g.activation(out=sqt[:, i * 512:(i + 1) * 512], in_=P[bk][:, :], func=AF.Square),
                          reads=[pk], writes=[f"{kind}sq{i}"])
                        if i == nt - 1:
                            def mmst(g, sqt=sqt, nt=nt, bank=bank):
                                for q in range(nt):
                                    r = g.matmul(P[bank][:, :], lhsT=onesb[:, :], rhs=sqt[:, q * 512:(q + 1) * 512], start=(q == 0), stop=(q == nt - 1))
                                return r
                            A("pe", mmst, reads=[f"{kind}sq{q}" for q in range(nt)] + ["onesb"], writes=[f"P{bank}"])
                            rsqrt_ops(Rt, P[bank][:, :], scl, [f"P{bank}"], rk_)
                            for q in range(nt):
                                A("pool", lambda g, raw=raw, Rt=Rt, q=q, dstT=dstT: g.tensor_tensor(
                                    out=dstT[:, q * T + sc0:q * T + sc0 + 512], in0=raw[:, q * 512:(q + 1) * 512], in1=Rt, op=ALU.mult),
                                  reads=[f"{kind}raw{q}", rk_], writes=[f"{kind}nT"])
                    elif kind == "kpe":
                        A("dve", lambda g, bk=bk: g.tensor_tensor(out=t1[0][0:32, :], in0=P[bk][0:32, :], in1=TABm[0:32, sc0:sc0 + 512], op=ALU.mult),
                          reads=[pk, "TABm"], writes=["t1_0"])
                        A("dve", lambda g, bk=bk: g.tensor_tensor(out=t2[0][0:32, :], in0=P[bk][32:64, :], in1=TABm[32:64, sc0:sc0 + 512], op=ALU.mult),
                          reads=[pk, "TABm"], writes=["t2_0"])
                        A("dve", lambda g: g.tensor_tensor(out=kpeT[:, sc0:sc0 + 512], in0=t1[0][0:32, :], in1=t2[0][0:32, :], op=ALU.add),
                          reads=["t1_0", "t2_0"], writes=["kpeT"])
                    else:
                        nm = kind[:2]
                        if kind[2] == "n":
                            A("dve", lambda g, bk=bk, i=i: g.tensor_tensor(out=t1[i], in0=P[bk][:, :], in1=Cr, op=ALU.mult),
                              reads=[pk, "Cr"], writes=[f"t1_{i}"])
                        else:
                            A("dve", lambda g, bk=bk, i=i: g.tensor_tensor(out=t2[i], in0=P[bk][:, :], in1=Sr, op=ALU.mult),
                              reads=[pk, "Sr"], writes=[f"t2_{i}"])
                            dstq = rqT if nm == "rq" else rkT
                            A("pool", lambda g, i=i, dstq=dstq: g.tensor_tensor(out=dstq[:, i * 512:(i + 1) * 512], in0=t1[i], in1=t2[i], op=ALU.add),
                              reads=[f"t1_{i}", f"t2_{i}"], writes=[nm + "T"])
                            if nm == "rq":
                                A("pool", lambda g, i=i: g.tensor_tensor(out=rqm[0:64, i * 512:(i + 1) * 512], in0=t1[i][0:64, :], in1=t2[i][0:64, :],
                                                                        op=ALU.add),
                                  reads=[f"t1_{i}", f"t2_{i}"], writes=["rqm"])
                                A("pool", lambda g, i=i: g.tensor_tensor(out=qwT[:, i * 512:(i + 1) * 512], in0=rqT[:, i * 512:(i + 1) * 512],
                                                                        in1=WQc[:, i * 512:(i + 1) * 512], op=ALU.mult),
                                  reads=["rqT", "WQc"], writes=["qwT"])
                                A("pool", lambda g, i=i: g.tensor_tensor(out=qwm[0:64, i * 512:(i + 1) * 512], in0=rqT[0:64, i * 512:(i + 1) * 512],
                                                                        in1=WQc[0:64, i * 512:(i + 1) * 512], op=ALU.mult),
                                  reads=["rqT", "WQc"], writes=["qwm"])
                ck(3)
                for j in range(4):
                    for which, off, bank in (("v", RV, 4), ("g", RG, 5)):
                        def mmt(g, j=j, off=off, bank=bank):
                            for c in range(8):
                                r = g.matmul(P[bank][:, :], lhsT=hT[:, c * 512 + j * 128:c * 512 + (j + 1) * 128], rhs=w1s(c, off, 512),
                                             start=(c == 0), stop=(c == 7))
                            return r
                        A("pe", mmt, reads=["W10", "W11", "hT"], writes=[f"P{bank}"])
                        if which == "v":
                            A("act", lambda g, j=j: g.activation(out=vtok[:, j * 512:(j + 1) * 512], in_=P[4][:, :], func=AF.Copy),
                              reads=["P4"], writes=["vtok"])
                        else:
                            A("act", lambda g, j=j: g.activation(out=sg[:, j * 512:(j + 1) * 512], in_=P[5][:, :], func=AF.Silu),
                              reads=["P5"], writes=["sg"])
                ck(4)
                for j in range(4):
                    jc = slice(j * 128, (j + 1) * 128)

                    def trk(g, j=j):
                        for i in range(2):
                            r = g.transpose(out=Pb[5][:, i * 128:(i + 1) * 128], in_=rkT[:, i * 512 + j * 128:i * 512 + (j + 1) * 128], identity=identb[:, :])
                        return r
                    A("pe", trk, reads=["rkT", "identb"], writes=["P5"])
                    A("dve", lambda g: g.tensor_tensor(out=kwtok, in0=Pb[5][:, 0:256], in1=WKc[:, :], op=ALU.mult),
                      reads=["P5", "WKc"], writes=["kwtok"])

                    ck(6)

                    def mmsc(g, j=j):
                        for h in range(4):
                            i, r0 = h // 2, 64 * (h % 2)
                            cs = slice(i * 512 + j * 128, i * 512 + (j + 1) * 128)
                            if r0 == 0:
                                r = g.matmul(P[6][:, h * 128:(h + 1) * 128], lhsT=rkT[:, cs], rhs=rqm[:, cs], start=True, stop=True)
                            else:
                                r = g.matmul(P[6][:, h * 128:(h + 1) * 128], lhsT=rkT[64:128, cs], rhs=rqT[64:128, cs], start=True, stop=True,
                                             tile_position=(64, 0))
                        return r
                    A("pe", mmsc, reads=["rkT", "rqT", "rqm"], writes=["P6"])
                    A("dve", lambda g: g.tensor_tensor(out=scTm, in0=P[6][:, :], in1=DTc[:, :], op=ALU.mult), reads=["P6", "DTc"], writes=["scTm"])

                    ck(7)

                    def mmo(g, j=j):
                        for h in range(4):
                            i, r0 = h // 2, 64 * (h % 2)
                            cs = slice(i * 512 + j * 128, i * 512 + (j + 1) * 128)
                            g.matmul(P[7][:, h * 128:(h + 1) * 128], lhsT=scTm[:, h * 128:(h + 1) * 128],
                                     rhs=vtok[:, j * 512 + h * 128:j * 512 + (h + 1) * 128], start=True, stop=False)
                            if r0 == 0:
                                r = g.matmul(P[7][:, h * 128:(h + 1) * 128], lhsT=qwm[:, cs], rhs=Sbf[:, i * 128:(i + 1) * 128], start=False, stop=True)
                            else:
                                r = g.matmul(P[7][:, h * 128:(h + 1) * 128], lhsT=qwT[64:128, cs], rhs=Sbf[64:128, i * 128:(i + 1) * 128],
                                             start=False, stop=True, tile_position=(64, 0))
                        return r
                    A("pe", mmo, reads=["scTm", "vtok", "qwT", "qwm", "Sbf"], writes=["P7"])
                    A("act", lambda g: g.activation(out=osb, in_=P[7][:, :], func=AF.Copy), reads=["P7"], writes=["osb"])

                    ck(8)

                    def mmu(g, j=j):
                        for h in range(4):
                            i, r0 = h // 2, 64 * (h % 2)
                            kw = dict(tile_position=(0, 64)) if r0 else {}
                            r = g.matmul(P[5][r0:r0 + 64, 256 + i * 128:256 + (i + 1) * 128], lhsT=kwtok[:, h * 64:(h + 1) * 64],
                                         rhs=vtok[:, j * 512 + h * 128:j * 512 + (h + 1) * 128], start=True, stop=True, **kw)
                        return r
                    A("pe", mmu, reads=["kwtok", "vtok"], writes=["P5"])
                    for i in range(2):
                        A("dve", lambda g, i=i: g.scalar_tensor_tensor(out=Sf[:, i * 128:(i + 1) * 128], in0=Sf[:, i * 128:(i + 1) * 128],
                                                                      scalar=DECc[:, i:i + 1], in1=P[5][:, 256 + i * 128:256 + (i + 1) * 128],
                                                                      op0=ALU.mult, op1=ALU.add),
                          reads=["P5", "DECc", "Sf"], writes=["Sf"])
                    A("pool", lambda g: g.tensor_copy(out=Sbf, in_=Sf), reads=["Sf"], writes=["Sbf"])
                    ck(9)
                    o3 = osb.rearrange("p (h v) -> p h v", h=4)
                    A("dve", lambda g, o3=o3: g.reduce_sum(out=st["osum"], in_=o3, axis=AX.X), reads=["osb"], writes=["osum"])
                    A("pool", lambda g: g.tensor_tensor(out=osq, in0=osb, in1=osb, op=ALU.mult), reads=["osb"], writes=["ynorm"])
                    A("dve", lambda g: g.reduce_sum(out=st["osqs"], in_=osq.rearrange("p (h v) -> p h v", h=4), axis=AX.X),
                      reads=["ynorm"], writes=["osqs"])
                    A("dve", lambda g: g.tensor_scalar(out=st["mean"], in0=st["osum"], scalar1=1.0 / 128, scalar2=None, op0=ALU.mult),
                      reads=["osum"], writes=["mean"])
                    A("dve", lambda g: g.tensor_tensor(out=st["msq"], in0=st["mean"], in1=st["mean"], op=ALU.mult), reads=["mean"], writes=["msq"])
                    A("dve", lambda g: g.scalar_tensor_tensor(out=st["var"], in0=st["osqs"], scalar=1.0 / 128, in1=st["msq"], op0=ALU.mult, op1=ALU.subtract),
                      reads=["osqs", "msq"], writes=["var"])
                    rsqrt_ops(st["rgn"], st["var"], 1.0, ["var"], "rgn")
                    for h in range(4):
                        A("dve", lambda g, h=h: g.tensor_scalar(out=ynorm[:, h * 128:(h + 1) * 128], in0=osb[:, h * 128:(h + 1) * 128],
                                                                scalar1=st["mean"][:, h:h + 1], scalar2=st["rgn"][:, h:h + 1],
                                                                op0=ALU.subtract, op1=ALU.mult),
                          reads=["osb", "mean", "rgn"], writes=["ynorm"])
                    A("pool", lambda g, j=j: g.tensor_tensor(out=ytok, in0=ynorm, in1=sg[:, j * 512:(j + 1) * 512], op=ALU.mult),
                      reads=["ynorm", "sg"], writes=["ytok"])

                    ck(10)

                    def try_(g):
                        for t in range(4):
                            r = g.transpose(out=Pb[4][:, t * 128:(t + 1) * 128], in_=ytok[:, t * 128:(t + 1) * 128], identity=identb[:, :])
                        return r
                    A("pe", try_, reads=["ytok", "identb"], writes=["P4"])
                    A("act", lambda g, j=j: g.activation(out=ysT.rearrange("p (t n) -> p t n", t=4)[:, :, j * 128:(j + 1) * 128],
                                                         in_=Pb[4][:, 0:512].rearrange("p (t n) -> p t n", t=4), func=AF.Copy),
                      reads=["P4"], writes=["ysT"])
                ck(5)
                dma("sp", yret_d[:, :, sc0:sc0 + 512].rearrange("t p n -> p t n"), ysT.rearrange("p (t n) -> p t n", t=4),
                    r=["ysT"], w=["yret_d"])
            tap("cqnT", cqnT, ["cqnT"])
            tap("ckvnT", ckvnT, ["ckvnT"])
            tap("kpeT", kpeT, ["kpeT"])
            tap("TABm", TABm, ["TABm"])
            tap("yret", yret_d, ["yret_d"])
            tap("hT", hT, ["hT"])
            tap("cqraw", cqraw, ["cqraw0", "cqraw1", "cqraw2"])
            tap("Rq", Rq, ["Rq"])
            tap("sq", sq, ["cqsq0", "cqsq1", "cqsq2"])
            tap("rqT", rqT, ["rqT"])
            tap("rkT", rkT, ["rkT"])
            tap("osb", osb, ["osb"])
            tap("ytok", ytok, ["ytok"])
            tap("Sf", Sf, ["Sf"])
            S.barrier()
            _phase[0] += 1
            if _phase[0] > STOP:
                raise _Stop()
            AR.off = P12

            ymlaT = AR.alloc(4 * T, BF16)
            P23 = AR.off
            Wq = AR.alloc(3 * 1024, BF16)
            Wkv = AR.alloc(2 * 1536, BF16)
            WU_OFF = ARN - (8 * DFF * 2) // 4
            Wu, _ = AR.alloc_at(WU_OFF, 8 * DFF, BF16)
            WU_PRE = 2
            dma("pool", Wq.rearrange("p (c n) -> p c n", c=3), wq_d.rearrange("(c p) n -> p c n", p=128), w=["Wq"])
            dma("pool", Wkv.rearrange("p (c n) -> p c n", c=2), wkv_d.rearrange("(c p) n -> p c n", p=128), w=["Wkv"])
            KT = [AR.alloc(T, BF16), AR.alloc(T, BF16)]
            QT = [AR.alloc(T, BF16), AR.alloc(T, BF16)]
            Vg = [AR.alloc(32 * 128, BF16), AR.alloc(32 * 128, BF16)]
            PT = [AR.alloc(GS * 512, BF16) for _ in range(NPT)]
            NSLOT = 4 // GS
            it_i = 0
            u1 = AR.alloc(512)
            u2 = AR.alloc(512)
            rec = AR.alloc(512)
            assert AR.off <= WU_OFF + (WU_PRE * DFF * 2) // 4, (AR.off, WU_OFF)
            dma("pool", Wu.rearrange("p (c n) -> p c n", c=8)[:, WU_PRE:8, :],
                wu_d.rearrange("(c p) n -> p c n", p=128)[:, WU_PRE:8, :], w=["Wu1"], nobar=True)
            for b in range(2):
                A("dve", lambda g, b=b: g.memset(KT[b][0:64, :], 0.0), writes=[f"KT{b}"])
                A("pool", lambda g, b=b: g.memset(QT[b][0:64, :], 0.0), writes=[f"QT{b}"])
                A("dve", lambda g, b=b: g.memset(Vg[b].rearrange("p (k v) -> p k v", v=128)[:, :, 64:128], 1.0), writes=[f"Vg{b}"])
            SCALE = float(96 ** -0.5)
            pt_i = 0
            for h in range(8):
                hb = h % 2
                kk, qk, vk = f"KT{hb}", f"QT{hb}", f"Vg{hb}"
                A("dve", lambda g, hb=hb: g.tensor_copy(out=KT[hb][0:32, :], in_=kpeT[:, :]), reads=["kpeT"], writes=[kk])
                for sbi in range(NSB):
                    sc0 = sbi * 512

                    def mmk(g, h=h, sc0=sc0):
                        for c in range(2):
                            r = g.matmul(P[6][:, :], lhsT=Wkv[:, c * 1536 + h * 128:c * 1536 + (h + 1) * 128], rhs=ckvnT[:, c * T + sc0:c * T + sc0 + 512],
                                         start=(c == 0), stop=(c == 1))
                        return r
                    A("pe", mmk, reads=["Wkv", "ckvnT"], writes=["P6"])
                    A("dve", lambda g, hb=hb, sc0=sc0: g.tensor_copy(out=KT[hb][64:128, sc0:sc0 + 512], in_=P[6][64:128, :]),
                      reads=["P6"], writes=[kk])

                    def mmq(g, h=h, sc0=sc0):
                        for c in range(3):
                            r = g.matmul(P[7][:, :], lhsT=Wq[:, c * 1024 + h * 128:c * 1024 + (h + 1) * 128], rhs=cqnT[:, c * T + sc0:c * T + sc0 + 512],
                                         start=(c == 0), stop=(c == 2))
                        return r
                    A("pe", mmq, reads=["Wq", "cqnT"], writes=["P7"])
                    A("dve", lambda g, sc0=sc0: g.tensor_tensor(out=u1[0:32, :], in0=P[7][0:32, :], in1=TABm[0:32, sc0:sc0 + 512], op=ALU.mult),
                      reads=["P7", "TABm"], writes=["u1"])
                    A("dve", lambda g, sc0=sc0: g.tensor_tensor(out=u2[0:32, :], in0=P[7][32:64, :], in1=TABm[32:64, sc0:sc0 + 512], op=ALU.mult),
                      reads=["P7", "TABm"], writes=["u2"])
                    A("pool", lambda g, hb=hb, sc0=sc0: g.tensor_tensor(out=QT[hb][0:32, sc0:sc0 + 512], in0=u1[0:32, :], in1=u2[0:32, :], op=ALU.add),
                      reads=["u1", "u2"], writes=[qk])
                    A("dve", lambda g, hb=hb, sc0=sc0: g.tensor_copy(out=QT[hb][64:128, sc0:sc0 + 512], in_=P[7][64:128, :]),
                      reads=["P7"], writes=[qk])
                for k8 in range(4):
                    def mmv(g, h=h, k8=k8):
                        for q in range(8):
                            kb = k8 * 8 + q
                            for c in range(2):
                                r = g.matmul(P[6][:, q * 64:(q + 1) * 64], lhsT=ckvnT[:, c * T + kb * 128:c * T + (kb + 1) * 128],
                                             rhs=Wkv[:, c * 1536 + 1024 + h * 64:c * 1536 + 1024 + (h + 1) * 64], start=(c == 0), stop=(c == 1))
                        return r
                    A("pe", mmv, reads=["Wkv", "ckvnT"], writes=["P6"])
                    A("dve", lambda g, hb=hb, k8=k8: g.tensor_copy(
                        out=Vg[hb].rearrange("p (k v) -> p k v", v=128)[:, k8 * 8:(k8 + 1) * 8, 0:64],
                        in_=P[6][:, :].rearrange("p (k v) -> p k v", v=64)), reads=["P6"], writes=[vk])
                for qi, qs in enumerate(QS_ORDER):
                    q0 = qs * 512
                    acc = 4 + qi % 2
                    ak = f"P{acc}"
                    nfull = 4 * qs
                    groups = [(kb, min(kb + GS, nfull)) for kb in range(0, nfull, GS)]
                    items = [("full", a, b) for a, b in groups] + [("diag", 4 * qs + d, d) for d in range(4)]
                    last_kb = 4 * qs + 3
                    for gi, it in enumerate(items):
                        slot = it_i % NSLOT
                        it_i += 1
                        sbank = GS * slot
                        sk = f"PS{slot}"
                        pt = PT[pt_i % NPT]
                        ptk = f"PT{pt_i % NPT}"
                        pt_i += 1
                        if it[0] == "full":
                            kbs = list(range(it[1], it[2]))

                            def mms(g, hb=hb, kbs=kbs, sbank=sbank, q0=q0):
                                for n_, kb in enumerate(kbs):
                                    r = g.matmul(P[sbank + n_][:, :], lhsT=KT[hb][:, kb * 128:(kb + 1) * 128], rhs=QT[hb][:, q0:q0 + 512],
                                                 start=True, stop=True)
                                return r
                            A("pe", mms, reads=[kk, qk], writes=[sk])
                            for n_ in range(len(kbs)):
                                A("act", lambda g, pt=pt, sbank=sbank, n_=n_: g.activation(out=pt[:, n_ * 512:(n_ + 1) * 512], in_=P[sbank + n_][:, :],
                                                                                            func=AF.Exp, scale=SCALE),
                                  reads=[sk], writes=[ptk])

                            def mmpv(g, hb=hb, kbs=kbs, pt=pt, acc=acc, last_kb=last_kb):
                                for n_, kb in enumerate(kbs):
                                    r = g.matmul(P[acc][:, :], lhsT=Vg[hb][:, kb * 128:(kb + 1) * 128], rhs=pt[:, n_ * 512:(n_ + 1) * 512],
                                                 start=(kb == 0), stop=(kb == last_kb))
                                return r
                            A("pe", mmpv, reads=[vk, ptk], writes=[ak])
                        else:
                            kb, d = it[1], it[2]
                            c0 = d * 128
                            A("pe", lambda g, hb=hb, kb=kb, c0=c0, sbank=sbank, q0=q0: g.matmul(
                                P[sbank][:, c0:512], lhsT=KT[hb][:, kb * 128:(kb + 1) * 128], rhs=QT[hb][:, q0 + c0:q0 + 512], start=True, stop=True),
                              reads=[kk, qk], writes=[sk])
                            A("act", lambda g, pt=pt, sbank=sbank, c0=c0: g.activation(out=pt[:, c0:512], in_=P[sbank][:, c0:512], func=AF.Exp, scale=SCALE),
                              reads=[sk], writes=[ptk])
                            A("pool", lambda g, pt=pt, c0=c0: g.tensor_tensor(out=pt[:, c0:c0 + 128], in0=pt[:, c0:c0 + 128], in1=trib[:, :], op=ALU.mult),
                              reads=[ptk, "trib"], writes=[ptk])
                            A("pe", lambda g, hb=hb, kb=kb, c0=c0, pt=pt, acc=acc, last_kb=last_kb: g.matmul(
                                P[acc][:, c0:512], lhsT=Vg[hb][:, kb * 128:(kb + 1) * 128], rhs=pt[:, c0:512], start=(kb == 0), stop=(kb == last_kb)),
                              reads=[vk, ptk], writes=[ak])
                    A("dve", lambda g, acc=acc: g.reciprocal(out=rec[0:64, :], in_=P[acc][64:128, :]), reads=[ak], writes=["rec"])
                    r0 = 64 * (h % 2)
                    A("dve", lambda g, acc=acc, r0=r0, h=h, q0=q0: g.tensor_tensor(
                        out=ymlaT[r0:r0 + 64, (h // 2) * T + q0:(h // 2) * T + q0 + 512], in0=P[acc][0:64, :], in1=rec[0:64, :], op=ALU.mult),
                      reads=[ak, "rec"], writes=["ymlaT"])
            S.barrier()
            _phase[0] += 1
            if _phase[0] > STOP:
                raise _Stop()

            AR.off = P23
            Wg, wg_end = AR.alloc_at(0, 8 * DFF, BF16)
            assert wg_end <= P12
            dma("pool", Wu.rearrange("p (c n) -> p c n", c=8)[:, 0:WU_PRE, :],
                wu_d.rearrange("(c p) n -> p c n", p=128)[:, 0:WU_PRE, :], w=["Wu0"], nobar=True)
            for hf in range(2):
                dma("pool", Wg.rearrange("p (c n) -> p c n", c=8)[:, hf * 4:(hf + 1) * 4, :],
                    wg_d.rearrange("(c p) n -> p c n", p=128)[:, hf * 4:(hf + 1) * 4, :], w=[f"Wg{hf}"], nobar=True)
            Wo = AR.alloc(8 * D, BF16)
            og = AR.alloc(8)
            yrTs = [AR.alloc(4 * 512, BF16), AR.alloc(4 * 512, BF16)]
            ysq = [AR.alloc(4 * 128, BF16), AR.alloc(4 * 128, BF16)]
            xt3 = [AR.alloc(D), AR.alloc(D)]
            x1 = [AR.alloc(D), AR.alloc(D)]
            mB = [AR.alloc(D)]
            mixs = [AR.alloc(D)]
            tt = [AR.alloc(D)]
            rm = [AR.alloc(1), AR.alloc(1)]
            r2 = [AR.alloc(1), AR.alloc(1)]
            hole = wg_end
            for lst in (mB, mixs, tt):
                v_, hole = AR.alloc_at(hole, D)
                lst.append(v_)
            assert hole <= P12
            dma("sp", og, og_d, w=["og"])
            for hf in range(2):
                dma("pool", Wo.rearrange("p (c n) -> p c n", c=8)[:, hf * 4:(hf + 1) * 4, :],
                    wout_d.rearrange("(c p) n -> p c n", p=128)[:, hf * 4:(hf + 1) * 4, :], w=[f"Wo{hf}"])
            for c in range(8):
                A("dve", lambda g, c=c: g.tensor_scalar(out=Wo[:, c * D:(c + 1) * D], in0=Wo[:, c * D:(c + 1) * D], scalar1=og[:, c:c + 1],
                                                         scalar2=None, op0=ALU.mult), reads=[f"Wo{c // 4}", "og"], writes=[f"Wo{c // 4}"])
            for tb in range(32):
                b2 = tb % 2
                tc0 = tb * 128
                sbi, j = tb // 4, tb % 4
                yb = yrTs[sbi % 2]
                ybk = f"yrT{sbi % 2}"
                pa = (0, 1) if b2 == 0 else (4, 5)
                pak = [f"P{pa[0]}", f"P{pa[1]}"]
                stb = 6 + b2
                if j == 0:
                    dma("sp", yb.rearrange("p (t n) -> p t n", t=4), yret_d[:, :, sbi * 512:(sbi + 1) * 512].rearrange("t p n -> p t n"),
                        r=["yret_d"], w=[ybk])
                dma("sp", xt3[b2], x_d[tc0:tc0 + 128, :], w=[f"xt3{b2}"])
                A("pool", lambda g, tc0=tc0, b2=b2: g.tensor_tensor(out=ysq[b2].rearrange("p (c n) -> p c n", c=4),
                                                                     in0=ymlaT.rearrange("p (c n) -> p c n", c=4)[:, :, tc0:tc0 + 128],
                                                                     in1=ymlaT.rearrange("p (c n) -> p c n", c=4)[:, :, tc0:tc0 + 128], op=ALU.mult),
                  reads=["ymlaT"], writes=[f"ysq{b2}"])

                def mmss(g, b2=b2, stb=stb):
                    for c in range(4):
                        r = g.matmul(P[stb][:, 0:1], lhsT=ysq[b2][:, c * 128:(c + 1) * 128], rhs=onesb[:, 0:1], start=(c == 0), stop=(c == 3))
                    return r
                A("pe", mmss, reads=[f"ysq{b2}", "onesb"], writes=[f"P{stb}"])
                rsqrt_ops(rm[b2], P[stb][:, 0:1], 1.0 / 512, [f"P{stb}"], f"rm{b2}")

                def mmA(g, tc0=tc0, pa=pa):
                    for hf in range(2):
                        for c in range(4):
                            r = g.matmul(P[pa[hf]][:, :], lhsT=ymlaT[:, c * T + tc0:c * T + tc0 + 128], rhs=Wo[:, c * D + hf * 512:c * D + (hf + 1) * 512],
                                         start=(c == 0), stop=(c == 3))
                    return r
                A("pe", mmA, reads=["ymlaT", "Wo0"], writes=pak)

                def mmB(g, yb=yb, j=j):
                    for hf in range(2):
                        for c in range(4):
                            r = g.matmul(P[2 + hf][:, :], lhsT=yb[:, c * 512 + j * 128:c * 512 + (j + 1) * 128],
                                         rhs=Wo[:, (4 + c) * D + hf * 512:(4 + c) * D + (hf + 1) * 512], start=(c == 0), stop=(c == 3))
                    return r
                A("pe", mmB, reads=[ybk, "Wo1"], writes=["PB"])
                for hf in range(2):
                    A("act", lambda g, hf=hf, b2=b2: g.activation(out=mB[b2][:, hf * 512:(hf + 1) * 512], in_=P[2 + hf][:, :], func=AF.Copy),
                      reads=["PB"], writes=[f"mB{b2}"])
                    A("dve", lambda g, hf=hf, b2=b2, pa=pa: g.scalar_tensor_tensor(out=mixs[b2][:, hf * 512:(hf + 1) * 512], in0=P[pa[hf]][:, :],
                                                                                scalar=rm[b2][:, 0:1], in1=mB[b2][:, hf * 512:(hf + 1) * 512],
                                                                                op0=ALU.mult, op1=ALU.add),
                      reads=[pak[hf], f"rm{b2}", f"mB{b2}"], writes=[f"mixs{b2}"])
                A("act", lambda g, b2=b2: g.activation(out=tt[b2].bitcast(BF16)[:, 0:D], in_=mixs[b2], func=AF.Square, accum_out=r2[b2]),
                  reads=[f"mixs{b2}"], writes=[f"tt{b2}", f"r2{b2}"])
                rsqrt_ops(r2[b2], r2[b2], 1.0 / D, [f"r2{b2}"], f"r2{b2}")
                A("dve", lambda g, b2=b2: g.scalar_tensor_tensor(out=tt[b2], in0=mixs[b2], scalar=r2[b2][:, 0:1], in1=G1b[:, :], op0=ALU.mult, op1=ALU.mult),
                  reads=[f"mixs{b2}", f"r2{b2}", "G1b"], writes=[f"tt{b2}"])
                A("pool", lambda g, b2=b2: g.tensor_tensor(out=x1[b2], in0=xt3[b2], in1=tt[b2], op=ALU.add), reads=[f"xt3{b2}", f"tt{b2}"], writes=[f"x1{b2}"])
                dma("sp", out_d[tc0:tc0 + 128, :], x1[b2], r=[f"x1{b2}"], w=[f"out{tb}"])
            assert AR.off <= WU_OFF, (AR.off, WU_OFF)
            S.barrier()
            _phase[0] += 1
            if _phase[0] > STOP:
                raise _Stop()

            Wd, wd_end = AR.alloc_at(wg_end, NJ * D, BF16)
            assert wd_end <= P23
            AR.off = P23
            xa = [AR.alloc(D), AR.alloc(D)]
            xb = [AR.alloc(D), AR.alloc(D)]
            xn4 = [AR.alloc(D, BF16), AR.alloc(D, BF16)]
            junk4 = AR.alloc(D, BF16)
            junk5 = junk4
            s4 = [AR.alloc(1), AR.alloc(1)]
            h2T = AR.alloc(8 * 512, BF16)
            h1T = AR.alloc(NJ * 512, BF16)
            sgt = [AR.alloc(512, BF16), AR.alloc(512, BF16)]
            t4 = [AR.alloc(D), AR.alloc(D)]
            r3 = [AR.alloc(1), AR.alloc(1)]
            assert AR.off <= WU_OFF, (AR.off, WU_OFF)
            for hf in range(2):
                dma("pool", Wd.rearrange("p (c n) -> p c n", c=NJ)[:, hf * 11:(hf + 1) * 11, :],
                    wd_d.rearrange("(c p) n -> p c n", p=128)[:, hf * 11:(hf + 1) * 11, :], w=[f"Wd{hf}"])
            fin = []
            for sbi in range(NSB):
                for j in range(4):
                    tb = sbi * 4 + j
                    b2 = tb % 2
                    xv = xa[b2]
                    dma("sp", xv, out_d[tb * 128:(tb + 1) * 128, :], r=[f"out{tb}"], w=[f"xa{b2}"])
                    A("act", lambda g, xv=xv, b2=b2: g.activation(out=xn4[b2], in_=xv, func=AF.Square, accum_out=s4[b2]),
                      reads=[f"xa{b2}"], writes=[f"xn4{b2}", f"s4{b2}"])
                    rsqrt_ops(s4[b2], s4[b2], 1.0 / D, [f"s4{b2}"], f"s4{b2}")
                    A("act", lambda g, xv=xv, b2=b2: g.activation(out=xn4[b2], in_=xv, func=AF.Copy, scale=s4[b2]),
                      reads=[f"xa{b2}", f"s4{b2}"], writes=[f"xn4{b2}"])

                    def tr8b(g, b2=b2):
                        for c in range(8):
                            r = g.transpose(out=Pb[b2][:, c * 128:(c + 1) * 128], in_=xn4[b2][:, c * 128:(c + 1) * 128], identity=identb[:, :])
                        return r
                    A("pe", tr8b, reads=[f"xn4{b2}", "identb"], writes=[f"P{b2}"])
                    for c in range(8):
                        dst = h2T[:, c * 512 + j * 128:c * 512 + (j + 1) * 128]
                        if b2 == 0:
                            A("dve", lambda g, c=c, dst=dst, b2=b2: g.tensor_scalar(out=dst, in0=Pb[b2][:, c * 128:(c + 1) * 128],
                                                                                   scalar1=a2[:, c:c + 1], scalar2=sh2[:, c:c + 1],
                                                                                   op0=ALU.mult, op1=ALU.add),
                              reads=[f"P{b2}", "a2", "modc"], writes=["h2T"])
                        else:
                            A("act", lambda g, c=c, dst=dst, b2=b2: g.activation(out=dst, in_=Pb[b2][:, c * 128:(c + 1) * 128], func=AF.Identity,
                                                                                scale=a2[:, c:c + 1], bias=sh2[:, c:c + 1]),
                              reads=[f"P{b2}", "a2", "modc"], writes=["h2T"])
                for jj in range(NJ):
                    gb = 2 + jj % 2
                    ub = 4 + jj % 2

                    def mmg(g, jj=jj, gb=gb, ub=ub):
                        for c in range(8):
                            g.matmul(P[gb][:, :], lhsT=Wg[:, c * DFF + jj * 128:c * DFF + (jj + 1) * 128], rhs=h2T[:, c * 512:(c + 1) * 512],
                                     start=(c == 0), stop=(c == 7))
                        for c in range(8):
                            r = g.matmul(P[ub][:, :], lhsT=Wu[:, c * DFF + jj * 128:c * DFF + (jj + 1) * 128], rhs=h2T[:, c * 512:(c + 1) * 512],
                                         start=(c == 0), stop=(c == 7))
                        return r
                    A("pe", mmg, reads=["Wg0", "Wg1", "Wu0", "Wu1", "h2T"], writes=[f"P{gb}", f"P{ub}"])
                    A("act", lambda g, jj=jj, gb=gb: g.activation(out=sgt[jj % 2], in_=P[gb][:, :], func=AF.Silu), reads=[f"P{gb}"], writes=[f"sgt{jj % 2}"])
                    A("dve", lambda g, jj=jj, ub=ub: g.tensor_tensor(out=h1T[:, jj * 512:(jj + 1) * 512], in0=P[ub][:, :], in1=sgt[jj % 2], op=ALU.mult),
                      reads=[f"P{ub}", f"sgt{jj % 2}"], writes=["h1T"])
                for j in range(4):
                    tb = sbi * 4 + j
                    b2 = tb % 2
                    xv = xb[b2]
                    tv = t4[b2]
                    dma("sp", xv, out_d[tb * 128:(tb + 1) * 128, :], r=[f"out{tb}"], w=[f"xb{b2}"])

                    fb = (6, 7) if j % 2 == 0 else (3, 5)
                    fk_ = [f"P{fb[0]}", f"P{fb[1]}"]

                    def mmd(g, j=j, fb=fb):
                        for hf in range(2):
                            for jj in range(NJ):
                                r = g.matmul(P[fb[hf]][:, :], lhsT=h1T[:, jj * 512 + j * 128:jj * 512 + (j + 1) * 128],
                                             rhs=Wd[:, jj * D + hf * 512:jj * D + (hf + 1) * 512], start=(jj == 0), stop=(jj == NJ - 1))
                        return r
                    A("pe", mmd, reads=["h1T", "Wd0", "Wd1"], writes=fk_)
                    A("act", lambda g, tv=tv, fb=fb: g.activation(out=tv[:, 0:512], in_=P[fb[0]][:, :], func=AF.Copy), reads=[fk_[0]], writes=[f"t4{b2}"])
                    A("dve", lambda g, tv=tv, fb=fb: g.tensor_copy(out=tv[:, 512:1024], in_=P[fb[1]][:, :]), reads=[fk_[1]], writes=[f"t4{b2}"])
                    A("act", lambda g, tv=tv, b2=b2: g.activation(out=junk5, in_=tv, func=AF.Square, accum_out=r3[b2]), reads=[f"t4{b2}"], writes=["junk4", f"r3{b2}"])
                    rsqrt_ops(r3[b2], r3[b2], 1.0 / D, [f"r3{b2}"], f"r3{b2}")
                    A("dve", lambda g, tv=tv, b2=b2: g.scalar_tensor_tensor(out=tv, in0=tv, scalar=r3[b2][:, 0:1], in1=G2b[:, :], op0=ALU.mult, op1=ALU.mult),
                      reads=[f"t4{b2}", f"r3{b2}", "G2b"], writes=[f"t4{b2}"])
                    A("pool", lambda g, xv=xv, tv=tv: g.tensor_tensor(out=xv, in0=xv, in1=tv, op=ALU.add), reads=[f"xb{b2}", f"t4{b2}"], writes=[f"xb{b2}"])
                    fin.append(dma("sp", out_d[tb * 128:(tb + 1) * 128, :], xv, r=[f"xb{b2}"], w=[f"out{tb}"]))
            A("sp", lambda g: None, deps=fin)

        except _Stop:
            pass
        with nc.Block() as block:
            S.emit_all(block, esem, dsem)
    return nc


def _consts():
    f = np.float32
    gam = 1.0 - 2.0 ** (-5.0 - np.arange(4, dtype=np.float64))
    idx = np.arange(128)
    ident = np.eye(128, dtype=f)
    tri = (idx[None, :] >= idx[:, None]).astype(f)
    dtc = np.zeros((128, 4, 128), np.float64)
    rel = idx[None, :] - idx[:, None]
    for h in range(4):
        dtc[:, h, :] = np.where(rel >= 0, gam[h] ** np.maximum(rel, 0), 0.0) * 0.125
    wqc = np.zeros((128, 2, 512), np.float64)
    wkc = np.zeros((128, 2, 128), np.float64)
    decc = np.zeros((128, 2), np.float64)
    for i in range(2):
        for r in range(128):
            h = 2 * i + r // 64
            wqc[r, i, :] = np.tile(gam[h] ** (idx + 1.0), 4)
            decc[r, i] = gam[h] ** 128
        for ft in range(128):
            h = 2 * i + ft // 64
            wkc[:, i, ft] = gam[h] ** (127.0 - idx) * 0.125
    inv_m = 10000.0 ** (-np.arange(16, dtype=np.float64) / 16.0)
    inv_r = 10000.0 ** (-np.arange(32, dtype=np.float64) / 32.0)
    invc = np.zeros((128, 3), np.float64)
    phc = np.zeros((128, 3), np.float64)
    for r in range(64):
        invc[r, 0] = inv_m[r % 16]
        phc[r, 0] = np.pi / 2 if r < 32 else (np.pi if r < 48 else 0.0)
    for r in range(128):
        invc[r, 1] = inv_r[r % 32]
        invc[r, 2] = inv_r[r % 32]
        phc[r, 1] = np.pi / 2
        phc[r, 2] = np.pi if (r % 64) < 32 else 0.0
    return dict(ident=ident, tri=tri, dtc=dtc.reshape(128, 512).astype(f), wqc=wqc.reshape(128, 1024).astype(f),
                wkc=wkc.reshape(128, 256).astype(f), decc=decc.astype(f), invc=invc.astype(f), phc=phc.astype(f))


def _colmajor(v, n):
    return np.ascontiguousarray(np.asarray(v, np.float32).reshape(n, 128).T)


def _prep_shared(inp):
    f = np.float32
    w_in = np.asarray(inp["w_in"], f)[0]
    cols = list(range(0, 640))
    cols += list(range(640, 672)) + [640 + k for k in list(range(16, 32)) + list(range(0, 16))]
    for base in (672, 928):
        for i in range(2):
            nat, sw = [], []
            for hh in (2 * i, 2 * i + 1):
                b = base + hh * 64
                nat += list(range(b, b + 64))
                sw += list(range(b + 32, b + 64)) + list(range(b, b + 32))
            cols += nat + sw
    cols += list(range(1184, 2208))
    w1 = np.ascontiguousarray(w_in[:, cols])
    assert w1.shape[1] == NC1
    wqb = np.asarray(inp["w_q_b"], f)[0]
    qc = []
    for h in range(8):
        b = h * 96
        qc += list(range(b + 64, b + 96)) + [b + 64 + k for k in list(range(16, 32)) + list(range(0, 16))] + list(range(b, b + 64))
    wq = np.ascontiguousarray(wqb[:, qc])
    wkvb = np.asarray(inp["w_kv_b"], f)[0]
    wkv = np.zeros((256, 1536), f)
    for h in range(8):
        wkv[:, h * 128 + 64:h * 128 + 128] = wkvb[:, h * 128:h * 128 + 64]
        wkv[:, 1024 + h * 64:1024 + (h + 1) * 64] = wkvb[:, h * 128 + 64:h * 128 + 128]
    sh = dict(
        w_ada=np.ascontiguousarray(np.asarray(inp["w_ada"], f)[0]),
        b_ada=np.ascontiguousarray(np.asarray(inp["b_ada"], f)[0][None, :]),
        gpre1=_colmajor(inp["pre_norm_mix"][0], 8), gpre2=_colmajor(inp["pre_norm_ffn"][0], 8),
        gpost1=np.ascontiguousarray(np.asarray(inp["post_norm_mix"], f)[0][None, :]),
        gpost2=np.ascontiguousarray(np.asarray(inp["post_norm_ffn"], f)[0][None, :]),
        qg=_colmajor(inp["q_a_norm"][0], 3), kvg=_colmajor(inp["kv_a_norm"][0], 2),
        og=_colmajor(np.concatenate([np.asarray(inp["mla_out_norm"], f)[0], np.asarray(inp["ret_gn_gain"], f)[0]]), 8),
        w1=w1, wq=wq, wkv=wkv,
        wout=np.ascontiguousarray(np.asarray(inp["w_out"], f)[0]),
        wg=np.ascontiguousarray(np.asarray(inp["w_gate"], f)[0]),
        wu=np.ascontiguousarray(np.asarray(inp["w_up"], f)[0]),
        wd=np.ascontiguousarray(np.asarray(inp["w_down"], f)[0]),
    )
    sh.update(_consts())
    return sh


def make_in_maps(inp, cores):
    sh = _prep_shared(inp)
    x = np.asarray(inp["x"], np.float32)
    c = np.asarray(inp["c"], np.float32)
    pos = np.asarray(inp["positions"], np.int32)
    maps = []
    for b in cores:
        m = dict(sh)
        m["x"] = np.ascontiguousarray(x[b])
        m["cT"] = _colmajor(c[b], 8)
        m["pos"] = np.ascontiguousarray(pos[b][None, :])
        maps.append(m)
    return maps


_NC = None


def kernel(**inputs):
    global _NC
    if _NC is None:
        _NC = build_nc()
    maps = make_in_maps(inputs, list(range(8)))
    res = run_bass_kernel_spmd(_NC, maps, core_ids=list(range(8)))
    return np.stack([np.asarray(r["out"], np.float32) for r in res.results], axis=0)
```
